# Optimizing a Trainium2 kernel written in Bass

```python
import math
import jax, jax.numpy as jnp
from jax import lax
import numpy as np

D_MODEL = 1024
BATCH = 2
SEQ = 8192
DEPTH = 4

POOL_WIDTH = D_MODEL // 4
FOURIER_WIDTH = D_MODEL // 4
ATTN_WIDTH = D_MODEL // 2
POOL_WINDOWS = (2, 4, 8, 16)
N_POOL_GROUPS = len(POOL_WINDOWS)
POOL_GROUP_DIM = POOL_WIDTH // N_POOL_GROUPS
N_FOURIER_GROUPS = 4
FOURIER_GROUP_DIM = FOURIER_WIDTH // N_FOURIER_GROUPS
ATTN_HEADS = 4
ATTN_HEAD_DIM = ATTN_WIDTH // (2 * ATTN_HEADS)
ATTN_V_DIM = 2 * ATTN_HEAD_DIM
Q_BLOCK = 128
D_FF = 2816
IN_COLS = POOL_WIDTH + FOURIER_WIDTH + 3 * ATTN_WIDTH
NORM_EPS = 1e-6

kernel_name = "hybrid_pool_fourier_diffattn_macaron_encoder"


def alibi_slopes(n_heads):
    return jnp.array([2.0 ** (-8.0 * (i + 1) / n_heads) for i in range(n_heads)], dtype=jnp.float32)


def lambda_init_fn(layer_idx):
    return 0.8 - 0.6 * math.exp(-0.3 * layer_idx)


def rmsnorm(x, g):
    xf = x.astype(jnp.float32)
    y = xf * lax.rsqrt(jnp.mean(xf * xf, axis=-1, keepdims=True) + NORM_EPS)
    return (y * g.astype(jnp.float32)).astype(x.dtype)


def swiglu(x, w_gate, w_up, w_down):
    return (jax.nn.silu(x @ w_gate) * (x @ w_up)) @ w_down


def pool_mixer(a, pool_w, pool_scale):
    B, S, _ = a.shape
    af = a.reshape(B, S, N_POOL_GROUPS, POOL_GROUP_DIM).astype(jnp.float32)
    prefix = jnp.concatenate(
        [jnp.zeros((B, 1, N_POOL_GROUPS, POOL_GROUP_DIM), jnp.float32), jnp.cumsum(af, axis=1)], axis=1)
    t = jnp.arange(S)
    outs = []
    for g, w in enumerate(POOL_WINDOWS):
        left = w // 2
        right = w - 1 - left
        hi = jnp.minimum(t + right + 1, S)
        lo = jnp.maximum(t - left, 0)
        pg = prefix[:, :, g]
        win_sum = jnp.take(pg, hi, axis=1) - jnp.take(pg, lo, axis=1)
        cnt = (hi - lo).astype(jnp.float32)[None, :, None]
        outs.append(win_sum / cnt - af[:, :, g])
    m = jnp.stack(outs, axis=2).astype(a.dtype)
    y = jnp.einsum('bsgc,gcd->bsgd', m, pool_w).reshape(B, S, POOL_WIDTH)
    return y * pool_scale


def fourier_mixer(f, fourier_w):
    B, S, _ = f.shape
    f4 = f.reshape(B, S, N_FOURIER_GROUPS, FOURIER_GROUP_DIM).astype(jnp.float32)
    y = jnp.real(jnp.fft.fft2(f4, axes=(1, 3), norm='ortho'))
    y = y.astype(f.dtype).reshape(B, S, FOURIER_WIDTH)
    return y @ fourier_w


def diff_attention(q, k, v, lam, lam_init, head_norm):
    B, S = q.shape[0], q.shape[1]
    n_blk = S // Q_BLOCK
    scale = ATTN_HEAD_DIM ** -0.5
    slopes = alibi_slopes(ATTN_HEADS)
    kpos = jnp.arange(S, dtype=jnp.float32)
    qb = q.reshape(B, n_blk, Q_BLOCK, ATTN_HEADS, 2, ATTN_HEAD_DIM).transpose(1, 0, 2, 3, 4, 5)

    def one_block(args):
        q_blk, i = args
        qpos = (i * Q_BLOCK + jnp.arange(Q_BLOCK)).astype(jnp.float32)
        bias = -slopes[:, None, None] * jnp.abs(qpos[:, None] - kpos[None, :])
        s = jnp.einsum('bqhjd,bkhjd->bhjqk', q_blk, k).astype(jnp.float32) * scale + bias[None, :, None]
        p = jax.nn.softmax(s, axis=-1)
        attn = p[:, :, 0] - lam.astype(jnp.float32) * p[:, :, 1]
        return jnp.einsum('bhqk,bkhe->bqhe', attn.astype(v.dtype), v)

    o = lax.map(one_block, (qb, jnp.arange(n_blk)))
    o = o.transpose(1, 0, 2, 3, 4).reshape(B, S, ATTN_HEADS, ATTN_V_DIM)
    o = rmsnorm(o, head_norm.reshape(ATTN_HEADS, ATTN_V_DIM)) * (1.0 - lam_init)
    return o.reshape(B, S, ATTN_WIDTH)


def setup_inputs(seed: int = 0) -> dict:
    key = jax.random.key(seed)
    ks = jax.random.split(key, 24)
    f32 = jnp.float32

    def nrm(k, shape, fan_in):
        return jax.random.normal(k, shape, f32) * (fan_in ** -0.5)

    def gain(k, shape):
        return 1.0 + 0.05 * jax.random.normal(k, shape, f32)

    return {
        "x": jax.random.normal(ks[0], (BATCH, SEQ, D_MODEL), f32),
        "ffn1_norm": gain(ks[1], (DEPTH, D_MODEL)),
        "ffn1_w_gate": nrm(ks[2], (DEPTH, D_MODEL, D_FF), D_MODEL),
        "ffn1_w_up": nrm(ks[3], (DEPTH, D_MODEL, D_FF), D_MODEL),
        "ffn1_w_down": nrm(ks[4], (DEPTH, D_FF, D_MODEL), D_FF),
        "mix_norm": gain(ks[5], (DEPTH, D_MODEL)),
        "w_in": nrm(ks[6], (DEPTH, D_MODEL, IN_COLS), D_MODEL),
        "pool_w": nrm(ks[7], (DEPTH, N_POOL_GROUPS, POOL_GROUP_DIM, POOL_GROUP_DIM), POOL_GROUP_DIM),
        "pool_scale": gain(ks[8], (DEPTH, POOL_WIDTH)),
        "fourier_w": nrm(ks[9], (DEPTH, FOURIER_WIDTH, FOURIER_WIDTH), FOURIER_WIDTH),
        "lam_q1": 0.1 * jax.random.normal(ks[10], (DEPTH, ATTN_HEAD_DIM), f32),
        "lam_k1": 0.1 * jax.random.normal(ks[11], (DEPTH, ATTN_HEAD_DIM), f32),
        "lam_q2": 0.1 * jax.random.normal(ks[12], (DEPTH, ATTN_HEAD_DIM), f32),
        "lam_k2": 0.1 * jax.random.normal(ks[13], (DEPTH, ATTN_HEAD_DIM), f32),
        "attn_head_norm": gain(ks[14], (DEPTH, ATTN_WIDTH)),
        "w_out": nrm(ks[15], (DEPTH, D_MODEL, D_MODEL), D_MODEL),
        "ffn2_norm": gain(ks[16], (DEPTH, D_MODEL)),
        "ffn2_w_gate": nrm(ks[17], (DEPTH, D_MODEL, D_FF), D_MODEL),
        "ffn2_w_up": nrm(ks[18], (DEPTH, D_MODEL, D_FF), D_MODEL),
        "ffn2_w_down": nrm(ks[19], (DEPTH, D_FF, D_MODEL), D_FF),
        "final_norm": gain(ks[20], (D_MODEL,)),
    }


def reference(x, ffn1_norm, ffn1_w_gate, ffn1_w_up, ffn1_w_down, mix_norm, w_in, pool_w, pool_scale,
              fourier_w, lam_q1, lam_k1, lam_q2, lam_k2, attn_head_norm, w_out,
              ffn2_norm, ffn2_w_gate, ffn2_w_up, ffn2_w_down, final_norm):
    B, S, _ = x.shape
    c0 = POOL_WIDTH
    c1 = c0 + FOURIER_WIDTH
    c2 = c1 + ATTN_WIDTH
    c3 = c2 + ATTN_WIDTH
    for l in range(DEPTH):
        h = rmsnorm(x, ffn1_norm[l])
        x = x + 0.5 * swiglu(h, ffn1_w_gate[l], ffn1_w_up[l], ffn1_w_down[l])

        h = rmsnorm(x, mix_norm[l])
        p = h @ w_in[l]
        a = p[..., :c0]
        f = p[..., c0:c1]
        q = p[..., c1:c2].reshape(B, S, ATTN_HEADS, 2, ATTN_HEAD_DIM)
        k = p[..., c2:c3].reshape(B, S, ATTN_HEADS, 2, ATTN_HEAD_DIM)
        v = p[..., c3:].reshape(B, S, ATTN_HEADS, ATTN_V_DIM)

        lam_init = lambda_init_fn(l)
        lam = (jnp.exp(jnp.sum(lam_q1[l].astype(jnp.float32) * lam_k1[l].astype(jnp.float32)))
               - jnp.exp(jnp.sum(lam_q2[l].astype(jnp.float32) * lam_k2[l].astype(jnp.float32)))
               + lam_init)

        y_pool = pool_mixer(a, pool_w[l], pool_scale[l])
        y_four = fourier_mixer(f, fourier_w[l])
        y_attn = diff_attention(q, k, v, lam, lam_init, attn_head_norm[l]).astype(x.dtype)
        y = jnp.concatenate([y_pool, y_four, y_attn], axis=-1) @ w_out[l]
        x = x + y

        h = rmsnorm(x, ffn2_norm[l])
        x = x + 0.5 * swiglu(h, ffn2_w_gate[l], ffn2_w_up[l], ffn2_w_down[l])
    return rmsnorm(x, final_norm)
```

```python
import math
from contextlib import ExitStack

import numpy as np
import ml_dtypes

import concourse.bass as bass
import concourse.mybir as mybir
from concourse.bass_utils import run_bass_kernel_spmd

F32 = mybir.dt.float32
BF16 = mybir.dt.bfloat16
AF = mybir.ActivationFunctionType
ALU = mybir.AluOpType

D_MODEL = 1024
BATCH = 2
SEQ = 8192
DEPTH = 4
D_FF = 2816
NCORES = 8
TOK = 2048
NTC = 4
KC = 8
EPS = 1e-6
HEADS = 4
SLOPES = [2.0 ** (-8.0 * (i + 1) / HEADS) for i in range(HEADS)]
POOL_W = (2, 4, 8, 16)
BAND = [4, 16, None, None]


def lambda_init_fn(layer_idx):
    return 0.8 - 0.6 * math.exp(-0.3 * layer_idx)


class Res:
    __slots__ = ("name", "last_w", "readers")

    def __init__(self, name):
        self.name = name
        self.last_w = None
        self.readers = []


class Op:
    __slots__ = ("eng", "fn", "deps", "kind", "signal", "has_dep", "idx")

    def __init__(self, eng, fn, kind):
        self.eng = eng
        self.fn = fn
        self.deps = set()
        self.kind = kind
        self.signal = None
        self.has_dep = False
        self.idx = -1


ENGS = ("pe", "act", "dve", "pool", "sp")


class FW:
    def __init__(self, nc, stack, n_dma_sems=24, n_cc_sems=4):
        self.nc = nc
        self.stack = stack
        self.ops = []
        self.last_op = {e: None for e in ENGS}
        self.pending = {e: [] for e in ENGS}
        self.outstanding_dma = []
        self.n_dma_sems = n_dma_sems
        self.n_cc_sems = n_cc_sems
        self.epoch_marks = []

    def res(self, name="r"):
        return Res(name)

    def op(self, eng, fn, reads=(), writes=(), kind="c", after_barrier=True):
        o = Op(eng, fn, kind)
        o.idx = len(self.ops)
        for r in reads:
            if r.last_w is not None:
                o.deps.add(r.last_w)
            if kind == "c":
                r.readers = [x for x in r.readers if not (x.kind == "c" and x.eng == eng)]
            r.readers.append(o)
        for w in writes:
            if w.last_w is not None:
                o.deps.add(w.last_w)
            for rd in w.readers:
                if rd is not o:
                    o.deps.add(rd)
            w.last_w = o
            w.readers = []
        if after_barrier and self.pending[eng]:
            o.deps.update(self.pending[eng])
            self.pending[eng] = []
        o.deps.discard(o)
        self.ops.append(o)
        if kind != "cc":
            self.last_op[eng] = o
        if kind == "d":
            self.outstanding_dma.append(o)
        return o

    def barrier(self):
        col = [o for o in self.last_op.values() if o is not None] + list(self.outstanding_dma)
        for e in ENGS:
            self.pending[e] = list(col) + self.pending[e]
        self.outstanding_dma = []

    def new_epoch(self):
        self.epoch_marks.append(len(self.ops))

    def emit(self):
        nc = self.nc
        st = self.stack
        for o in self.ops:
            keep = set()
            for p in o.deps:
                if p.eng == "pe" and o.eng == "pe" and p.kind == "c" and o.kind == "c":
                    continue
                keep.add(p)
                p.has_dep = True
            o.deps = keep
        n_epochs = len(self.epoch_marks) + 1
        eng_sems = {e: [st.enter_context(nc.semaphore(f"s_{e}_{k}")) for k in range(n_epochs)] for e in ENGS}
        dma_sems = [st.enter_context(nc.semaphore(f"s_dma_{k}")) for k in range(self.n_dma_sems)]
        n_sw = 8
        pool_of = {"pool": list(range(0, n_sw)), "sp": list(range(n_sw, self.n_dma_sems))}
        rr = {"pool": 0, "sp": 0}
        cc_sems = [st.enter_context(nc.semaphore(f"s_cc_{k}")) for k in range(self.n_cc_sems)]
        epoch = 0
        marks = list(self.epoch_marks)
        counters = {e: 0 for e in ENGS}
        dma_rr = 0
        cc_rr = 0
        dma_tot = [0] * self.n_dma_sems
        dma_prev = [None] * self.n_dma_sems
        cc_tot = [0] * self.n_cc_sems
        cc_prev = [None] * self.n_cc_sems
        pre_wait = {}
        for o in self.ops:
            while marks and o.idx >= marks[0]:
                marks.pop(0)
                epoch += 1
                counters = {e: 0 for e in ENGS}
            if o.kind == "d":
                lst = pool_of[o.eng]
                k = lst[rr[o.eng] % len(lst)]
                rr[o.eng] += 1
                if dma_prev[k] is not None:
                    pre_wait[o] = dma_prev[k].signal
                dma_tot[k] += 16
                o.signal = (dma_sems[k], dma_tot[k])
                dma_prev[k] = o
            elif o.kind == "cc":
                k = cc_rr
                cc_rr = (cc_rr + 1) % self.n_cc_sems
                if cc_prev[k] is not None:
                    pre_wait[o] = cc_prev[k].signal
                cc_tot[k] += 1
                o.signal = (cc_sems[k], cc_tot[k])
                cc_prev[k] = o
            elif o.has_dep:
                counters[o.eng] += 1
                o.signal = (eng_sems[o.eng][epoch], counters[o.eng])
                self.max_count = max(getattr(self, "max_count", 0), counters[o.eng])
                self.count_log = getattr(self, "count_log", {})
                self.count_log[(epoch, o.eng)] = counters[o.eng]
        by_eng = {e: [o for o in self.ops if o.eng == e] for e in ENGS}
        self.n_waits = 0

        def run(eng_name, eng):
            waited = {}
            for o in by_eng[eng_name]:
                need = [p.signal for p in o.deps]
                if o in pre_wait:
                    need.append(pre_wait[o])
                for (sem, val) in need:
                    key = id(sem)
                    if waited.get(key, 0) < val:
                        eng.wait_ge(sem, val)
                        waited[key] = val
                        self.n_waits += 1
                if o.fn is None:
                    continue
                ins = o.fn(eng)
                if o.kind == "d":
                    ins.then_inc(o.signal[0], 16)
                elif o.kind == "cc":
                    ins.then_inc(o.signal[0], 1)
                elif o.has_dep:
                    ins.then_inc(o.signal[0], 1)

        with nc.Block() as block:
            @block.tensor
            def _(e):
                run("pe", e)

            @block.scalar
            def _(e):
                run("act", e)

            @block.vector
            def _(e):
                run("dve", e)

            @block.gpsimd
            def _(e):
                run("pool", e)

            @block.sync
            def _(e):
                if getattr(self, "sp_prologue", None) is not None:
                    self.sp_prologue(e)
                run("sp", e)


ARENA_BYTES = 111 * 1024


class Ctx:
    pass


def carve(ctx, off_bytes, shape, dtype):
    esz = 2 if dtype == BF16 else 4
    n = int(np.prod(shape))
    assert off_bytes % 4 == 0 and off_bytes + n * esz <= ARENA_BYTES, (off_bytes, shape)
    v = ctx.arena[:, off_bytes // 2: off_bytes // 2 + n * esz // 2]
    if dtype != BF16:
        v = v.bitcast(dtype)
    if len(shape) == 1:
        return v
    names = " ".join(f"d{i}" for i in range(len(shape)))
    kw = {f"d{i}": shape[i] for i in range(len(shape))}
    return v.rearrange(f"p ({names}) -> p {names}", **kw)


OFF_CONST = 0
OFF_DYN = 4096


def setup_memory(nc, stack, ctx):
    ctx.XT = stack.enter_context(nc.sbuf_tensor("XT", [128, KC, TOK], F32))
    ctx.HY = stack.enter_context(nc.sbuf_tensor("HY", [128, KC, TOK], BF16))
    ctx.arena = stack.enter_context(nc.sbuf_tensor("ARENA", [128, ARENA_BYTES // 2], BF16))
    ctx.PS = stack.enter_context(nc.psum_tensor("PS", [128, 8 * 512], F32))
    fw = ctx.fw
    ctx.r_xt = [[fw.res(f"xt{k}_{t}") for t in range(NTC)] for k in range(KC)]
    ctx.r_hy = [[fw.res(f"hy{k}_{t}") for t in range(NTC)] for k in range(KC)]
    ctx.r_ps = [fw.res(f"ps{b}") for b in range(8)]
    ctx.ones_bf = carve(ctx, OFF_CONST + 0, [128], BF16)
    ctx.ident_bf = carve(ctx, OFF_CONST + 256, [128], BF16)
    ctx.gains = carve(ctx, OFF_CONST + 512, [KC], F32)
    ctx.eps_col = carve(ctx, OFF_CONST + 768, [1], F32)
    ctx.ones_f = carve(ctx, OFF_CONST + 1024, [128], F32)
    ctx.r_const = fw.res("const")
    ctx.r_gain = fw.res("gain")


def bank(ctx, b):
    return ctx.PS[:, b * 512:(b + 1) * 512]


def emit_consts(ctx):
    fw = ctx.fw
    fw.op("pool", lambda e: e.memset(ctx.ones_bf, 1.0), writes=[ctx.r_const])
    fw.op("pool", lambda e: e.memset(ctx.eps_col, EPS), writes=[ctx.r_const])
    fw.op("pool", lambda e: e.memset(ctx.ones_f, 1.0), writes=[ctx.r_const])


def emit_load_x(ctx, x_dram):
    fw = ctx.fw
    src = x_dram.rearrange("(k p) t -> p k t", p=128)
    for k in range(KC):
        fw.op("sp", lambda e, k=k: e.dma_start(out=ctx.XT[:, k, :], in_=src[:, k, :]),
              writes=ctx.r_xt[k], kind="d")


def emit_store_x(ctx, y_dram):
    fw = ctx.fw
    dst = y_dram.rearrange("(k p) t -> p k t", p=128)
    ops = []
    for k in range(KC):
        ops.append(fw.op("sp", lambda e, k=k: e.dma_start(out=dst[:, k, :], in_=ctx.XT[:, k, :]),
                         reads=ctx.r_xt[k], kind="d"))
    return ops


def emit_rmsnorm(ctx, gain_dram_row, off):
    fw = ctx.fw
    rstd = carve(ctx, off, [NTC, 512], F32)
    sq = [carve(ctx, off + 8192 + i * 1024, [512], BF16) for i in range(4)]
    r_rstd = [fw.res(f"rstd{t}") for t in range(NTC)]
    r_sq = [fw.res(f"sq{i}") for i in range(4)]
    g_src = gain_dram_row.rearrange("(k p) -> p k", p=128)
    fw.op("sp", lambda e: e.dma_start(out=ctx.gains, in_=g_src, allow_slow_non_contiguous=True), writes=[ctx.r_gain], kind="d")
    cnt = 0
    for t in range(NTC):
        pb = 6 + (t % 2)
        ps = bank(ctx, pb)
        for k in range(KC):
            i = cnt % 4
            cnt += 1
            fw.op("act", lambda e, k=k, t=t, i=i: e.activation(out=sq[i], in_=ctx.XT[:, k, t * 512:(t + 1) * 512],
                                                              func=AF.Square),
                  reads=[ctx.r_xt[k][t]], writes=[r_sq[i]])
            fw.op("pe", lambda e, k=k, i=i, ps=ps: e.matmul(ps, lhsT=ctx.ones_bf, rhs=sq[i], start=(k == 0), stop=(k == KC - 1)),
                  reads=[r_sq[i], ctx.r_const], writes=[ctx.r_ps[pb]])
        fw.op("act", lambda e, t=t, ps=ps: e.activation(out=rstd[:, t, :], in_=ps, func=AF.Sqrt, bias=ctx.eps_col,
                                                       scale=1.0 / D_MODEL),
              reads=[ctx.r_ps[pb], ctx.r_const], writes=[r_rstd[t]])
        fw.op("dve", lambda e, t=t: e.reciprocal(out=rstd[:, t, :], in_=rstd[:, t, :]),
              reads=[r_rstd[t]], writes=[r_rstd[t]])
        for k in range(KC):
            eng = "dve"
            fw.op(eng, lambda e, k=k, t=t: e.scalar_tensor_tensor(
                out=ctx.HY[:, k, t * 512:(t + 1) * 512], in0=ctx.XT[:, k, t * 512:(t + 1) * 512],
                scalar=ctx.gains[:, k:k + 1], in1=rstd[:, t, :], op0=ALU.mult, op1=ALU.mult),
                reads=[ctx.r_xt[k][t], r_rstd[t], ctx.r_gain], writes=[ctx.r_hy[k][t]])


FF_GROUPS = [(0, 4), (4, 4), (8, 4), (12, 4), (16, 4), (20, 2)]


def emit_ffn(ctx, norm_row, wg, wu, wd, off):
    fw = ctx.fw
    emit_rmsnorm(ctx, norm_row, off)
    o = off + 12288
    WG = [carve(ctx, o + s * 16384, [KC, 512], BF16) for s in range(2)]
    WU = [carve(ctx, o + s * 16384 + 8192, [KC, 512], BF16) for s in range(2)]
    o += 32768
    WD = [carve(ctx, o + s * 8192, [4, 1024], BF16) for s in range(2)]
    o += 16384
    AT = [carve(ctx, o + s * 16384, [4, TOK], BF16) for s in range(2)]
    o += 32768
    SG = [carve(ctx, o + s * 2048, [512], F32) for s in range(2)]
    o += 4096
    r_wg = [fw.res("wg") for _ in range(2)]
    r_wu = [fw.res("wu") for _ in range(2)]
    r_wd = [fw.res("wd") for _ in range(2)]
    r_at = [[[fw.res("at") for _ in range(NTC)] for _ in range(4)] for _ in range(2)]
    r_sg = [fw.res("sg") for _ in range(2)]
    wg_v = wg.rearrange("(k p) f -> p k f", p=128)
    wu_v = wu.rearrange("(k p) f -> p k f", p=128)
    wd_v = wd.rearrange("(c p) d -> p c d", p=128)
    state = {"sg": 0, "gu": 0, "y": 0}

    def load_w(g):
        f0, n = FF_GROUPS[g]
        s = g % 2
        c0, c1 = f0 * 128, (f0 + n) * 128
        fw.op("pool", lambda e: e.dma_start(out=WG[s][:, :, 0:c1 - c0], in_=wg_v[:, :, c0:c1]),
              writes=[r_wg[s]], kind="d", after_barrier=True)
        fw.op("pool", lambda e: e.dma_start(out=WU[s][:, :, 0:c1 - c0], in_=wu_v[:, :, c0:c1]),
              writes=[r_wu[s]], kind="d")
        fw.op("pool", lambda e: e.dma_start(out=WD[s][:, 0:n, :], in_=wd_v[:, f0:f0 + n, :]),
              writes=[r_wd[s]], kind="d")

    def up(g):
        f0, n = FF_GROUPS[g]
        s = g % 2
        for fc in range(n):
            for t in range(NTC):
                gb = 2 * (state["gu"] % 2)
                state["gu"] += 1
                gps, ups = bank(ctx, gb), bank(ctx, gb + 1)
                for k in range(KC):
                    fw.op("pe", lambda e, k=k, fc=fc, t=t, gps=gps: e.matmul(
                        gps, lhsT=WG[s][:, k, fc * 128:(fc + 1) * 128], rhs=ctx.HY[:, k, t * 512:(t + 1) * 512],
                        start=(k == 0), stop=(k == KC - 1)),
                        reads=[r_wg[s], ctx.r_hy[k][t]], writes=[ctx.r_ps[gb]])
                for k in range(KC):
                    fw.op("pe", lambda e, k=k, fc=fc, t=t, ups=ups: e.matmul(
                        ups, lhsT=WU[s][:, k, fc * 128:(fc + 1) * 128], rhs=ctx.HY[:, k, t * 512:(t + 1) * 512],
                        start=(k == 0), stop=(k == KC - 1)),
                        reads=[r_wu[s], ctx.r_hy[k][t]], writes=[ctx.r_ps[gb + 1]])
                si = state["sg"] % 2
                state["sg"] += 1
                fw.op("act", lambda e, gps=gps, si=si: e.activation(out=SG[si], in_=gps, func=AF.Silu),
                      reads=[ctx.r_ps[gb]], writes=[r_sg[si]])
                fw.op("dve", lambda e, ups=ups, si=si, fc=fc, t=t: e.tensor_tensor(
                    out=AT[s][:, fc, t * 512:(t + 1) * 512], in0=SG[si], in1=ups, op=ALU.mult),
                    reads=[ctx.r_ps[gb + 1], r_sg[si]], writes=[r_at[s][fc][t]])

    def down(g):
        f0, n = FF_GROUPS[g]
        s = g % 2
        for dc in range(KC):
            for t in range(NTC):
                yb = 4 + (state["y"] % 2)
                state["y"] += 1
                yps = bank(ctx, yb)
                for fc in range(n):
                    fw.op("pe", lambda e, fc=fc, dc=dc, t=t, yps=yps: e.matmul(
                        yps, lhsT=WD[s][:, fc, dc * 128:(dc + 1) * 128], rhs=AT[s][:, fc, t * 512:(t + 1) * 512],
                        start=(fc == 0), stop=(fc == n - 1)),
                        reads=[r_wd[s], r_at[s][fc][t]], writes=[ctx.r_ps[yb]])
                fw.op("dve", lambda e, dc=dc, t=t, yps=yps: e.scalar_tensor_tensor(
                    out=ctx.XT[:, dc, t * 512:(t + 1) * 512], in0=yps, scalar=0.5,
                    in1=ctx.XT[:, dc, t * 512:(t + 1) * 512], op0=ALU.mult, op1=ALU.add),
                    reads=[ctx.r_ps[yb]], writes=[ctx.r_xt[dc][t]])

    ng = len(FF_GROUPS)
    load_w(0)
    load_w(1)
    up(0)
    for g in range(1, ng):
        up(g)
        down(g - 1)
        if g + 1 < ng:
            load_w(g + 1)
    down(ng - 1)


OFF_ET = 32768
OFF_QC = 49280
OFF_HI = 65664
ETW = TOK + 16
FNORM = 1.0 / math.sqrt(SEQ * 64.0)


def emit_final_norm(ctx, gain_row, y_dram, off):
    fw = ctx.fw
    rstd = carve(ctx, off, [NTC, 512], F32)
    sq = [carve(ctx, off + 8192 + i * 1024, [512], BF16) for i in range(4)]
    ob = [carve(ctx, off + 12288 + i * 2048, [512], F32) for i in range(4)]
    r_rstd = [fw.res("rstd") for t in range(NTC)]
    r_sq = [fw.res("sq") for i in range(4)]
    r_ob = [fw.res("ob") for i in range(4)]
    g_src = gain_row.rearrange("(k p) -> p k", p=128)
    fw.op("sp", lambda e: e.dma_start(out=ctx.gains, in_=g_src, allow_slow_non_contiguous=True),
          writes=[ctx.r_gain], kind="d")
    dst = y_dram.rearrange("(k p) t -> p k t", p=128)
    cnt = 0
    outs = []
    for t in range(NTC):
        pb = 6 + (t % 2)
        ps = bank(ctx, pb)
        for k in range(KC):
            i = cnt % 4
            cnt += 1
            fw.op("act", lambda e, k=k, t=t, i=i: e.activation(out=sq[i], in_=ctx.XT[:, k, t * 512:(t + 1) * 512],
                                                              func=AF.Square),
                  reads=[ctx.r_xt[k][t]], writes=[r_sq[i]])
            fw.op("pe", lambda e, k=k, i=i, ps=ps: e.matmul(ps, lhsT=ctx.ones_bf, rhs=sq[i], start=(k == 0), stop=(k == KC - 1)),
                  reads=[r_sq[i], ctx.r_const], writes=[ctx.r_ps[pb]])
        fw.op("act", lambda e, t=t, ps=ps: e.activation(out=rstd[:, t, :], in_=ps, func=AF.Sqrt, bias=ctx.eps_col,
                                                       scale=1.0 / D_MODEL),
              reads=[ctx.r_ps[pb], ctx.r_const], writes=[r_rstd[t]])
        fw.op("dve", lambda e, t=t: e.reciprocal(out=rstd[:, t, :], in_=rstd[:, t, :]),
              reads=[r_rstd[t]], writes=[r_rstd[t]])
        for k in range(KC):
            i = (t * KC + k) % 4
            fw.op("dve", lambda e, k=k, t=t, i=i: e.scalar_tensor_tensor(
                out=ob[i], in0=ctx.XT[:, k, t * 512:(t + 1) * 512],
                scalar=ctx.gains[:, k:k + 1], in1=rstd[:, t, :], op0=ALU.mult, op1=ALU.mult),
                reads=[ctx.r_xt[k][t], r_rstd[t], ctx.r_gain], writes=[r_ob[i]])
            outs.append(fw.op("sp", lambda e, k=k, t=t, i=i: e.dma_start(out=dst[:, k, t * 512:(t + 1) * 512], in_=ob[i]),
                              reads=[r_ob[i]], kind="d"))
    return outs


def emit_proj(ctx, l, io, after_f=None, after_kv=None):
    fw = ctx.fw
    rp = io.get("r_pay", None)
    wr = (lambda key: [rp[key]]) if rp is not None else (lambda key: [])
    emit_rmsnorm(ctx, io["mix_norm"][l], OFF_DYN)
    WB = [carve(ctx, 16384 + s * 8192, [KC, 512], BF16) for s in range(2)]
    r_wb = [fw.res("wb") for _ in range(2)]
    ET = carve(ctx, OFF_ET, [2, ETW], F32)
    QC = carve(ctx, OFF_QC, [4, TOK], BF16)
    KST = carve(ctx, OFF_HI, [4, TOK], BF16)
    VST = carve(ctx, OFF_HI + 16384, [4, 16, 128], BF16)
    FST = carve(ctx, OFF_HI + 32768, [16, 256], BF16)
    ctx.r_et = [fw.res("et") for _ in range(2)]
    ctx.r_qc = [fw.res("qc") for _ in range(4)]
    r_kst = [fw.res("kst") for _ in range(4)]
    r_vst = fw.res("vst")
    r_fst = fw.res("fst")
    win = io["w_in"][l].rearrange("(k p) f -> p k f", p=128)
    st = {"b": 0}

    def load(blk):
        s = blk % 2
        fw.op("pool", lambda e: e.dma_start(out=WB[s], in_=win[:, :, blk * 512:(blk + 1) * 512]),
              writes=[r_wb[s]], kind="d")

    def nb():
        b = st["b"] % 4
        st["b"] += 1
        return b

    def fmajor(blk, c0, nchunk, evac):
        s = blk % 2
        for ck in range(nchunk):
            for t in range(NTC):
                b = nb()
                ps = bank(ctx, b)
                for k in range(KC):
                    fw.op("pe", lambda e, k=k, ck=ck, t=t, ps=ps: e.matmul(
                        ps, lhsT=WB[s][:, k, c0 + ck * 128:c0 + (ck + 1) * 128], rhs=ctx.HY[:, k, t * 512:(t + 1) * 512],
                        start=(k == 0), stop=(k == KC - 1)),
                        reads=[r_wb[s], ctx.r_hy[k][t]], writes=[ctx.r_ps[b]])
                evac(ck, t, ps, b)

    def tmajor(blk, c0, ncol, evac):
        s = blk % 2
        for tile in range(16):
            b = nb()
            ps = bank(ctx, b)[:, 0:ncol]
            t = tile // 4
            for k in range(KC):
                fw.op("pe", lambda e, k=k, tile=tile, ps=ps: e.matmul(
                    ps, lhsT=ctx.HY[:, k, tile * 128:(tile + 1) * 128], rhs=WB[s][:, k, c0:c0 + ncol],
                    start=(k == 0), stop=(k == KC - 1)),
                    reads=[r_wb[s], ctx.r_hy[k][t]], writes=[ctx.r_ps[b]])
            evac(tile, ps, b)

    outs = []
    load(0)
    load(3)
    fmajor(0, 0, 2, lambda ck, t, ps, b: fw.op(
        "act", lambda e: e.copy(out=ET[:, ck, 8 + t * 512:8 + (t + 1) * 512], in_=ps),
        reads=[ctx.r_ps[b]], writes=[ctx.r_et[ck]]))
    for ck in range(2):
        outs.append(fw.op("sp", lambda e, ck=ck: e.dma_start(out=io["hpay"][ck, :, 0:8], in_=ET[:, ck, 8:16]),
                          reads=[ctx.r_et[ck]], writes=wr("halo"), kind="d"))
        outs.append(fw.op("sp", lambda e, ck=ck: e.dma_start(out=io["hpay"][ck, :, 8:16], in_=ET[:, ck, TOK:TOK + 8]),
                          reads=[ctx.r_et[ck]], writes=wr("halo"), kind="d"))
    if after_f is not None:
        after_f()
    tmajor(0, 256, 256, lambda tile, ps, b: fw.op(
        "dve", lambda e: e.tensor_copy(out=FST[:, tile, :], in_=ps), reads=[ctx.r_ps[b]], writes=[r_fst]))
    for g in range(4):
        outs.append(fw.op("sp", lambda e, g=g: e.dma_start(
            out=io["fpay"][g].rearrange("(t p) c -> p t c", p=128), in_=FST[:, :, g * 64:(g + 1) * 64]),
            reads=[r_fst], writes=wr("f"), kind="d"))
    load(2)
    tmajor(3, 0, 512, lambda tile, ps, b: fw.op(
        "act" if tile % 2 else "dve",
        (lambda e: e.copy(out=VST[:, :, tile, :], in_=ps.rearrange("p (h e) -> p h e", h=4))) if tile % 2
        else (lambda e: e.tensor_copy(out=VST[:, :, tile, :], in_=ps.rearrange("p (h e) -> p h e", h=4))),
        reads=[ctx.r_ps[b]], writes=[r_vst]))
    load(1)
    fmajor(2, 0, 4, lambda ck, t, ps, b: fw.op(
        "dve", lambda e: e.tensor_copy(out=KST[:, ck, t * 512:(t + 1) * 512], in_=ps),
        reads=[ctx.r_ps[b]], writes=[r_kst[ck]]))
    for h in range(4):
        outs.append(fw.op("sp", lambda e, h=h: e.dma_start(out=io["kpay"][h], in_=KST[:, h, :]),
                          reads=[r_kst[h]], writes=wr(("kv", h)), kind="d"))
        outs.append(fw.op("sp", lambda e, h=h: e.dma_start(out=io["vpay"][h], in_=VST[:, h, :, :]),
                          reads=[r_vst], writes=wr(("kv", h)), kind="d"))
    if after_kv is not None:
        after_kv()
    fmajor(1, 0, 4, lambda ck, t, ps, b: fw.op(
        "act", lambda e: e.mul(out=QC[:, ck, t * 512:(t + 1) * 512], in_=ps, mul=0.125),
        reads=[ctx.r_ps[b]], writes=[ctx.r_qc[ck]]))
    return outs


def _stash_view(ctx):
    return ctx.HY[:, 0:4, :].rearrange("p k t -> p (k t)").bitcast(F32).rearrange("p (c t) -> p c t", c=2)


def emit_stash_et(ctx):
    fw = ctx.fw
    ET = carve(ctx, OFF_ET, [2, ETW], F32)
    SV = _stash_view(ctx)
    for ck in range(2):
        fw.op("dve" if ck == 0 else "act",
              (lambda e, ck=ck: e.tensor_copy(out=SV[:, ck, :], in_=ET[:, ck, 8:8 + TOK])) if ck == 0
              else (lambda e, ck=ck: e.copy(out=SV[:, ck, :], in_=ET[:, ck, 8:8 + TOK])),
              reads=[ctx.r_et[ck]], writes=[r for k in (2 * ck, 2 * ck + 1) for r in ctx.r_hy[k]])


def emit_restore_et(ctx):
    fw = ctx.fw
    ET = carve(ctx, OFF_ET, [2, ETW], F32)
    SV = _stash_view(ctx)
    for ck in range(2):
        fw.op("dve" if ck == 0 else "act",
              (lambda e, ck=ck: e.tensor_copy(out=ET[:, ck, 8:8 + TOK], in_=SV[:, ck, :])) if ck == 0
              else (lambda e, ck=ck: e.copy(out=ET[:, ck, 8:8 + TOK], in_=SV[:, ck, :])),
              reads=[r for k in (2 * ck, 2 * ck + 1) for r in ctx.r_hy[k]], writes=[ctx.r_et[ck]])


def emit_pool(ctx, l, io):
    fw = ctx.fw
    ET = carve(ctx, OFF_ET, [2, ETW], F32)
    T1 = carve(ctx, OFF_HI, [ETW], F32)
    T2 = carve(ctx, OFF_HI + 8256, [ETW], F32)
    o = OFF_HI + 16512
    PW = carve(ctx, o, [2, 128], BF16)
    PWF = carve(ctx, o + 512, [2, 128], F32)
    PSC = carve(ctx, o + 1536, [2], F32)
    CORR = carve(ctx, o + 1544, [2, 16], F32)
    MASK = carve(ctx, o + 1672, [8], F32)
    HALO = carve(ctx, o + 1704, [2, 4, 16], F32)
    MT = carve(ctx, o + 2304, [2, TOK], BF16)
    WIN = carve(ctx, o + 2304 + 8192, [TOK], F32)
    r_t1, r_t2, r_pw, r_cst, r_halo, r_win = (fw.res(n) for n in ("t1", "t2", "pw", "pcst", "halo", "win"))
    r_mt = [fw.res("mt") for _ in range(2)]
    fw.op("pool", lambda e: e.memset(PWF, 0.0), writes=[r_pw])
    for g in range(4):
        ck, gl = g // 2, g % 2
        fw.op("sp", lambda e, g=g, ck=ck, gl=gl: e.dma_start(
            out=PWF[gl * 64:(gl + 1) * 64, ck, gl * 64:(gl + 1) * 64], in_=io["pool_w"][l][g]),
            writes=[r_pw], kind="d")
    fw.op("dve", lambda e: e.tensor_copy(out=PW, in_=PWF), reads=[r_pw], writes=[r_pw])
    fw.op("sp", lambda e: e.dma_start(out=PSC, in_=io["pool_scale_t"][l]), writes=[r_cst], kind="d")
    fw.op("sp", lambda e: e.dma_start(out=CORR, in_=io["pcorr"]), writes=[r_cst], kind="d")
    fw.op("sp", lambda e: e.dma_start(out=MASK, in_=io["pmask"]), writes=[r_cst], kind="d")
    hrd = [io["r_gath"]["halo"]] if "r_gath" in io else []
    for k_ in range(2):
        for r_ in range(4):
            fw.op("sp", lambda e, k_=k_, r_=r_: e.dma_start(out=HALO[:, k_, r_, :], in_=io["halo_all"][k_, :, r_, :]),
                  reads=hrd, writes=[r_halo], kind="d")
    for ck in range(2):
        fw.op("dve", lambda e, ck=ck: e.tensor_scalar(out=ET[:, ck, 0:8], in0=HALO[:, ck, 0, 8:16], scalar1=MASK[:, 0:1],
                                                      scalar2=None, op0=ALU.mult),
              reads=[r_halo, r_cst], writes=[ctx.r_et[ck]])
        fw.op("dve", lambda e, ck=ck: e.tensor_scalar(out=ET[:, ck, TOK + 8:TOK + 16], in0=HALO[:, ck, 0, 0:8],
                                                      scalar1=MASK[:, 4:5], scalar2=None, op0=ALU.mult),
              reads=[r_halo, r_cst], writes=[ctx.r_et[ck]])
        for r in range(1, 4):
            fw.op("dve", lambda e, ck=ck, r=r: e.scalar_tensor_tensor(
                out=ET[:, ck, 0:8], in0=HALO[:, ck, r, 8:16], scalar=MASK[:, r:r + 1], in1=ET[:, ck, 0:8],
                op0=ALU.mult, op1=ALU.add), reads=[r_halo, r_cst], writes=[ctx.r_et[ck]])
            fw.op("dve", lambda e, ck=ck, r=r: e.scalar_tensor_tensor(
                out=ET[:, ck, TOK + 8:TOK + 16], in0=HALO[:, ck, r, 0:8], scalar=MASK[:, 4 + r:5 + r],
                in1=ET[:, ck, TOK + 8:TOK + 16], op0=ALU.mult, op1=ALU.add),
                reads=[r_halo, r_cst], writes=[ctx.r_et[ck]])
        E = ET[:, ck, :]
        fw.op("pool", lambda e, E=E: e.tensor_tensor(out=T1[:, 0:ETW - 1], in0=E[:, 0:ETW - 1], in1=E[:, 1:ETW], op=ALU.add),
              reads=[ctx.r_et[ck]], writes=[r_t1])
        if ck == 0:
            lv = {0: (T1, r_t1, 2)}
            fw.op("pool", lambda e: e.tensor_tensor(out=T2[:, 0:ETW - 3], in0=T1[:, 0:ETW - 3], in1=T1[:, 2:ETW - 1], op=ALU.add),
                  reads=[r_t1], writes=[r_t2])
            lv[1] = (T2, r_t2, 4)
        else:
            fw.op("pool", lambda e: e.tensor_tensor(out=T2[:, 0:ETW - 3], in0=T1[:, 0:ETW - 3], in1=T1[:, 2:ETW - 1], op=ALU.add),
                  reads=[r_t1], writes=[r_t2])
            fw.op("pool", lambda e: e.tensor_tensor(out=T1[:, 0:ETW - 7], in0=T2[:, 0:ETW - 7], in1=T2[:, 4:ETW - 3], op=ALU.add),
                  reads=[r_t2], writes=[r_t1])
            fw.op("pool", lambda e: e.tensor_tensor(out=T2[:, 0:ETW - 15], in0=T1[:, 0:ETW - 15], in1=T1[:, 8:ETW - 7], op=ALU.add),
                  reads=[r_t1], writes=[r_t2])
            lv = {0: (T1, r_t1, 8), 1: (T2, r_t2, 16)}
        for gl in range(2):
            src, rsrc, w = lv[gl]
            sh = 8 - w // 2
            ps_ = slice(gl * 64, (gl + 1) * 64)
            fw.op("dve", lambda e, src=src, sh=sh, ps_=ps_: e.tensor_copy(out=WIN[ps_, :], in_=src[ps_, sh:sh + TOK]),
                  reads=[rsrc], writes=[r_win])
            fw.op("dve", lambda e, ck=ck, ps_=ps_: e.tensor_tensor(out=WIN[ps_, 0:8], in0=WIN[ps_, 0:8],
                                                                  in1=CORR[ps_, ck, 0:8], op=ALU.mult),
                  reads=[r_cst], writes=[r_win])
            fw.op("dve", lambda e, ck=ck, ps_=ps_: e.tensor_tensor(out=WIN[ps_, TOK - 8:TOK], in0=WIN[ps_, TOK - 8:TOK],
                                                                  in1=CORR[ps_, ck, 8:16], op=ALU.mult),
                  reads=[r_cst], writes=[r_win])
            fw.op("dve", lambda e, ck=ck, ps_=ps_, w=w: e.scalar_tensor_tensor(
                out=MT[ps_, ck, :], in0=WIN[ps_, :], scalar=1.0 / w, in1=ET[ps_, ck, 8:8 + TOK],
                op0=ALU.mult, op1=ALU.subtract), reads=[r_win, ctx.r_et[ck]], writes=[r_mt[ck]])
    for ck in range(2):
        for t in range(NTC):
            b = t % 4
            ps = bank(ctx, b)
            fw.op("pe", lambda e, ck=ck, t=t, ps=ps: e.matmul(ps, lhsT=PW[:, ck, :], rhs=MT[:, ck, t * 512:(t + 1) * 512],
                                                            start=True, stop=True),
                  reads=[r_pw, r_mt[ck]], writes=[ctx.r_ps[b]])
            fw.op("act", lambda e, ck=ck, t=t, ps=ps: e.mul(out=ctx.HY[:, ck, t * 512:(t + 1) * 512], in_=ps, mul=PSC[:, ck:ck + 1]),
                  reads=[ctx.r_ps[b], r_cst], writes=[ctx.r_hy[ck][t]])


def emit_fourier(ctx, l, io):
    fw = ctx.fw
    XS = carve(ctx, 4096, [128, 64], BF16)
    BB = carve(ctx, 20480, [2, 64, 64], BF16)
    W3 = carve(ctx, 36864, [64, 3, 32], BF16)
    UT = carve(ctx, OFF_HI, [4, 2, TOK], BF16)
    MM = carve(ctx, OFF_HI + 32768, [4, 2, 256], BF16)
    WFS = carve(ctx, OFF_HI + 36864, [4, 256], BF16)
    tb = OFF_HI + 38912
    WA = carve(ctx, tb, [128], BF16)
    C64 = carve(ctx, tb + 960, [2, 64], BF16)
    r_tab, r_wfs, r_mm, r_xs, r_bb = (fw.res(n) for n in ("ftab", "wfs", "mm", "xs", "bb"))
    r_ut = [fw.res("ut") for _ in range(4)]
    fw.op("sp", lambda e: e.dma_start(out=WA[0:64, :], in_=io["f_wa"]), writes=[r_tab], kind="d")
    fw.op("sp", lambda e: e.dma_start(out=W3, in_=io["f_w3"]), writes=[r_tab], kind="d")
    fw.op("sp", lambda e: e.dma_start(out=C64[0:64], in_=io["f_c64"]), writes=[r_tab], kind="d")
    fw.op("pool", lambda e: e.dma_start(out=WFS[0:64], in_=io["fourier_w"][l].rearrange("(g p) c -> p g c", p=64)),
          writes=[r_wfs], kind="d")
    for g in range(4):
        for comp in range(2):
            b = (g * 2 + comp) % 4
            ps = bank(ctx, b)[0:64, 0:256]
            fw.op("pe", lambda e, g=g, comp=comp, ps=ps: e.matmul(ps, lhsT=C64[0:64, comp, :], rhs=WFS[0:64, g, :],
                                                                 start=True, stop=True),
                  reads=[r_tab, r_wfs], writes=[ctx.r_ps[b]])
            fw.op("dve", lambda e, g=g, comp=comp, ps=ps: e.tensor_copy(out=MM[0:64, g, comp, :], in_=ps),
                  reads=[ctx.r_ps[b]], writes=[r_mm])
    for g in range(4):
        if "load_xs" in io:
            io["load_xs"](g, XS, r_xs)
        else:
            fw.op("sp", lambda e, g=g: e.dma_start(out=XS[0:64], in_=io["fg"][g].rearrange("(s1 s2) c -> s1 s2 c", s2=128)),
                  writes=[r_xs], kind="d")
        for rd in range(4):
            pb0 = 4 * (rd % 2)
            PSV = ctx.PS[:, pb0 * 512:(pb0 + 4) * 512].rearrange("p (c x) -> p c x", x=128)
            rps = [ctx.r_ps[pb0 + i] for i in range(4)]
            for ci in range(16):
                c = rd * 16 + ci
                fw.op("pe", lambda e, c=c, ci=ci, PSV=PSV: e.matmul(PSV[:, ci, :], lhsT=XS[0:64, :, c], rhs=WA[0:64, :],
                                                                   start=True, stop=True),
                      reads=[r_xs, r_tab], writes=[rps[ci // 4]])
            AR = PSV[:, :, 0:64]
            AI = PSV[:, :, 64:128]
            cs = slice(rd * 16, (rd + 1) * 16)
            BRv = BB[:, 0, :, cs].rearrange("p k c -> p c k")
            BIv = BB[:, 1, :, cs].rearrange("p k c -> p c k")
            fw.op("act", lambda e, AR=AR, BRv=BRv: e.copy(out=BRv, in_=AR), reads=rps, writes=[r_bb])
            fw.op("dve", lambda e, AI=AI, BIv=BIv: e.tensor_copy(out=BIv, in_=AI), reads=rps, writes=[r_bb])
        for q4 in range(4):
            pr, pi = (q4 % 2) * 2, (q4 % 2) * 2 + 1
            for kk in range(16):
                k1 = q4 * 16 + kk
                outr = bank(ctx, pr)[0:64, kk * 32:(kk + 1) * 32]
                outi = bank(ctx, pi)[0:64, kk * 32:(kk + 1) * 32]
                fw.op("pe", lambda e, k1=k1, outr=outr: e.matmul(outr, lhsT=BB[:, 0, k1, :], rhs=W3[:, k1, 0, :], start=True, stop=False),
                      reads=[r_bb, r_tab], writes=[ctx.r_ps[pr]])
                fw.op("pe", lambda e, k1=k1, outr=outr: e.matmul(outr, lhsT=BB[:, 1, k1, :], rhs=W3[:, k1, 2, :], start=False, stop=True),
                      reads=[r_bb, r_tab], writes=[ctx.r_ps[pr]])
                fw.op("pe", lambda e, k1=k1, outi=outi: e.matmul(outi, lhsT=BB[:, 1, k1, :], rhs=W3[:, k1, 0, :], start=True, stop=False),
                      reads=[r_bb, r_tab], writes=[ctx.r_ps[pi]])
                fw.op("pe", lambda e, k1=k1, outi=outi: e.matmul(outi, lhsT=BB[:, 0, k1, :], rhs=W3[:, k1, 1, :], start=False, stop=True),
                      reads=[r_bb, r_tab], writes=[ctx.r_ps[pi]])
            for comp, pbk in ((0, pr), (1, pi)):
                src = bank(ctx, pbk)[0:64, :].rearrange("p (k j) -> p k j", j=32)
                dstv = UT[0:64, g, comp, :].rearrange("p (j k) -> p k j", k=64)[:, q4 * 16:(q4 + 1) * 16, :]
                fw.op("act" if comp else "dve",
                      (lambda e, src=src, dstv=dstv: e.copy(out=dstv, in_=src)) if comp
                      else (lambda e, src=src, dstv=dstv: e.tensor_copy(out=dstv, in_=src)),
                      reads=[ctx.r_ps[pbk]], writes=[r_ut[g]])
    for ck in range(2):
        for t in range(NTC):
            b = 4 + (ck * NTC + t) % 4
            ps = bank(ctx, b)
            n = 0
            for g in range(4):
                for comp in range(2):
                    fw.op("pe", lambda e, g=g, comp=comp, ck=ck, t=t, ps=ps, n=n: e.matmul(
                        ps, lhsT=MM[0:64, g, comp, ck * 128:(ck + 1) * 128], rhs=UT[0:64, g, comp, t * 512:(t + 1) * 512],
                        start=(n == 0), stop=(n == 7)),
                        reads=[r_mm, r_ut[g]], writes=[ctx.r_ps[b]])
                    n += 1
            fw.op("act", lambda e, ck=ck, t=t, ps=ps: e.copy(out=ctx.HY[:, 2 + ck, t * 512:(t + 1) * 512], in_=ps),
                  reads=[ctx.r_ps[b]], writes=[ctx.r_hy[2 + ck][t]])


def emit_attn(ctx, l, io):
    fw = ctx.fw
    K0 = carve(ctx, 4096, [SEQ], BF16)
    K1 = carve(ctx, 20480, [SEQ], BF16)
    Q0 = carve(ctx, 36864, [TOK], BF16)
    Q1 = carve(ctx, 40960, [TOK], BF16)
    DT = carve(ctx, 45056, [4, 128], BF16)
    o = 46080
    LAMV = carve(ctx, o, [256], F32)
    LTMP = carve(ctx, o + 1024, [64], F32)
    LS = carve(ctx, o + 1280, [8], F32)
    GN = carve(ctx, o + 1312, [4], F32)
    SQH = carve(ctx, o + 1344, [512], BF16)
    QC = carve(ctx, OFF_QC, [4, TOK], BF16)
    V = carve(ctx, OFF_HI, [64, 128], BF16)
    PT = [carve(ctx, OFF_HI + 16384 + i * 2048, [1024], BF16) for i in range(3)]
    FT = [carve(ctx, OFF_HI + 22528 + i * 2048, [512], F32) for i in range(4)]
    ACC = carve(ctx, OFF_HI + 30720, [1024], F32)
    r_acc2 = [fw.res("acc0"), fw.res("acc1")]
    r_k, r_v, r_q, r_d, r_lam = (fw.res(n) for n in ("k", "v", "q", "dt", "lam"))
    r_pt = [fw.res("pt") for _ in range(3)]
    r_ft = [fw.res("ft") for _ in range(4)]
    r_sqh = fw.res("sqh")
    LI = carve(ctx, o + 2368, [2], F32)
    fw.op("sp", lambda e: e.dma_start(out=DT, in_=io["dtile"]), writes=[r_d], kind="d")
    fw.op("sp", lambda e: e.dma_start(out=ctx.ident_bf, in_=io["ident"]), writes=[ctx.r_const], kind="d")
    fw.op("sp", lambda e: e.dma_start(out=LAMV, in_=io["lamvec"][l].partition_broadcast(128)), writes=[r_lam], kind="d")
    fw.op("sp", lambda e: e.dma_start(out=GN, in_=io["head_norm_t"][l]), writes=[r_lam], kind="d")
    fw.op("sp", lambda e: e.dma_start(out=LI, in_=io["laminit"][l]), writes=[r_lam], kind="d")
    for i in range(2):
        fw.op("dve", lambda e, i=i: e.tensor_tensor(out=LTMP, in0=LAMV[:, i * 128:i * 128 + 64],
                                                    in1=LAMV[:, i * 128 + 64:i * 128 + 128], op=ALU.mult),
              reads=[r_lam], writes=[r_lam])
        fw.op("dve", lambda e, i=i: e.reduce_sum(out=LS[:, i:i + 1], in_=LTMP, axis=mybir.AxisListType.X),
              reads=[r_lam], writes=[r_lam])
    fw.op("act", lambda e: e.activation(out=LS[:, 2:4], in_=LS[:, 0:2], func=AF.Exp), reads=[r_lam], writes=[r_lam])
    fw.op("dve", lambda e: e.tensor_tensor(out=LS[:, 4:5], in0=LS[:, 3:4], in1=LS[:, 2:3], op=ALU.subtract),
          reads=[r_lam], writes=[r_lam])
    fw.op("dve", lambda e: e.tensor_tensor(out=LS[:, 5:6], in0=LS[:, 4:5], in1=LI[:, 0:1], op=ALU.add),
          reads=[r_lam], writes=[r_lam])
    fw.op("dve", lambda e: e.tensor_scalar(out=GN, in0=GN, scalar1=LI[:, 1:2], scalar2=None, op0=ALU.mult),
          reads=[r_lam], writes=[r_lam])
    NEGLAM = LS[:, 5:6]

    it = {"n": 0}
    for h in range(HEADS):
        if "load_kv" in io:
            io["load_kv"](h, K0, K1, V, r_k, r_v)
        else:
            fw.op("sp", lambda e, h=h: e.dma_start(out=K0[0:64, :], in_=io["kg"][h, 0:64, :]), writes=[r_k], kind="d")
            fw.op("sp", lambda e, h=h: e.dma_start(out=K1[0:64, :], in_=io["kg"][h, 64:128, :]), writes=[r_k], kind="d")
            fw.op("sp", lambda e, h=h: e.dma_start(out=V, in_=io["vg"][h]), writes=[r_v], kind="d")
        fw.op("sp", lambda e, h=h: e.dma_start(out=K0[64:73, :], in_=io["kaug0"][h]), writes=[r_k], kind="d")
        fw.op("sp", lambda e, h=h: e.dma_start(out=K1[64:73, :], in_=io["kaug0"][h]), writes=[r_k], kind="d")
        fw.op("sp", lambda e, h=h: e.dma_start(out=Q0[64:73, :], in_=io["qaug0"][h]), writes=[r_q], kind="d")
        fw.op("sp", lambda e, h=h: e.dma_start(out=Q1[64:73, :], in_=io["qaug0"][h]), writes=[r_q], kind="d")
        fw.op("dve", lambda e, h=h: e.tensor_copy(out=Q0[0:64, :], in_=QC[0:64, h, :]), reads=[ctx.r_qc[h]], writes=[r_q])
        fw.op("sp", lambda e, h=h: e.dma_start(out=Q1[0:64, :], in_=QC[64:128, h, :]), reads=[ctx.r_qc[h]], writes=[r_q], kind="d")

        def s_mm(Q, L, sb, hh=h):
            ks = slice(L * 128, (L + 1) * 128)
            for j in range(2):
                KT, QT = (K0, Q0) if j == 0 else (K1, Q1)
                ps = bank(ctx, sb + j)

                def rng(mode):
                    return {"diag": slice(0, 65), "below": slice(0, 69), "above": slice(0, 73)}[mode]

                def mm(out, mode, qs, start=True, stop=True, KT=KT, QT=QT, j=j):
                    pr = rng(mode)
                    rb = ctx.r_ps[sb + j]
                    fw.op("pe", lambda e: e.matmul(out, lhsT=KT[pr, ks], rhs=QT[pr, qs], start=start, stop=stop),
                          reads=[r_k, r_q], writes=[rb])

                if L >= 16 or L < 4 * Q:
                    mm(ps, "below", slice(Q * 512, (Q + 1) * 512))
                elif L >= 4 * Q + 4:
                    mm(ps, "above", slice(Q * 512, (Q + 1) * 512))
                else:
                    us = L - 4 * Q
                    for u in range(4):
                        qs = slice(Q * 512 + u * 128, Q * 512 + (u + 1) * 128)
                        out = ps[:, u * 128:(u + 1) * 128]
                        if u > us:
                            mm(out, "below", qs)
                        elif u < us:
                            mm(out, "above", qs)
                        else:
                            mm(out, "diag", qs, start=True, stop=False)
                            fw.op("pe", lambda e, out=out, hh=hh: e.matmul(out, lhsT=ctx.ident_bf, rhs=DT[:, hh, :], start=False, stop=True),
                                  reads=[ctx.r_const, r_d], writes=[ctx.r_ps[sb + j]])

        for Q in range(4):
            qcols = slice(Q * 512, (Q + 1) * 512)
            if BAND[h] is None:
                Ls = list(range(64))
            else:
                Ls = [(4 * Q + d_) % 64 for d_ in range(-BAND[h], BAND[h] + 4)]
            nL = len(Ls)
            s_mm(Q, Ls[0], 0)
            for li, L in enumerate(Ls):
                sb = 2 * (li % 2)
                if li + 1 < nL:
                    s_mm(Q, Ls[li + 1], 2 * ((li + 1) % 2))
                pi = it["n"] % 3
                it["n"] += 1
                fw.op("act", lambda e, sb=sb, pi=pi: e.activation(out=PT[pi], in_=ctx.PS[:, sb * 512:(sb + 2) * 512], func=AF.Exp),
                      reads=[ctx.r_ps[sb], ctx.r_ps[sb + 1]], writes=[r_pt[pi]])
                for j in range(2):
                    fw.op("pe", lambda e, L=L, j=j, pi=pi, li=li, nL=nL: e.matmul(bank(ctx, 4 + j), lhsT=V[:, L, :], rhs=PT[pi][:, j * 512:(j + 1) * 512],
                                                                   start=(li == 0), stop=(li == nL - 1)),
                          reads=[r_v, r_pt[pi]], writes=[ctx.r_ps[4 + j]])
                fw.op("pe", lambda e, pi=pi, li=li, nL=nL: e.matmul(bank(ctx, 6), lhsT=ctx.ones_bf, rhs=PT[pi][:, 0:512],
                                                              start=(li == 0), stop=(li == nL - 1)),
                      reads=[ctx.r_const, r_pt[pi]], writes=[ctx.r_ps[6]])
                if li == 0:
                    fw.op("dve", lambda e, pi=pi: e.tensor_copy(out=ACC[:, 512:1024], in_=PT[pi][:, 512:1024]),
                          reads=[r_pt[pi]], writes=[r_acc2[1]])
                else:
                    fw.op("dve", lambda e, pi=pi: e.tensor_tensor(out=ACC[:, 512:1024], in0=ACC[:, 512:1024],
                                                                  in1=PT[pi][:, 512:1024], op=ALU.add),
                          reads=[r_pt[pi]], writes=[r_acc2[1]])
            fw.op("pe", lambda e: e.matmul(bank(ctx, 7), lhsT=ctx.ones_f, rhs=ACC[:, 512:1024], start=True, stop=True),
                  reads=[ctx.r_const, r_acc2[1]], writes=[ctx.r_ps[7]])
            fw.op("dve", lambda e: e.reciprocal(out=FT[0], in_=bank(ctx, 6)), reads=[ctx.r_ps[6]], writes=[r_ft[0]])
            fw.op("dve", lambda e: e.reciprocal(out=FT[1], in_=bank(ctx, 7)), reads=[ctx.r_ps[7]], writes=[r_ft[1]])
            fw.op("dve", lambda e: e.tensor_tensor(out=FT[0], in0=bank(ctx, 4), in1=FT[0], op=ALU.mult),
                  reads=[ctx.r_ps[4], r_ft[0]], writes=[r_ft[0]])
            fw.op("dve", lambda e: e.tensor_tensor(out=FT[1], in0=bank(ctx, 5), in1=FT[1], op=ALU.mult),
                  reads=[ctx.r_ps[5], r_ft[1]], writes=[r_ft[1]])
            fw.op("dve", lambda e: e.scalar_tensor_tensor(out=FT[2], in0=FT[1], scalar=NEGLAM, in1=FT[0],
                                                          op0=ALU.mult, op1=ALU.add),
                  reads=[r_ft[0], r_ft[1], r_lam], writes=[r_ft[2]])
            fw.op("act", lambda e: e.activation(out=SQH, in_=FT[2], func=AF.Square), reads=[r_ft[2]], writes=[r_sqh])
            fw.op("pe", lambda e: e.matmul(bank(ctx, 6), lhsT=ctx.ones_bf, rhs=SQH, start=True, stop=True),
                  reads=[ctx.r_const, r_sqh], writes=[ctx.r_ps[6]])
            fw.op("act", lambda e: e.activation(out=FT[3], in_=bank(ctx, 6), func=AF.Sqrt, bias=ctx.eps_col, scale=1.0 / 128.0),
                  reads=[ctx.r_ps[6], ctx.r_const], writes=[r_ft[3]])
            fw.op("dve", lambda e: e.reciprocal(out=FT[3], in_=FT[3]), reads=[r_ft[3]], writes=[r_ft[3]])
            t = Q
            fw.op("dve", lambda e, h=h, qcols=qcols: e.scalar_tensor_tensor(
                out=ctx.HY[:, 4 + h, qcols], in0=FT[2], scalar=GN[:, h:h + 1], in1=FT[3], op0=ALU.mult, op1=ALU.mult),
                reads=[r_ft[2], r_ft[3], r_lam], writes=[ctx.r_hy[4 + h][t]])


def emit_wout(ctx, l, io):
    fw = ctx.fw
    WO = carve(ctx, 4096, [KC, 1024], BF16)
    r_wo = [fw.res("wo") for _ in range(2)]
    wv = io["w_out"][l].rearrange("(k p) d -> p k d", p=128)
    for hlf in range(2):
        fw.op("pool", lambda e, hlf=hlf: e.dma_start(out=WO[:, hlf * 4:(hlf + 1) * 4, :], in_=wv[:, hlf * 4:(hlf + 1) * 4, :]),
              writes=[r_wo[hlf]], kind="d")
    n = 0
    for dc in range(KC):
        for t in range(NTC):
            b = n % 4
            n += 1
            ps = bank(ctx, b)
            for k in range(KC):
                fw.op("pe", lambda e, k=k, dc=dc, t=t, ps=ps: e.matmul(
                    ps, lhsT=WO[:, k, dc * 128:(dc + 1) * 128], rhs=ctx.HY[:, k, t * 512:(t + 1) * 512],
                    start=(k == 0), stop=(k == KC - 1)),
                    reads=[r_wo[k // 4], ctx.r_hy[k][t]], writes=[ctx.r_ps[b]])
            fw.op("dve", lambda e, dc=dc, t=t, ps=ps: e.tensor_tensor(
                out=ctx.XT[:, dc, t * 512:(t + 1) * 512], in0=ps, in1=ctx.XT[:, dc, t * 512:(t + 1) * 512], op=ALU.add),
                reads=[ctx.r_ps[b]], writes=[ctx.r_xt[dc][t]])


BIG_A = {"ffn2_w_gate": [D_MODEL, D_FF], "ffn2_w_up": [D_MODEL, D_FF], "ffn2_w_down": [D_FF, D_MODEL],
         "w_out": [D_MODEL, D_MODEL]}
BIG_B = {"ffn1_w_gate": [D_MODEL, D_FF], "ffn1_w_up": [D_MODEL, D_FF], "ffn1_w_down": [D_FF, D_MODEL],
         "w_in": [D_MODEL, 2048]}
SMALL_A = {"ffn2_norm": [D_MODEL], "pool_w": [4, 64, 64], "fourier_w": [256, 256], "pool_scale_t": [128, 2],
           "head_norm_t": [128, 4], "lamvec": [256], "laminit": [128, 2]}
SMALL_B = {"ffn1_norm": [D_MODEL], "mix_norm": [D_MODEL]}
TABLE_SPECS = {
    "pcorr": ([128, 2, 16], F32), "pmask": ([128, 8], F32),
    "f_wa": ([64, 128], BF16),
    "f_w3": ([128, 64, 3, 32], BF16), "f_c64": ([64, 2, 64], BF16),
    "dtile": ([128, 4, 128], BF16), "ident": ([128, 128], BF16),
    "kaug0": ([4, 9, SEQ], BF16), "qaug0": ([4, 9, TOK], BF16),
}
PAY_SPECS = {
    "kpay": ([4, 128, TOK], BF16), "vpay": ([4, 128, 16, 128], BF16), "fpay": ([4, TOK, 64], BF16),
    "hpay": ([2, 128, 16], F32), "qc_out": ([128, 4, TOK], BF16), "et_out": ([128, 2, TOK], F32),
    "x_out": ([D_MODEL, TOK], F32),
}
GATH_SPECS = {
    "kg": ([4, 128, SEQ], BF16), "vg": ([4, 128, 64, 128], BF16), "fg": ([4, SEQ, 64], BF16),
    "halo_all": ([2, 128, 4, 16], F32), "qc_in": ([128, 4, TOK], BF16), "et_in": ([128, 2, TOK], F32),
}


class _One:
    def __init__(self, ap):
        self.ap = ap

    def __getitem__(self, _):
        return self.ap


def build_launch(kind, dbg_phases=None):
    nc = bass.Bass("TRN2", target_bir_lowering=False)
    io = {}

    def inp(name, shp, dt=F32):
        return nc.dram_tensor(name, shp, dt, kind="ExternalInput").ap()

    io["x_in"] = inp("x_in", [D_MODEL, TOK])
    if kind != "first":
        for n, shp in {**BIG_A, **SMALL_A}.items():
            io[n] = _One(inp(n, shp))
        for n, (shp, dt) in TABLE_SPECS.items():
            io[n] = inp(n, shp, dt)
        for n, (shp, dt) in GATH_SPECS.items():
            io[n] = inp(n, shp, dt)
    if kind != "last":
        for n, shp in {**BIG_B, **SMALL_B}.items():
            io[n] = _One(inp(n, shp))
        for n, (shp, dt) in PAY_SPECS.items():
            io[n] = nc.dram_tensor(n, shp, dt, kind="ExternalOutput").ap()
    else:
        io["final_norm"] = inp("final_norm", [D_MODEL])
        io["y"] = nc.dram_tensor("y", [D_MODEL, TOK], F32, kind="ExternalOutput").ap()
    with ExitStack() as stack:
        ctx = Ctx()
        ctx.nc = nc
        fw = ctx.fw = FW(nc, stack)
        setup_memory(nc, stack, ctx)
        emit_consts(ctx)
        emit_load_x(ctx, io["x_in"])
        if kind != "first":
            ET = carve(ctx, OFF_ET, [2, ETW], F32)
            QC = carve(ctx, OFF_QC, [4, TOK], BF16)
            ctx.r_et = [fw.res("et") for _ in range(2)]
            ctx.r_qc = [fw.res("qc") for _ in range(4)]
            for ck in range(2):
                fw.op("sp", lambda e, ck=ck: e.dma_start(out=ET[:, ck, 8:8 + TOK], in_=io["et_in"][:, ck, :]),
                      writes=[ctx.r_et[ck]], kind="d")
            for h in range(4):
                fw.op("sp", lambda e, h=h: e.dma_start(out=QC[:, h, :], in_=io["qc_in"][:, h, :]),
                      writes=[ctx.r_qc[h]], kind="d")
            if dbg_phases is None or "pool" in dbg_phases:
                emit_pool(ctx, 0, io)
                fw.barrier()
            if dbg_phases is None or "fourier" in dbg_phases:
                emit_fourier(ctx, 0, io)
                fw.barrier()
            if dbg_phases is None or "attn" in dbg_phases:
                emit_attn(ctx, 0, io)
                fw.barrier()
            if dbg_phases is not None:
                hy_out = nc.dram_tensor("hy_out", [128, KC, TOK], BF16, kind="ExternalOutput").ap()
                for k in range(KC):
                    fw.op("sp", lambda e, k=k: e.dma_start(out=hy_out[:, k, :], in_=ctx.HY[:, k, :]),
                          reads=ctx.r_hy[k], kind="d")
                fw.barrier()
                fw.op("sp", None)
                fw.emit()
                return nc
            emit_wout(ctx, 0, io)
            fw.barrier()
            emit_ffn(ctx, io["ffn2_norm"][0], io["ffn2_w_gate"][0], io["ffn2_w_up"][0], io["ffn2_w_down"][0], OFF_DYN)
            fw.barrier()
            fw.new_epoch()
        if kind == "last":
            emit_final_norm(ctx, io["final_norm"], io["y"], OFF_DYN)
        else:
            emit_ffn(ctx, io["ffn1_norm"][0], io["ffn1_w_gate"][0], io["ffn1_w_up"][0], io["ffn1_w_down"][0], OFF_DYN)
            fw.barrier()
            emit_proj(ctx, 0, io)
            ET = carve(ctx, OFF_ET, [2, ETW], F32)
            QC = carve(ctx, OFF_QC, [4, TOK], BF16)
            for ck in range(2):
                fw.op("sp", lambda e, ck=ck: e.dma_start(out=io["et_out"][:, ck, :], in_=ET[:, ck, 8:8 + TOK]),
                      reads=[ctx.r_et[ck]], kind="d")
            for h in range(4):
                fw.op("sp", lambda e, h=h: e.dma_start(out=io["qc_out"][:, h, :], in_=QC[:, h, :]),
                      reads=[ctx.r_qc[h]], kind="d")
            fw.barrier()
            emit_store_x(ctx, io["x_out"])
        fw.barrier()
        fw.op("sp", None)
        fw.emit()
        nc._fw_stats = (len(fw.ops), fw.n_waits, dict(fw.count_log), max(dma_v for dma_v in [0]))
    return nc


def _bf(a):
    return np.asarray(a, dtype=np.float32).astype(ml_dtypes.bfloat16)


def make_tables(r):
    t = {}
    pcorr = np.ones((128, 2, 16), np.float32)
    for ck in range(2):
        for p in range(128):
            w = POOL_W[2 * ck + p // 64]
            left = w // 2
            right = w - 1 - left
            for i in range(8):
                if r == 0:
                    tt = i
                    cnt = min(tt + right + 1, SEQ) - max(tt - left, 0)
                    pcorr[p, ck, i] = w / cnt
                if r == 3:
                    tt = SEQ - 8 + i
                    cnt = min(tt + right + 1, SEQ) - max(tt - left, 0)
                    pcorr[p, ck, 8 + i] = w / cnt
    t["pcorr"] = pcorr
    pmask = np.zeros((128, 8), np.float32)
    if r - 1 >= 0:
        pmask[:, r - 1] = 1.0
    if r + 1 <= 3:
        pmask[:, 4 + r + 1] = 1.0
    t["pmask"] = pmask
    s1 = np.arange(64)[:, None]
    k1 = np.arange(64)[None, :]
    ang = 2 * np.pi * ((s1 * k1) % 64) / 64.0
    t["f_wa"] = _bf(np.concatenate([np.cos(ang), -np.sin(ang)], axis=1))
    s2 = np.arange(128)[:, None]
    ang = 2 * np.pi * ((s2 * k1) % SEQ) / float(SEQ)
    t["f_tr"] = (np.cos(ang) * FNORM).astype(np.float32)
    t["f_ti"] = (-np.sin(ang) * FNORM).astype(np.float32)
    del t["f_tr"], t["f_ti"]
    s2c = np.arange(128, dtype=np.int64)[:, None, None]
    k1c = np.arange(64, dtype=np.int64)[None, :, None]
    k2c = (32 * r + np.arange(32, dtype=np.int64))[None, None, :]
    ph = 2 * np.pi * (((k1c * s2c) + 64 * (k2c * s2c)) % SEQ) / float(SEQ)
    wr_ = np.cos(ph) * FNORM
    wi_ = -np.sin(ph) * FNORM
    t["f_w3"] = _bf(np.stack([wr_, wi_, -wi_], axis=2))
    c = np.arange(64)[:, None]
    cp = np.arange(64)[None, :]
    ang = 2 * np.pi * ((c * cp) % 64) / 64.0
    t["f_c64"] = _bf(np.stack([np.cos(ang), np.sin(ang)], axis=1))
    p = np.arange(128)
    dt = np.zeros((128, 4, 128), np.float32)
    for h in range(4):
        dt[:, h, :] = -SLOPES[h] * np.abs(p[:, None] - p[None, :])
    t["dtile"] = _bf(dt)
    t["ident"] = _bf(np.eye(128))
    L = np.arange(64)
    n = (16 * r + L) % 64
    sig = np.where(L < 16, 1.0, np.where(n < 16 * r, 1.0, -1.0))
    ncol = np.repeat(n, 128).astype(np.float64)
    sigc = np.repeat(sig, 128)
    pcol = np.tile(p, 64).astype(np.float64)
    kaug0 = np.zeros((4, 9, SEQ), np.float32)
    kaug1 = np.zeros((4, 64, SEQ), np.float32)
    qaug0 = np.zeros((4, 9, TOK), np.float32)
    qaug1 = np.zeros((4, 64, TOK), np.float32)
    tq = 2048 * r + np.arange(TOK)
    nq = (tq // 256).astype(np.float64)
    bq = (tq % 256).astype(np.float64)
    for h in range(4):
        m = SLOPES[h]
        A = np.stack([sigc * m * 128.0 * ncol, sigc * m * pcol, sigc, sigc])
        B = np.stack([np.ones(TOK), np.ones(TOK), -m * 256.0 * nq, -m * bq])
        kaug0[h, 0] = 1.0
        kaug0[h, 1:5] = A
        kaug0[h, 5:9] = A
        kaug1[h, 0:4] = A
        kaug1[h, 32:36] = A
        qaug0[h, 0] = 0.0
        qaug0[h, 1:5] = B
        qaug0[h, 5:9] = -2.0 * B
        qaug1[h, 32:36] = B
        qaug1[h, 0:4] = -2.0 * B
    for nm, a in (("kaug0", kaug0), ("qaug0", qaug0)):
        b = _bf(a)
        assert np.array_equal(b.astype(np.float32), a), nm
        t[nm] = b
    return t


def _layer_small(inputs, l):
    f32 = np.float32
    d = {}
    d["pool_scale_t"] = np.ascontiguousarray(np.asarray(inputs["pool_scale"][l], f32).reshape(2, 128).T)
    d["head_norm_t"] = np.ascontiguousarray(np.asarray(inputs["attn_head_norm"][l], f32).reshape(4, 128).T)
    d["lamvec"] = np.concatenate([np.asarray(inputs[k][l], f32) for k in ("lam_q1", "lam_k1", "lam_q2", "lam_k2")])
    li = lambda_init_fn(l)
    d["laminit"] = np.tile(np.array([[-li, 1.0 - li]], f32), (128, 1))
    return d


def _run(nc, in_maps):
    res = run_bass_kernel_spmd(nc, in_maps, core_ids=list(range(NCORES)))
    return res.results


class _Lay:
    def __init__(self, ap):
        self.ap = ap

    def __getitem__(self, l):
        return self.ap[l]


FUSED_W = {
    "ffn1_norm": [DEPTH, D_MODEL], "ffn1_w_gate": [DEPTH, D_MODEL, D_FF], "ffn1_w_up": [DEPTH, D_MODEL, D_FF],
    "ffn1_w_down": [DEPTH, D_FF, D_MODEL], "mix_norm": [DEPTH, D_MODEL], "w_in": [DEPTH, D_MODEL, 2048],
    "pool_w": [DEPTH, 4, 64, 64], "fourier_w": [DEPTH, 256, 256], "w_out": [DEPTH, D_MODEL, D_MODEL],
    "ffn2_norm": [DEPTH, D_MODEL], "ffn2_w_gate": [DEPTH, D_MODEL, D_FF], "ffn2_w_up": [DEPTH, D_MODEL, D_FF],
    "ffn2_w_down": [DEPTH, D_FF, D_MODEL],
    "pool_scale_t": [DEPTH, 128, 2], "head_norm_t": [DEPTH, 128, 4], "lamvec": [DEPTH, 256], "laminit": [DEPTH, 128, 2],
}
GROUPS = [[0, 1, 2, 3], [4, 5, 6, 7]]


def build_fused(depth=DEPTH):
    nc = bass.Bass("TRN2", target_bir_lowering=False)
    io = {}

    def inp(name, shp, dt=F32):
        return nc.dram_tensor(name, shp, dt, kind="ExternalInput").ap()

    io["x_in"] = inp("x_in", [D_MODEL, TOK])
    for n, shp in FUSED_W.items():
        io[n] = _Lay(inp(n, shp))
    io["final_norm"] = inp("final_norm", [D_MODEL])
    for n, (shp, dt) in TABLE_SPECS.items():
        io[n] = inp(n, shp, dt)
    io["y"] = nc.dram_tensor("y", [D_MODEL, TOK], F32, kind="ExternalOutput").ap()
    pay_kv = [nc.dram_tensor(f"pay_kv{h}", [256, TOK], BF16) for h in range(4)]
    kvg = [nc.dram_tensor(f"kvg{h}", [4 * 256, TOK], BF16) for h in range(4)]
    pay_f = nc.dram_tensor("pay_f", [4 * TOK, 64], BF16)
    fgat = nc.dram_tensor("fgat", [4 * 4 * TOK, 64], BF16)
    pay_h = nc.dram_tensor("pay_h", [256, 16], F32)
    hgat = nc.dram_tensor("hgat", [4 * 256, 16], F32)
    io["kpay"] = [pay_kv[h].ap()[0:128, :] for h in range(4)]
    io["vpay"] = [pay_kv[h].ap()[128:256, :].rearrange("p (t e) -> p t e", e=128) for h in range(4)]
    io["fpay"] = [pay_f.ap()[g * TOK:(g + 1) * TOK, :] for g in range(4)]
    io["hpay"] = pay_h.ap().rearrange("(k p) j -> k p j", p=128)
    io["halo_all"] = hgat.ap().rearrange("(r k p) j -> k p r j", r=4, k=2)
    with ExitStack() as stack:
        ctx = Ctx()
        ctx.nc = nc
        fw = ctx.fw = FW(nc, stack)
        setup_memory(nc, stack, ctx)
        rp = {"f": fw.res("pay_f"), "halo": fw.res("pay_h")}
        rg = {"f": fw.res("fgat"), "halo": fw.res("hgat")}
        for h in range(4):
            rp[("kv", h)] = fw.res("pay_kv")
            rg[("kv", h)] = fw.res("kvg")
        io["r_pay"] = rp
        io["r_gath"] = rg

        def cc(src, dst, key):
            fw.op("pool", lambda e: e.collective_compute("AllGather", ALU.bypass, replica_groups=GROUPS,
                                                         ins=[src.ap().opt()], outs=[dst.ap().opt()]),
                  reads=[rp[key]], writes=[rg[key]], kind="cc")

        def after_f():
            cc(pay_h, hgat, "halo")

        def after_kv():
            for h in range(4):
                cc(pay_kv[h], kvg[h], ("kv", h))
            cc(pay_f, fgat, "f")

        def load_xs(g, XS, r_xs):
            fv = fgat.ap()
            for j in range(4):
                base = j * 4 * TOK + g * TOK
                fw.op("sp", lambda e, j=j, base=base: e.dma_start(
                    out=XS[16 * j:16 * (j + 1)], in_=fv[base:base + TOK, :].rearrange("(s1 s2) c -> s1 s2 c", s2=128)),
                    reads=[rg["f"]], writes=[r_xs], kind="d")

        def load_kv(h, K0, K1, V, r_k, r_v):
            kv = kvg[h].ap()
            for i in range(4):
                def mk(i=i, part=0):
                    def f(e):
                        rank = (ctx.pid + i) % 4
                        if part == 0:
                            return e.dma_start(out=K0[0:64, i * TOK:(i + 1) * TOK], in_=kv[bass.ds(rank * 256, 64), :])
                        if part == 1:
                            return e.dma_start(out=K1[0:64, i * TOK:(i + 1) * TOK], in_=kv[bass.ds(rank * 256 + 64, 64), :])
                        return e.dma_start(out=V[:, 16 * i:16 * (i + 1), :],
                                           in_=kv[bass.ds(rank * 256 + 128, 128), :].rearrange("p (t e) -> p t e", e=128))
                    return f
                fw.op("sp", mk(i, 0), reads=[rg[("kv", h)]], writes=[r_k], kind="d")
                fw.op("sp", mk(i, 1), reads=[rg[("kv", h)]], writes=[r_k], kind="d")
                fw.op("sp", mk(i, 2), reads=[rg[("kv", h)]], writes=[r_v], kind="d")

        io["load_xs"] = load_xs
        io["load_kv"] = load_kv

        def _pro(e):
            ctx.pid = nc.partition_id([mybir.EngineType.SP])
        fw.sp_prologue = _pro
        io["fg"] = None
        emit_consts(ctx)
        emit_load_x(ctx, io["x_in"])
        for l in range(depth):
            emit_ffn(ctx, io["ffn1_norm"][l], io["ffn1_w_gate"][l], io["ffn1_w_up"][l], io["ffn1_w_down"][l], OFF_DYN)
            fw.barrier()
            emit_proj(ctx, l, io, after_f=after_f, after_kv=after_kv)
            fw.barrier()
            fw.new_epoch()
            emit_pool(ctx, l, io)
            fw.barrier()
            emit_attn(ctx, l, io)
            fw.barrier()
            emit_fourier(ctx, l, io)
            fw.barrier()
            emit_wout(ctx, l, io)
            fw.barrier()
            emit_ffn(ctx, io["ffn2_norm"][l], io["ffn2_w_gate"][l], io["ffn2_w_up"][l], io["ffn2_w_down"][l], OFF_DYN)
            fw.barrier()
            fw.new_epoch()
        emit_final_norm(ctx, io["final_norm"], io["y"], OFF_DYN)
        fw.barrier()
        fw.op("sp", None)
        fw.emit()
        nc._fw_stats = (len(fw.ops), fw.n_waits, dict(fw.count_log))
    return nc


def fused_inputs(inputs, depth=DEPTH):
    f32 = np.float32
    x = np.asarray(inputs["x"], f32)
    shared = {}
    for n in FUSED_W:
        if n in inputs:
            shared[n] = np.ascontiguousarray(np.asarray(inputs[n], f32))
    sm = [_layer_small(inputs, l) for l in range(DEPTH)]
    for n in ("pool_scale_t", "head_norm_t", "lamvec", "laminit"):
        shared[n] = np.ascontiguousarray(np.stack([sm[l][n] for l in range(DEPTH)]).astype(f32))
    shared["final_norm"] = np.asarray(inputs["final_norm"], f32)
    tables = [make_tables(r) for r in range(4)]
    in_maps = []
    for c in range(NCORES):
        b, r = c // 4, c % 4
        d = {"x_in": np.ascontiguousarray(x[b, r * TOK:(r + 1) * TOK, :].T)}
        d.update(shared)
        d.update(tables[r])
        in_maps.append(d)
    return in_maps


def kernel(**inputs):
    in_maps = fused_inputs(inputs)
    outs = _run(_prog("fused"), in_maps)
    out = np.empty((BATCH, SEQ, D_MODEL), np.float32)
    for c in range(NCORES):
        b, r = c // 4, c % 4
        out[b, r * TOK:(r + 1) * TOK, :] = np.asarray(outs[c]["y"], np.float32).T
    return out


_PROGS = {}


def _prog(kind):
    if kind not in _PROGS:
        _PROGS[kind] = build_fused() if kind == "fused" else build_launch(kind)
    return _PROGS[kind]


def kernel_unfused(**inputs):
    f32 = np.float32
    x = np.asarray(inputs["x"], f32)
    tables = [make_tables(r) for r in range(4)]

    def wA(l):
        d = {n: np.ascontiguousarray(np.asarray(inputs[n][l], f32)) for n in BIG_A}
        d["ffn2_norm"] = np.asarray(inputs["ffn2_norm"][l], f32)
        d["pool_w"] = np.asarray(inputs["pool_w"][l], f32)
        d["fourier_w"] = np.asarray(inputs["fourier_w"][l], f32)
        d.update(_layer_small(inputs, l))
        return d

    def wB(l):
        d = {n: np.ascontiguousarray(np.asarray(inputs[n][l], f32)) for n in BIG_B}
        d["ffn1_norm"] = np.asarray(inputs["ffn1_norm"][l], f32)
        d["mix_norm"] = np.asarray(inputs["mix_norm"][l], f32)
        return d

    def gathered(outs):
        g = []
        for c in range(NCORES):
            b, r = c // 4, c % 4
            grp = [outs[4 * b + j] for j in range(4)]
            rot = [grp[(r + i) % 4] for i in range(4)]
            d = {}
            d["kg"] = np.ascontiguousarray(np.concatenate([o["kpay"] for o in rot], axis=2))
            d["vg"] = np.ascontiguousarray(np.concatenate([o["vpay"] for o in rot], axis=2))
            d["fg"] = np.ascontiguousarray(np.concatenate([o["fpay"] for o in grp], axis=1))
            d["halo_all"] = np.ascontiguousarray(np.stack([o["hpay"] for o in grp], axis=2))
            d["qc_in"] = outs[c]["qc_out"]
            d["et_in"] = outs[c]["et_out"]
            d["x_in"] = outs[c]["x_out"]
            g.append(d)
        return g

    b0 = wB(0)
    in_maps = []
    for c in range(NCORES):
        b, r = c // 4, c % 4
        d = {"x_in": np.ascontiguousarray(x[b, r * TOK:(r + 1) * TOK, :].T)}
        d.update(b0)
        in_maps.append(d)
    outs = _run(_prog("first"), in_maps)
    for l in range(1, DEPTH):
        g = gathered(outs)
        a, bb = wA(l - 1), wB(l)
        in_maps = []
        for c in range(NCORES):
            d = dict(g[c])
            d.update(a)
            d.update(bb)
            d.update(tables[c % 4])
            in_maps.append(d)
        outs = _run(_prog("mid"), in_maps)
    g = gathered(outs)
    a = wA(DEPTH - 1)
    in_maps = []
    for c in range(NCORES):
        d = dict(g[c])
        d.update(a)
        d.update(tables[c % 4])
        d["final_norm"] = np.asarray(inputs["final_norm"], f32)
        in_maps.append(d)
    outs = _run(_prog("last"), in_maps)
    out = np.empty((BATCH, SEQ, D_MODEL), f32)
    for c in range(NCORES):
        b, r = c // 4, c % 4
        out[b, r * TOK:(r + 1) * TOK, :] = np.asarray(outs[c]["y"], f32).T
    return out
```

```python
import math
from contextlib import ExitStack

import numpy as np
import ml_dtypes

import concourse.bass as bass
import concourse.mybir as mybir
from concourse.bass_utils import run_bass_kernel_spmd

F32 = mybir.dt.float32
BF16 = mybir.dt.bfloat16
AF = mybir.ActivationFunctionType
ALU = mybir.AluOpType

D_MODEL = 1024
BATCH = 2
SEQ = 8192
DEPTH = 4
D_FF = 2816
NCORES = 8
TOK = 2048
NTC = 4
KC = 8
EPS = 1e-6
HEADS = 4
SLOPES = [2.0 ** (-8.0 * (i + 1) / HEADS) for i in range(HEADS)]
POOL_W = (2, 4, 8, 16)
BAND = [4, 16, None, None]


def lambda_init_fn(layer_idx):
    return 0.8 - 0.6 * math.exp(-0.3 * layer_idx)


class Res:
    __slots__ = ("name", "last_w", "readers")

    def __init__(self, name):
        self.name = name
        self.last_w = None
        self.readers = []


class Op:
    __slots__ = ("eng", "fn", "deps", "kind", "signal", "has_dep", "idx")

    def __init__(self, eng, fn, kind):
        self.eng = eng
        self.fn = fn
        self.deps = set()
        self.kind = kind
        self.signal = None
        self.has_dep = False
        self.idx = -1


ENGS = ("pe", "act", "dve", "pool", "sp")


class FW:
    def __init__(self, nc, stack, n_dma_sems=24, n_cc_sems=4):
        self.nc = nc
        self.stack = stack
        self.ops = []
        self.last_op = {e: None for e in ENGS}
        self.pending = {e: [] for e in ENGS}
        self.outstanding_dma = []
        self.n_dma_sems = n_dma_sems
        self.n_cc_sems = n_cc_sems
        self.epoch_marks = []

    def res(self, name="r"):
        return Res(name)

    def op(self, eng, fn, reads=(), writes=(), kind="c", after_barrier=True):
        o = Op(eng, fn, kind)
        o.idx = len(self.ops)
        for r in reads:
            if r.last_w is not None:
                o.deps.add(r.last_w)
            if kind == "c":
                r.readers = [x for x in r.readers if not (x.kind == "c" and x.eng == eng)]
            r.readers.append(o)
        for w in writes:
            if w.last_w is not None:
                o.deps.add(w.last_w)
            for rd in w.readers:
                if rd is not o:
                    o.deps.add(rd)
            w.last_w = o
            w.readers = []
        if after_barrier and self.pending[eng]:
            o.deps.update(self.pending[eng])
            self.pending[eng] = []
        o.deps.discard(o)
        self.ops.append(o)
        if kind != "cc":
            self.last_op[eng] = o
        if kind == "d":
            self.outstanding_dma.append(o)
        return o

    def barrier(self):
        col = [o for o in self.last_op.values() if o is not None] + list(self.outstanding_dma)
        for e in ENGS:
            self.pending[e] = list(col) + self.pending[e]
        self.outstanding_dma = []

    def new_epoch(self):
        self.epoch_marks.append(len(self.ops))

    def emit(self):
        nc = self.nc
        st = self.stack
        for o in self.ops:
            keep = set()
            for p in o.deps:
                if p.eng == "pe" and o.eng == "pe" and p.kind == "c" and o.kind == "c":
                    continue
                keep.add(p)
                p.has_dep = True
            o.deps = keep
        n_epochs = len(self.epoch_marks) + 1
        eng_sems = {e: [st.enter_context(nc.semaphore(f"s_{e}_{k}")) for k in range(n_epochs)] for e in ENGS}
        dma_sems = [st.enter_context(nc.semaphore(f"s_dma_{k}")) for k in range(self.n_dma_sems)]
        n_sw = 8
        pool_of = {"pool": list(range(0, n_sw)), "sp": list(range(n_sw, self.n_dma_sems))}
        rr = {"pool": 0, "sp": 0}
        cc_sems = [st.enter_context(nc.semaphore(f"s_cc_{k}")) for k in range(self.n_cc_sems)]
        epoch = 0
        marks = list(self.epoch_marks)
        counters = {e: 0 for e in ENGS}
        dma_rr = 0
        cc_rr = 0
        dma_tot = [0] * self.n_dma_sems
        dma_prev = [None] * self.n_dma_sems
        cc_tot = [0] * self.n_cc_sems
        cc_prev = [None] * self.n_cc_sems
        pre_wait = {}
        for o in self.ops:
            while marks and o.idx >= marks[0]:
                marks.pop(0)
                epoch += 1
                counters = {e: 0 for e in ENGS}
            if o.kind == "d":
                lst = pool_of[o.eng]
                k = lst[rr[o.eng] % len(lst)]
                rr[o.eng] += 1
                if dma_prev[k] is not None:
                    pre_wait[o] = dma_prev[k].signal
                dma_tot[k] += 16
                o.signal = (dma_sems[k], dma_tot[k])
                dma_prev[k] = o
            elif o.kind == "cc":
                k = cc_rr
                cc_rr = (cc_rr + 1) % self.n_cc_sems
                if cc_prev[k] is not None:
                    pre_wait[o] = cc_prev[k].signal
                cc_tot[k] += 1
                o.signal = (cc_sems[k], cc_tot[k])
                cc_prev[k] = o
            elif o.has_dep:
                counters[o.eng] += 1
                o.signal = (eng_sems[o.eng][epoch], counters[o.eng])
                self.max_count = max(getattr(self, "max_count", 0), counters[o.eng])
                self.count_log = getattr(self, "count_log", {})
                self.count_log[(epoch, o.eng)] = counters[o.eng]
        by_eng = {e: [o for o in self.ops if o.eng == e] for e in ENGS}
        self.n_waits = 0

        def run(eng_name, eng):
            waited = {}
            for o in by_eng[eng_name]:
                need = [p.signal for p in o.deps]
                if o in pre_wait:
                    need.append(pre_wait[o])
                for (sem, val) in need:
                    key = id(sem)
                    if waited.get(key, 0) < val:
                        eng.wait_ge(sem, val)
                        waited[key] = val
                        self.n_waits += 1
                if o.fn is None:
                    continue
                ins = o.fn(eng)
                if o.kind == "d":
                    ins.then_inc(o.signal[0], 16)
                elif o.kind == "cc":
                    ins.then_inc(o.signal[0], 1)
                elif o.has_dep:
                    ins.then_inc(o.signal[0], 1)

        with nc.Block() as block:
            @block.tensor
            def _(e):
                run("pe", e)

            @block.scalar
            def _(e):
                run("act", e)

            @block.vector
            def _(e):
                run("dve", e)

            @block.gpsimd
            def _(e):
                run("pool", e)

            @block.sync
            def _(e):
                if getattr(self, "sp_prologue", None) is not None:
                    self.sp_prologue(e)
                run("sp", e)


ARENA_BYTES = 111 * 1024


class Ctx:
    pass


def carve(ctx, off_bytes, shape, dtype):
    esz = 2 if dtype == BF16 else 4
    n = int(np.prod(shape))
    assert off_bytes % 4 == 0 and off_bytes + n * esz <= ARENA_BYTES, (off_bytes, shape)
    v = ctx.arena[:, off_bytes // 2: off_bytes // 2 + n * esz // 2]
    if dtype != BF16:
        v = v.bitcast(dtype)
    if len(shape) == 1:
        return v
    names = " ".join(f"d{i}" for i in range(len(shape)))
    kw = {f"d{i}": shape[i] for i in range(len(shape))}
    return v.rearrange(f"p ({names}) -> p {names}", **kw)


OFF_CONST = 0
OFF_DYN = 4096


def setup_memory(nc, stack, ctx):
    ctx.XT = stack.enter_context(nc.sbuf_tensor("XT", [128, KC, TOK], F32))
    ctx.HY = stack.enter_context(nc.sbuf_tensor("HY", [128, KC, TOK], BF16))
    ctx.arena = stack.enter_context(nc.sbuf_tensor("ARENA", [128, ARENA_BYTES // 2], BF16))
    ctx.PS = stack.enter_context(nc.psum_tensor("PS", [128, 8 * 512], F32))
    fw = ctx.fw
    ctx.r_xt = [[fw.res(f"xt{k}_{t}") for t in range(NTC)] for k in range(KC)]
    ctx.r_hy = [[fw.res(f"hy{k}_{t}") for t in range(NTC)] for k in range(KC)]
    ctx.r_ps = [fw.res(f"ps{b}") for b in range(8)]
    ctx.ones_bf = carve(ctx, OFF_CONST + 0, [128], BF16)
    ctx.ident_bf = carve(ctx, OFF_CONST + 256, [128], BF16)
    ctx.gains = carve(ctx, OFF_CONST + 512, [KC], F32)
    ctx.eps_col = carve(ctx, OFF_CONST + 768, [1], F32)
    ctx.ones_f = carve(ctx, OFF_CONST + 1024, [128], F32)
    ctx.r_const = fw.res("const")
    ctx.r_gain = fw.res("gain")


def bank(ctx, b):
    return ctx.PS[:, b * 512:(b + 1) * 512]


def emit_consts(ctx):
    fw = ctx.fw
    fw.op("pool", lambda e: e.memset(ctx.ones_bf, 1.0), writes=[ctx.r_const])
    fw.op("pool", lambda e: e.memset(ctx.eps_col, EPS), writes=[ctx.r_const])
    fw.op("pool", lambda e: e.memset(ctx.ones_f, 1.0), writes=[ctx.r_const])


def emit_load_x(ctx, x_dram):
    fw = ctx.fw
    src = x_dram.rearrange("(k p) t -> p k t", p=128)
    for k in range(KC):
        fw.op("sp", lambda e, k=k: e.dma_start(out=ctx.XT[:, k, :], in_=src[:, k, :]),
              writes=ctx.r_xt[k], kind="d")


def emit_store_x(ctx, y_dram):
    fw = ctx.fw
    dst = y_dram.rearrange("(k p) t -> p k t", p=128)
    ops = []
    for k in range(KC):
        ops.append(fw.op("sp", lambda e, k=k: e.dma_start(out=dst[:, k, :], in_=ctx.XT[:, k, :]),
                         reads=ctx.r_xt[k], kind="d"))
    return ops


def emit_rmsnorm(ctx, gain_dram_row, off):
    fw = ctx.fw
    rstd = carve(ctx, off, [NTC, 512], F32)
    sq = [carve(ctx, off + 8192 + i * 1024, [512], BF16) for i in range(4)]
    r_rstd = [fw.res(f"rstd{t}") for t in range(NTC)]
    r_sq = [fw.res(f"sq{i}") for i in range(4)]
    g_src = gain_dram_row.rearrange("(k p) -> p k", p=128)
    fw.op("sp", lambda e: e.dma_start(out=ctx.gains, in_=g_src, allow_slow_non_contiguous=True), writes=[ctx.r_gain], kind="d")
    cnt = 0
    for t in range(NTC):
        pb = 6 + (t % 2)
        ps = bank(ctx, pb)
        for k in range(KC):
            i = cnt % 4
            cnt += 1
            fw.op("act", lambda e, k=k, t=t, i=i: e.activation(out=sq[i], in_=ctx.XT[:, k, t * 512:(t + 1) * 512],
                                                              func=AF.Square),
                  reads=[ctx.r_xt[k][t]], writes=[r_sq[i]])
            fw.op("pe", lambda e, k=k, i=i, ps=ps: e.matmul(ps, lhsT=ctx.ones_bf, rhs=sq[i], start=(k == 0), stop=(k == KC - 1)),
                  reads=[r_sq[i], ctx.r_const], writes=[ctx.r_ps[pb]])
        fw.op("act", lambda e, t=t, ps=ps: e.activation(out=rstd[:, t, :], in_=ps, func=AF.Sqrt, bias=ctx.eps_col,
                                                       scale=1.0 / D_MODEL),
              reads=[ctx.r_ps[pb], ctx.r_const], writes=[r_rstd[t]])
        fw.op("dve", lambda e, t=t: e.reciprocal(out=rstd[:, t, :], in_=rstd[:, t, :]),
              reads=[r_rstd[t]], writes=[r_rstd[t]])
        for k in range(KC):
            eng = "dve"
            fw.op(eng, lambda e, k=k, t=t: e.scalar_tensor_tensor(
                out=ctx.HY[:, k, t * 512:(t + 1) * 512], in0=ctx.XT[:, k, t * 512:(t + 1) * 512],
                scalar=ctx.gains[:, k:k + 1], in1=rstd[:, t, :], op0=ALU.mult, op1=ALU.mult),
                reads=[ctx.r_xt[k][t], r_rstd[t], ctx.r_gain], writes=[ctx.r_hy[k][t]])


FF_GROUPS = [(0, 4), (4, 4), (8, 4), (12, 4), (16, 4), (20, 2)]


def emit_ffn(ctx, norm_row, wg, wu, wd, off):
    fw = ctx.fw
    emit_rmsnorm(ctx, norm_row, off)
    o = off + 12288
    WG = [carve(ctx, o + s * 16384, [KC, 512], BF16) for s in range(2)]
    WU = [carve(ctx, o + s * 16384 + 8192, [KC, 512], BF16) for s in range(2)]
    o += 32768
    WD = [carve(ctx, o + s * 8192, [4, 1024], BF16) for s in range(2)]
    o += 16384
    AT = [carve(ctx, o + s * 16384, [4, TOK], BF16) for s in range(2)]
    o += 32768
    SG = [carve(ctx, o + s * 2048, [512], F32) for s in range(2)]
    o += 4096
    r_wg = [fw.res("wg") for _ in range(2)]
    r_wu = [fw.res("wu") for _ in range(2)]
    r_wd = [fw.res("wd") for _ in range(2)]
    r_at = [[[fw.res("at") for _ in range(NTC)] for _ in range(4)] for _ in range(2)]
    r_sg = [fw.res("sg") for _ in range(2)]
    wg_v = wg.rearrange("(k p) f -> p k f", p=128)
    wu_v = wu.rearrange("(k p) f -> p k f", p=128)
    wd_v = wd.rearrange("(c p) d -> p c d", p=128)
    state = {"sg": 0, "gu": 0, "y": 0}

    def load_w(g):
        f0, n = FF_GROUPS[g]
        s = g % 2
        c0, c1 = f0 * 128, (f0 + n) * 128
        fw.op("pool", lambda e: e.dma_start(out=WG[s][:, :, 0:c1 - c0], in_=wg_v[:, :, c0:c1]),
              writes=[r_wg[s]], kind="d", after_barrier=True)
        fw.op("pool", lambda e: e.dma_start(out=WU[s][:, :, 0:c1 - c0], in_=wu_v[:, :, c0:c1]),
              writes=[r_wu[s]], kind="d")
        fw.op("pool", lambda e: e.dma_start(out=WD[s][:, 0:n, :], in_=wd_v[:, f0:f0 + n, :]),
              writes=[r_wd[s]], kind="d")

    def up(g):
        f0, n = FF_GROUPS[g]
        s = g % 2
        for fc in range(n):
            for t in range(NTC):
                gb = 2 * (state["gu"] % 2)
                state["gu"] += 1
                gps, ups = bank(ctx, gb), bank(ctx, gb + 1)
                for k in range(KC):
                    fw.op("pe", lambda e, k=k, fc=fc, t=t, gps=gps: e.matmul(
                        gps, lhsT=WG[s][:, k, fc * 128:(fc + 1) * 128], rhs=ctx.HY[:, k, t * 512:(t + 1) * 512],
                        start=(k == 0), stop=(k == KC - 1)),
                        reads=[r_wg[s], ctx.r_hy[k][t]], writes=[ctx.r_ps[gb]])
                for k in range(KC):
                    fw.op("pe", lambda e, k=k, fc=fc, t=t, ups=ups: e.matmul(
                        ups, lhsT=WU[s][:, k, fc * 128:(fc + 1) * 128], rhs=ctx.HY[:, k, t * 512:(t + 1) * 512],
                        start=(k == 0), stop=(k == KC - 1)),
                        reads=[r_wu[s], ctx.r_hy[k][t]], writes=[ctx.r_ps[gb + 1]])
                si = state["sg"] % 2
                state["sg"] += 1
                fw.op("act", lambda e, gps=gps, si=si: e.activation(out=SG[si], in_=gps, func=AF.Silu),
                      reads=[ctx.r_ps[gb]], writes=[r_sg[si]])
                fw.op("dve", lambda e, ups=ups, si=si, fc=fc, t=t: e.tensor_tensor(
                    out=AT[s][:, fc, t * 512:(t + 1) * 512], in0=SG[si], in1=ups, op=ALU.mult),
                    reads=[ctx.r_ps[gb + 1], r_sg[si]], writes=[r_at[s][fc][t]])

    def down(g):
        f0, n = FF_GROUPS[g]
        s = g % 2
        for dc in range(KC):
            for t in range(NTC):
                yb = 4 + (state["y"] % 2)
                state["y"] += 1
                yps = bank(ctx, yb)
                for fc in range(n):
                    fw.op("pe", lambda e, fc=fc, dc=dc, t=t, yps=yps: e.matmul(
                        yps, lhsT=WD[s][:, fc, dc * 128:(dc + 1) * 128], rhs=AT[s][:, fc, t * 512:(t + 1) * 512],
                        start=(fc == 0), stop=(fc == n - 1)),
                        reads=[r_wd[s], r_at[s][fc][t]], writes=[ctx.r_ps[yb]])
                fw.op("dve", lambda e, dc=dc, t=t, yps=yps: e.scalar_tensor_tensor(
                    out=ctx.XT[:, dc, t * 512:(t + 1) * 512], in0=yps, scalar=0.5,
                    in1=ctx.XT[:, dc, t * 512:(t + 1) * 512], op0=ALU.mult, op1=ALU.add),
                    reads=[ctx.r_ps[yb]], writes=[ctx.r_xt[dc][t]])

    ng = len(FF_GROUPS)
    load_w(0)
    load_w(1)
    up(0)
    for g in range(1, ng):
        up(g)
        down(g - 1)
        if g + 1 < ng:
            load_w(g + 1)
    down(ng - 1)


OFF_ET = 32768
OFF_QC = 49280
OFF_HI = 65664
ETW = TOK + 16
FNORM = 1.0 / math.sqrt(SEQ * 64.0)


def emit_final_norm(ctx, gain_row, y_dram, off):
    fw = ctx.fw
    rstd = carve(ctx, off, [NTC, 512], F32)
    sq = [carve(ctx, off + 8192 + i * 1024, [512], BF16) for i in range(4)]
    ob = [carve(ctx, off + 12288 + i * 2048, [512], F32) for i in range(4)]
    r_rstd = [fw.res("rstd") for t in range(NTC)]
    r_sq = [fw.res("sq") for i in range(4)]
    r_ob = [fw.res("ob") for i in range(4)]
    g_src = gain_row.rearrange("(k p) -> p k", p=128)
    fw.op("sp", lambda e: e.dma_start(out=ctx.gains, in_=g_src, allow_slow_non_contiguous=True),
          writes=[ctx.r_gain], kind="d")
    dst = y_dram.rearrange("(k p) t -> p k t", p=128)
    cnt = 0
    outs = []
    for t in range(NTC):
        pb = 6 + (t % 2)
        ps = bank(ctx, pb)
        for k in range(KC):
            i = cnt % 4
            cnt += 1
            fw.op("act", lambda e, k=k, t=t, i=i: e.activation(out=sq[i], in_=ctx.XT[:, k, t * 512:(t + 1) * 512],
                                                              func=AF.Square),
                  reads=[ctx.r_xt[k][t]], writes=[r_sq[i]])
            fw.op("pe", lambda e, k=k, i=i, ps=ps: e.matmul(ps, lhsT=ctx.ones_bf, rhs=sq[i], start=(k == 0), stop=(k == KC - 1)),
                  reads=[r_sq[i], ctx.r_const], writes=[ctx.r_ps[pb]])
        fw.op("act", lambda e, t=t, ps=ps: e.activation(out=rstd[:, t, :], in_=ps, func=AF.Sqrt, bias=ctx.eps_col,
                                                       scale=1.0 / D_MODEL),
              reads=[ctx.r_ps[pb], ctx.r_const], writes=[r_rstd[t]])
        fw.op("dve", lambda e, t=t: e.reciprocal(out=rstd[:, t, :], in_=rstd[:, t, :]),
              reads=[r_rstd[t]], writes=[r_rstd[t]])
        for k in range(KC):
            i = (t * KC + k) % 4
            fw.op("dve", lambda e, k=k, t=t, i=i: e.scalar_tensor_tensor(
                out=ob[i], in0=ctx.XT[:, k, t * 512:(t + 1) * 512],
                scalar=ctx.gains[:, k:k + 1], in1=rstd[:, t, :], op0=ALU.mult, op1=ALU.mult),
                reads=[ctx.r_xt[k][t], r_rstd[t], ctx.r_gain], writes=[r_ob[i]])
            outs.append(fw.op("sp", lambda e, k=k, t=t, i=i: e.dma_start(out=dst[:, k, t * 512:(t + 1) * 512], in_=ob[i]),
                              reads=[r_ob[i]], kind="d"))
    return outs


def emit_proj(ctx, l, io, after_f=None, after_kv=None):
    fw = ctx.fw
    rp = io.get("r_pay", None)
    wr = (lambda key: [rp[key]]) if rp is not None else (lambda key: [])
    emit_rmsnorm(ctx, io["mix_norm"][l], OFF_DYN)
    WB = [carve(ctx, 16384 + s * 8192, [KC, 512], BF16) for s in range(2)]
    r_wb = [fw.res("wb") for _ in range(2)]
    ET = carve(ctx, OFF_ET, [2, ETW], F32)
    QC = carve(ctx, OFF_QC, [4, TOK], BF16)
    KST = carve(ctx, OFF_HI, [4, TOK], BF16)
    VST = carve(ctx, OFF_HI + 16384, [4, 16, 128], BF16)
    FST = carve(ctx, OFF_HI + 32768, [16, 256], BF16)
    ctx.r_et = [fw.res("et") for _ in range(2)]
    ctx.r_qc = [fw.res("qc") for _ in range(4)]
    r_kst = [fw.res("kst") for _ in range(4)]
    r_vst = fw.res("vst")
    r_fst = fw.res("fst")
    win = io["w_in"][l].rearrange("(k p) f -> p k f", p=128)
    st = {"b": 0}

    def load(blk):
        s = blk % 2
        fw.op("pool", lambda e: e.dma_start(out=WB[s], in_=win[:, :, blk * 512:(blk + 1) * 512]),
              writes=[r_wb[s]], kind="d")

    def nb():
        b = st["b"] % 4
        st["b"] += 1
        return b

    def fmajor(blk, c0, nchunk, evac):
        s = blk % 2
        for ck in range(nchunk):
            for t in range(NTC):
                b = nb()
                ps = bank(ctx, b)
                for k in range(KC):
                    fw.op("pe", lambda e, k=k, ck=ck, t=t, ps=ps: e.matmul(
                        ps, lhsT=WB[s][:, k, c0 + ck * 128:c0 + (ck + 1) * 128], rhs=ctx.HY[:, k, t * 512:(t + 1) * 512],
                        start=(k == 0), stop=(k == KC - 1)),
                        reads=[r_wb[s], ctx.r_hy[k][t]], writes=[ctx.r_ps[b]])
                evac(ck, t, ps, b)

    def tmajor(blk, c0, ncol, evac):
        s = blk % 2
        for tile in range(16):
            b = nb()
            ps = bank(ctx, b)[:, 0:ncol]
            t = tile // 4
            for k in range(KC):
                fw.op("pe", lambda e, k=k, tile=tile, ps=ps: e.matmul(
                    ps, lhsT=ctx.HY[:, k, tile * 128:(tile + 1) * 128], rhs=WB[s][:, k, c0:c0 + ncol],
                    start=(k == 0), stop=(k == KC - 1)),
                    reads=[r_wb[s], ctx.r_hy[k][t]], writes=[ctx.r_ps[b]])
            evac(tile, ps, b)

    outs = []
    load(0)
    load(3)
    fmajor(0, 0, 2, lambda ck, t, ps, b: fw.op(
        "act", lambda e: e.copy(out=ET[:, ck, 8 + t * 512:8 + (t + 1) * 512], in_=ps),
        reads=[ctx.r_ps[b]], writes=[ctx.r_et[ck]]))
    for ck in range(2):
        outs.append(fw.op("sp", lambda e, ck=ck: e.dma_start(out=io["hpay"][ck, :, 0:8], in_=ET[:, ck, 8:16]),
                          reads=[ctx.r_et[ck]], writes=wr("halo"), kind="d"))
        outs.append(fw.op("sp", lambda e, ck=ck: e.dma_start(out=io["hpay"][ck, :, 8:16], in_=ET[:, ck, TOK:TOK + 8]),
                          reads=[ctx.r_et[ck]], writes=wr("halo"), kind="d"))
    if after_f is not None:
        after_f()
    tmajor(0, 256, 256, lambda tile, ps, b: fw.op(
        "dve", lambda e: e.tensor_copy(out=FST[:, tile, :], in_=ps), reads=[ctx.r_ps[b]], writes=[r_fst]))
    for g in range(4):
        outs.append(fw.op("sp", lambda e, g=g: e.dma_start(
            out=io["fpay"][g].rearrange("(t p) c -> p t c", p=128), in_=FST[:, :, g * 64:(g + 1) * 64]),
            reads=[r_fst], writes=wr("f"), kind="d"))
    load(2)
    tmajor(3, 0, 512, lambda tile, ps, b: fw.op(
        "act" if tile % 2 else "dve",
        (lambda e: e.copy(out=VST[:, :, tile, :], in_=ps.rearrange("p (h e) -> p h e", h=4))) if tile % 2
        else (lambda e: e.tensor_copy(out=VST[:, :, tile, :], in_=ps.rearrange("p (h e) -> p h e", h=4))),
        reads=[ctx.r_ps[b]], writes=[r_vst]))
    load(1)
    fmajor(2, 0, 4, lambda ck, t, ps, b: fw.op(
        "dve", lambda e: e.tensor_copy(out=KST[:, ck, t * 512:(t + 1) * 512], in_=ps),
        reads=[ctx.r_ps[b]], writes=[r_kst[ck]]))
    for h in range(4):
        outs.append(fw.op("sp", lambda e, h=h: e.dma_start(out=io["kpay"][h], in_=KST[:, h, :]),
                          reads=[r_kst[h]], writes=wr(("kv", h)), kind="d"))
        outs.append(fw.op("sp", lambda e, h=h: e.dma_start(out=io["vpay"][h], in_=VST[:, h, :, :]),
                          reads=[r_vst], writes=wr(("kv", h)), kind="d"))
    if after_kv is not None:
        after_kv()
    fmajor(1, 0, 4, lambda ck, t, ps, b: fw.op(
        "act", lambda e: e.mul(out=QC[:, ck, t * 512:(t + 1) * 512], in_=ps, mul=0.125),
        reads=[ctx.r_ps[b]], writes=[ctx.r_qc[ck]]))
    return outs


def _stash_view(ctx):
    return ctx.HY[:, 0:4, :].rearrange("p k t -> p (k t)").bitcast(F32).rearrange("p (c t) -> p c t", c=2)


def emit_stash_et(ctx):
    fw = ctx.fw
    ET = carve(ctx, OFF_ET, [2, ETW], F32)
    SV = _stash_view(ctx)
    for ck in range(2):
        fw.op("dve" if ck == 0 else "act",
              (lambda e, ck=ck: e.tensor_copy(out=SV[:, ck, :], in_=ET[:, ck, 8:8 + TOK])) if ck == 0
              else (lambda e, ck=ck: e.copy(out=SV[:, ck, :], in_=ET[:, ck, 8:8 + TOK])),
              reads=[ctx.r_et[ck]], writes=[r for k in (2 * ck, 2 * ck + 1) for r in ctx.r_hy[k]])


def emit_restore_et(ctx):
    fw = ctx.fw
    ET = carve(ctx, OFF_ET, [2, ETW], F32)
    SV = _stash_view(ctx)
    for ck in range(2):
        fw.op("dve" if ck == 0 else "act",
              (lambda e, ck=ck: e.tensor_copy(out=ET[:, ck, 8:8 + TOK], in_=SV[:, ck, :])) if ck == 0
              else (lambda e, ck=ck: e.copy(out=ET[:, ck, 8:8 + TOK], in_=SV[:, ck, :])),
              reads=[r for k in (2 * ck, 2 * ck + 1) for r in ctx.r_hy[k]], writes=[ctx.r_et[ck]])


def _pool_bufs(ctx):
    fw = ctx.fw
    o = OFF_CONST + 1536
    d = dict(
        PW=carve(ctx, o, [2, 128], BF16),
        PWF=carve(ctx, o + 512, [2, 128], F32),
        PSC=carve(ctx, o + 1536, [2], F32),
        CORR=carve(ctx, o + 1544, [2, 16], F32),
        MASK=carve(ctx, o + 1672, [8], F32),
        HALO=carve(ctx, o + 1704, [2, 4, 16], F32),
    )
    if not hasattr(ctx, "r_pool"):
        ctx.r_pool = {n: fw.res(n) for n in ("pw", "pcst", "halo")}
    return d


def emit_pool_loads(ctx, l, io):
    fw = ctx.fw
    B = _pool_bufs(ctx)
    PW, PWF, PSC, CORR, MASK = B["PW"], B["PWF"], B["PSC"], B["CORR"], B["MASK"]
    r_pw, r_cst = ctx.r_pool["pw"], ctx.r_pool["pcst"]
    fw.op("pool", lambda e: e.memset(PWF, 0.0), writes=[r_pw])
    for g in range(4):
        ck, gl = g // 2, g % 2
        fw.op("sp", lambda e, g=g, ck=ck, gl=gl: e.dma_start(
            out=PWF[gl * 64:(gl + 1) * 64, ck, gl * 64:(gl + 1) * 64], in_=io["pool_w"][l][g]),
            writes=[r_pw], kind="d")
    fw.op("dve", lambda e: e.tensor_copy(out=PW, in_=PWF), reads=[r_pw], writes=[r_pw])
    fw.op("sp", lambda e: e.dma_start(out=PSC, in_=io["pool_scale_t"][l]), writes=[r_cst], kind="d")
    fw.op("sp", lambda e: e.dma_start(out=CORR, in_=io["pcorr"]), writes=[r_cst], kind="d")
    fw.op("sp", lambda e: e.dma_start(out=MASK, in_=io["pmask"]), writes=[r_cst], kind="d")


def emit_pool_halo_load(ctx, io):
    fw = ctx.fw
    HALO = _pool_bufs(ctx)["HALO"]
    hrd = [io["r_gath"]["halo"]] if "r_gath" in io else []
    for k_ in range(2):
        for r_ in range(4):
            fw.op("sp", lambda e, k_=k_, r_=r_: e.dma_start(out=HALO[:, k_, r_, :], in_=io["halo_all"][k_, :, r_, :]),
                  reads=hrd, writes=[ctx.r_pool["halo"]], kind="d")


def emit_pool(ctx, l, io, preloaded=False):
    fw = ctx.fw
    ET = carve(ctx, OFF_ET, [2, ETW], F32)
    T1 = carve(ctx, OFF_HI, [ETW], F32)
    T2 = carve(ctx, OFF_HI + 8256, [ETW], F32)
    o = OFF_HI + 16512
    MT = carve(ctx, o + 2304, [2, TOK], BF16)
    WIN = carve(ctx, o + 2304 + 8192, [TOK], F32)
    B = _pool_bufs(ctx)
    PW, PSC, CORR, MASK, HALO = B["PW"], B["PSC"], B["CORR"], B["MASK"], B["HALO"]
    if not preloaded:
        emit_pool_loads(ctx, l, io)
        emit_pool_halo_load(ctx, io)
    r_pw, r_cst, r_halo = ctx.r_pool["pw"], ctx.r_pool["pcst"], ctx.r_pool["halo"]
    r_t1, r_t2, r_win = (fw.res(n) for n in ("t1", "t2", "win"))
    r_mt = [fw.res("mt") for _ in range(2)]
    for ck in range(2):
        fw.op("dve", lambda e, ck=ck: e.tensor_scalar(out=ET[:, ck, 0:8], in0=HALO[:, ck, 0, 8:16], scalar1=MASK[:, 0:1],
                                                      scalar2=None, op0=ALU.mult),
              reads=[r_halo, r_cst], writes=[ctx.r_et[ck]])
        fw.op("dve", lambda e, ck=ck: e.tensor_scalar(out=ET[:, ck, TOK + 8:TOK + 16], in0=HALO[:, ck, 0, 0:8],
                                                      scalar1=MASK[:, 4:5], scalar2=None, op0=ALU.mult),
              reads=[r_halo, r_cst], writes=[ctx.r_et[ck]])
        for r in range(1, 4):
            fw.op("dve", lambda e, ck=ck, r=r: e.scalar_tensor_tensor(
                out=ET[:, ck, 0:8], in0=HALO[:, ck, r, 8:16], scalar=MASK[:, r:r + 1], in1=ET[:, ck, 0:8],
                op0=ALU.mult, op1=ALU.add), reads=[r_halo, r_cst], writes=[ctx.r_et[ck]])
            fw.op("dve", lambda e, ck=ck, r=r: e.scalar_tensor_tensor(
                out=ET[:, ck, TOK + 8:TOK + 16], in0=HALO[:, ck, r, 0:8], scalar=MASK[:, 4 + r:5 + r],
                in1=ET[:, ck, TOK + 8:TOK + 16], op0=ALU.mult, op1=ALU.add),
                reads=[r_halo, r_cst], writes=[ctx.r_et[ck]])
        E = ET[:, ck, :]
        fw.op("pool", lambda e, E=E: e.tensor_tensor(out=T1[:, 0:ETW - 1], in0=E[:, 0:ETW - 1], in1=E[:, 1:ETW], op=ALU.add),
              reads=[ctx.r_et[ck]], writes=[r_t1])
        if ck == 0:
            lv = {0: (T1, r_t1, 2)}
            fw.op("pool", lambda e: e.tensor_tensor(out=T2[:, 0:ETW - 3], in0=T1[:, 0:ETW - 3], in1=T1[:, 2:ETW - 1], op=ALU.add),
                  reads=[r_t1], writes=[r_t2])
            lv[1] = (T2, r_t2, 4)
        else:
            fw.op("pool", lambda e: e.tensor_tensor(out=T2[:, 0:ETW - 3], in0=T1[:, 0:ETW - 3], in1=T1[:, 2:ETW - 1], op=ALU.add),
                  reads=[r_t1], writes=[r_t2])
            fw.op("pool", lambda e: e.tensor_tensor(out=T1[:, 0:ETW - 7], in0=T2[:, 0:ETW - 7], in1=T2[:, 4:ETW - 3], op=ALU.add),
                  reads=[r_t2], writes=[r_t1])
            fw.op("pool", lambda e: e.tensor_tensor(out=T2[:, 0:ETW - 15], in0=T1[:, 0:ETW - 15], in1=T1[:, 8:ETW - 7], op=ALU.add),
                  reads=[r_t1], writes=[r_t2])
            lv = {0: (T1, r_t1, 8), 1: (T2, r_t2, 16)}
        for gl in range(2):
            src, rsrc, w = lv[gl]
            sh = 8 - w // 2
            ps_ = slice(gl * 64, (gl + 1) * 64)
            fw.op("dve", lambda e, src=src, sh=sh, ps_=ps_: e.tensor_copy(out=WIN[ps_, :], in_=src[ps_, sh:sh + TOK]),
                  reads=[rsrc], writes=[r_win])
            fw.op("dve", lambda e, ck=ck, ps_=ps_: e.tensor_tensor(out=WIN[ps_, 0:8], in0=WIN[ps_, 0:8],
                                                                  in1=CORR[ps_, ck, 0:8], op=ALU.mult),
                  reads=[r_cst], writes=[r_win])
            fw.op("dve", lambda e, ck=ck, ps_=ps_: e.tensor_tensor(out=WIN[ps_, TOK - 8:TOK], in0=WIN[ps_, TOK - 8:TOK],
                                                                  in1=CORR[ps_, ck, 8:16], op=ALU.mult),
                  reads=[r_cst], writes=[r_win])
            fw.op("dve", lambda e, ck=ck, ps_=ps_, w=w: e.scalar_tensor_tensor(
                out=MT[ps_, ck, :], in0=WIN[ps_, :], scalar=1.0 / w, in1=ET[ps_, ck, 8:8 + TOK],
                op0=ALU.mult, op1=ALU.subtract), reads=[r_win, ctx.r_et[ck]], writes=[r_mt[ck]])
    for ck in range(2):
        for t in range(NTC):
            b = t % 4
            ps = bank(ctx, b)
            fw.op("pe", lambda e, ck=ck, t=t, ps=ps: e.matmul(ps, lhsT=PW[:, ck, :], rhs=MT[:, ck, t * 512:(t + 1) * 512],
                                                            start=True, stop=True),
                  reads=[r_pw, r_mt[ck]], writes=[ctx.r_ps[b]])
            fw.op("act", lambda e, ck=ck, t=t, ps=ps: e.mul(out=ctx.HY[:, ck, t * 512:(t + 1) * 512], in_=ps, mul=PSC[:, ck:ck + 1]),
                  reads=[ctx.r_ps[b], r_cst], writes=[ctx.r_hy[ck][t]])


def emit_fourier(ctx, l, io):
    fw = ctx.fw
    XS = carve(ctx, 4096, [128, 64], BF16)
    BB = carve(ctx, 20480, [2, 64, 64], BF16)
    W3 = carve(ctx, 36864, [64, 3, 32], BF16)
    UT = carve(ctx, OFF_HI, [4, 2, TOK], BF16)
    MM = carve(ctx, OFF_HI + 32768, [4, 2, 256], BF16)
    WFS = carve(ctx, OFF_HI + 36864, [4, 256], BF16)
    tb = OFF_HI + 38912
    WA = carve(ctx, tb, [128], BF16)
    C64 = carve(ctx, tb + 960, [2, 64], BF16)
    r_tab, r_wfs, r_mm, r_xs, r_bb = (fw.res(n) for n in ("ftab", "wfs", "mm", "xs", "bb"))
    r_ut = [fw.res("ut") for _ in range(4)]
    fw.op("sp", lambda e: e.dma_start(out=WA[0:64, :], in_=io["f_wa"]), writes=[r_tab], kind="d")
    fw.op("sp", lambda e: e.dma_start(out=W3, in_=io["f_w3"]), writes=[r_tab], kind="d")
    fw.op("sp", lambda e: e.dma_start(out=C64[0:64], in_=io["f_c64"]), writes=[r_tab], kind="d")
    fw.op("pool", lambda e: e.dma_start(out=WFS[0:64], in_=io["fourier_w"][l].rearrange("(g p) c -> p g c", p=64)),
          writes=[r_wfs], kind="d")
    for g in range(4):
        for comp in range(2):
            b = (g * 2 + comp) % 4
            ps = bank(ctx, b)[0:64, 0:256]
            fw.op("pe", lambda e, g=g, comp=comp, ps=ps: e.matmul(ps, lhsT=C64[0:64, comp, :], rhs=WFS[0:64, g, :],
                                                                 start=True, stop=True),
                  reads=[r_tab, r_wfs], writes=[ctx.r_ps[b]])
            fw.op("dve", lambda e, g=g, comp=comp, ps=ps: e.tensor_copy(out=MM[0:64, g, comp, :], in_=ps),
                  reads=[ctx.r_ps[b]], writes=[r_mm])
    for g in range(4):
        if "load_xs" in io:
            io["load_xs"](g, XS, r_xs)
        else:
            fw.op("sp", lambda e, g=g: e.dma_start(out=XS[0:64], in_=io["fg"][g].rearrange("(s1 s2) c -> s1 s2 c", s2=128)),
                  writes=[r_xs], kind="d")
        for rd in range(4):
            pb0 = 4 * (rd % 2)
            PSV = ctx.PS[:, pb0 * 512:(pb0 + 4) * 512].rearrange("p (c x) -> p c x", x=128)
            rps = [ctx.r_ps[pb0 + i] for i in range(4)]
            for ci in range(16):
                c = rd * 16 + ci
                fw.op("pe", lambda e, c=c, ci=ci, PSV=PSV: e.matmul(PSV[:, ci, :], lhsT=XS[0:64, :, c], rhs=WA[0:64, :],
                                                                   start=True, stop=True),
                      reads=[r_xs, r_tab], writes=[rps[ci // 4]])
            AR = PSV[:, :, 0:64]
            AI = PSV[:, :, 64:128]
            cs = slice(rd * 16, (rd + 1) * 16)
            BRv = BB[:, 0, :, cs].rearrange("p k c -> p c k")
            BIv = BB[:, 1, :, cs].rearrange("p k c -> p c k")
            fw.op("act", lambda e, AR=AR, BRv=BRv: e.copy(out=BRv, in_=AR), reads=rps, writes=[r_bb])
            fw.op("dve", lambda e, AI=AI, BIv=BIv: e.tensor_copy(out=BIv, in_=AI), reads=rps, writes=[r_bb])
        for q4 in range(4):
            pr, pi = (q4 % 2) * 2, (q4 % 2) * 2 + 1
            for kk in range(16):
                k1 = q4 * 16 + kk
                outr = bank(ctx, pr)[0:64, kk * 32:(kk + 1) * 32]
                outi = bank(ctx, pi)[0:64, kk * 32:(kk + 1) * 32]
                fw.op("pe", lambda e, k1=k1, outr=outr: e.matmul(outr, lhsT=BB[:, 0, k1, :], rhs=W3[:, k1, 0, :], start=True, stop=False),
                      reads=[r_bb, r_tab], writes=[ctx.r_ps[pr]])
                fw.op("pe", lambda e, k1=k1, outr=outr: e.matmul(outr, lhsT=BB[:, 1, k1, :], rhs=W3[:, k1, 2, :], start=False, stop=True),
                      reads=[r_bb, r_tab], writes=[ctx.r_ps[pr]])
                fw.op("pe", lambda e, k1=k1, outi=outi: e.matmul(outi, lhsT=BB[:, 1, k1, :], rhs=W3[:, k1, 0, :], start=True, stop=False),
                      reads=[r_bb, r_tab], writes=[ctx.r_ps[pi]])
                fw.op("pe", lambda e, k1=k1, outi=outi: e.matmul(outi, lhsT=BB[:, 0, k1, :], rhs=W3[:, k1, 1, :], start=False, stop=True),
                      reads=[r_bb, r_tab], writes=[ctx.r_ps[pi]])
            for comp, pbk in ((0, pr), (1, pi)):
                src = bank(ctx, pbk)[0:64, :].rearrange("p (k j) -> p k j", j=32)
                dstv = UT[0:64, g, comp, :].rearrange("p (j k) -> p k j", k=64)[:, q4 * 16:(q4 + 1) * 16, :]
                fw.op("act" if comp else "dve",
                      (lambda e, src=src, dstv=dstv: e.copy(out=dstv, in_=src)) if comp
                      else (lambda e, src=src, dstv=dstv: e.tensor_copy(out=dstv, in_=src)),
                      reads=[ctx.r_ps[pbk]], writes=[r_ut[g]])
    for ck in range(2):
        for t in range(NTC):
            b = 4 + (ck * NTC + t) % 4
            ps = bank(ctx, b)
            n = 0
            for g in range(4):
                for comp in range(2):
                    fw.op("pe", lambda e, g=g, comp=comp, ck=ck, t=t, ps=ps, n=n: e.matmul(
                        ps, lhsT=MM[0:64, g, comp, ck * 128:(ck + 1) * 128], rhs=UT[0:64, g, comp, t * 512:(t + 1) * 512],
                        start=(n == 0), stop=(n == 7)),
                        reads=[r_mm, r_ut[g]], writes=[ctx.r_ps[b]])
                    n += 1
            fw.op("act", lambda e, ck=ck, t=t, ps=ps: e.copy(out=ctx.HY[:, 2 + ck, t * 512:(t + 1) * 512], in_=ps),
                  reads=[ctx.r_ps[b]], writes=[ctx.r_hy[2 + ck][t]])


def emit_attn(ctx, l, io):
    fw = ctx.fw
    K0 = carve(ctx, 4096, [SEQ], BF16)
    K1 = carve(ctx, 20480, [SEQ], BF16)
    Q0 = carve(ctx, 36864, [TOK], BF16)
    Q1 = carve(ctx, 40960, [TOK], BF16)
    DT = carve(ctx, 45056, [4, 128], BF16)
    o = 46080
    LAMV = carve(ctx, o, [256], F32)
    LTMP = carve(ctx, o + 1024, [64], F32)
    LS = carve(ctx, o + 1280, [8], F32)
    GN = carve(ctx, o + 1312, [4], F32)
    SQH = carve(ctx, o + 1344, [512], BF16)
    QC = carve(ctx, OFF_QC, [4, TOK], BF16)
    V = carve(ctx, OFF_HI, [64, 128], BF16)
    PT = [carve(ctx, OFF_HI + 16384 + i * 2048, [1024], BF16) for i in range(3)]
    FT = [carve(ctx, OFF_HI + 22528 + i * 2048, [512], F32) for i in range(4)]
    ACC = carve(ctx, OFF_HI + 30720, [1024], F32)
    r_acc2 = [fw.res("acc0"), fw.res("acc1")]
    r_k, r_v, r_q, r_d, r_lam = (fw.res(n) for n in ("k", "v", "q", "dt", "lam"))
    r_pt = [fw.res("pt") for _ in range(3)]
    r_ft = [fw.res("ft") for _ in range(4)]
    r_sqh = fw.res("sqh")
    LI = carve(ctx, o + 2368, [2], F32)
    fw.op("sp", lambda e: e.dma_start(out=DT, in_=io["dtile"]), writes=[r_d], kind="d")
    fw.op("sp", lambda e: e.dma_start(out=ctx.ident_bf, in_=io["ident"]), writes=[ctx.r_const], kind="d")
    fw.op("sp", lambda e: e.dma_start(out=LAMV, in_=io["lamvec"][l].partition_broadcast(128)), writes=[r_lam], kind="d")
    fw.op("sp", lambda e: e.dma_start(out=GN, in_=io["head_norm_t"][l]), writes=[r_lam], kind="d")
    fw.op("sp", lambda e: e.dma_start(out=LI, in_=io["laminit"][l]), writes=[r_lam], kind="d")
    for i in range(2):
        fw.op("dve", lambda e, i=i: e.tensor_tensor(out=LTMP, in0=LAMV[:, i * 128:i * 128 + 64],
                                                    in1=LAMV[:, i * 128 + 64:i * 128 + 128], op=ALU.mult),
              reads=[r_lam], writes=[r_lam])
        fw.op("dve", lambda e, i=i: e.reduce_sum(out=LS[:, i:i + 1], in_=LTMP, axis=mybir.AxisListType.X),
              reads=[r_lam], writes=[r_lam])
    fw.op("act", lambda e: e.activation(out=LS[:, 2:4], in_=LS[:, 0:2], func=AF.Exp), reads=[r_lam], writes=[r_lam])
    fw.op("dve", lambda e: e.tensor_tensor(out=LS[:, 4:5], in0=LS[:, 3:4], in1=LS[:, 2:3], op=ALU.subtract),
          reads=[r_lam], writes=[r_lam])
    fw.op("dve", lambda e: e.tensor_tensor(out=LS[:, 5:6], in0=LS[:, 4:5], in1=LI[:, 0:1], op=ALU.add),
          reads=[r_lam], writes=[r_lam])
    fw.op("dve", lambda e: e.tensor_scalar(out=GN, in0=GN, scalar1=LI[:, 1:2], scalar2=None, op0=ALU.mult),
          reads=[r_lam], writes=[r_lam])
    NEGLAM = LS[:, 5:6]

    it = {"n": 0}
    for h in range(HEADS):
        if "load_kv" in io:
            io["load_kv"](h, K0, K1, V, r_k, r_v)
        else:
            fw.op("sp", lambda e, h=h: e.dma_start(out=K0[0:64, :], in_=io["kg"][h, 0:64, :]), writes=[r_k], kind="d")
            fw.op("sp", lambda e, h=h: e.dma_start(out=K1[0:64, :], in_=io["kg"][h, 64:128, :]), writes=[r_k], kind="d")
            fw.op("sp", lambda e, h=h: e.dma_start(out=V, in_=io["vg"][h]), writes=[r_v], kind="d")
        fw.op("sp", lambda e, h=h: e.dma_start(out=K0[64:73, :], in_=io["kaug0"][h]), writes=[r_k], kind="d")
        fw.op("sp", lambda e, h=h: e.dma_start(out=K1[64:73, :], in_=io["kaug0"][h]), writes=[r_k], kind="d")
        fw.op("sp", lambda e, h=h: e.dma_start(out=Q0[64:73, :], in_=io["qaug0"][h]), writes=[r_q], kind="d")
        fw.op("sp", lambda e, h=h: e.dma_start(out=Q1[64:73, :], in_=io["qaug0"][h]), writes=[r_q], kind="d")
        fw.op("dve", lambda e, h=h: e.tensor_copy(out=Q0[0:64, :], in_=QC[0:64, h, :]), reads=[ctx.r_qc[h]], writes=[r_q])
        fw.op("sp", lambda e, h=h: e.dma_start(out=Q1[0:64, :], in_=QC[64:128, h, :]), reads=[ctx.r_qc[h]], writes=[r_q], kind="d")

        def s_mm(Q, L, sb, hh=h):
            ks = slice(L * 128, (L + 1) * 128)
            for j in range(2):
                KT, QT = (K0, Q0) if j == 0 else (K1, Q1)
                ps = bank(ctx, sb + j)

                def rng(mode):
                    return {"diag": slice(0, 65), "below": slice(0, 69), "above": slice(0, 73)}[mode]

                def mm(out, mode, qs, start=True, stop=True, KT=KT, QT=QT, j=j):
                    pr = rng(mode)
                    rb = ctx.r_ps[sb + j]
                    fw.op("pe", lambda e: e.matmul(out, lhsT=KT[pr, ks], rhs=QT[pr, qs], start=start, stop=stop),
                          reads=[r_k, r_q], writes=[rb])

                if L >= 16 or L < 4 * Q:
                    mm(ps, "below", slice(Q * 512, (Q + 1) * 512))
                elif L >= 4 * Q + 4:
                    mm(ps, "above", slice(Q * 512, (Q + 1) * 512))
                else:
                    us = L - 4 * Q
                    for u in range(4):
                        qs = slice(Q * 512 + u * 128, Q * 512 + (u + 1) * 128)
                        out = ps[:, u * 128:(u + 1) * 128]
                        if u > us:
                            mm(out, "below", qs)
                        elif u < us:
                            mm(out, "above", qs)
                        else:
                            mm(out, "diag", qs, start=True, stop=False)
                            fw.op("pe", lambda e, out=out, hh=hh: e.matmul(out, lhsT=ctx.ident_bf, rhs=DT[:, hh, :], start=False, stop=True),
                                  reads=[ctx.r_const, r_d], writes=[ctx.r_ps[sb + j]])

        for Q in range(4):
            qcols = slice(Q * 512, (Q + 1) * 512)
            if BAND[h] is None:
                Ls = list(range(64))
            else:
                Ls = [(4 * Q + d_) % 64 for d_ in range(-BAND[h], BAND[h] + 4)]
            nL = len(Ls)
            s_mm(Q, Ls[0], 0)
            for li, L in enumerate(Ls):
                sb = 2 * (li % 2)
                if li + 1 < nL:
                    s_mm(Q, Ls[li + 1], 2 * ((li + 1) % 2))
                pi = it["n"] % 3
                it["n"] += 1
                fw.op("act", lambda e, sb=sb, pi=pi: e.activation(out=PT[pi], in_=ctx.PS[:, sb * 512:(sb + 2) * 512], func=AF.Exp),
                      reads=[ctx.r_ps[sb], ctx.r_ps[sb + 1]], writes=[r_pt[pi]])
                for j in range(2):
                    fw.op("pe", lambda e, L=L, j=j, pi=pi, li=li, nL=nL: e.matmul(bank(ctx, 4 + j), lhsT=V[:, L, :], rhs=PT[pi][:, j * 512:(j + 1) * 512],
                                                                   start=(li == 0), stop=(li == nL - 1)),
                          reads=[r_v, r_pt[pi]], writes=[ctx.r_ps[4 + j]])
                fw.op("pe", lambda e, pi=pi, li=li, nL=nL: e.matmul(bank(ctx, 6), lhsT=ctx.ones_bf, rhs=PT[pi][:, 0:512],
                                                              start=(li == 0), stop=(li == nL - 1)),
                      reads=[ctx.r_const, r_pt[pi]], writes=[ctx.r_ps[6]])
                if li == 0:
                    fw.op("dve", lambda e, pi=pi: e.tensor_copy(out=ACC[:, 512:1024], in_=PT[pi][:, 512:1024]),
                          reads=[r_pt[pi]], writes=[r_acc2[1]])
                else:
                    fw.op("dve", lambda e, pi=pi: e.tensor_tensor(out=ACC[:, 512:1024], in0=ACC[:, 512:1024],
                                                                  in1=PT[pi][:, 512:1024], op=ALU.add),
                          reads=[r_pt[pi]], writes=[r_acc2[1]])
            fw.op("pe", lambda e: e.matmul(bank(ctx, 7), lhsT=ctx.ones_f, rhs=ACC[:, 512:1024], start=True, stop=True),
                  reads=[ctx.r_const, r_acc2[1]], writes=[ctx.r_ps[7]])
            fw.op("dve", lambda e: e.reciprocal(out=FT[0], in_=bank(ctx, 6)), reads=[ctx.r_ps[6]], writes=[r_ft[0]])
            fw.op("dve", lambda e: e.reciprocal(out=FT[1], in_=bank(ctx, 7)), reads=[ctx.r_ps[7]], writes=[r_ft[1]])
            fw.op("dve", lambda e: e.tensor_tensor(out=FT[0], in0=bank(ctx, 4), in1=FT[0], op=ALU.mult),
                  reads=[ctx.r_ps[4], r_ft[0]], writes=[r_ft[0]])
            fw.op("dve", lambda e: e.tensor_tensor(out=FT[1], in0=bank(ctx, 5), in1=FT[1], op=ALU.mult),
                  reads=[ctx.r_ps[5], r_ft[1]], writes=[r_ft[1]])
            fw.op("dve", lambda e: e.scalar_tensor_tensor(out=FT[2], in0=FT[1], scalar=NEGLAM, in1=FT[0],
                                                          op0=ALU.mult, op1=ALU.add),
                  reads=[r_ft[0], r_ft[1], r_lam], writes=[r_ft[2]])
            fw.op("act", lambda e: e.activation(out=SQH, in_=FT[2], func=AF.Square), reads=[r_ft[2]], writes=[r_sqh])
            fw.op("pe", lambda e: e.matmul(bank(ctx, 6), lhsT=ctx.ones_bf, rhs=SQH, start=True, stop=True),
                  reads=[ctx.r_const, r_sqh], writes=[ctx.r_ps[6]])
            fw.op("act", lambda e: e.activation(out=FT[3], in_=bank(ctx, 6), func=AF.Sqrt, bias=ctx.eps_col, scale=1.0 / 128.0),
                  reads=[ctx.r_ps[6], ctx.r_const], writes=[r_ft[3]])
            fw.op("dve", lambda e: e.reciprocal(out=FT[3], in_=FT[3]), reads=[r_ft[3]], writes=[r_ft[3]])
            t = Q
            fw.op("dve", lambda e, h=h, qcols=qcols: e.scalar_tensor_tensor(
                out=ctx.HY[:, 4 + h, qcols], in0=FT[2], scalar=GN[:, h:h + 1], in1=FT[3], op0=ALU.mult, op1=ALU.mult),
                reads=[r_ft[2], r_ft[3], r_lam], writes=[ctx.r_hy[4 + h][t]])


def emit_wout(ctx, l, io):
    fw = ctx.fw
    WO = carve(ctx, 4096, [KC, 1024], BF16)
    r_wo = [fw.res("wo") for _ in range(2)]
    wv = io["w_out"][l].rearrange("(k p) d -> p k d", p=128)
    for hlf in range(2):
        fw.op("pool", lambda e, hlf=hlf: e.dma_start(out=WO[:, hlf * 4:(hlf + 1) * 4, :], in_=wv[:, hlf * 4:(hlf + 1) * 4, :]),
              writes=[r_wo[hlf]], kind="d")
    n = 0
    for dc in range(KC):
        for t in range(NTC):
            b = n % 4
            n += 1
            ps = bank(ctx, b)
            for k in range(KC):
                fw.op("pe", lambda e, k=k, dc=dc, t=t, ps=ps: e.matmul(
                    ps, lhsT=WO[:, k, dc * 128:(dc + 1) * 128], rhs=ctx.HY[:, k, t * 512:(t + 1) * 512],
                    start=(k == 0), stop=(k == KC - 1)),
                    reads=[r_wo[k // 4], ctx.r_hy[k][t]], writes=[ctx.r_ps[b]])
            fw.op("dve", lambda e, dc=dc, t=t, ps=ps: e.tensor_tensor(
                out=ctx.XT[:, dc, t * 512:(t + 1) * 512], in0=ps, in1=ctx.XT[:, dc, t * 512:(t + 1) * 512], op=ALU.add),
                reads=[ctx.r_ps[b]], writes=[ctx.r_xt[dc][t]])


BIG_A = {"ffn2_w_gate": [D_MODEL, D_FF], "ffn2_w_up": [D_MODEL, D_FF], "ffn2_w_down": [D_FF, D_MODEL],
         "w_out": [D_MODEL, D_MODEL]}
BIG_B = {"ffn1_w_gate": [D_MODEL, D_FF], "ffn1_w_up": [D_MODEL, D_FF], "ffn1_w_down": [D_FF, D_MODEL],
         "w_in": [D_MODEL, 2048]}
SMALL_A = {"ffn2_norm": [D_MODEL], "pool_w": [4, 64, 64], "fourier_w": [256, 256], "pool_scale_t": [128, 2],
           "head_norm_t": [128, 4], "lamvec": [256], "laminit": [128, 2]}
SMALL_B = {"ffn1_norm": [D_MODEL], "mix_norm": [D_MODEL]}
TABLE_SPECS = {
    "pcorr": ([128, 2, 16], F32), "pmask": ([128, 8], F32),
    "f_wa": ([64, 128], BF16),
    "f_w3": ([128, 64, 3, 32], BF16), "f_c64": ([64, 2, 64], BF16),
    "dtile": ([128, 4, 128], BF16), "ident": ([128, 128], BF16),
    "kaug0": ([4, 9, SEQ], BF16), "qaug0": ([4, 9, TOK], BF16),
}
PAY_SPECS = {
    "kpay": ([4, 128, TOK], BF16), "vpay": ([4, 128, 16, 128], BF16), "fpay": ([4, TOK, 64], BF16),
    "hpay": ([2, 128, 16], F32), "qc_out": ([128, 4, TOK], BF16), "et_out": ([128, 2, TOK], F32),
    "x_out": ([D_MODEL, TOK], F32),
}
GATH_SPECS = {
    "kg": ([4, 128, SEQ], BF16), "vg": ([4, 128, 64, 128], BF16), "fg": ([4, SEQ, 64], BF16),
    "halo_all": ([2, 128, 4, 16], F32), "qc_in": ([128, 4, TOK], BF16), "et_in": ([128, 2, TOK], F32),
}


class _One:
    def __init__(self, ap):
        self.ap = ap

    def __getitem__(self, _):
        return self.ap


def build_launch(kind, dbg_phases=None):
    nc = bass.Bass("TRN2", target_bir_lowering=False)
    io = {}

    def inp(name, shp, dt=F32):
        return nc.dram_tensor(name, shp, dt, kind="ExternalInput").ap()

    io["x_in"] = inp("x_in", [D_MODEL, TOK])
    if kind != "first":
        for n, shp in {**BIG_A, **SMALL_A}.items():
            io[n] = _One(inp(n, shp))
        for n, (shp, dt) in TABLE_SPECS.items():
            io[n] = inp(n, shp, dt)
        for n, (shp, dt) in GATH_SPECS.items():
            io[n] = inp(n, shp, dt)
    if kind != "last":
        for n, shp in {**BIG_B, **SMALL_B}.items():
            io[n] = _One(inp(n, shp))
        for n, (shp, dt) in PAY_SPECS.items():
            io[n] = nc.dram_tensor(n, shp, dt, kind="ExternalOutput").ap()
    else:
        io["final_norm"] = inp("final_norm", [D_MODEL])
        io["y"] = nc.dram_tensor("y", [D_MODEL, TOK], F32, kind="ExternalOutput").ap()
    with ExitStack() as stack:
        ctx = Ctx()
        ctx.nc = nc
        fw = ctx.fw = FW(nc, stack)
        setup_memory(nc, stack, ctx)
        emit_consts(ctx)
        emit_load_x(ctx, io["x_in"])
        if kind != "first":
            ET = carve(ctx, OFF_ET, [2, ETW], F32)
            QC = carve(ctx, OFF_QC, [4, TOK], BF16)
            ctx.r_et = [fw.res("et") for _ in range(2)]
            ctx.r_qc = [fw.res("qc") for _ in range(4)]
            for ck in range(2):
                fw.op("sp", lambda e, ck=ck: e.dma_start(out=ET[:, ck, 8:8 + TOK], in_=io["et_in"][:, ck, :]),
                      writes=[ctx.r_et[ck]], kind="d")
            for h in range(4):
                fw.op("sp", lambda e, h=h: e.dma_start(out=QC[:, h, :], in_=io["qc_in"][:, h, :]),
                      writes=[ctx.r_qc[h]], kind="d")
            if dbg_phases is None or "pool" in dbg_phases:
                emit_pool(ctx, 0, io)
                fw.barrier()
            if dbg_phases is None or "fourier" in dbg_phases:
                emit_fourier(ctx, 0, io)
                fw.barrier()
            if dbg_phases is None or "attn" in dbg_phases:
                emit_attn(ctx, 0, io)
                fw.barrier()
            if dbg_phases is not None:
                hy_out = nc.dram_tensor("hy_out", [128, KC, TOK], BF16, kind="ExternalOutput").ap()
                for k in range(KC):
                    fw.op("sp", lambda e, k=k: e.dma_start(out=hy_out[:, k, :], in_=ctx.HY[:, k, :]),
                          reads=ctx.r_hy[k], kind="d")
                fw.barrier()
                fw.op("sp", None)
                fw.emit()
                return nc
            emit_wout(ctx, 0, io)
            fw.barrier()
            emit_ffn(ctx, io["ffn2_norm"][0], io["ffn2_w_gate"][0], io["ffn2_w_up"][0], io["ffn2_w_down"][0], OFF_DYN)
            fw.barrier()
            fw.new_epoch()
        if kind == "last":
            emit_final_norm(ctx, io["final_norm"], io["y"], OFF_DYN)
        else:
            emit_ffn(ctx, io["ffn1_norm"][0], io["ffn1_w_gate"][0], io["ffn1_w_up"][0], io["ffn1_w_down"][0], OFF_DYN)
            fw.barrier()
            emit_proj(ctx, 0, io)
            ET = carve(ctx, OFF_ET, [2, ETW], F32)
            QC = carve(ctx, OFF_QC, [4, TOK], BF16)
            for ck in range(2):
                fw.op("sp", lambda e, ck=ck: e.dma_start(out=io["et_out"][:, ck, :], in_=ET[:, ck, 8:8 + TOK]),
                      reads=[ctx.r_et[ck]], kind="d")
            for h in range(4):
                fw.op("sp", lambda e, h=h: e.dma_start(out=io["qc_out"][:, h, :], in_=QC[:, h, :]),
                      reads=[ctx.r_qc[h]], kind="d")
            fw.barrier()
            emit_store_x(ctx, io["x_out"])
        fw.barrier()
        fw.op("sp", None)
        fw.emit()
        nc._fw_stats = (len(fw.ops), fw.n_waits, dict(fw.count_log), max(dma_v for dma_v in [0]))
    return nc


def _bf(a):
    return np.asarray(a, dtype=np.float32).astype(ml_dtypes.bfloat16)


def make_tables(r):
    t = {}
    pcorr = np.ones((128, 2, 16), np.float32)
    for ck in range(2):
        for p in range(128):
            w = POOL_W[2 * ck + p // 64]
            left = w // 2
            right = w - 1 - left
            for i in range(8):
                if r == 0:
                    tt = i
                    cnt = min(tt + right + 1, SEQ) - max(tt - left, 0)
                    pcorr[p, ck, i] = w / cnt
                if r == 3:
                    tt = SEQ - 8 + i
                    cnt = min(tt + right + 1, SEQ) - max(tt - left, 0)
                    pcorr[p, ck, 8 + i] = w / cnt
    t["pcorr"] = pcorr
    pmask = np.zeros((128, 8), np.float32)
    if r - 1 >= 0:
        pmask[:, r - 1] = 1.0
    if r + 1 <= 3:
        pmask[:, 4 + r + 1] = 1.0
    t["pmask"] = pmask
    s1 = np.arange(64)[:, None]
    k1 = np.arange(64)[None, :]
    ang = 2 * np.pi * ((s1 * k1) % 64) / 64.0
    t["f_wa"] = _bf(np.concatenate([np.cos(ang), -np.sin(ang)], axis=1))
    s2 = np.arange(128)[:, None]
    ang = 2 * np.pi * ((s2 * k1) % SEQ) / float(SEQ)
    t["f_tr"] = (np.cos(ang) * FNORM).astype(np.float32)
    t["f_ti"] = (-np.sin(ang) * FNORM).astype(np.float32)
    del t["f_tr"], t["f_ti"]
    s2c = np.arange(128, dtype=np.int64)[:, None, None]
    k1c = np.arange(64, dtype=np.int64)[None, :, None]
    k2c = (32 * r + np.arange(32, dtype=np.int64))[None, None, :]
    ph = 2 * np.pi * (((k1c * s2c) + 64 * (k2c * s2c)) % SEQ) / float(SEQ)
    wr_ = np.cos(ph) * FNORM
    wi_ = -np.sin(ph) * FNORM
    t["f_w3"] = _bf(np.stack([wr_, wi_, -wi_], axis=2))
    c = np.arange(64)[:, None]
    cp = np.arange(64)[None, :]
    ang = 2 * np.pi * ((c * cp) % 64) / 64.0
    t["f_c64"] = _bf(np.stack([np.cos(ang), np.sin(ang)], axis=1))
    p = np.arange(128)
    dt = np.zeros((128, 4, 128), np.float32)
    for h in range(4):
        dt[:, h, :] = -SLOPES[h] * np.abs(p[:, None] - p[None, :])
    t["dtile"] = _bf(dt)
    t["ident"] = _bf(np.eye(128))
    L = np.arange(64)
    n = (16 * r + L) % 64
    sig = np.where(L < 16, 1.0, np.where(n < 16 * r, 1.0, -1.0))
    ncol = np.repeat(n, 128).astype(np.float64)
    sigc = np.repeat(sig, 128)
    pcol = np.tile(p, 64).astype(np.float64)
    kaug0 = np.zeros((4, 9, SEQ), np.float32)
    kaug1 = np.zeros((4, 64, SEQ), np.float32)
    qaug0 = np.zeros((4, 9, TOK), np.float32)
    qaug1 = np.zeros((4, 64, TOK), np.float32)
    tq = 2048 * r + np.arange(TOK)
    nq = (tq // 256).astype(np.float64)
    bq = (tq % 256).astype(np.float64)
    for h in range(4):
        m = SLOPES[h]
        A = np.stack([sigc * m * 128.0 * ncol, sigc * m * pcol, sigc, sigc])
        B = np.stack([np.ones(TOK), np.ones(TOK), -m * 256.0 * nq, -m * bq])
        kaug0[h, 0] = 1.0
        kaug0[h, 1:5] = A
        kaug0[h, 5:9] = A
        kaug1[h, 0:4] = A
        kaug1[h, 32:36] = A
        qaug0[h, 0] = 0.0
        qaug0[h, 1:5] = B
        qaug0[h, 5:9] = -2.0 * B
        qaug1[h, 32:36] = B
        qaug1[h, 0:4] = -2.0 * B
    for nm, a in (("kaug0", kaug0), ("qaug0", qaug0)):
        b = _bf(a)
        assert np.array_equal(b.astype(np.float32), a), nm
        t[nm] = b
    return t


def _layer_small(inputs, l):
    f32 = np.float32
    d = {}
    d["pool_scale_t"] = np.ascontiguousarray(np.asarray(inputs["pool_scale"][l], f32).reshape(2, 128).T)
    d["head_norm_t"] = np.ascontiguousarray(np.asarray(inputs["attn_head_norm"][l], f32).reshape(4, 128).T)
    d["lamvec"] = np.concatenate([np.asarray(inputs[k][l], f32) for k in ("lam_q1", "lam_k1", "lam_q2", "lam_k2")])
    li = lambda_init_fn(l)
    d["laminit"] = np.tile(np.array([[-li, 1.0 - li]], f32), (128, 1))
    return d


def _run(nc, in_maps):
    res = run_bass_kernel_spmd(nc, in_maps, core_ids=list(range(NCORES)))
    return res.results


class _Lay:
    def __init__(self, ap):
        self.ap = ap

    def __getitem__(self, l):
        return self.ap[l]


FUSED_W = {
    "ffn1_norm": [DEPTH, D_MODEL], "ffn1_w_gate": [DEPTH, D_MODEL, D_FF], "ffn1_w_up": [DEPTH, D_MODEL, D_FF],
    "ffn1_w_down": [DEPTH, D_FF, D_MODEL], "mix_norm": [DEPTH, D_MODEL], "w_in": [DEPTH, D_MODEL, 2048],
    "pool_w": [DEPTH, 4, 64, 64], "fourier_w": [DEPTH, 256, 256], "w_out": [DEPTH, D_MODEL, D_MODEL],
    "ffn2_norm": [DEPTH, D_MODEL], "ffn2_w_gate": [DEPTH, D_MODEL, D_FF], "ffn2_w_up": [DEPTH, D_MODEL, D_FF],
    "ffn2_w_down": [DEPTH, D_FF, D_MODEL],
    "pool_scale_t": [DEPTH, 128, 2], "head_norm_t": [DEPTH, 128, 4], "lamvec": [DEPTH, 256], "laminit": [DEPTH, 128, 2],
}
GROUPS = [[0, 1, 2, 3], [4, 5, 6, 7]]


def build_fused(depth=DEPTH):
    nc = bass.Bass("TRN2", target_bir_lowering=False)
    io = {}

    def inp(name, shp, dt=F32):
        return nc.dram_tensor(name, shp, dt, kind="ExternalInput").ap()

    io["x_in"] = inp("x_in", [D_MODEL, TOK])
    for n, shp in FUSED_W.items():
        io[n] = _Lay(inp(n, shp))
    io["final_norm"] = inp("final_norm", [D_MODEL])
    for n, (shp, dt) in TABLE_SPECS.items():
        io[n] = inp(n, shp, dt)
    io["y"] = nc.dram_tensor("y", [D_MODEL, TOK], F32, kind="ExternalOutput").ap()
    pay_kv = [nc.dram_tensor(f"pay_kv{h}", [256, TOK], BF16) for h in range(4)]
    kvg = [nc.dram_tensor(f"kvg{h}", [4 * 256, TOK], BF16) for h in range(4)]
    pay_f = nc.dram_tensor("pay_f", [4 * TOK, 64], BF16)
    fgat = nc.dram_tensor("fgat", [4 * 4 * TOK, 64], BF16)
    pay_h = nc.dram_tensor("pay_h", [256, 16], F32)
    hgat = nc.dram_tensor("hgat", [4 * 256, 16], F32)
    io["kpay"] = [pay_kv[h].ap()[0:128, :] for h in range(4)]
    io["vpay"] = [pay_kv[h].ap()[128:256, :].rearrange("p (t e) -> p t e", e=128) for h in range(4)]
    io["fpay"] = [pay_f.ap()[g * TOK:(g + 1) * TOK, :] for g in range(4)]
    io["hpay"] = pay_h.ap().rearrange("(k p) j -> k p j", p=128)
    io["halo_all"] = hgat.ap().rearrange("(r k p) j -> k p r j", r=4, k=2)
    with ExitStack() as stack:
        ctx = Ctx()
        ctx.nc = nc
        fw = ctx.fw = FW(nc, stack)
        setup_memory(nc, stack, ctx)
        rp = {"f": fw.res("pay_f"), "halo": fw.res("pay_h")}
        rg = {"f": fw.res("fgat"), "halo": fw.res("hgat")}
        for h in range(4):
            rp[("kv", h)] = fw.res("pay_kv")
            rg[("kv", h)] = fw.res("kvg")
        io["r_pay"] = rp
        io["r_gath"] = rg

        def cc(src, dst, key):
            fw.op("pool", lambda e: e.collective_compute("AllGather", ALU.bypass, replica_groups=GROUPS,
                                                         ins=[src.ap().opt()], outs=[dst.ap().opt()]),
                  reads=[rp[key]], writes=[rg[key]], kind="cc")

        def after_f():
            cc(pay_h, hgat, "halo")
            emit_pool_halo_load(ctx, io)

        def after_kv():
            for h in range(4):
                cc(pay_kv[h], kvg[h], ("kv", h))
            cc(pay_f, fgat, "f")

        def load_xs(g, XS, r_xs):
            fv = fgat.ap()
            for j in range(4):
                base = j * 4 * TOK + g * TOK
                fw.op("sp", lambda e, j=j, base=base: e.dma_start(
                    out=XS[16 * j:16 * (j + 1)], in_=fv[base:base + TOK, :].rearrange("(s1 s2) c -> s1 s2 c", s2=128)),
                    reads=[rg["f"]], writes=[r_xs], kind="d")

        def load_kv(h, K0, K1, V, r_k, r_v):
            kv = kvg[h].ap()
            for i in range(4):
                def mk(i=i, part=0):
                    def f(e):
                        rank = (ctx.pid + i) % 4
                        if part == 0:
                            return e.dma_start(out=K0[0:64, i * TOK:(i + 1) * TOK], in_=kv[bass.ds(rank * 256, 64), :])
                        if part == 1:
                            return e.dma_start(out=K1[0:64, i * TOK:(i + 1) * TOK], in_=kv[bass.ds(rank * 256 + 64, 64), :])
                        return e.dma_start(out=V[:, 16 * i:16 * (i + 1), :],
                                           in_=kv[bass.ds(rank * 256 + 128, 128), :].rearrange("p (t e) -> p t e", e=128))
                    return f
                fw.op("sp", mk(i, 0), reads=[rg[("kv", h)]], writes=[r_k], kind="d")
                fw.op("sp", mk(i, 1), reads=[rg[("kv", h)]], writes=[r_k], kind="d")
                fw.op("sp", mk(i, 2), reads=[rg[("kv", h)]], writes=[r_v], kind="d")

        io["load_xs"] = load_xs
        io["load_kv"] = load_kv

        def _pro(e):
            ctx.pid = nc.partition_id([mybir.EngineType.SP])
        fw.sp_prologue = _pro
        io["fg"] = None
        emit_consts(ctx)
        emit_load_x(ctx, io["x_in"])
        for l in range(depth):
            emit_ffn(ctx, io["ffn1_norm"][l], io["ffn1_w_gate"][l], io["ffn1_w_up"][l], io["ffn1_w_down"][l], OFF_DYN)
            fw.barrier()
            emit_pool_loads(ctx, l, io)
            emit_proj(ctx, l, io, after_f=after_f, after_kv=after_kv)
            fw.barrier()
            fw.new_epoch()
            emit_pool(ctx, l, io, preloaded=True)
            fw.barrier()
            emit_attn(ctx, l, io)
            fw.barrier()
            emit_fourier(ctx, l, io)
            fw.barrier()
            emit_wout(ctx, l, io)
            fw.barrier()
            emit_ffn(ctx, io["ffn2_norm"][l], io["ffn2_w_gate"][l], io["ffn2_w_up"][l], io["ffn2_w_down"][l], OFF_DYN)
            fw.barrier()
            fw.new_epoch()
        emit_final_norm(ctx, io["final_norm"], io["y"], OFF_DYN)
        fw.barrier()
        fw.op("sp", None)
        fw.emit()
        nc._fw_stats = (len(fw.ops), fw.n_waits, dict(fw.count_log))
    return nc


def fused_inputs(inputs, depth=DEPTH):
    f32 = np.float32
    x = np.asarray(inputs["x"], f32)
    shared = {}
    for n in FUSED_W:
        if n in inputs:
            shared[n] = np.ascontiguousarray(np.asarray(inputs[n], f32))
    sm = [_layer_small(inputs, l) for l in range(DEPTH)]
    for n in ("pool_scale_t", "head_norm_t", "lamvec", "laminit"):
        shared[n] = np.ascontiguousarray(np.stack([sm[l][n] for l in range(DEPTH)]).astype(f32))
    shared["final_norm"] = np.asarray(inputs["final_norm"], f32)
    tables = [make_tables(r) for r in range(4)]
    in_maps = []
    for c in range(NCORES):
        b, r = c // 4, c % 4
        d = {"x_in": np.ascontiguousarray(x[b, r * TOK:(r + 1) * TOK, :].T)}
        d.update(shared)
        d.update(tables[r])
        in_maps.append(d)
    return in_maps


def kernel(**inputs):
    in_maps = fused_inputs(inputs)
    outs = _run(_prog("fused"), in_maps)
    out = np.empty((BATCH, SEQ, D_MODEL), np.float32)
    for c in range(NCORES):
        b, r = c // 4, c % 4
        out[b, r * TOK:(r + 1) * TOK, :] = np.asarray(outs[c]["y"], np.float32).T
    return out


_PROGS = {}


def _prog(kind):
    if kind not in _PROGS:
        _PROGS[kind] = build_fused() if kind == "fused" else build_launch(kind)
    return _PROGS[kind]


def kernel_unfused(**inputs):
    f32 = np.float32
    x = np.asarray(inputs["x"], f32)
    tables = [make_tables(r) for r in range(4)]

    def wA(l):
        d = {n: np.ascontiguousarray(np.asarray(inputs[n][l], f32)) for n in BIG_A}
        d["ffn2_norm"] = np.asarray(inputs["ffn2_norm"][l], f32)
        d["pool_w"] = np.asarray(inputs["pool_w"][l], f32)
        d["fourier_w"] = np.asarray(inputs["fourier_w"][l], f32)
        d.update(_layer_small(inputs, l))
        return d

    def wB(l):
        d = {n: np.ascontiguousarray(np.asarray(inputs[n][l], f32)) for n in BIG_B}
        d["ffn1_norm"] = np.asarray(inputs["ffn1_norm"][l], f32)
        d["mix_norm"] = np.asarray(inputs["mix_norm"][l], f32)
        return d

    def gathered(outs):
        g = []
        for c in range(NCORES):
            b, r = c // 4, c % 4
            grp = [outs[4 * b + j] for j in range(4)]
            rot = [grp[(r + i) % 4] for i in range(4)]
            d = {}
            d["kg"] = np.ascontiguousarray(np.concatenate([o["kpay"] for o in rot], axis=2))
            d["vg"] = np.ascontiguousarray(np.concatenate([o["vpay"] for o in rot], axis=2))
            d["fg"] = np.ascontiguousarray(np.concatenate([o["fpay"] for o in grp], axis=1))
            d["halo_all"] = np.ascontiguousarray(np.stack([o["hpay"] for o in grp], axis=2))
            d["qc_in"] = outs[c]["qc_out"]
            d["et_in"] = outs[c]["et_out"]
            d["x_in"] = outs[c]["x_out"]
            g.append(d)
        return g

    b0 = wB(0)
    in_maps = []
    for c in range(NCORES):
        b, r = c // 4, c % 4
        d = {"x_in": np.ascontiguousarray(x[b, r * TOK:(r + 1) * TOK, :].T)}
        d.update(b0)
        in_maps.append(d)
    outs = _run(_prog("first"), in_maps)
    for l in range(1, DEPTH):
        g = gathered(outs)
        a, bb = wA(l - 1), wB(l)
        in_maps = []
        for c in range(NCORES):
            d = dict(g[c])
            d.update(a)
            d.update(bb)
            d.update(tables[c % 4])
            in_maps.append(d)
        outs = _run(_prog("mid"), in_maps)
    g = gathered(outs)
    a = wA(DEPTH - 1)
    in_maps = []
    for c in range(NCORES):
        d = dict(g[c])
        d.update(a)
        d.update(tables[c % 4])
        d["final_norm"] = np.asarray(inputs["final_norm"], f32)
        in_maps.append(d)
    outs = _run(_prog("last"), in_maps)
    out = np.empty((BATCH, SEQ, D_MODEL), f32)
    for c in range(NCORES):
        b, r = c // 4, c % 4
        out[b, r * TOK:(r + 1) * TOK, :] = np.asarray(outs[c]["y"], f32).T
    return out
```

```python
import math
from contextlib import ExitStack

import numpy as np
import ml_dtypes

import concourse.bass as bass
import concourse.mybir as mybir
from concourse.bass_utils import run_bass_kernel_spmd

F32 = mybir.dt.float32
BF16 = mybir.dt.bfloat16
AF = mybir.ActivationFunctionType
ALU = mybir.AluOpType

D_MODEL = 1024
BATCH = 2
SEQ = 8192
DEPTH = 4
D_FF = 2816
NCORES = 8
TOK = 2048
NTC = 4
KC = 8
EPS = 1e-6
HEADS = 4
SLOPES = [2.0 ** (-8.0 * (i + 1) / HEADS) for i in range(HEADS)]
POOL_W = (2, 4, 8, 16)
BAND = [4, 16, None, None]


def lambda_init_fn(layer_idx):
    return 0.8 - 0.6 * math.exp(-0.3 * layer_idx)


class Res:
    __slots__ = ("name", "last_w", "readers")

    def __init__(self, name):
        self.name = name
        self.last_w = None
        self.readers = []


class Op:
    __slots__ = ("eng", "fn", "deps", "kind", "signal", "has_dep", "idx")

    def __init__(self, eng, fn, kind):
        self.eng = eng
        self.fn = fn
        self.deps = set()
        self.kind = kind
        self.signal = None
        self.has_dep = False
        self.idx = -1


ENGS = ("pe", "act", "dve", "pool", "sp")


class FW:
    def __init__(self, nc, stack, n_dma_sems=24, n_cc_sems=4):
        self.nc = nc
        self.stack = stack
        self.ops = []
        self.last_op = {e: None for e in ENGS}
        self.pending = {e: [] for e in ENGS}
        self.outstanding_dma = []
        self.n_dma_sems = n_dma_sems
        self.n_cc_sems = n_cc_sems
        self.epoch_marks = []

    def res(self, name="r"):
        return Res(name)

    def op(self, eng, fn, reads=(), writes=(), kind="c", after_barrier=True):
        o = Op(eng, fn, kind)
        o.idx = len(self.ops)
        for r in reads:
            if r.last_w is not None:
                o.deps.add(r.last_w)
            if kind == "c":
                r.readers = [x for x in r.readers if not (x.kind == "c" and x.eng == eng)]
            r.readers.append(o)
        for w in writes:
            if w.last_w is not None:
                o.deps.add(w.last_w)
            for rd in w.readers:
                if rd is not o:
                    o.deps.add(rd)
            w.last_w = o
            w.readers = []
        if after_barrier and self.pending[eng]:
            o.deps.update(self.pending[eng])
            self.pending[eng] = []
        o.deps.discard(o)
        self.ops.append(o)
        if kind != "cc":
            self.last_op[eng] = o
        if kind == "d":
            self.outstanding_dma.append(o)
        return o

    def barrier(self):
        col = [o for o in self.last_op.values() if o is not None] + list(self.outstanding_dma)
        for e in ENGS:
            self.pending[e] = list(col) + self.pending[e]
        self.outstanding_dma = []

    def new_epoch(self):
        self.epoch_marks.append(len(self.ops))

    def emit(self):
        nc = self.nc
        st = self.stack
        for o in self.ops:
            keep = set()
            for p in o.deps:
                if p.eng == "pe" and o.eng == "pe" and p.kind == "c" and o.kind == "c":
                    continue
                keep.add(p)
                p.has_dep = True
            o.deps = keep
        n_epochs = len(self.epoch_marks) + 1
        eng_sems = {e: [st.enter_context(nc.semaphore(f"s_{e}_{k}")) for k in range(n_epochs)] for e in ENGS}
        dma_sems = [st.enter_context(nc.semaphore(f"s_dma_{k}")) for k in range(self.n_dma_sems)]
        n_sw = 8
        pool_of = {"pool": list(range(0, n_sw)), "sp": list(range(n_sw, self.n_dma_sems))}
        rr = {"pool": 0, "sp": 0}
        cc_sems = [st.enter_context(nc.semaphore(f"s_cc_{k}")) for k in range(self.n_cc_sems)]
        epoch = 0
        marks = list(self.epoch_marks)
        counters = {e: 0 for e in ENGS}
        dma_rr = 0
        cc_rr = 0
        dma_tot = [0] * self.n_dma_sems
        dma_prev = [None] * self.n_dma_sems
        cc_tot = [0] * self.n_cc_sems
        cc_prev = [None] * self.n_cc_sems
        pre_wait = {}
        for o in self.ops:
            while marks and o.idx >= marks[0]:
                marks.pop(0)
                epoch += 1
                counters = {e: 0 for e in ENGS}
            if o.kind == "d":
                lst = pool_of[o.eng]
                k = lst[rr[o.eng] % len(lst)]
                rr[o.eng] += 1
                if dma_prev[k] is not None:
                    pre_wait[o] = dma_prev[k].signal
                dma_tot[k] += 16
                o.signal = (dma_sems[k], dma_tot[k])
                dma_prev[k] = o
            elif o.kind == "cc":
                k = cc_rr
                cc_rr = (cc_rr + 1) % self.n_cc_sems
                if cc_prev[k] is not None:
                    pre_wait[o] = cc_prev[k].signal
                cc_tot[k] += 1
                o.signal = (cc_sems[k], cc_tot[k])
                cc_prev[k] = o
            elif o.has_dep:
                counters[o.eng] += 1
                o.signal = (eng_sems[o.eng][epoch], counters[o.eng])
                self.max_count = max(getattr(self, "max_count", 0), counters[o.eng])
                self.count_log = getattr(self, "count_log", {})
                self.count_log[(epoch, o.eng)] = counters[o.eng]
        by_eng = {e: [o for o in self.ops if o.eng == e] for e in ENGS}
        self.n_waits = 0

        def run(eng_name, eng):
            waited = {}
            for o in by_eng[eng_name]:
                need = [p.signal for p in o.deps]
                if o in pre_wait:
                    need.append(pre_wait[o])
                for (sem, val) in need:
                    key = id(sem)
                    if waited.get(key, 0) < val:
                        eng.wait_ge(sem, val)
                        waited[key] = val
                        self.n_waits += 1
                if o.fn is None:
                    continue
                ins = o.fn(eng)
                if o.kind == "d":
                    ins.then_inc(o.signal[0], 16)
                elif o.kind == "cc":
                    ins.then_inc(o.signal[0], 1)
                elif o.has_dep:
                    ins.then_inc(o.signal[0], 1)

        with nc.Block() as block:
            @block.tensor
            def _(e):
                run("pe", e)

            @block.scalar
            def _(e):
                run("act", e)

            @block.vector
            def _(e):
                run("dve", e)

            @block.gpsimd
            def _(e):
                run("pool", e)

            @block.sync
            def _(e):
                if getattr(self, "sp_prologue", None) is not None:
                    self.sp_prologue(e)
                run("sp", e)


ARENA_BYTES = 111 * 1024


class Ctx:
    pass


def carve(ctx, off_bytes, shape, dtype):
    esz = 2 if dtype == BF16 else 4
    n = int(np.prod(shape))
    assert off_bytes % 4 == 0 and off_bytes + n * esz <= ARENA_BYTES, (off_bytes, shape)
    v = ctx.arena[:, off_bytes // 2: off_bytes // 2 + n * esz // 2]
    if dtype != BF16:
        v = v.bitcast(dtype)
    if len(shape) == 1:
        return v
    names = " ".join(f"d{i}" for i in range(len(shape)))
    kw = {f"d{i}": shape[i] for i in range(len(shape))}
    return v.rearrange(f"p ({names}) -> p {names}", **kw)


OFF_CONST = 0
OFF_DYN = 4096


def setup_memory(nc, stack, ctx):
    ctx.XT = stack.enter_context(nc.sbuf_tensor("XT", [128, KC, TOK], F32))
    ctx.HY = stack.enter_context(nc.sbuf_tensor("HY", [128, KC, TOK], BF16))
    ctx.arena = stack.enter_context(nc.sbuf_tensor("ARENA", [128, ARENA_BYTES // 2], BF16))
    ctx.PS = stack.enter_context(nc.psum_tensor("PS", [128, 8 * 512], F32))
    fw = ctx.fw
    ctx.r_xt = [[fw.res(f"xt{k}_{t}") for t in range(NTC)] for k in range(KC)]
    ctx.r_hy = [[fw.res(f"hy{k}_{t}") for t in range(NTC)] for k in range(KC)]
    ctx.r_ps = [fw.res(f"ps{b}") for b in range(8)]
    ctx.ones_bf = carve(ctx, OFF_CONST + 0, [128], BF16)
    ctx.ident_bf = carve(ctx, OFF_CONST + 256, [128], BF16)
    ctx.gains = carve(ctx, OFF_CONST + 512, [KC], F32)
    ctx.eps_col = carve(ctx, OFF_CONST + 768, [1], F32)
    ctx.ones_f = carve(ctx, OFF_CONST + 1024, [128], F32)
    ctx.r_const = fw.res("const")
    ctx.r_gain = fw.res("gain")


def bank(ctx, b):
    return ctx.PS[:, b * 512:(b + 1) * 512]


def emit_consts(ctx):
    fw = ctx.fw
    fw.op("pool", lambda e: e.memset(ctx.ones_bf, 1.0), writes=[ctx.r_const])
    fw.op("pool", lambda e: e.memset(ctx.eps_col, EPS), writes=[ctx.r_const])
    fw.op("pool", lambda e: e.memset(ctx.ones_f, 1.0), writes=[ctx.r_const])


def emit_load_x(ctx, x_dram):
    fw = ctx.fw
    src = x_dram.rearrange("(k p) t -> p k t", p=128)
    for k in range(KC):
        fw.op("sp", lambda e, k=k: e.dma_start(out=ctx.XT[:, k, :], in_=src[:, k, :]),
              writes=ctx.r_xt[k], kind="d")


def emit_store_x(ctx, y_dram):
    fw = ctx.fw
    dst = y_dram.rearrange("(k p) t -> p k t", p=128)
    ops = []
    for k in range(KC):
        ops.append(fw.op("sp", lambda e, k=k: e.dma_start(out=dst[:, k, :], in_=ctx.XT[:, k, :]),
                         reads=ctx.r_xt[k], kind="d"))
    return ops


def emit_rmsnorm(ctx, gain_dram_row, off):
    fw = ctx.fw
    rstd = carve(ctx, off, [NTC, 512], F32)
    sq = [carve(ctx, off + 8192 + i * 1024, [512], BF16) for i in range(4)]
    r_rstd = [fw.res(f"rstd{t}") for t in range(NTC)]
    r_sq = [fw.res(f"sq{i}") for i in range(4)]
    g_src = gain_dram_row.rearrange("(k p) -> p k", p=128)
    fw.op("sp", lambda e: e.dma_start(out=ctx.gains, in_=g_src, allow_slow_non_contiguous=True), writes=[ctx.r_gain], kind="d")
    cnt = 0
    for t in range(NTC):
        pb = 6 + (t % 2)
        ps = bank(ctx, pb)
        for k in range(KC):
            i = cnt % 4
            cnt += 1
            fw.op("act", lambda e, k=k, t=t, i=i: e.activation(out=sq[i], in_=ctx.XT[:, k, t * 512:(t + 1) * 512],
                                                              func=AF.Square),
                  reads=[ctx.r_xt[k][t]], writes=[r_sq[i]])
            fw.op("pe", lambda e, k=k, i=i, ps=ps: e.matmul(ps, lhsT=ctx.ones_bf, rhs=sq[i], start=(k == 0), stop=(k == KC - 1)),
                  reads=[r_sq[i], ctx.r_const], writes=[ctx.r_ps[pb]])
        fw.op("act", lambda e, t=t, ps=ps: e.activation(out=rstd[:, t, :], in_=ps, func=AF.Sqrt, bias=ctx.eps_col,
                                                       scale=1.0 / D_MODEL),
              reads=[ctx.r_ps[pb], ctx.r_const], writes=[r_rstd[t]])
        fw.op("dve", lambda e, t=t: e.reciprocal(out=rstd[:, t, :], in_=rstd[:, t, :]),
              reads=[r_rstd[t]], writes=[r_rstd[t]])
        for k in range(KC):
            eng = "dve"
            fw.op(eng, lambda e, k=k, t=t: e.scalar_tensor_tensor(
                out=ctx.HY[:, k, t * 512:(t + 1) * 512], in0=ctx.XT[:, k, t * 512:(t + 1) * 512],
                scalar=ctx.gains[:, k:k + 1], in1=rstd[:, t, :], op0=ALU.mult, op1=ALU.mult),
                reads=[ctx.r_xt[k][t], r_rstd[t], ctx.r_gain], writes=[ctx.r_hy[k][t]])


FF_GROUPS = [(0, 4), (4, 4), (8, 4), (12, 4), (16, 4), (20, 2)]


def emit_ffn(ctx, norm_row, wg, wu, wd, off):
    fw = ctx.fw
    emit_rmsnorm(ctx, norm_row, off)
    o = off + 12288
    WG = [carve(ctx, o + s * 16384, [KC, 512], BF16) for s in range(2)]
    WU = [carve(ctx, o + s * 16384 + 8192, [KC, 512], BF16) for s in range(2)]
    o += 32768
    WD = [carve(ctx, o + s * 8192, [4, 1024], BF16) for s in range(2)]
    o += 16384
    AT = [carve(ctx, o + s * 16384, [4, TOK], BF16) for s in range(2)]
    o += 32768
    SG = [carve(ctx, o + s * 2048, [512], F32) for s in range(2)]
    o += 4096
    r_wg = [fw.res("wg") for _ in range(2)]
    r_wu = [fw.res("wu") for _ in range(2)]
    r_wd = [fw.res("wd") for _ in range(2)]
    r_at = [[[fw.res("at") for _ in range(NTC)] for _ in range(4)] for _ in range(2)]
    r_sg = [fw.res("sg") for _ in range(2)]
    wg_v = wg.rearrange("(k p) f -> p k f", p=128)
    wu_v = wu.rearrange("(k p) f -> p k f", p=128)
    wd_v = wd.rearrange("(c p) d -> p c d", p=128)
    state = {"sg": 0, "gu": 0, "y": 0}

    def load_w(g):
        f0, n = FF_GROUPS[g]
        s = g % 2
        c0, c1 = f0 * 128, (f0 + n) * 128
        fw.op("pool", lambda e: e.dma_start(out=WG[s][:, :, 0:c1 - c0], in_=wg_v[:, :, c0:c1]),
              writes=[r_wg[s]], kind="d", after_barrier=True)
        fw.op("pool", lambda e: e.dma_start(out=WU[s][:, :, 0:c1 - c0], in_=wu_v[:, :, c0:c1]),
              writes=[r_wu[s]], kind="d")
        fw.op("pool", lambda e: e.dma_start(out=WD[s][:, 0:n, :], in_=wd_v[:, f0:f0 + n, :]),
              writes=[r_wd[s]], kind="d")

    def up(g):
        f0, n = FF_GROUPS[g]
        s = g % 2
        for fc in range(n):
            for t in range(NTC):
                gb = 2 * (state["gu"] % 2)
                state["gu"] += 1
                gps, ups = bank(ctx, gb), bank(ctx, gb + 1)
                for k in range(KC):
                    fw.op("pe", lambda e, k=k, fc=fc, t=t, gps=gps: e.matmul(
                        gps, lhsT=WG[s][:, k, fc * 128:(fc + 1) * 128], rhs=ctx.HY[:, k, t * 512:(t + 1) * 512],
                        start=(k == 0), stop=(k == KC - 1)),
                        reads=[r_wg[s], ctx.r_hy[k][t]], writes=[ctx.r_ps[gb]])
                for k in range(KC):
                    fw.op("pe", lambda e, k=k, fc=fc, t=t, ups=ups: e.matmul(
                        ups, lhsT=WU[s][:, k, fc * 128:(fc + 1) * 128], rhs=ctx.HY[:, k, t * 512:(t + 1) * 512],
                        start=(k == 0), stop=(k == KC - 1)),
                        reads=[r_wu[s], ctx.r_hy[k][t]], writes=[ctx.r_ps[gb + 1]])
                si = state["sg"] % 2
                state["sg"] += 1
                fw.op("act", lambda e, gps=gps, si=si: e.activation(out=SG[si], in_=gps, func=AF.Silu),
                      reads=[ctx.r_ps[gb]], writes=[r_sg[si]])
                fw.op("dve", lambda e, ups=ups, si=si, fc=fc, t=t: e.tensor_tensor(
                    out=AT[s][:, fc, t * 512:(t + 1) * 512], in0=SG[si], in1=ups, op=ALU.mult),
                    reads=[ctx.r_ps[gb + 1], r_sg[si]], writes=[r_at[s][fc][t]])

    def down(g):
        f0, n = FF_GROUPS[g]
        s = g % 2
        for dc in range(KC):
            for t in range(NTC):
                yb = 4 + (state["y"] % 2)
                state["y"] += 1
                yps = bank(ctx, yb)
                for fc in range(n):
                    fw.op("pe", lambda e, fc=fc, dc=dc, t=t, yps=yps: e.matmul(
                        yps, lhsT=WD[s][:, fc, dc * 128:(dc + 1) * 128], rhs=AT[s][:, fc, t * 512:(t + 1) * 512],
                        start=(fc == 0), stop=(fc == n - 1)),
                        reads=[r_wd[s], r_at[s][fc][t]], writes=[ctx.r_ps[yb]])
                fw.op("dve", lambda e, dc=dc, t=t, yps=yps: e.scalar_tensor_tensor(
                    out=ctx.XT[:, dc, t * 512:(t + 1) * 512], in0=yps, scalar=0.5,
                    in1=ctx.XT[:, dc, t * 512:(t + 1) * 512], op0=ALU.mult, op1=ALU.add),
                    reads=[ctx.r_ps[yb]], writes=[ctx.r_xt[dc][t]])

    ng = len(FF_GROUPS)
    load_w(0)
    load_w(1)
    up(0)
    for g in range(1, ng):
        up(g)
        down(g - 1)
        if g + 1 < ng:
            load_w(g + 1)
    down(ng - 1)


OFF_ET = 32768
OFF_QC = 49280
OFF_HI = 65664
ETW = TOK + 16
FNORM = 1.0 / math.sqrt(SEQ * 64.0)


def emit_final_norm(ctx, gain_row, y_dram, off):
    fw = ctx.fw
    rstd = carve(ctx, off, [NTC, 512], F32)
    sq = [carve(ctx, off + 8192 + i * 1024, [512], BF16) for i in range(4)]
    ob = [carve(ctx, off + 12288 + i * 2048, [512], F32) for i in range(4)]
    r_rstd = [fw.res("rstd") for t in range(NTC)]
    r_sq = [fw.res("sq") for i in range(4)]
    r_ob = [fw.res("ob") for i in range(4)]
    g_src = gain_row.rearrange("(k p) -> p k", p=128)
    fw.op("sp", lambda e: e.dma_start(out=ctx.gains, in_=g_src, allow_slow_non_contiguous=True),
          writes=[ctx.r_gain], kind="d")
    dst = y_dram.rearrange("(k p) t -> p k t", p=128)
    cnt = 0
    outs = []
    for t in range(NTC):
        pb = 6 + (t % 2)
        ps = bank(ctx, pb)
        for k in range(KC):
            i = cnt % 4
            cnt += 1
            fw.op("act", lambda e, k=k, t=t, i=i: e.activation(out=sq[i], in_=ctx.XT[:, k, t * 512:(t + 1) * 512],
                                                              func=AF.Square),
                  reads=[ctx.r_xt[k][t]], writes=[r_sq[i]])
            fw.op("pe", lambda e, k=k, i=i, ps=ps: e.matmul(ps, lhsT=ctx.ones_bf, rhs=sq[i], start=(k == 0), stop=(k == KC - 1)),
                  reads=[r_sq[i], ctx.r_const], writes=[ctx.r_ps[pb]])
        fw.op("act", lambda e, t=t, ps=ps: e.activation(out=rstd[:, t, :], in_=ps, func=AF.Sqrt, bias=ctx.eps_col,
                                                       scale=1.0 / D_MODEL),
              reads=[ctx.r_ps[pb], ctx.r_const], writes=[r_rstd[t]])
        fw.op("dve", lambda e, t=t: e.reciprocal(out=rstd[:, t, :], in_=rstd[:, t, :]),
              reads=[r_rstd[t]], writes=[r_rstd[t]])
        for k in range(KC):
            i = (t * KC + k) % 4
            fw.op("dve", lambda e, k=k, t=t, i=i: e.scalar_tensor_tensor(
                out=ob[i], in0=ctx.XT[:, k, t * 512:(t + 1) * 512],
                scalar=ctx.gains[:, k:k + 1], in1=rstd[:, t, :], op0=ALU.mult, op1=ALU.mult),
                reads=[ctx.r_xt[k][t], r_rstd[t], ctx.r_gain], writes=[r_ob[i]])
            outs.append(fw.op("sp", lambda e, k=k, t=t, i=i: e.dma_start(out=dst[:, k, t * 512:(t + 1) * 512], in_=ob[i]),
                              reads=[r_ob[i]], kind="d"))
    return outs


def emit_proj(ctx, l, io, after_f=None, after_kv=None):
    fw = ctx.fw
    rp = io.get("r_pay", None)
    wr = (lambda key: [rp[key]]) if rp is not None else (lambda key: [])
    emit_rmsnorm(ctx, io["mix_norm"][l], OFF_DYN)
    WB = [carve(ctx, 16384 + s * 8192, [KC, 512], BF16) for s in range(2)]
    r_wb = [fw.res("wb") for _ in range(2)]
    ET = carve(ctx, OFF_ET, [2, ETW], F32)
    QC = carve(ctx, OFF_QC, [4, TOK], BF16)
    KST = carve(ctx, OFF_HI, [4, TOK], BF16)
    VST = carve(ctx, OFF_HI + 16384, [4, 16, 128], BF16)
    FST = carve(ctx, OFF_HI + 32768, [16, 256], BF16)
    ctx.r_et = [fw.res("et") for _ in range(2)]
    ctx.r_qc = [fw.res("qc") for _ in range(4)]
    r_kst = [fw.res("kst") for _ in range(4)]
    r_vst = fw.res("vst")
    r_fst = fw.res("fst")
    win = io["w_in"][l].rearrange("(k p) f -> p k f", p=128)
    st = {"b": 0}

    def load(blk):
        s = blk % 2
        fw.op("pool", lambda e: e.dma_start(out=WB[s], in_=win[:, :, blk * 512:(blk + 1) * 512]),
              writes=[r_wb[s]], kind="d")

    def nb():
        b = st["b"] % 4
        st["b"] += 1
        return b

    def fmajor(blk, c0, nchunk, evac):
        s = blk % 2
        for ck in range(nchunk):
            for t in range(NTC):
                b = nb()
                ps = bank(ctx, b)
                for k in range(KC):
                    fw.op("pe", lambda e, k=k, ck=ck, t=t, ps=ps: e.matmul(
                        ps, lhsT=WB[s][:, k, c0 + ck * 128:c0 + (ck + 1) * 128], rhs=ctx.HY[:, k, t * 512:(t + 1) * 512],
                        start=(k == 0), stop=(k == KC - 1)),
                        reads=[r_wb[s], ctx.r_hy[k][t]], writes=[ctx.r_ps[b]])
                evac(ck, t, ps, b)

    def tmajor(blk, c0, ncol, evac):
        s = blk % 2
        for tile in range(16):
            b = nb()
            ps = bank(ctx, b)[:, 0:ncol]
            t = tile // 4
            for k in range(KC):
                fw.op("pe", lambda e, k=k, tile=tile, ps=ps: e.matmul(
                    ps, lhsT=ctx.HY[:, k, tile * 128:(tile + 1) * 128], rhs=WB[s][:, k, c0:c0 + ncol],
                    start=(k == 0), stop=(k == KC - 1)),
                    reads=[r_wb[s], ctx.r_hy[k][t]], writes=[ctx.r_ps[b]])
            evac(tile, ps, b)

    outs = []
    load(0)
    load(3)
    fmajor(0, 0, 2, lambda ck, t, ps, b: fw.op(
        "act", lambda e: e.copy(out=ET[:, ck, 8 + t * 512:8 + (t + 1) * 512], in_=ps),
        reads=[ctx.r_ps[b]], writes=[ctx.r_et[ck]]))
    for ck in range(2):
        outs.append(fw.op("sp", lambda e, ck=ck: e.dma_start(out=io["hpay"][ck, :, 0:8], in_=ET[:, ck, 8:16]),
                          reads=[ctx.r_et[ck]], writes=wr("halo"), kind="d"))
        outs.append(fw.op("sp", lambda e, ck=ck: e.dma_start(out=io["hpay"][ck, :, 8:16], in_=ET[:, ck, TOK:TOK + 8]),
                          reads=[ctx.r_et[ck]], writes=wr("halo"), kind="d"))
    if after_f is not None:
        after_f()
    tmajor(0, 256, 256, lambda tile, ps, b: fw.op(
        "dve", lambda e: e.tensor_copy(out=FST[:, tile, :], in_=ps), reads=[ctx.r_ps[b]], writes=[r_fst]))
    for g in range(4):
        outs.append(fw.op("sp", lambda e, g=g: e.dma_start(
            out=io["fpay"][g].rearrange("(t p) c -> p t c", p=128), in_=FST[:, :, g * 64:(g + 1) * 64]),
            reads=[r_fst], writes=wr("f"), kind="d"))
    load(2)
    tmajor(3, 0, 512, lambda tile, ps, b: fw.op(
        "act" if tile % 2 else "dve",
        (lambda e: e.copy(out=VST[:, :, tile, :], in_=ps.rearrange("p (h e) -> p h e", h=4))) if tile % 2
        else (lambda e: e.tensor_copy(out=VST[:, :, tile, :], in_=ps.rearrange("p (h e) -> p h e", h=4))),
        reads=[ctx.r_ps[b]], writes=[r_vst]))
    load(1)
    fmajor(2, 0, 4, lambda ck, t, ps, b: fw.op(
        "dve", lambda e: e.tensor_copy(out=KST[:, ck, t * 512:(t + 1) * 512], in_=ps),
        reads=[ctx.r_ps[b]], writes=[r_kst[ck]]))
    for h in range(4):
        outs.append(fw.op("sp", lambda e, h=h: e.dma_start(out=io["kpay"][h], in_=KST[:, h, :]),
                          reads=[r_kst[h]], writes=wr(("kv", h)), kind="d"))
        outs.append(fw.op("sp", lambda e, h=h: e.dma_start(out=io["vpay"][h], in_=VST[:, h, :, :]),
                          reads=[r_vst], writes=wr(("kv", h)), kind="d"))
    if after_kv is not None:
        after_kv()
    fmajor(1, 0, 4, lambda ck, t, ps, b: fw.op(
        "act", lambda e: e.mul(out=QC[:, ck, t * 512:(t + 1) * 512], in_=ps, mul=0.125),
        reads=[ctx.r_ps[b]], writes=[ctx.r_qc[ck]]))
    return outs


def _stash_view(ctx):
    return ctx.HY[:, 0:4, :].rearrange("p k t -> p (k t)").bitcast(F32).rearrange("p (c t) -> p c t", c=2)


def emit_stash_et(ctx):
    fw = ctx.fw
    ET = carve(ctx, OFF_ET, [2, ETW], F32)
    SV = _stash_view(ctx)
    for ck in range(2):
        fw.op("dve" if ck == 0 else "act",
              (lambda e, ck=ck: e.tensor_copy(out=SV[:, ck, :], in_=ET[:, ck, 8:8 + TOK])) if ck == 0
              else (lambda e, ck=ck: e.copy(out=SV[:, ck, :], in_=ET[:, ck, 8:8 + TOK])),
              reads=[ctx.r_et[ck]], writes=[r for k in (2 * ck, 2 * ck + 1) for r in ctx.r_hy[k]])


def emit_restore_et(ctx):
    fw = ctx.fw
    ET = carve(ctx, OFF_ET, [2, ETW], F32)
    SV = _stash_view(ctx)
    for ck in range(2):
        fw.op("dve" if ck == 0 else "act",
              (lambda e, ck=ck: e.tensor_copy(out=ET[:, ck, 8:8 + TOK], in_=SV[:, ck, :])) if ck == 0
              else (lambda e, ck=ck: e.copy(out=ET[:, ck, 8:8 + TOK], in_=SV[:, ck, :])),
              reads=[r for k in (2 * ck, 2 * ck + 1) for r in ctx.r_hy[k]], writes=[ctx.r_et[ck]])


def _pool_bufs(ctx):
    fw = ctx.fw
    o = OFF_CONST + 1536
    d = dict(
        PW=carve(ctx, o, [2, 128], BF16),
        PWF=carve(ctx, o + 512, [2, 128], F32),
        PSC=carve(ctx, o + 1536, [2], F32),
        CORR=carve(ctx, o + 1544, [2, 16], F32),
        MASK=carve(ctx, o + 1672, [8], F32),
        HALO=carve(ctx, o + 1704, [2, 4, 16], F32),
    )
    if not hasattr(ctx, "r_pool"):
        ctx.r_pool = {n: fw.res(n) for n in ("pw", "pcst", "halo")}
    return d


def emit_pool_loads(ctx, l, io):
    fw = ctx.fw
    B = _pool_bufs(ctx)
    PW, PWF, PSC, CORR, MASK = B["PW"], B["PWF"], B["PSC"], B["CORR"], B["MASK"]
    r_pw, r_cst = ctx.r_pool["pw"], ctx.r_pool["pcst"]
    fw.op("pool", lambda e: e.memset(PWF, 0.0), writes=[r_pw])
    for g in range(4):
        ck, gl = g // 2, g % 2
        fw.op("sp", lambda e, g=g, ck=ck, gl=gl: e.dma_start(
            out=PWF[gl * 64:(gl + 1) * 64, ck, gl * 64:(gl + 1) * 64], in_=io["pool_w"][l][g]),
            writes=[r_pw], kind="d")
    fw.op("dve", lambda e: e.tensor_copy(out=PW, in_=PWF), reads=[r_pw], writes=[r_pw])
    fw.op("sp", lambda e: e.dma_start(out=PSC, in_=io["pool_scale_t"][l]), writes=[r_cst], kind="d")
    fw.op("sp", lambda e: e.dma_start(out=CORR, in_=io["pcorr"]), writes=[r_cst], kind="d")
    fw.op("sp", lambda e: e.dma_start(out=MASK, in_=io["pmask"]), writes=[r_cst], kind="d")


def emit_pool_halo_load(ctx, io):
    fw = ctx.fw
    HALO = _pool_bufs(ctx)["HALO"]
    hrd = [io["r_gath"]["halo"]] if "r_gath" in io else []
    for k_ in range(2):
        for r_ in range(4):
            fw.op("sp", lambda e, k_=k_, r_=r_: e.dma_start(out=HALO[:, k_, r_, :], in_=io["halo_all"][k_, :, r_, :]),
                  reads=hrd, writes=[ctx.r_pool["halo"]], kind="d")


def emit_pool(ctx, l, io, preloaded=False):
    fw = ctx.fw
    ET = carve(ctx, OFF_ET, [2, ETW], F32)
    T1 = carve(ctx, OFF_HI, [ETW], F32)
    T2 = carve(ctx, OFF_HI + 8256, [ETW], F32)
    o = OFF_HI + 16512
    MT = carve(ctx, o + 2304, [2, TOK], BF16)
    WIN = carve(ctx, o + 2304 + 8192, [TOK], F32)
    B = _pool_bufs(ctx)
    PW, PSC, CORR, MASK, HALO = B["PW"], B["PSC"], B["CORR"], B["MASK"], B["HALO"]
    if not preloaded:
        emit_pool_loads(ctx, l, io)
        emit_pool_halo_load(ctx, io)
    r_pw, r_cst, r_halo = ctx.r_pool["pw"], ctx.r_pool["pcst"], ctx.r_pool["halo"]
    r_t1, r_t2, r_win = (fw.res(n) for n in ("t1", "t2", "win"))
    r_mt = [fw.res("mt") for _ in range(2)]
    for ck in range(2):
        fw.op("dve", lambda e, ck=ck: e.tensor_scalar(out=ET[:, ck, 0:8], in0=HALO[:, ck, 0, 8:16], scalar1=MASK[:, 0:1],
                                                      scalar2=None, op0=ALU.mult),
              reads=[r_halo, r_cst], writes=[ctx.r_et[ck]])
        fw.op("dve", lambda e, ck=ck: e.tensor_scalar(out=ET[:, ck, TOK + 8:TOK + 16], in0=HALO[:, ck, 0, 0:8],
                                                      scalar1=MASK[:, 4:5], scalar2=None, op0=ALU.mult),
              reads=[r_halo, r_cst], writes=[ctx.r_et[ck]])
        for r in range(1, 4):
            fw.op("dve", lambda e, ck=ck, r=r: e.scalar_tensor_tensor(
                out=ET[:, ck, 0:8], in0=HALO[:, ck, r, 8:16], scalar=MASK[:, r:r + 1], in1=ET[:, ck, 0:8],
                op0=ALU.mult, op1=ALU.add), reads=[r_halo, r_cst], writes=[ctx.r_et[ck]])
            fw.op("dve", lambda e, ck=ck, r=r: e.scalar_tensor_tensor(
                out=ET[:, ck, TOK + 8:TOK + 16], in0=HALO[:, ck, r, 0:8], scalar=MASK[:, 4 + r:5 + r],
                in1=ET[:, ck, TOK + 8:TOK + 16], op0=ALU.mult, op1=ALU.add),
                reads=[r_halo, r_cst], writes=[ctx.r_et[ck]])
        E = ET[:, ck, :]
        fw.op("pool", lambda e, E=E: e.tensor_tensor(out=T1[:, 0:ETW - 1], in0=E[:, 0:ETW - 1], in1=E[:, 1:ETW], op=ALU.add),
              reads=[ctx.r_et[ck]], writes=[r_t1])
        if ck == 0:
            lv = {0: (T1, r_t1, 2)}
            fw.op("pool", lambda e: e.tensor_tensor(out=T2[:, 0:ETW - 3], in0=T1[:, 0:ETW - 3], in1=T1[:, 2:ETW - 1], op=ALU.add),
                  reads=[r_t1], writes=[r_t2])
            lv[1] = (T2, r_t2, 4)
        else:
            fw.op("pool", lambda e: e.tensor_tensor(out=T2[:, 0:ETW - 3], in0=T1[:, 0:ETW - 3], in1=T1[:, 2:ETW - 1], op=ALU.add),
                  reads=[r_t1], writes=[r_t2])
            fw.op("pool", lambda e: e.tensor_tensor(out=T1[:, 0:ETW - 7], in0=T2[:, 0:ETW - 7], in1=T2[:, 4:ETW - 3], op=ALU.add),
                  reads=[r_t2], writes=[r_t1])
            fw.op("pool", lambda e: e.tensor_tensor(out=T2[:, 0:ETW - 15], in0=T1[:, 0:ETW - 15], in1=T1[:, 8:ETW - 7], op=ALU.add),
                  reads=[r_t1], writes=[r_t2])
            lv = {0: (T1, r_t1, 8), 1: (T2, r_t2, 16)}
        for gl in range(2):
            src, rsrc, w = lv[gl]
            sh = 8 - w // 2
            ps_ = slice(gl * 64, (gl + 1) * 64)
            fw.op("dve", lambda e, src=src, sh=sh, ps_=ps_: e.tensor_copy(out=WIN[ps_, :], in_=src[ps_, sh:sh + TOK]),
                  reads=[rsrc], writes=[r_win])
            fw.op("dve", lambda e, ck=ck, ps_=ps_: e.tensor_tensor(out=WIN[ps_, 0:8], in0=WIN[ps_, 0:8],
                                                                  in1=CORR[ps_, ck, 0:8], op=ALU.mult),
                  reads=[r_cst], writes=[r_win])
            fw.op("dve", lambda e, ck=ck, ps_=ps_: e.tensor_tensor(out=WIN[ps_, TOK - 8:TOK], in0=WIN[ps_, TOK - 8:TOK],
                                                                  in1=CORR[ps_, ck, 8:16], op=ALU.mult),
                  reads=[r_cst], writes=[r_win])
            fw.op("dve", lambda e, ck=ck, ps_=ps_, w=w: e.scalar_tensor_tensor(
                out=MT[ps_, ck, :], in0=WIN[ps_, :], scalar=1.0 / w, in1=ET[ps_, ck, 8:8 + TOK],
                op0=ALU.mult, op1=ALU.subtract), reads=[r_win, ctx.r_et[ck]], writes=[r_mt[ck]])
    for ck in range(2):
        for t in range(NTC):
            b = t % 4
            ps = bank(ctx, b)
            fw.op("pe", lambda e, ck=ck, t=t, ps=ps: e.matmul(ps, lhsT=PW[:, ck, :], rhs=MT[:, ck, t * 512:(t + 1) * 512],
                                                            start=True, stop=True),
                  reads=[r_pw, r_mt[ck]], writes=[ctx.r_ps[b]])
            fw.op("act", lambda e, ck=ck, t=t, ps=ps: e.mul(out=ctx.HY[:, ck, t * 512:(t + 1) * 512], in_=ps, mul=PSC[:, ck:ck + 1]),
                  reads=[ctx.r_ps[b], r_cst], writes=[ctx.r_hy[ck][t]])


def emit_fourier(ctx, l, io):
    fw = ctx.fw
    XS = carve(ctx, 4096, [128, 64], BF16)
    BB = carve(ctx, 20480, [2, 64, 64], BF16)
    W3 = carve(ctx, 36864, [64, 3, 32], BF16)
    UT = carve(ctx, OFF_HI, [4, 2, TOK], BF16)
    MM = carve(ctx, OFF_HI + 32768, [4, 2, 256], BF16)
    WFS = carve(ctx, OFF_HI + 36864, [4, 256], BF16)
    tb = OFF_HI + 38912
    WA = carve(ctx, tb, [128], BF16)
    C64 = carve(ctx, tb + 960, [2, 64], BF16)
    r_tab, r_wfs, r_mm, r_xs, r_bb = (fw.res(n) for n in ("ftab", "wfs", "mm", "xs", "bb"))
    r_ut = [fw.res("ut") for _ in range(4)]
    fw.op("sp", lambda e: e.dma_start(out=WA[0:64, :], in_=io["f_wa"]), writes=[r_tab], kind="d")
    fw.op("sp", lambda e: e.dma_start(out=W3, in_=io["f_w3"]), writes=[r_tab], kind="d")
    fw.op("sp", lambda e: e.dma_start(out=C64[0:64], in_=io["f_c64"]), writes=[r_tab], kind="d")
    fw.op("pool", lambda e: e.dma_start(out=WFS[0:64], in_=io["fourier_w"][l].rearrange("(g p) c -> p g c", p=64)),
          writes=[r_wfs], kind="d")
    for g in range(4):
        for comp in range(2):
            b = (g * 2 + comp) % 4
            ps = bank(ctx, b)[0:64, 0:256]
            fw.op("pe", lambda e, g=g, comp=comp, ps=ps: e.matmul(ps, lhsT=C64[0:64, comp, :], rhs=WFS[0:64, g, :],
                                                                 start=True, stop=True),
                  reads=[r_tab, r_wfs], writes=[ctx.r_ps[b]])
            fw.op("dve", lambda e, g=g, comp=comp, ps=ps: e.tensor_copy(out=MM[0:64, g, comp, :], in_=ps),
                  reads=[ctx.r_ps[b]], writes=[r_mm])
    for g in range(4):
        if "load_xs" in io:
            io["load_xs"](g, XS, r_xs)
        else:
            fw.op("sp", lambda e, g=g: e.dma_start(out=XS[0:64], in_=io["fg"][g].rearrange("(s1 s2) c -> s1 s2 c", s2=128)),
                  writes=[r_xs], kind="d")
        for rd in range(4):
            pb0 = 4 * (rd % 2)
            PSV = ctx.PS[:, pb0 * 512:(pb0 + 4) * 512].rearrange("p (c x) -> p c x", x=128)
            rps = [ctx.r_ps[pb0 + i] for i in range(4)]
            for ci in range(16):
                c = rd * 16 + ci
                fw.op("pe", lambda e, c=c, ci=ci, PSV=PSV: e.matmul(PSV[:, ci, :], lhsT=XS[0:64, :, c], rhs=WA[0:64, :],
                                                                   start=True, stop=True),
                      reads=[r_xs, r_tab], writes=[rps[ci // 4]])
            AR = PSV[:, :, 0:64]
            AI = PSV[:, :, 64:128]
            cs = slice(rd * 16, (rd + 1) * 16)
            BRv = BB[:, 0, :, cs].rearrange("p k c -> p c k")
            BIv = BB[:, 1, :, cs].rearrange("p k c -> p c k")
            fw.op("act", lambda e, AR=AR, BRv=BRv: e.copy(out=BRv, in_=AR), reads=rps, writes=[r_bb])
            fw.op("dve", lambda e, AI=AI, BIv=BIv: e.tensor_copy(out=BIv, in_=AI), reads=rps, writes=[r_bb])
        for q4 in range(4):
            pr, pi = (q4 % 2) * 2, (q4 % 2) * 2 + 1
            for kk in range(16):
                k1 = q4 * 16 + kk
                outr = bank(ctx, pr)[0:64, kk * 32:(kk + 1) * 32]
                outi = bank(ctx, pi)[0:64, kk * 32:(kk + 1) * 32]
                fw.op("pe", lambda e, k1=k1, outr=outr: e.matmul(outr, lhsT=BB[:, 0, k1, :], rhs=W3[:, k1, 0, :], start=True, stop=False),
                      reads=[r_bb, r_tab], writes=[ctx.r_ps[pr]])
                fw.op("pe", lambda e, k1=k1, outr=outr: e.matmul(outr, lhsT=BB[:, 1, k1, :], rhs=W3[:, k1, 2, :], start=False, stop=True),
                      reads=[r_bb, r_tab], writes=[ctx.r_ps[pr]])
                fw.op("pe", lambda e, k1=k1, outi=outi: e.matmul(outi, lhsT=BB[:, 1, k1, :], rhs=W3[:, k1, 0, :], start=True, stop=False),
                      reads=[r_bb, r_tab], writes=[ctx.r_ps[pi]])
                fw.op("pe", lambda e, k1=k1, outi=outi: e.matmul(outi, lhsT=BB[:, 0, k1, :], rhs=W3[:, k1, 1, :], start=False, stop=True),
                      reads=[r_bb, r_tab], writes=[ctx.r_ps[pi]])
            for comp, pbk in ((0, pr), (1, pi)):
                src = bank(ctx, pbk)[0:64, :].rearrange("p (k j) -> p k j", j=32)
                dstv = UT[0:64, g, comp, :].rearrange("p (j k) -> p k j", k=64)[:, q4 * 16:(q4 + 1) * 16, :]
                fw.op("act" if comp else "dve",
                      (lambda e, src=src, dstv=dstv: e.copy(out=dstv, in_=src)) if comp
                      else (lambda e, src=src, dstv=dstv: e.tensor_copy(out=dstv, in_=src)),
                      reads=[ctx.r_ps[pbk]], writes=[r_ut[g]])
    for ck in range(2):
        for t in range(NTC):
            b = 4 + (ck * NTC + t) % 4
            ps = bank(ctx, b)
            n = 0
            for g in range(4):
                for comp in range(2):
                    fw.op("pe", lambda e, g=g, comp=comp, ck=ck, t=t, ps=ps, n=n: e.matmul(
                        ps, lhsT=MM[0:64, g, comp, ck * 128:(ck + 1) * 128], rhs=UT[0:64, g, comp, t * 512:(t + 1) * 512],
                        start=(n == 0), stop=(n == 7)),
                        reads=[r_mm, r_ut[g]], writes=[ctx.r_ps[b]])
                    n += 1
            fw.op("act", lambda e, ck=ck, t=t, ps=ps: e.copy(out=ctx.HY[:, 2 + ck, t * 512:(t + 1) * 512], in_=ps),
                  reads=[ctx.r_ps[b]], writes=[ctx.r_hy[2 + ck][t]])


def emit_attn(ctx, l, io):
    fw = ctx.fw
    K0 = carve(ctx, 4096, [SEQ], BF16)
    K1 = carve(ctx, 20480, [SEQ], BF16)
    Q0 = carve(ctx, 36864, [TOK], BF16)
    Q1 = carve(ctx, 40960, [TOK], BF16)
    DT = carve(ctx, 45056, [4, 128], BF16)
    o = 46080
    LAMV = carve(ctx, o, [256], F32)
    LTMP = carve(ctx, o + 1024, [64], F32)
    LS = carve(ctx, o + 1280, [8], F32)
    GN = carve(ctx, o + 1312, [4], F32)
    SQH = carve(ctx, o + 1344, [512], BF16)
    QC = carve(ctx, OFF_QC, [4, TOK], BF16)
    V = carve(ctx, OFF_HI, [64, 128], BF16)
    PT = [carve(ctx, OFF_HI + 16384 + i * 2048, [1024], BF16) for i in range(3)] + [carve(ctx, OFF_HI + 34816, [1024], BF16)]
    FT = [carve(ctx, OFF_HI + 22528 + i * 2048, [512], F32) for i in range(4)]
    ACC = carve(ctx, OFF_HI + 30720, [1024], F32)
    r_acc2 = [fw.res("acc0"), fw.res("acc1")]
    r_k, r_v, r_q, r_d, r_lam = (fw.res(n) for n in ("k", "v", "q", "dt", "lam"))
    r_pt = [fw.res("pt") for _ in range(4)]
    r_ft = [fw.res("ft") for _ in range(4)]
    r_sqh = fw.res("sqh")
    LI = carve(ctx, o + 2368, [2], F32)
    fw.op("sp", lambda e: e.dma_start(out=DT, in_=io["dtile"]), writes=[r_d], kind="d")
    fw.op("sp", lambda e: e.dma_start(out=ctx.ident_bf, in_=io["ident"]), writes=[ctx.r_const], kind="d")
    fw.op("sp", lambda e: e.dma_start(out=LAMV, in_=io["lamvec"][l].partition_broadcast(128)), writes=[r_lam], kind="d")
    fw.op("sp", lambda e: e.dma_start(out=GN, in_=io["head_norm_t"][l]), writes=[r_lam], kind="d")
    fw.op("sp", lambda e: e.dma_start(out=LI, in_=io["laminit"][l]), writes=[r_lam], kind="d")
    for i in range(2):
        fw.op("dve", lambda e, i=i: e.tensor_tensor(out=LTMP, in0=LAMV[:, i * 128:i * 128 + 64],
                                                    in1=LAMV[:, i * 128 + 64:i * 128 + 128], op=ALU.mult),
              reads=[r_lam], writes=[r_lam])
        fw.op("dve", lambda e, i=i: e.reduce_sum(out=LS[:, i:i + 1], in_=LTMP, axis=mybir.AxisListType.X),
              reads=[r_lam], writes=[r_lam])
    fw.op("act", lambda e: e.activation(out=LS[:, 2:4], in_=LS[:, 0:2], func=AF.Exp), reads=[r_lam], writes=[r_lam])
    fw.op("dve", lambda e: e.tensor_tensor(out=LS[:, 4:5], in0=LS[:, 3:4], in1=LS[:, 2:3], op=ALU.subtract),
          reads=[r_lam], writes=[r_lam])
    fw.op("dve", lambda e: e.tensor_tensor(out=LS[:, 5:6], in0=LS[:, 4:5], in1=LI[:, 0:1], op=ALU.add),
          reads=[r_lam], writes=[r_lam])
    fw.op("dve", lambda e: e.tensor_scalar(out=GN, in0=GN, scalar1=LI[:, 1:2], scalar2=None, op0=ALU.mult),
          reads=[r_lam], writes=[r_lam])
    NEGLAM = LS[:, 5:6]

    it = {"n": 0}
    for h in range(HEADS):
        if "load_kv" in io:
            io["load_kv"](h, K0, K1, V, r_k, r_v)
        else:
            fw.op("sp", lambda e, h=h: e.dma_start(out=K0[0:64, :], in_=io["kg"][h, 0:64, :]), writes=[r_k], kind="d")
            fw.op("sp", lambda e, h=h: e.dma_start(out=K1[0:64, :], in_=io["kg"][h, 64:128, :]), writes=[r_k], kind="d")
            fw.op("sp", lambda e, h=h: e.dma_start(out=V, in_=io["vg"][h]), writes=[r_v], kind="d")
        fw.op("sp", lambda e, h=h: e.dma_start(out=K0[64:73, :], in_=io["kaug0"][h]), writes=[r_k], kind="d")
        fw.op("sp", lambda e, h=h: e.dma_start(out=K1[64:73, :], in_=io["kaug0"][h]), writes=[r_k], kind="d")
        fw.op("sp", lambda e, h=h: e.dma_start(out=Q0[64:73, :], in_=io["qaug0"][h]), writes=[r_q], kind="d")
        fw.op("sp", lambda e, h=h: e.dma_start(out=Q1[64:73, :], in_=io["qaug0"][h]), writes=[r_q], kind="d")
        fw.op("dve", lambda e, h=h: e.tensor_copy(out=Q0[0:64, :], in_=QC[0:64, h, :]), reads=[ctx.r_qc[h]], writes=[r_q])
        fw.op("sp", lambda e, h=h: e.dma_start(out=Q1[0:64, :], in_=QC[64:128, h, :]), reads=[ctx.r_qc[h]], writes=[r_q], kind="d")

        def s_mm(Q, L, sb, hh=h):
            ks = slice(L * 128, (L + 1) * 128)
            for j in range(2):
                KT, QT = (K0, Q0) if j == 0 else (K1, Q1)
                ps = bank(ctx, sb + j)

                def rng(mode):
                    return {"diag": slice(0, 65), "below": slice(0, 69), "above": slice(0, 73)}[mode]

                def mm(out, mode, qs, start=True, stop=True, KT=KT, QT=QT, j=j):
                    pr = rng(mode)
                    rb = ctx.r_ps[sb + j]
                    fw.op("pe", lambda e: e.matmul(out, lhsT=KT[pr, ks], rhs=QT[pr, qs], start=start, stop=stop),
                          reads=[r_k, r_q], writes=[rb])

                if L >= 16 or L < 4 * Q:
                    mm(ps, "below", slice(Q * 512, (Q + 1) * 512))
                elif L >= 4 * Q + 4:
                    mm(ps, "above", slice(Q * 512, (Q + 1) * 512))
                else:
                    us = L - 4 * Q
                    for u in range(4):
                        qs = slice(Q * 512 + u * 128, Q * 512 + (u + 1) * 128)
                        out = ps[:, u * 128:(u + 1) * 128]
                        if u > us:
                            mm(out, "below", qs)
                        elif u < us:
                            mm(out, "above", qs)
                        else:
                            mm(out, "diag", qs, start=True, stop=False)
                            fw.op("pe", lambda e, out=out, hh=hh: e.matmul(out, lhsT=ctx.ident_bf, rhs=DT[:, hh, :], start=False, stop=True),
                                  reads=[ctx.r_const, r_d], writes=[ctx.r_ps[sb + j]])

        for Q in range(4):
            qcols = slice(Q * 512, (Q + 1) * 512)
            if BAND[h] is None:
                Ls = list(range(64))
            else:
                Ls = [(4 * Q + d_) % 64 for d_ in range(-BAND[h], BAND[h] + 4)]
            nL = len(Ls)
            s_mm(Q, Ls[0], 0)
            for li, L in enumerate(Ls):
                sb = 2 * (li % 2)
                if li + 1 < nL:
                    s_mm(Q, Ls[li + 1], 2 * ((li + 1) % 2))
                pi = it["n"] % 4
                it["n"] += 1
                fw.op("act", lambda e, sb=sb, pi=pi: e.activation(out=PT[pi], in_=ctx.PS[:, sb * 512:(sb + 2) * 512], func=AF.Exp),
                      reads=[ctx.r_ps[sb], ctx.r_ps[sb + 1]], writes=[r_pt[pi]])
                for j in range(2):
                    fw.op("pe", lambda e, L=L, j=j, pi=pi, li=li, nL=nL: e.matmul(bank(ctx, 4 + j), lhsT=V[:, L, :], rhs=PT[pi][:, j * 512:(j + 1) * 512],
                                                                   start=(li == 0), stop=(li == nL - 1)),
                          reads=[r_v, r_pt[pi]], writes=[ctx.r_ps[4 + j]])
                fw.op("pe", lambda e, pi=pi, li=li, nL=nL: e.matmul(bank(ctx, 6), lhsT=ctx.ones_bf, rhs=PT[pi][:, 0:512],
                                                              start=(li == 0), stop=(li == nL - 1)),
                      reads=[ctx.r_const, r_pt[pi]], writes=[ctx.r_ps[6]])
                if li == 0:
                    fw.op("dve", lambda e, pi=pi: e.tensor_copy(out=ACC[:, 512:1024], in_=PT[pi][:, 512:1024]),
                          reads=[r_pt[pi]], writes=[r_acc2[1]])
                else:
                    fw.op("dve", lambda e, pi=pi: e.tensor_tensor(out=ACC[:, 512:1024], in0=ACC[:, 512:1024],
                                                                  in1=PT[pi][:, 512:1024], op=ALU.add),
                          reads=[r_pt[pi]], writes=[r_acc2[1]])
            fw.op("pe", lambda e: e.matmul(bank(ctx, 7), lhsT=ctx.ones_f, rhs=ACC[:, 512:1024], start=True, stop=True),
                  reads=[ctx.r_const, r_acc2[1]], writes=[ctx.r_ps[7]])
            fw.op("dve", lambda e: e.reciprocal(out=FT[0], in_=bank(ctx, 6)), reads=[ctx.r_ps[6]], writes=[r_ft[0]])
            fw.op("dve", lambda e: e.reciprocal(out=FT[1], in_=bank(ctx, 7)), reads=[ctx.r_ps[7]], writes=[r_ft[1]])
            fw.op("dve", lambda e: e.tensor_tensor(out=FT[0], in0=bank(ctx, 4), in1=FT[0], op=ALU.mult),
                  reads=[ctx.r_ps[4], r_ft[0]], writes=[r_ft[0]])
            fw.op("dve", lambda e: e.tensor_tensor(out=FT[1], in0=bank(ctx, 5), in1=FT[1], op=ALU.mult),
                  reads=[ctx.r_ps[5], r_ft[1]], writes=[r_ft[1]])
            fw.op("dve", lambda e: e.scalar_tensor_tensor(out=FT[2], in0=FT[1], scalar=NEGLAM, in1=FT[0],
                                                          op0=ALU.mult, op1=ALU.add),
                  reads=[r_ft[0], r_ft[1], r_lam], writes=[r_ft[2]])
            fw.op("act", lambda e: e.activation(out=SQH, in_=FT[2], func=AF.Square), reads=[r_ft[2]], writes=[r_sqh])
            fw.op("pe", lambda e: e.matmul(bank(ctx, 6), lhsT=ctx.ones_bf, rhs=SQH, start=True, stop=True),
                  reads=[ctx.r_const, r_sqh], writes=[ctx.r_ps[6]])
            fw.op("act", lambda e: e.activation(out=FT[3], in_=bank(ctx, 6), func=AF.Sqrt, bias=ctx.eps_col, scale=1.0 / 128.0),
                  reads=[ctx.r_ps[6], ctx.r_const], writes=[r_ft[3]])
            fw.op("dve", lambda e: e.reciprocal(out=FT[3], in_=FT[3]), reads=[r_ft[3]], writes=[r_ft[3]])
            t = Q
            fw.op("dve", lambda e, h=h, qcols=qcols: e.scalar_tensor_tensor(
                out=ctx.HY[:, 4 + h, qcols], in0=FT[2], scalar=GN[:, h:h + 1], in1=FT[3], op0=ALU.mult, op1=ALU.mult),
                reads=[r_ft[2], r_ft[3], r_lam], writes=[ctx.r_hy[4 + h][t]])


def emit_wout(ctx, l, io):
    fw = ctx.fw
    WO = carve(ctx, 4096, [KC, 1024], BF16)
    r_wo = [fw.res("wo") for _ in range(2)]
    wv = io["w_out"][l].rearrange("(k p) d -> p k d", p=128)
    for hlf in range(2):
        fw.op("pool", lambda e, hlf=hlf: e.dma_start(out=WO[:, hlf * 4:(hlf + 1) * 4, :], in_=wv[:, hlf * 4:(hlf + 1) * 4, :]),
              writes=[r_wo[hlf]], kind="d")
    n = 0
    for dc in range(KC):
        for t in range(NTC):
            b = n % 4
            n += 1
            ps = bank(ctx, b)
            for k in range(KC):
                fw.op("pe", lambda e, k=k, dc=dc, t=t, ps=ps: e.matmul(
                    ps, lhsT=WO[:, k, dc * 128:(dc + 1) * 128], rhs=ctx.HY[:, k, t * 512:(t + 1) * 512],
                    start=(k == 0), stop=(k == KC - 1)),
                    reads=[r_wo[k // 4], ctx.r_hy[k][t]], writes=[ctx.r_ps[b]])
            fw.op("dve", lambda e, dc=dc, t=t, ps=ps: e.tensor_tensor(
                out=ctx.XT[:, dc, t * 512:(t + 1) * 512], in0=ps, in1=ctx.XT[:, dc, t * 512:(t + 1) * 512], op=ALU.add),
                reads=[ctx.r_ps[b]], writes=[ctx.r_xt[dc][t]])


BIG_A = {"ffn2_w_gate": [D_MODEL, D_FF], "ffn2_w_up": [D_MODEL, D_FF], "ffn2_w_down": [D_FF, D_MODEL],
         "w_out": [D_MODEL, D_MODEL]}
BIG_B = {"ffn1_w_gate": [D_MODEL, D_FF], "ffn1_w_up": [D_MODEL, D_FF], "ffn1_w_down": [D_FF, D_MODEL],
         "w_in": [D_MODEL, 2048]}
SMALL_A = {"ffn2_norm": [D_MODEL], "pool_w": [4, 64, 64], "fourier_w": [256, 256], "pool_scale_t": [128, 2],
           "head_norm_t": [128, 4], "lamvec": [256], "laminit": [128, 2]}
SMALL_B = {"ffn1_norm": [D_MODEL], "mix_norm": [D_MODEL]}
TABLE_SPECS = {
    "pcorr": ([128, 2, 16], F32), "pmask": ([128, 8], F32),
    "f_wa": ([64, 128], BF16),
    "f_w3": ([128, 64, 3, 32], BF16), "f_c64": ([64, 2, 64], BF16),
    "dtile": ([128, 4, 128], BF16), "ident": ([128, 128], BF16),
    "kaug0": ([4, 9, SEQ], BF16), "qaug0": ([4, 9, TOK], BF16),
}
PAY_SPECS = {
    "kpay": ([4, 128, TOK], BF16), "vpay": ([4, 128, 16, 128], BF16), "fpay": ([4, TOK, 64], BF16),
    "hpay": ([2, 128, 16], F32), "qc_out": ([128, 4, TOK], BF16), "et_out": ([128, 2, TOK], F32),
    "x_out": ([D_MODEL, TOK], F32),
}
GATH_SPECS = {
    "kg": ([4, 128, SEQ], BF16), "vg": ([4, 128, 64, 128], BF16), "fg": ([4, SEQ, 64], BF16),
    "halo_all": ([2, 128, 4, 16], F32), "qc_in": ([128, 4, TOK], BF16), "et_in": ([128, 2, TOK], F32),
}


class _One:
    def __init__(self, ap):
        self.ap = ap

    def __getitem__(self, _):
        return self.ap


def build_launch(kind, dbg_phases=None):
    nc = bass.Bass("TRN2", target_bir_lowering=False)
    io = {}

    def inp(name, shp, dt=F32):
        return nc.dram_tensor(name, shp, dt, kind="ExternalInput").ap()

    io["x_in"] = inp("x_in", [D_MODEL, TOK])
    if kind != "first":
        for n, shp in {**BIG_A, **SMALL_A}.items():
            io[n] = _One(inp(n, shp))
        for n, (shp, dt) in TABLE_SPECS.items():
            io[n] = inp(n, shp, dt)
        for n, (shp, dt) in GATH_SPECS.items():
            io[n] = inp(n, shp, dt)
    if kind != "last":
        for n, shp in {**BIG_B, **SMALL_B}.items():
            io[n] = _One(inp(n, shp))
        for n, (shp, dt) in PAY_SPECS.items():
            io[n] = nc.dram_tensor(n, shp, dt, kind="ExternalOutput").ap()
    else:
        io["final_norm"] = inp("final_norm", [D_MODEL])
        io["y"] = nc.dram_tensor("y", [D_MODEL, TOK], F32, kind="ExternalOutput").ap()
    with ExitStack() as stack:
        ctx = Ctx()
        ctx.nc = nc
        fw = ctx.fw = FW(nc, stack)
        setup_memory(nc, stack, ctx)
        emit_consts(ctx)
        emit_load_x(ctx, io["x_in"])
        if kind != "first":
            ET = carve(ctx, OFF_ET, [2, ETW], F32)
            QC = carve(ctx, OFF_QC, [4, TOK], BF16)
            ctx.r_et = [fw.res("et") for _ in range(2)]
            ctx.r_qc = [fw.res("qc") for _ in range(4)]
            for ck in range(2):
                fw.op("sp", lambda e, ck=ck: e.dma_start(out=ET[:, ck, 8:8 + TOK], in_=io["et_in"][:, ck, :]),
                      writes=[ctx.r_et[ck]], kind="d")
            for h in range(4):
                fw.op("sp", lambda e, h=h: e.dma_start(out=QC[:, h, :], in_=io["qc_in"][:, h, :]),
                      writes=[ctx.r_qc[h]], kind="d")
            if dbg_phases is None or "pool" in dbg_phases:
                emit_pool(ctx, 0, io)
                fw.barrier()
            if dbg_phases is None or "fourier" in dbg_phases:
                emit_fourier(ctx, 0, io)
                fw.barrier()
            if dbg_phases is None or "attn" in dbg_phases:
                emit_attn(ctx, 0, io)
                fw.barrier()
            if dbg_phases is not None:
                hy_out = nc.dram_tensor("hy_out", [128, KC, TOK], BF16, kind="ExternalOutput").ap()
                for k in range(KC):
                    fw.op("sp", lambda e, k=k: e.dma_start(out=hy_out[:, k, :], in_=ctx.HY[:, k, :]),
                          reads=ctx.r_hy[k], kind="d")
                fw.barrier()
                fw.op("sp", None)
                fw.emit()
                return nc
            emit_wout(ctx, 0, io)
            fw.barrier()
            emit_ffn(ctx, io["ffn2_norm"][0], io["ffn2_w_gate"][0], io["ffn2_w_up"][0], io["ffn2_w_down"][0], OFF_DYN)
            fw.barrier()
            fw.new_epoch()
        if kind == "last":
            emit_final_norm(ctx, io["final_norm"], io["y"], OFF_DYN)
        else:
            emit_ffn(ctx, io["ffn1_norm"][0], io["ffn1_w_gate"][0], io["ffn1_w_up"][0], io["ffn1_w_down"][0], OFF_DYN)
            fw.barrier()
            emit_proj(ctx, 0, io)
            ET = carve(ctx, OFF_ET, [2, ETW], F32)
            QC = carve(ctx, OFF_QC, [4, TOK], BF16)
            for ck in range(2):
                fw.op("sp", lambda e, ck=ck: e.dma_start(out=io["et_out"][:, ck, :], in_=ET[:, ck, 8:8 + TOK]),
                      reads=[ctx.r_et[ck]], kind="d")
            for h in range(4):
                fw.op("sp", lambda e, h=h: e.dma_start(out=io["qc_out"][:, h, :], in_=QC[:, h, :]),
                      reads=[ctx.r_qc[h]], kind="d")
            fw.barrier()
            emit_store_x(ctx, io["x_out"])
        fw.barrier()
        fw.op("sp", None)
        fw.emit()
        nc._fw_stats = (len(fw.ops), fw.n_waits, dict(fw.count_log), max(dma_v for dma_v in [0]))
    return nc


def _bf(a):
    return np.asarray(a, dtype=np.float32).astype(ml_dtypes.bfloat16)


def make_tables(r):
    t = {}
    pcorr = np.ones((128, 2, 16), np.float32)
    for ck in range(2):
        for p in range(128):
            w = POOL_W[2 * ck + p // 64]
            left = w // 2
            right = w - 1 - left
            for i in range(8):
                if r == 0:
                    tt = i
                    cnt = min(tt + right + 1, SEQ) - max(tt - left, 0)
                    pcorr[p, ck, i] = w / cnt
                if r == 3:
                    tt = SEQ - 8 + i
                    cnt = min(tt + right + 1, SEQ) - max(tt - left, 0)
                    pcorr[p, ck, 8 + i] = w / cnt
    t["pcorr"] = pcorr
    pmask = np.zeros((128, 8), np.float32)
    if r - 1 >= 0:
        pmask[:, r - 1] = 1.0
    if r + 1 <= 3:
        pmask[:, 4 + r + 1] = 1.0
    t["pmask"] = pmask
    s1 = np.arange(64)[:, None]
    k1 = np.arange(64)[None, :]
    ang = 2 * np.pi * ((s1 * k1) % 64) / 64.0
    t["f_wa"] = _bf(np.concatenate([np.cos(ang), -np.sin(ang)], axis=1))
    s2 = np.arange(128)[:, None]
    ang = 2 * np.pi * ((s2 * k1) % SEQ) / float(SEQ)
    t["f_tr"] = (np.cos(ang) * FNORM).astype(np.float32)
    t["f_ti"] = (-np.sin(ang) * FNORM).astype(np.float32)
    del t["f_tr"], t["f_ti"]
    s2c = np.arange(128, dtype=np.int64)[:, None, None]
    k1c = np.arange(64, dtype=np.int64)[None, :, None]
    k2c = (32 * r + np.arange(32, dtype=np.int64))[None, None, :]
    ph = 2 * np.pi * (((k1c * s2c) + 64 * (k2c * s2c)) % SEQ) / float(SEQ)
    wr_ = np.cos(ph) * FNORM
    wi_ = -np.sin(ph) * FNORM
    t["f_w3"] = _bf(np.stack([wr_, wi_, -wi_], axis=2))
    c = np.arange(64)[:, None]
    cp = np.arange(64)[None, :]
    ang = 2 * np.pi * ((c * cp) % 64) / 64.0
    t["f_c64"] = _bf(np.stack([np.cos(ang), np.sin(ang)], axis=1))
    p = np.arange(128)
    dt = np.zeros((128, 4, 128), np.float32)
    for h in range(4):
        dt[:, h, :] = -SLOPES[h] * np.abs(p[:, None] - p[None, :])
    t["dtile"] = _bf(dt)
    t["ident"] = _bf(np.eye(128))
    L = np.arange(64)
    n = (16 * r + L) % 64
    sig = np.where(L < 16, 1.0, np.where(n < 16 * r, 1.0, -1.0))
    ncol = np.repeat(n, 128).astype(np.float64)
    sigc = np.repeat(sig, 128)
    pcol = np.tile(p, 64).astype(np.float64)
    kaug0 = np.zeros((4, 9, SEQ), np.float32)
    kaug1 = np.zeros((4, 64, SEQ), np.float32)
    qaug0 = np.zeros((4, 9, TOK), np.float32)
    qaug1 = np.zeros((4, 64, TOK), np.float32)
    tq = 2048 * r + np.arange(TOK)
    nq = (tq // 256).astype(np.float64)
    bq = (tq % 256).astype(np.float64)
    for h in range(4):
        m = SLOPES[h]
        A = np.stack([sigc * m * 128.0 * ncol, sigc * m * pcol, sigc, sigc])
        B = np.stack([np.ones(TOK), np.ones(TOK), -m * 256.0 * nq, -m * bq])
        kaug0[h, 0] = 1.0
        kaug0[h, 1:5] = A
        kaug0[h, 5:9] = A
        kaug1[h, 0:4] = A
        kaug1[h, 32:36] = A
        qaug0[h, 0] = 0.0
        qaug0[h, 1:5] = B
        qaug0[h, 5:9] = -2.0 * B
        qaug1[h, 32:36] = B
        qaug1[h, 0:4] = -2.0 * B
    for nm, a in (("kaug0", kaug0), ("qaug0", qaug0)):
        b = _bf(a)
        assert np.array_equal(b.astype(np.float32), a), nm
        t[nm] = b
    return t


def _layer_small(inputs, l):
    f32 = np.float32
    d = {}
    d["pool_scale_t"] = np.ascontiguousarray(np.asarray(inputs["pool_scale"][l], f32).reshape(2, 128).T)
    d["head_norm_t"] = np.ascontiguousarray(np.asarray(inputs["attn_head_norm"][l], f32).reshape(4, 128).T)
    d["lamvec"] = np.concatenate([np.asarray(inputs[k][l], f32) for k in ("lam_q1", "lam_k1", "lam_q2", "lam_k2")])
    li = lambda_init_fn(l)
    d["laminit"] = np.tile(np.array([[-li, 1.0 - li]], f32), (128, 1))
    return d


def _run(nc, in_maps):
    res = run_bass_kernel_spmd(nc, in_maps, core_ids=list(range(NCORES)))
    return res.results


class _Lay:
    def __init__(self, ap):
        self.ap = ap

    def __getitem__(self, l):
        return self.ap[l]


FUSED_W = {
    "ffn1_norm": [DEPTH, D_MODEL], "ffn1_w_gate": [DEPTH, D_MODEL, D_FF], "ffn1_w_up": [DEPTH, D_MODEL, D_FF],
    "ffn1_w_down": [DEPTH, D_FF, D_MODEL], "mix_norm": [DEPTH, D_MODEL], "w_in": [DEPTH, D_MODEL, 2048],
    "pool_w": [DEPTH, 4, 64, 64], "fourier_w": [DEPTH, 256, 256], "w_out": [DEPTH, D_MODEL, D_MODEL],
    "ffn2_norm": [DEPTH, D_MODEL], "ffn2_w_gate": [DEPTH, D_MODEL, D_FF], "ffn2_w_up": [DEPTH, D_MODEL, D_FF],
    "ffn2_w_down": [DEPTH, D_FF, D_MODEL],
    "pool_scale_t": [DEPTH, 128, 2], "head_norm_t": [DEPTH, 128, 4], "lamvec": [DEPTH, 256], "laminit": [DEPTH, 128, 2],
}
GROUPS = [[0, 1, 2, 3], [4, 5, 6, 7]]


def build_fused(depth=DEPTH):
    nc = bass.Bass("TRN2", target_bir_lowering=False)
    io = {}

    def inp(name, shp, dt=F32):
        return nc.dram_tensor(name, shp, dt, kind="ExternalInput").ap()

    io["x_in"] = inp("x_in", [D_MODEL, TOK])
    for n, shp in FUSED_W.items():
        io[n] = _Lay(inp(n, shp))
    io["final_norm"] = inp("final_norm", [D_MODEL])
    for n, (shp, dt) in TABLE_SPECS.items():
        io[n] = inp(n, shp, dt)
    io["y"] = nc.dram_tensor("y", [D_MODEL, TOK], F32, kind="ExternalOutput").ap()
    pay_kv = [nc.dram_tensor(f"pay_kv{h}", [256, TOK], BF16) for h in range(4)]
    kvg = [nc.dram_tensor(f"kvg{h}", [4 * 256, TOK], BF16) for h in range(4)]
    pay_f = nc.dram_tensor("pay_f", [4 * TOK, 64], BF16)
    fgat = nc.dram_tensor("fgat", [4 * 4 * TOK, 64], BF16)
    pay_h = nc.dram_tensor("pay_h", [256, 16], F32)
    hgat = nc.dram_tensor("hgat", [4 * 256, 16], F32)
    io["kpay"] = [pay_kv[h].ap()[0:128, :] for h in range(4)]
    io["vpay"] = [pay_kv[h].ap()[128:256, :].rearrange("p (t e) -> p t e", e=128) for h in range(4)]
    io["fpay"] = [pay_f.ap()[g * TOK:(g + 1) * TOK, :] for g in range(4)]
    io["hpay"] = pay_h.ap().rearrange("(k p) j -> k p j", p=128)
    io["halo_all"] = hgat.ap().rearrange("(r k p) j -> k p r j", r=4, k=2)
    with ExitStack() as stack:
        ctx = Ctx()
        ctx.nc = nc
        fw = ctx.fw = FW(nc, stack)
        setup_memory(nc, stack, ctx)
        rp = {"f": fw.res("pay_f"), "halo": fw.res("pay_h")}
        rg = {"f": fw.res("fgat"), "halo": fw.res("hgat")}
        for h in range(4):
            rp[("kv", h)] = fw.res("pay_kv")
            rg[("kv", h)] = fw.res("kvg")
        io["r_pay"] = rp
        io["r_gath"] = rg

        def cc(src, dst, key):
            fw.op("pool", lambda e: e.collective_compute("AllGather", ALU.bypass, replica_groups=GROUPS,
                                                         ins=[src.ap().opt()], outs=[dst.ap().opt()]),
                  reads=[rp[key]], writes=[rg[key]], kind="cc")

        def after_f():
            cc(pay_h, hgat, "halo")
            emit_pool_halo_load(ctx, io)

        def after_kv():
            for h in range(4):
                cc(pay_kv[h], kvg[h], ("kv", h))
            cc(pay_f, fgat, "f")

        def load_xs(g, XS, r_xs):
            fv = fgat.ap()
            for j in range(4):
                base = j * 4 * TOK + g * TOK
                fw.op("sp", lambda e, j=j, base=base: e.dma_start(
                    out=XS[16 * j:16 * (j + 1)], in_=fv[base:base + TOK, :].rearrange("(s1 s2) c -> s1 s2 c", s2=128)),
                    reads=[rg["f"]], writes=[r_xs], kind="d")

        def load_kv(h, K0, K1, V, r_k, r_v):
            kv = kvg[h].ap()
            for i in range(4):
                def mk(i=i, part=0):
                    def f(e):
                        rank = (ctx.pid + i) % 4
                        if part == 0:
                            return e.dma_start(out=K0[0:64, i * TOK:(i + 1) * TOK], in_=kv[bass.ds(rank * 256, 64), :])
                        if part == 1:
                            return e.dma_start(out=K1[0:64, i * TOK:(i + 1) * TOK], in_=kv[bass.ds(rank * 256 + 64, 64), :])
                        return e.dma_start(out=V[:, 16 * i:16 * (i + 1), :],
                                           in_=kv[bass.ds(rank * 256 + 128, 128), :].rearrange("p (t e) -> p t e", e=128))
                    return f
                fw.op("sp", mk(i, 0), reads=[rg[("kv", h)]], writes=[r_k], kind="d")
                fw.op("sp", mk(i, 1), reads=[rg[("kv", h)]], writes=[r_k], kind="d")
                fw.op("sp", mk(i, 2), reads=[rg[("kv", h)]], writes=[r_v], kind="d")

        io["load_xs"] = load_xs
        io["load_kv"] = load_kv

        def _pro(e):
            ctx.pid = nc.partition_id([mybir.EngineType.SP])
        fw.sp_prologue = _pro
        io["fg"] = None
        emit_consts(ctx)
        emit_load_x(ctx, io["x_in"])
        for l in range(depth):
            emit_ffn(ctx, io["ffn1_norm"][l], io["ffn1_w_gate"][l], io["ffn1_w_up"][l], io["ffn1_w_down"][l], OFF_DYN)
            fw.barrier()
            emit_pool_loads(ctx, l, io)
            emit_proj(ctx, l, io, after_f=after_f, after_kv=after_kv)
            fw.barrier()
            fw.new_epoch()
            emit_pool(ctx, l, io, preloaded=True)
            fw.barrier()
            emit_attn(ctx, l, io)
            fw.barrier()
            emit_fourier(ctx, l, io)
            fw.barrier()
            emit_wout(ctx, l, io)
            fw.barrier()
            emit_ffn(ctx, io["ffn2_norm"][l], io["ffn2_w_gate"][l], io["ffn2_w_up"][l], io["ffn2_w_down"][l], OFF_DYN)
            fw.barrier()
            fw.new_epoch()
        emit_final_norm(ctx, io["final_norm"], io["y"], OFF_DYN)
        fw.barrier()
        fw.op("sp", None)
        fw.emit()
        nc._fw_stats = (len(fw.ops), fw.n_waits, dict(fw.count_log))
    return nc


def fused_inputs(inputs, depth=DEPTH):
    f32 = np.float32
    x = np.asarray(inputs["x"], f32)
    shared = {}
    for n in FUSED_W:
        if n in inputs:
            shared[n] = np.ascontiguousarray(np.asarray(inputs[n], f32))
    sm = [_layer_small(inputs, l) for l in range(DEPTH)]
    for n in ("pool_scale_t", "head_norm_t", "lamvec", "laminit"):
        shared[n] = np.ascontiguousarray(np.stack([sm[l][n] for l in range(DEPTH)]).astype(f32))
    shared["final_norm"] = np.asarray(inputs["final_norm"], f32)
    tables = [make_tables(r) for r in range(4)]
    in_maps = []
    for c in range(NCORES):
        b, r = c // 4, c % 4
        d = {"x_in": np.ascontiguousarray(x[b, r * TOK:(r + 1) * TOK, :].T)}
        d.update(shared)
        d.update(tables[r])
        in_maps.append(d)
    return in_maps


def kernel(**inputs):
    in_maps = fused_inputs(inputs)
    outs = _run(_prog("fused"), in_maps)
    out = np.empty((BATCH, SEQ, D_MODEL), np.float32)
    for c in range(NCORES):
        b, r = c // 4, c % 4
        out[b, r * TOK:(r + 1) * TOK, :] = np.asarray(outs[c]["y"], np.float32).T
    return out


_PROGS = {}


def _prog(kind):
    if kind not in _PROGS:
        _PROGS[kind] = build_fused() if kind == "fused" else build_launch(kind)
    return _PROGS[kind]


def kernel_unfused(**inputs):
    f32 = np.float32
    x = np.asarray(inputs["x"], f32)
    tables = [make_tables(r) for r in range(4)]

    def wA(l):
        d = {n: np.ascontiguousarray(np.asarray(inputs[n][l], f32)) for n in BIG_A}
        d["ffn2_norm"] = np.asarray(inputs["ffn2_norm"][l], f32)
        d["pool_w"] = np.asarray(inputs["pool_w"][l], f32)
        d["fourier_w"] = np.asarray(inputs["fourier_w"][l], f32)
        d.update(_layer_small(inputs, l))
        return d

    def wB(l):
        d = {n: np.ascontiguousarray(np.asarray(inputs[n][l], f32)) for n in BIG_B}
        d["ffn1_norm"] = np.asarray(inputs["ffn1_norm"][l], f32)
        d["mix_norm"] = np.asarray(inputs["mix_norm"][l], f32)
        return d

    def gathered(outs):
        g = []
        for c in range(NCORES):
            b, r = c // 4, c % 4
            grp = [outs[4 * b + j] for j in range(4)]
            rot = [grp[(r + i) % 4] for i in range(4)]
            d = {}
            d["kg"] = np.ascontiguousarray(np.concatenate([o["kpay"] for o in rot], axis=2))
            d["vg"] = np.ascontiguousarray(np.concatenate([o["vpay"] for o in rot], axis=2))
            d["fg"] = np.ascontiguousarray(np.concatenate([o["fpay"] for o in grp], axis=1))
            d["halo_all"] = np.ascontiguousarray(np.stack([o["hpay"] for o in grp], axis=2))
            d["qc_in"] = outs[c]["qc_out"]
            d["et_in"] = outs[c]["et_out"]
            d["x_in"] = outs[c]["x_out"]
            g.append(d)
        return g

    b0 = wB(0)
    in_maps = []
    for c in range(NCORES):
        b, r = c // 4, c % 4
        d = {"x_in": np.ascontiguousarray(x[b, r * TOK:(r + 1) * TOK, :].T)}
        d.update(b0)
        in_maps.append(d)
    outs = _run(_prog("first"), in_maps)
    for l in range(1, DEPTH):
        g = gathered(outs)
        a, bb = wA(l - 1), wB(l)
        in_maps = []
        for c in range(NCORES):
            d = dict(g[c])
            d.update(a)
            d.update(bb)
            d.update(tables[c % 4])
            in_maps.append(d)
        outs = _run(_prog("mid"), in_maps)
    g = gathered(outs)
    a = wA(DEPTH - 1)
    in_maps = []
    for c in range(NCORES):
        d = dict(g[c])
        d.update(a)
        d.update(tables[c % 4])
        d["final_norm"] = np.asarray(inputs["final_norm"], f32)
        in_maps.append(d)
    outs = _run(_prog("last"), in_maps)
    out = np.empty((BATCH, SEQ, D_MODEL), f32)
    for c in range(NCORES):
        b, r = c // 4, c % 4
        out[b, r * TOK:(r + 1) * TOK, :] = np.asarray(outs[c]["y"], f32).T
    return out
```

```python
import math
from contextlib import ExitStack

import numpy as np
import ml_dtypes

import concourse.bass as bass
import concourse.mybir as mybir
from concourse.bass_utils import run_bass_kernel_spmd

F32 = mybir.dt.float32
BF16 = mybir.dt.bfloat16
AF = mybir.ActivationFunctionType
ALU = mybir.AluOpType

D_MODEL = 1024
BATCH = 2
SEQ = 8192
DEPTH = 4
D_FF = 2816
NCORES = 8
TOK = 2048
NTC = 4
KC = 8
EPS = 1e-6
HEADS = 4
SLOPES = [2.0 ** (-8.0 * (i + 1) / HEADS) for i in range(HEADS)]
POOL_W = (2, 4, 8, 16)
BAND = [4, 16, None, None]


def lambda_init_fn(layer_idx):
    return 0.8 - 0.6 * math.exp(-0.3 * layer_idx)


class Res:
    __slots__ = ("name", "last_w", "readers")

    def __init__(self, name):
        self.name = name
        self.last_w = None
        self.readers = []


class Op:
    __slots__ = ("eng", "fn", "deps", "kind", "signal", "has_dep", "idx")

    def __init__(self, eng, fn, kind):
        self.eng = eng
        self.fn = fn
        self.deps = set()
        self.kind = kind
        self.signal = None
        self.has_dep = False
        self.idx = -1


ENGS = ("pe", "act", "dve", "pool", "sp")


class FW:
    def __init__(self, nc, stack, n_dma_sems=24, n_cc_sems=4):
        self.nc = nc
        self.stack = stack
        self.ops = []
        self.last_op = {e: None for e in ENGS}
        self.pending = {e: [] for e in ENGS}
        self.outstanding_dma = []
        self.n_dma_sems = n_dma_sems
        self.n_cc_sems = n_cc_sems
        self.epoch_marks = []

    def res(self, name="r"):
        return Res(name)

    def op(self, eng, fn, reads=(), writes=(), kind="c", after_barrier=True):
        o = Op(eng, fn, kind)
        o.idx = len(self.ops)
        for r in reads:
            if r.last_w is not None:
                o.deps.add(r.last_w)
            if kind == "c":
                r.readers = [x for x in r.readers if not (x.kind == "c" and x.eng == eng)]
            r.readers.append(o)
        for w in writes:
            if w.last_w is not None:
                o.deps.add(w.last_w)
            for rd in w.readers:
                if rd is not o:
                    o.deps.add(rd)
            w.last_w = o
            w.readers = []
        if after_barrier and self.pending[eng]:
            o.deps.update(self.pending[eng])
            self.pending[eng] = []
        o.deps.discard(o)
        self.ops.append(o)
        if kind != "cc":
            self.last_op[eng] = o
        if kind == "d":
            self.outstanding_dma.append(o)
        return o

    def barrier(self):
        col = [o for o in self.last_op.values() if o is not None] + list(self.outstanding_dma)
        for e in ENGS:
            self.pending[e] = list(col) + self.pending[e]
        self.outstanding_dma = []

    def new_epoch(self):
        self.epoch_marks.append(len(self.ops))

    def emit(self):
        nc = self.nc
        st = self.stack
        for o in self.ops:
            keep = set()
            for p in o.deps:
                if p.eng == "pe" and o.eng == "pe" and p.kind == "c" and o.kind == "c":
                    continue
                keep.add(p)
                p.has_dep = True
            o.deps = keep
        n_epochs = len(self.epoch_marks) + 1
        eng_sems = {e: [st.enter_context(nc.semaphore(f"s_{e}_{k}")) for k in range(n_epochs)] for e in ENGS}
        dma_sems = [st.enter_context(nc.semaphore(f"s_dma_{k}")) for k in range(self.n_dma_sems)]
        n_sw = 8
        pool_of = {"pool": list(range(0, n_sw)), "sp": list(range(n_sw, self.n_dma_sems))}
        rr = {"pool": 0, "sp": 0}
        cc_sems = [st.enter_context(nc.semaphore(f"s_cc_{k}")) for k in range(self.n_cc_sems)]
        epoch = 0
        marks = list(self.epoch_marks)
        counters = {e: 0 for e in ENGS}
        dma_rr = 0
        cc_rr = 0
        dma_tot = [0] * self.n_dma_sems
        dma_prev = [None] * self.n_dma_sems
        cc_tot = [0] * self.n_cc_sems
        cc_prev = [None] * self.n_cc_sems
        pre_wait = {}
        for o in self.ops:
            while marks and o.idx >= marks[0]:
                marks.pop(0)
                epoch += 1
                counters = {e: 0 for e in ENGS}
            if o.kind == "d":
                lst = pool_of[o.eng]
                k = lst[rr[o.eng] % len(lst)]
                rr[o.eng] += 1
                if dma_prev[k] is not None:
                    pre_wait[o] = dma_prev[k].signal
                dma_tot[k] += 16
                o.signal = (dma_sems[k], dma_tot[k])
                dma_prev[k] = o
            elif o.kind == "cc":
                k = cc_rr
                cc_rr = (cc_rr + 1) % self.n_cc_sems
                if cc_prev[k] is not None:
                    pre_wait[o] = cc_prev[k].signal
                cc_tot[k] += 1
                o.signal = (cc_sems[k], cc_tot[k])
                cc_prev[k] = o
            elif o.has_dep:
                counters[o.eng] += 1
                o.signal = (eng_sems[o.eng][epoch], counters[o.eng])
                self.max_count = max(getattr(self, "max_count", 0), counters[o.eng])
                self.count_log = getattr(self, "count_log", {})
                self.count_log[(epoch, o.eng)] = counters[o.eng]
        by_eng = {e: [o for o in self.ops if o.eng == e] for e in ENGS}
        self.n_waits = 0

        def run(eng_name, eng):
            waited = {}
            for o in by_eng[eng_name]:
                need = [p.signal for p in o.deps]
                if o in pre_wait:
                    need.append(pre_wait[o])
                for (sem, val) in need:
                    key = id(sem)
                    if waited.get(key, 0) < val:
                        eng.wait_ge(sem, val)
                        waited[key] = val
                        self.n_waits += 1
                if o.fn is None:
                    continue
                ins = o.fn(eng)
                if o.kind == "d":
                    ins.then_inc(o.signal[0], 16)
                elif o.kind == "cc":
                    ins.then_inc(o.signal[0], 1)
                elif o.has_dep:
                    ins.then_inc(o.signal[0], 1)

        with nc.Block() as block:
            @block.tensor
            def _(e):
                run("pe", e)

            @block.scalar
            def _(e):
                run("act", e)

            @block.vector
            def _(e):
                run("dve", e)

            @block.gpsimd
            def _(e):
                run("pool", e)

            @block.sync
            def _(e):
                if getattr(self, "sp_prologue", None) is not None:
                    self.sp_prologue(e)
                run("sp", e)


ARENA_BYTES = 111 * 1024


class Ctx:
    pass


def carve(ctx, off_bytes, shape, dtype):
    esz = 2 if dtype == BF16 else 4
    n = int(np.prod(shape))
    assert off_bytes % 4 == 0 and off_bytes + n * esz <= ARENA_BYTES, (off_bytes, shape)
    v = ctx.arena[:, off_bytes // 2: off_bytes // 2 + n * esz // 2]
    if dtype != BF16:
        v = v.bitcast(dtype)
    if len(shape) == 1:
        return v
    names = " ".join(f"d{i}" for i in range(len(shape)))
    kw = {f"d{i}": shape[i] for i in range(len(shape))}
    return v.rearrange(f"p ({names}) -> p {names}", **kw)


OFF_CONST = 0
OFF_DYN = 4096


def setup_memory(nc, stack, ctx):
    ctx.XT = stack.enter_context(nc.sbuf_tensor("XT", [128, KC, TOK], F32))
    ctx.HY = stack.enter_context(nc.sbuf_tensor("HY", [128, KC, TOK], BF16))
    ctx.arena = stack.enter_context(nc.sbuf_tensor("ARENA", [128, ARENA_BYTES // 2], BF16))
    ctx.PS = stack.enter_context(nc.psum_tensor("PS", [128, 8 * 512], F32))
    fw = ctx.fw
    ctx.r_xt = [[fw.res(f"xt{k}_{t}") for t in range(NTC)] for k in range(KC)]
    ctx.r_hy = [[fw.res(f"hy{k}_{t}") for t in range(NTC)] for k in range(KC)]
    ctx.r_ps = [fw.res(f"ps{b}") for b in range(8)]
    ctx.ones_bf = carve(ctx, OFF_CONST + 0, [128], BF16)
    ctx.ident_bf = carve(ctx, OFF_CONST + 256, [128], BF16)
    ctx.gains = carve(ctx, OFF_CONST + 512, [KC], F32)
    ctx.eps_col = carve(ctx, OFF_CONST + 768, [1], F32)
    ctx.ones_f = carve(ctx, OFF_CONST + 1024, [128], F32)
    ctx.r_const = fw.res("const")
    ctx.r_gain = fw.res("gain")


def bank(ctx, b):
    return ctx.PS[:, b * 512:(b + 1) * 512]


def emit_consts(ctx):
    fw = ctx.fw
    fw.op("pool", lambda e: e.memset(ctx.ones_bf, 1.0), writes=[ctx.r_const])
    fw.op("pool", lambda e: e.memset(ctx.eps_col, EPS), writes=[ctx.r_const])
    fw.op("pool", lambda e: e.memset(ctx.ones_f, 1.0), writes=[ctx.r_const])


def emit_load_x(ctx, x_dram):
    fw = ctx.fw
    src = x_dram.rearrange("(k p) t -> p k t", p=128)
    for k in range(KC):
        fw.op("sp", lambda e, k=k: e.dma_start(out=ctx.XT[:, k, :], in_=src[:, k, :]),
              writes=ctx.r_xt[k], kind="d")


def emit_store_x(ctx, y_dram):
    fw = ctx.fw
    dst = y_dram.rearrange("(k p) t -> p k t", p=128)
    ops = []
    for k in range(KC):
        ops.append(fw.op("sp", lambda e, k=k: e.dma_start(out=dst[:, k, :], in_=ctx.XT[:, k, :]),
                         reads=ctx.r_xt[k], kind="d"))
    return ops


def emit_rmsnorm(ctx, gain_dram_row, off):
    fw = ctx.fw
    rstd = carve(ctx, off, [NTC, 512], F32)
    sq = [carve(ctx, off + 8192 + i * 1024, [512], BF16) for i in range(4)]
    r_rstd = [fw.res(f"rstd{t}") for t in range(NTC)]
    r_sq = [fw.res(f"sq{i}") for i in range(4)]
    g_src = gain_dram_row.rearrange("(k p) -> p k", p=128)
    fw.op("sp", lambda e: e.dma_start(out=ctx.gains, in_=g_src, allow_slow_non_contiguous=True), writes=[ctx.r_gain], kind="d")
    cnt = 0
    for t in range(NTC):
        pb = 6 + (t % 2)
        ps = bank(ctx, pb)
        for k in range(KC):
            i = cnt % 4
            cnt += 1
            fw.op("act", lambda e, k=k, t=t, i=i: e.activation(out=sq[i], in_=ctx.XT[:, k, t * 512:(t + 1) * 512],
                                                              func=AF.Square),
                  reads=[ctx.r_xt[k][t]], writes=[r_sq[i]])
            fw.op("pe", lambda e, k=k, i=i, ps=ps: e.matmul(ps, lhsT=ctx.ones_bf, rhs=sq[i], start=(k == 0), stop=(k == KC - 1)),
                  reads=[r_sq[i], ctx.r_const], writes=[ctx.r_ps[pb]])
        fw.op("act", lambda e, t=t, ps=ps: e.activation(out=rstd[:, t, :], in_=ps, func=AF.Sqrt, bias=ctx.eps_col,
                                                       scale=1.0 / D_MODEL),
              reads=[ctx.r_ps[pb], ctx.r_const], writes=[r_rstd[t]])
        fw.op("dve", lambda e, t=t: e.reciprocal(out=rstd[:, t, :], in_=rstd[:, t, :]),
              reads=[r_rstd[t]], writes=[r_rstd[t]])
        for k in range(KC):
            eng = "dve"
            fw.op(eng, lambda e, k=k, t=t: e.scalar_tensor_tensor(
                out=ctx.HY[:, k, t * 512:(t + 1) * 512], in0=ctx.XT[:, k, t * 512:(t + 1) * 512],
                scalar=ctx.gains[:, k:k + 1], in1=rstd[:, t, :], op0=ALU.mult, op1=ALU.mult),
                reads=[ctx.r_xt[k][t], r_rstd[t], ctx.r_gain], writes=[ctx.r_hy[k][t]])


FF_GROUPS = [(0, 4), (4, 4), (8, 4), (12, 4), (16, 4), (20, 2)]


def emit_ffn(ctx, norm_row, wg, wu, wd, off):
    fw = ctx.fw
    emit_rmsnorm(ctx, norm_row, off)
    o = off + 12288
    WG = [carve(ctx, o + s * 16384, [KC, 512], BF16) for s in range(2)]
    WU = [carve(ctx, o + s * 16384 + 8192, [KC, 512], BF16) for s in range(2)]
    o += 32768
    WD = [carve(ctx, o + s * 8192, [4, 1024], BF16) for s in range(2)]
    o += 16384
    AT = [carve(ctx, o + s * 16384, [4, TOK], BF16) for s in range(2)]
    o += 32768
    SG = [carve(ctx, o + s * 2048, [512], F32) for s in range(2)]
    o += 4096
    r_wg = [fw.res("wg") for _ in range(2)]
    r_wu = [fw.res("wu") for _ in range(2)]
    r_wd = [fw.res("wd") for _ in range(2)]
    r_at = [[[fw.res("at") for _ in range(NTC)] for _ in range(4)] for _ in range(2)]
    r_sg = [fw.res("sg") for _ in range(2)]
    wg_v = wg.rearrange("(k p) f -> p k f", p=128)
    wu_v = wu.rearrange("(k p) f -> p k f", p=128)
    wd_v = wd.rearrange("(c p) d -> p c d", p=128)
    state = {"sg": 0, "gu": 0, "y": 0}

    def load_w(g):
        f0, n = FF_GROUPS[g]
        s = g % 2
        c0, c1 = f0 * 128, (f0 + n) * 128
        fw.op("pool", lambda e: e.dma_start(out=WG[s][:, :, 0:c1 - c0], in_=wg_v[:, :, c0:c1]),
              writes=[r_wg[s]], kind="d", after_barrier=True)
        fw.op("pool", lambda e: e.dma_start(out=WU[s][:, :, 0:c1 - c0], in_=wu_v[:, :, c0:c1]),
              writes=[r_wu[s]], kind="d")
        fw.op("pool", lambda e: e.dma_start(out=WD[s][:, 0:n, :], in_=wd_v[:, f0:f0 + n, :]),
              writes=[r_wd[s]], kind="d")

    def up(g):
        f0, n = FF_GROUPS[g]
        s = g % 2
        for fc in range(n):
            for t in range(NTC):
                gb = 2 * (state["gu"] % 2)
                state["gu"] += 1
                gps, ups = bank(ctx, gb), bank(ctx, gb + 1)
                for k in range(KC):
                    fw.op("pe", lambda e, k=k, fc=fc, t=t, gps=gps: e.matmul(
                        gps, lhsT=WG[s][:, k, fc * 128:(fc + 1) * 128], rhs=ctx.HY[:, k, t * 512:(t + 1) * 512],
                        start=(k == 0), stop=(k == KC - 1)),
                        reads=[r_wg[s], ctx.r_hy[k][t]], writes=[ctx.r_ps[gb]])
                for k in range(KC):
                    fw.op("pe", lambda e, k=k, fc=fc, t=t, ups=ups: e.matmul(
                        ups, lhsT=WU[s][:, k, fc * 128:(fc + 1) * 128], rhs=ctx.HY[:, k, t * 512:(t + 1) * 512],
                        start=(k == 0), stop=(k == KC - 1)),
                        reads=[r_wu[s], ctx.r_hy[k][t]], writes=[ctx.r_ps[gb + 1]])
                si = state["sg"] % 2
                state["sg"] += 1
                fw.op("act", lambda e, gps=gps, si=si: e.activation(out=SG[si], in_=gps, func=AF.Silu),
                      reads=[ctx.r_ps[gb]], writes=[r_sg[si]])
                fw.op("dve", lambda e, ups=ups, si=si, fc=fc, t=t: e.tensor_tensor(
                    out=AT[s][:, fc, t * 512:(t + 1) * 512], in0=SG[si], in1=ups, op=ALU.mult),
                    reads=[ctx.r_ps[gb + 1], r_sg[si]], writes=[r_at[s][fc][t]])

    def down(g):
        f0, n = FF_GROUPS[g]
        s = g % 2
        for dc in range(KC):
            for t in range(NTC):
                yb = 4 + (state["y"] % 2)
                state["y"] += 1
                yps = bank(ctx, yb)
                for fc in range(n):
                    fw.op("pe", lambda e, fc=fc, dc=dc, t=t, yps=yps: e.matmul(
                        yps, lhsT=WD[s][:, fc, dc * 128:(dc + 1) * 128], rhs=AT[s][:, fc, t * 512:(t + 1) * 512],
                        start=(fc == 0), stop=(fc == n - 1)),
                        reads=[r_wd[s], r_at[s][fc][t]], writes=[ctx.r_ps[yb]])
                fw.op("dve", lambda e, dc=dc, t=t, yps=yps: e.scalar_tensor_tensor(
                    out=ctx.XT[:, dc, t * 512:(t + 1) * 512], in0=yps, scalar=0.5,
                    in1=ctx.XT[:, dc, t * 512:(t + 1) * 512], op0=ALU.mult, op1=ALU.add),
                    reads=[ctx.r_ps[yb]], writes=[ctx.r_xt[dc][t]])

    ng = len(FF_GROUPS)
    load_w(0)
    load_w(1)
    up(0)
    for g in range(1, ng):
        up(g)
        down(g - 1)
        if g + 1 < ng:
            load_w(g + 1)
    down(ng - 1)


OFF_ET = 32768
OFF_QC = 49280
OFF_HI = 65664
ETW = TOK + 16
FNORM = 1.0 / math.sqrt(SEQ * 64.0)


def emit_final_norm(ctx, gain_row, y_dram, off):
    fw = ctx.fw
    rstd = carve(ctx, off, [NTC, 512], F32)
    sq = [carve(ctx, off + 8192 + i * 1024, [512], BF16) for i in range(4)]
    ob = [carve(ctx, off + 12288 + i * 2048, [512], F32) for i in range(4)]
    r_rstd = [fw.res("rstd") for t in range(NTC)]
    r_sq = [fw.res("sq") for i in range(4)]
    r_ob = [fw.res("ob") for i in range(4)]
    g_src = gain_row.rearrange("(k p) -> p k", p=128)
    fw.op("sp", lambda e: e.dma_start(out=ctx.gains, in_=g_src, allow_slow_non_contiguous=True),
          writes=[ctx.r_gain], kind="d")
    dst = y_dram.rearrange("(k p) t -> p k t", p=128)
    cnt = 0
    outs = []
    for t in range(NTC):
        pb = 6 + (t % 2)
        ps = bank(ctx, pb)
        for k in range(KC):
            i = cnt % 4
            cnt += 1
            fw.op("act", lambda e, k=k, t=t, i=i: e.activation(out=sq[i], in_=ctx.XT[:, k, t * 512:(t + 1) * 512],
                                                              func=AF.Square),
                  reads=[ctx.r_xt[k][t]], writes=[r_sq[i]])
            fw.op("pe", lambda e, k=k, i=i, ps=ps: e.matmul(ps, lhsT=ctx.ones_bf, rhs=sq[i], start=(k == 0), stop=(k == KC - 1)),
                  reads=[r_sq[i], ctx.r_const], writes=[ctx.r_ps[pb]])
        fw.op("act", lambda e, t=t, ps=ps: e.activation(out=rstd[:, t, :], in_=ps, func=AF.Sqrt, bias=ctx.eps_col,
                                                       scale=1.0 / D_MODEL),
              reads=[ctx.r_ps[pb], ctx.r_const], writes=[r_rstd[t]])
        fw.op("dve", lambda e, t=t: e.reciprocal(out=rstd[:, t, :], in_=rstd[:, t, :]),
              reads=[r_rstd[t]], writes=[r_rstd[t]])
        for k in range(KC):
            i = (t * KC + k) % 4
            fw.op("dve", lambda e, k=k, t=t, i=i: e.scalar_tensor_tensor(
                out=ob[i], in0=ctx.XT[:, k, t * 512:(t + 1) * 512],
                scalar=ctx.gains[:, k:k + 1], in1=rstd[:, t, :], op0=ALU.mult, op1=ALU.mult),
                reads=[ctx.r_xt[k][t], r_rstd[t], ctx.r_gain], writes=[r_ob[i]])
            outs.append(fw.op("sp", lambda e, k=k, t=t, i=i: e.dma_start(out=dst[:, k, t * 512:(t + 1) * 512], in_=ob[i]),
                              reads=[r_ob[i]], kind="d"))
    return outs


def emit_proj(ctx, l, io, after_f=None, after_kv=None):
    fw = ctx.fw
    rp = io.get("r_pay", None)
    wr = (lambda key: [rp[key]]) if rp is not None else (lambda key: [])
    emit_rmsnorm(ctx, io["mix_norm"][l], OFF_DYN)
    WB = [carve(ctx, 16384 + s * 8192, [KC, 512], BF16) for s in range(2)]
    r_wb = [fw.res("wb") for _ in range(2)]
    ET = carve(ctx, OFF_ET, [2, ETW], F32)
    QC = carve(ctx, OFF_QC, [4, TOK], BF16)
    KST = carve(ctx, OFF_HI, [4, TOK], BF16)
    VST = carve(ctx, OFF_HI + 16384, [4, 16, 128], BF16)
    FST = carve(ctx, OFF_HI + 32768, [16, 256], BF16)
    ctx.r_et = [fw.res("et") for _ in range(2)]
    ctx.r_qc = [fw.res("qc") for _ in range(4)]
    r_kst = [fw.res("kst") for _ in range(4)]
    r_vst = fw.res("vst")
    r_fst = fw.res("fst")
    win = io["w_in"][l].rearrange("(k p) f -> p k f", p=128)
    st = {"b": 0}

    def load(blk):
        s = blk % 2
        fw.op("pool", lambda e: e.dma_start(out=WB[s], in_=win[:, :, blk * 512:(blk + 1) * 512]),
              writes=[r_wb[s]], kind="d")

    def nb():
        b = st["b"] % 4
        st["b"] += 1
        return b

    def fmajor(blk, c0, nchunk, evac):
        s = blk % 2
        for ck in range(nchunk):
            for t in range(NTC):
                b = nb()
                ps = bank(ctx, b)
                for k in range(KC):
                    fw.op("pe", lambda e, k=k, ck=ck, t=t, ps=ps: e.matmul(
                        ps, lhsT=WB[s][:, k, c0 + ck * 128:c0 + (ck + 1) * 128], rhs=ctx.HY[:, k, t * 512:(t + 1) * 512],
                        start=(k == 0), stop=(k == KC - 1)),
                        reads=[r_wb[s], ctx.r_hy[k][t]], writes=[ctx.r_ps[b]])
                evac(ck, t, ps, b)

    def tmajor(blk, c0, ncol, evac):
        s = blk % 2
        for tile in range(16):
            b = nb()
            ps = bank(ctx, b)[:, 0:ncol]
            t = tile // 4
            for k in range(KC):
                fw.op("pe", lambda e, k=k, tile=tile, ps=ps: e.matmul(
                    ps, lhsT=ctx.HY[:, k, tile * 128:(tile + 1) * 128], rhs=WB[s][:, k, c0:c0 + ncol],
                    start=(k == 0), stop=(k == KC - 1)),
                    reads=[r_wb[s], ctx.r_hy[k][t]], writes=[ctx.r_ps[b]])
            evac(tile, ps, b)

    outs = []
    load(0)
    load(3)
    fmajor(0, 0, 2, lambda ck, t, ps, b: fw.op(
        "act", lambda e: e.copy(out=ET[:, ck, 8 + t * 512:8 + (t + 1) * 512], in_=ps),
        reads=[ctx.r_ps[b]], writes=[ctx.r_et[ck]]))
    for ck in range(2):
        outs.append(fw.op("sp", lambda e, ck=ck: e.dma_start(out=io["hpay"][ck, :, 0:8], in_=ET[:, ck, 8:16]),
                          reads=[ctx.r_et[ck]], writes=wr("halo"), kind="d"))
        outs.append(fw.op("sp", lambda e, ck=ck: e.dma_start(out=io["hpay"][ck, :, 8:16], in_=ET[:, ck, TOK:TOK + 8]),
                          reads=[ctx.r_et[ck]], writes=wr("halo"), kind="d"))
    if after_f is not None:
        after_f()
    tmajor(0, 256, 256, lambda tile, ps, b: fw.op(
        "dve", lambda e: e.tensor_copy(out=FST[:, tile, :], in_=ps), reads=[ctx.r_ps[b]], writes=[r_fst]))
    for g in range(4):
        outs.append(fw.op("sp", lambda e, g=g: e.dma_start(
            out=io["fpay"][g].rearrange("(t p) c -> p t c", p=128), in_=FST[:, :, g * 64:(g + 1) * 64]),
            reads=[r_fst], writes=wr("f"), kind="d"))
    load(2)
    tmajor(3, 0, 512, lambda tile, ps, b: fw.op(
        "act" if tile % 2 else "dve",
        (lambda e: e.copy(out=VST[:, :, tile, :], in_=ps.rearrange("p (h e) -> p h e", h=4))) if tile % 2
        else (lambda e: e.tensor_copy(out=VST[:, :, tile, :], in_=ps.rearrange("p (h e) -> p h e", h=4))),
        reads=[ctx.r_ps[b]], writes=[r_vst]))
    load(1)
    fmajor(2, 0, 4, lambda ck, t, ps, b: fw.op(
        "dve", lambda e: e.tensor_copy(out=KST[:, ck, t * 512:(t + 1) * 512], in_=ps),
        reads=[ctx.r_ps[b]], writes=[r_kst[ck]]))
    for h in range(4):
        outs.append(fw.op("sp", lambda e, h=h: e.dma_start(out=io["kpay"][h], in_=KST[:, h, :]),
                          reads=[r_kst[h]], writes=wr(("kv", h)), kind="d"))
        outs.append(fw.op("sp", lambda e, h=h: e.dma_start(out=io["vpay"][h], in_=VST[:, h, :, :]),
                          reads=[r_vst], writes=wr(("kv", h)), kind="d"))
    if after_kv is not None:
        after_kv()
    fmajor(1, 0, 4, lambda ck, t, ps, b: fw.op(
        "act", lambda e: e.mul(out=QC[:, ck, t * 512:(t + 1) * 512], in_=ps, mul=0.125),
        reads=[ctx.r_ps[b]], writes=[ctx.r_qc[ck]]))
    return outs


def _stash_view(ctx):
    return ctx.HY[:, 0:4, :].rearrange("p k t -> p (k t)").bitcast(F32).rearrange("p (c t) -> p c t", c=2)


def emit_stash_et(ctx):
    fw = ctx.fw
    ET = carve(ctx, OFF_ET, [2, ETW], F32)
    SV = _stash_view(ctx)
    for ck in range(2):
        fw.op("dve" if ck == 0 else "act",
              (lambda e, ck=ck: e.tensor_copy(out=SV[:, ck, :], in_=ET[:, ck, 8:8 + TOK])) if ck == 0
              else (lambda e, ck=ck: e.copy(out=SV[:, ck, :], in_=ET[:, ck, 8:8 + TOK])),
              reads=[ctx.r_et[ck]], writes=[r for k in (2 * ck, 2 * ck + 1) for r in ctx.r_hy[k]])


def emit_restore_et(ctx):
    fw = ctx.fw
    ET = carve(ctx, OFF_ET, [2, ETW], F32)
    SV = _stash_view(ctx)
    for ck in range(2):
        fw.op("dve" if ck == 0 else "act",
              (lambda e, ck=ck: e.tensor_copy(out=ET[:, ck, 8:8 + TOK], in_=SV[:, ck, :])) if ck == 0
              else (lambda e, ck=ck: e.copy(out=ET[:, ck, 8:8 + TOK], in_=SV[:, ck, :])),
              reads=[r for k in (2 * ck, 2 * ck + 1) for r in ctx.r_hy[k]], writes=[ctx.r_et[ck]])


def _pool_bufs(ctx):
    fw = ctx.fw
    o = OFF_CONST + 1536
    d = dict(
        PW=carve(ctx, o, [2, 128], BF16),
        PWF=carve(ctx, o + 512, [2, 128], F32),
        PSC=carve(ctx, o + 1536, [2], F32),
        CORR=carve(ctx, o + 1544, [2, 16], F32),
        MASK=carve(ctx, o + 1672, [8], F32),
        HALO=carve(ctx, o + 1704, [2, 4, 16], F32),
    )
    if not hasattr(ctx, "r_pool"):
        ctx.r_pool = {n: fw.res(n) for n in ("pw", "pcst", "halo")}
    return d


def emit_pool_loads(ctx, l, io):
    fw = ctx.fw
    B = _pool_bufs(ctx)
    PW, PWF, PSC, CORR, MASK = B["PW"], B["PWF"], B["PSC"], B["CORR"], B["MASK"]
    r_pw, r_cst = ctx.r_pool["pw"], ctx.r_pool["pcst"]
    fw.op("pool", lambda e: e.memset(PWF, 0.0), writes=[r_pw])
    for g in range(4):
        ck, gl = g // 2, g % 2
        fw.op("sp", lambda e, g=g, ck=ck, gl=gl: e.dma_start(
            out=PWF[gl * 64:(gl + 1) * 64, ck, gl * 64:(gl + 1) * 64], in_=io["pool_w"][l][g]),
            writes=[r_pw], kind="d")
    fw.op("dve", lambda e: e.tensor_copy(out=PW, in_=PWF), reads=[r_pw], writes=[r_pw])
    fw.op("sp", lambda e: e.dma_start(out=PSC, in_=io["pool_scale_t"][l]), writes=[r_cst], kind="d")
    fw.op("sp", lambda e: e.dma_start(out=CORR, in_=io["pcorr"]), writes=[r_cst], kind="d")
    fw.op("sp", lambda e: e.dma_start(out=MASK, in_=io["pmask"]), writes=[r_cst], kind="d")


def emit_pool_halo_load(ctx, io):
    fw = ctx.fw
    HALO = _pool_bufs(ctx)["HALO"]
    hrd = [io["r_gath"]["halo"]] if "r_gath" in io else []
    for k_ in range(2):
        for r_ in range(4):
            fw.op("sp", lambda e, k_=k_, r_=r_: e.dma_start(out=HALO[:, k_, r_, :], in_=io["halo_all"][k_, :, r_, :]),
                  reads=hrd, writes=[ctx.r_pool["halo"]], kind="d")


def emit_pool(ctx, l, io, preloaded=False):
    fw = ctx.fw
    ET = carve(ctx, OFF_ET, [2, ETW], F32)
    T1 = carve(ctx, OFF_HI, [ETW], F32)
    T2 = carve(ctx, OFF_HI + 8256, [ETW], F32)
    o = OFF_HI + 16512
    MT = carve(ctx, o + 2304, [2, TOK], BF16)
    WIN = carve(ctx, o + 2304 + 8192, [TOK], F32)
    B = _pool_bufs(ctx)
    PW, PSC, CORR, MASK, HALO = B["PW"], B["PSC"], B["CORR"], B["MASK"], B["HALO"]
    if not preloaded:
        emit_pool_loads(ctx, l, io)
        emit_pool_halo_load(ctx, io)
    r_pw, r_cst, r_halo = ctx.r_pool["pw"], ctx.r_pool["pcst"], ctx.r_pool["halo"]
    r_t1, r_t2, r_win = (fw.res(n) for n in ("t1", "t2", "win"))
    r_mt = [fw.res("mt") for _ in range(2)]
    for ck in range(2):
        fw.op("dve", lambda e, ck=ck: e.tensor_scalar(out=ET[:, ck, 0:8], in0=HALO[:, ck, 0, 8:16], scalar1=MASK[:, 0:1],
                                                      scalar2=None, op0=ALU.mult),
              reads=[r_halo, r_cst], writes=[ctx.r_et[ck]])
        fw.op("dve", lambda e, ck=ck: e.tensor_scalar(out=ET[:, ck, TOK + 8:TOK + 16], in0=HALO[:, ck, 0, 0:8],
                                                      scalar1=MASK[:, 4:5], scalar2=None, op0=ALU.mult),
              reads=[r_halo, r_cst], writes=[ctx.r_et[ck]])
        for r in range(1, 4):
            fw.op("dve", lambda e, ck=ck, r=r: e.scalar_tensor_tensor(
                out=ET[:, ck, 0:8], in0=HALO[:, ck, r, 8:16], scalar=MASK[:, r:r + 1], in1=ET[:, ck, 0:8],
                op0=ALU.mult, op1=ALU.add), reads=[r_halo, r_cst], writes=[ctx.r_et[ck]])
            fw.op("dve", lambda e, ck=ck, r=r: e.scalar_tensor_tensor(
                out=ET[:, ck, TOK + 8:TOK + 16], in0=HALO[:, ck, r, 0:8], scalar=MASK[:, 4 + r:5 + r],
                in1=ET[:, ck, TOK + 8:TOK + 16], op0=ALU.mult, op1=ALU.add),
                reads=[r_halo, r_cst], writes=[ctx.r_et[ck]])
        E = ET[:, ck, :]
        fw.op("pool", lambda e, E=E: e.tensor_tensor(out=T1[:, 0:ETW - 1], in0=E[:, 0:ETW - 1], in1=E[:, 1:ETW], op=ALU.add),
              reads=[ctx.r_et[ck]], writes=[r_t1])
        if ck == 0:
            lv = {0: (T1, r_t1, 2)}
            fw.op("pool", lambda e: e.tensor_tensor(out=T2[:, 0:ETW - 3], in0=T1[:, 0:ETW - 3], in1=T1[:, 2:ETW - 1], op=ALU.add),
                  reads=[r_t1], writes=[r_t2])
            lv[1] = (T2, r_t2, 4)
        else:
            fw.op("pool", lambda e: e.tensor_tensor(out=T2[:, 0:ETW - 3], in0=T1[:, 0:ETW - 3], in1=T1[:, 2:ETW - 1], op=ALU.add),
                  reads=[r_t1], writes=[r_t2])
            fw.op("pool", lambda e: e.tensor_tensor(out=T1[:, 0:ETW - 7], in0=T2[:, 0:ETW - 7], in1=T2[:, 4:ETW - 3], op=ALU.add),
                  reads=[r_t2], writes=[r_t1])
            fw.op("pool", lambda e: e.tensor_tensor(out=T2[:, 0:ETW - 15], in0=T1[:, 0:ETW - 15], in1=T1[:, 8:ETW - 7], op=ALU.add),
                  reads=[r_t1], writes=[r_t2])
            lv = {0: (T1, r_t1, 8), 1: (T2, r_t2, 16)}
        for gl in range(2):
            src, rsrc, w = lv[gl]
            sh = 8 - w // 2
            ps_ = slice(gl * 64, (gl + 1) * 64)
            fw.op("dve", lambda e, src=src, sh=sh, ps_=ps_: e.tensor_copy(out=WIN[ps_, :], in_=src[ps_, sh:sh + TOK]),
                  reads=[rsrc], writes=[r_win])
            fw.op("dve", lambda e, ck=ck, ps_=ps_: e.tensor_tensor(out=WIN[ps_, 0:8], in0=WIN[ps_, 0:8],
                                                                  in1=CORR[ps_, ck, 0:8], op=ALU.mult),
                  reads=[r_cst], writes=[r_win])
            fw.op("dve", lambda e, ck=ck, ps_=ps_: e.tensor_tensor(out=WIN[ps_, TOK - 8:TOK], in0=WIN[ps_, TOK - 8:TOK],
                                                                  in1=CORR[ps_, ck, 8:16], op=ALU.mult),
                  reads=[r_cst], writes=[r_win])
            fw.op("dve", lambda e, ck=ck, ps_=ps_, w=w: e.scalar_tensor_tensor(
                out=MT[ps_, ck, :], in0=WIN[ps_, :], scalar=1.0 / w, in1=ET[ps_, ck, 8:8 + TOK],
                op0=ALU.mult, op1=ALU.subtract), reads=[r_win, ctx.r_et[ck]], writes=[r_mt[ck]])
    for ck in range(2):
        for t in range(NTC):
            b = t % 4
            ps = bank(ctx, b)
            fw.op("pe", lambda e, ck=ck, t=t, ps=ps: e.matmul(ps, lhsT=PW[:, ck, :], rhs=MT[:, ck, t * 512:(t + 1) * 512],
                                                            start=True, stop=True),
                  reads=[r_pw, r_mt[ck]], writes=[ctx.r_ps[b]])
            fw.op("act", lambda e, ck=ck, t=t, ps=ps: e.mul(out=ctx.HY[:, ck, t * 512:(t + 1) * 512], in_=ps, mul=PSC[:, ck:ck + 1]),
                  reads=[ctx.r_ps[b], r_cst], writes=[ctx.r_hy[ck][t]])


def emit_fourier(ctx, l, io):
    fw = ctx.fw
    XS = carve(ctx, 4096, [128, 64], BF16)
    BB = carve(ctx, 20480, [2, 64, 64], BF16)
    W3 = carve(ctx, 36864, [64, 3, 32], BF16)
    UT = carve(ctx, OFF_HI, [4, 2, TOK], BF16)
    MM = carve(ctx, OFF_HI + 32768, [4, 2, 256], BF16)
    WFS = carve(ctx, OFF_HI + 36864, [4, 256], BF16)
    tb = OFF_HI + 38912
    WA = carve(ctx, tb, [128], BF16)
    C64 = carve(ctx, tb + 960, [2, 64], BF16)
    r_tab, r_wfs, r_mm, r_xs, r_bb = (fw.res(n) for n in ("ftab", "wfs", "mm", "xs", "bb"))
    r_ut = [fw.res("ut") for _ in range(4)]
    fw.op("sp", lambda e: e.dma_start(out=WA[0:64, :], in_=io["f_wa"]), writes=[r_tab], kind="d")
    fw.op("sp", lambda e: e.dma_start(out=W3, in_=io["f_w3"]), writes=[r_tab], kind="d")
    fw.op("sp", lambda e: e.dma_start(out=C64[0:64], in_=io["f_c64"]), writes=[r_tab], kind="d")
    fw.op("pool", lambda e: e.dma_start(out=WFS[0:64], in_=io["fourier_w"][l].rearrange("(g p) c -> p g c", p=64)),
          writes=[r_wfs], kind="d")
    for g in range(4):
        for comp in range(2):
            b = (g * 2 + comp) % 4
            ps = bank(ctx, b)[0:64, 0:256]
            fw.op("pe", lambda e, g=g, comp=comp, ps=ps: e.matmul(ps, lhsT=C64[0:64, comp, :], rhs=WFS[0:64, g, :],
                                                                 start=True, stop=True),
                  reads=[r_tab, r_wfs], writes=[ctx.r_ps[b]])
            fw.op("dve", lambda e, g=g, comp=comp, ps=ps: e.tensor_copy(out=MM[0:64, g, comp, :], in_=ps),
                  reads=[ctx.r_ps[b]], writes=[r_mm])
    for g in range(4):
        if "load_xs" in io:
            io["load_xs"](g, XS, r_xs)
        else:
            fw.op("sp", lambda e, g=g: e.dma_start(out=XS[0:64], in_=io["fg"][g].rearrange("(s1 s2) c -> s1 s2 c", s2=128)),
                  writes=[r_xs], kind="d")
        for rd in range(4):
            pb0 = 4 * (rd % 2)
            PSV = ctx.PS[:, pb0 * 512:(pb0 + 4) * 512].rearrange("p (c x) -> p c x", x=128)
            rps = [ctx.r_ps[pb0 + i] for i in range(4)]
            for ci in range(16):
                c = rd * 16 + ci
                fw.op("pe", lambda e, c=c, ci=ci, PSV=PSV: e.matmul(PSV[:, ci, :], lhsT=XS[0:64, :, c], rhs=WA[0:64, :],
                                                                   start=True, stop=True),
                      reads=[r_xs, r_tab], writes=[rps[ci // 4]])
            AR = PSV[:, :, 0:64]
            AI = PSV[:, :, 64:128]
            cs = slice(rd * 16, (rd + 1) * 16)
            BRv = BB[:, 0, :, cs].rearrange("p k c -> p c k")
            BIv = BB[:, 1, :, cs].rearrange("p k c -> p c k")
            fw.op("act", lambda e, AR=AR, BRv=BRv: e.copy(out=BRv, in_=AR), reads=rps, writes=[r_bb])
            fw.op("dve", lambda e, AI=AI, BIv=BIv: e.tensor_copy(out=BIv, in_=AI), reads=rps, writes=[r_bb])
        for q4 in range(4):
            pr, pi = (q4 % 2) * 2, (q4 % 2) * 2 + 1
            for kk in range(16):
                k1 = q4 * 16 + kk
                outr = bank(ctx, pr)[0:64, kk * 32:(kk + 1) * 32]
                outi = bank(ctx, pi)[0:64, kk * 32:(kk + 1) * 32]
                fw.op("pe", lambda e, k1=k1, outr=outr: e.matmul(outr, lhsT=BB[:, 0, k1, :], rhs=W3[:, k1, 0, :], start=True, stop=False),
                      reads=[r_bb, r_tab], writes=[ctx.r_ps[pr]])
                fw.op("pe", lambda e, k1=k1, outr=outr: e.matmul(outr, lhsT=BB[:, 1, k1, :], rhs=W3[:, k1, 2, :], start=False, stop=True),
                      reads=[r_bb, r_tab], writes=[ctx.r_ps[pr]])
                fw.op("pe", lambda e, k1=k1, outi=outi: e.matmul(outi, lhsT=BB[:, 1, k1, :], rhs=W3[:, k1, 0, :], start=True, stop=False),
                      reads=[r_bb, r_tab], writes=[ctx.r_ps[pi]])
                fw.op("pe", lambda e, k1=k1, outi=outi: e.matmul(outi, lhsT=BB[:, 0, k1, :], rhs=W3[:, k1, 1, :], start=False, stop=True),
                      reads=[r_bb, r_tab], writes=[ctx.r_ps[pi]])
            for comp, pbk in ((0, pr), (1, pi)):
                src = bank(ctx, pbk)[0:64, :].rearrange("p (k j) -> p k j", j=32)
                dstv = UT[0:64, g, comp, :].rearrange("p (j k) -> p k j", k=64)[:, q4 * 16:(q4 + 1) * 16, :]
                fw.op("act" if comp else "dve",
                      (lambda e, src=src, dstv=dstv: e.copy(out=dstv, in_=src)) if comp
                      else (lambda e, src=src, dstv=dstv: e.tensor_copy(out=dstv, in_=src)),
                      reads=[ctx.r_ps[pbk]], writes=[r_ut[g]])
    for ck in range(2):
        for t in range(NTC):
            b = 4 + (ck * NTC + t) % 4
            ps = bank(ctx, b)
            n = 0
            for g in range(4):
                for comp in range(2):
                    fw.op("pe", lambda e, g=g, comp=comp, ck=ck, t=t, ps=ps, n=n: e.matmul(
                        ps, lhsT=MM[0:64, g, comp, ck * 128:(ck + 1) * 128], rhs=UT[0:64, g, comp, t * 512:(t + 1) * 512],
                        start=(n == 0), stop=(n == 7)),
                        reads=[r_mm, r_ut[g]], writes=[ctx.r_ps[b]])
                    n += 1
            fw.op("act", lambda e, ck=ck, t=t, ps=ps: e.copy(out=ctx.HY[:, 2 + ck, t * 512:(t + 1) * 512], in_=ps),
                  reads=[ctx.r_ps[b]], writes=[ctx.r_hy[2 + ck][t]])


def emit_attn(ctx, l, io):
    fw = ctx.fw
    K0 = carve(ctx, 4096, [SEQ], BF16)
    K1 = carve(ctx, 20480, [SEQ], BF16)
    Q0 = carve(ctx, 36864, [TOK], BF16)
    Q1 = carve(ctx, 40960, [TOK], BF16)
    DT = carve(ctx, 45056, [4, 128], BF16)
    o = 46080
    LAMV = carve(ctx, o, [256], F32)
    LTMP = carve(ctx, o + 1024, [64], F32)
    LS = carve(ctx, o + 1280, [8], F32)
    GN = carve(ctx, o + 1312, [4], F32)
    SQH = carve(ctx, o + 1344, [512], BF16)
    QC = carve(ctx, OFF_QC, [4, TOK], BF16)
    V = carve(ctx, OFF_HI, [64, 128], BF16)
    PT = [carve(ctx, OFF_HI + 16384 + i * 2048, [1024], BF16) for i in range(3)]
    FT = [carve(ctx, OFF_HI + 22528 + i * 2048, [512], F32) for i in range(4)]
    ACC = carve(ctx, OFF_HI + 30720, [1024], F32)
    OS = [carve(ctx, OFF_HI + 34816 + i * 2048, [512], F32) for i in range(2)]
    r_os = [fw.res("os0"), fw.res("os1")]
    r_acc2 = [fw.res("acc0"), fw.res("acc1")]
    r_d, r_lam = (fw.res(n) for n in ("dt", "lam"))
    r_k = [fw.res("k") for _ in range(10)]
    r_v = [fw.res("v") for _ in range(4)]
    r_q = [fw.res("q") for _ in range(4)]
    r_pt = [[fw.res("pt0"), fw.res("pt1")] for _ in range(3)]
    r_ft = [fw.res("ft") for _ in range(4)]
    r_sqh = fw.res("sqh")
    LI = carve(ctx, o + 2368, [2], F32)
    fw.op("sp", lambda e: e.dma_start(out=DT, in_=io["dtile"]), writes=[r_d], kind="d")
    fw.op("sp", lambda e: e.dma_start(out=ctx.ident_bf, in_=io["ident"]), writes=[ctx.r_const], kind="d")
    fw.op("sp", lambda e: e.dma_start(out=LAMV, in_=io["lamvec"][l].partition_broadcast(128)), writes=[r_lam], kind="d")
    fw.op("sp", lambda e: e.dma_start(out=GN, in_=io["head_norm_t"][l]), writes=[r_lam], kind="d")
    fw.op("sp", lambda e: e.dma_start(out=LI, in_=io["laminit"][l]), writes=[r_lam], kind="d")
    for i in range(2):
        fw.op("dve", lambda e, i=i: e.tensor_tensor(out=LTMP, in0=LAMV[:, i * 128:i * 128 + 64],
                                                    in1=LAMV[:, i * 128 + 64:i * 128 + 128], op=ALU.mult),
              reads=[r_lam], writes=[r_lam])
        fw.op("dve", lambda e, i=i: e.reduce_sum(out=LS[:, i:i + 1], in_=LTMP, axis=mybir.AxisListType.X),
              reads=[r_lam], writes=[r_lam])
    fw.op("act", lambda e: e.activation(out=LS[:, 2:4], in_=LS[:, 0:2], func=AF.Exp), reads=[r_lam], writes=[r_lam])
    fw.op("dve", lambda e: e.tensor_tensor(out=LS[:, 4:5], in0=LS[:, 3:4], in1=LS[:, 2:3], op=ALU.subtract),
          reads=[r_lam], writes=[r_lam])
    fw.op("dve", lambda e: e.tensor_tensor(out=LS[:, 5:6], in0=LS[:, 4:5], in1=LI[:, 0:1], op=ALU.add),
          reads=[r_lam], writes=[r_lam])
    fw.op("dve", lambda e: e.tensor_scalar(out=GN, in0=GN, scalar1=LI[:, 1:2], scalar2=None, op0=ALU.mult),
          reads=[r_lam], writes=[r_lam])
    NEGLAM = LS[:, 5:6]

    it = {"n": 0}
    pend = {"tail": None}
    for hi_, h in enumerate(io.get("head_order", list(range(HEADS)))):
        if "load_kv" in io:
            io["load_kv"](h, K0, K1, V, r_k, r_v)
        else:
            fw.op("sp", lambda e, h=h: e.dma_start(out=K0[0:64, :], in_=io["kg"][h, 0:64, :]), writes=r_k[0:4], kind="d")
            fw.op("sp", lambda e, h=h: e.dma_start(out=K1[0:64, :], in_=io["kg"][h, 64:128, :]), writes=r_k[4:8], kind="d")
            fw.op("sp", lambda e, h=h: e.dma_start(out=V, in_=io["vg"][h]), writes=r_v, kind="d")
        fw.op("sp", lambda e, h=h: e.dma_start(out=K0[64:73, :], in_=io["kaug0"][h]), writes=[r_k[8]], kind="d")
        fw.op("sp", lambda e, h=h: e.dma_start(out=K1[64:73, :], in_=io["kaug0"][h]), writes=[r_k[9]], kind="d")
        if hi_ == 0 and "after_first_loads" in io:
            io["after_first_loads"](r_k, r_v)
        fw.op("sp", lambda e, h=h: e.dma_start(out=Q0[64:73, :], in_=io["qaug0"][h]), writes=[r_q[0]], kind="d")
        fw.op("sp", lambda e, h=h: e.dma_start(out=Q1[64:73, :], in_=io["qaug0"][h]), writes=[r_q[1]], kind="d")
        fw.op("dve", lambda e, h=h: e.tensor_copy(out=Q0[0:64, :], in_=QC[0:64, h, :]), reads=[ctx.r_qc[h]], writes=[r_q[2]])
        fw.op("sp", lambda e, h=h: e.dma_start(out=Q1[0:64, :], in_=QC[64:128, h, :]), reads=[ctx.r_qc[h]], writes=[r_q[3]], kind="d")

        def s_mm(Q, L, sb, hh=h):
            ks = slice(L * 128, (L + 1) * 128)
            for j in range(2):
                KT, QT = (K0, Q0) if j == 0 else (K1, Q1)
                ps = bank(ctx, sb + j)

                def rng(mode):
                    return {"diag": slice(0, 65), "below": slice(0, 69), "above": slice(0, 73)}[mode]

                def mm(out, mode, qs, start=True, stop=True, KT=KT, QT=QT, j=j):
                    pr = rng(mode)
                    rb = ctx.r_ps[sb + j]
                    fw.op("pe", lambda e: e.matmul(out, lhsT=KT[pr, ks], rhs=QT[pr, qs], start=start, stop=stop),
                          reads=r_k + r_q, writes=[rb])

                if L >= 16 or L < 4 * Q:
                    mm(ps, "below", slice(Q * 512, (Q + 1) * 512))
                elif L >= 4 * Q + 4:
                    mm(ps, "above", slice(Q * 512, (Q + 1) * 512))
                else:
                    us = L - 4 * Q
                    for u in range(4):
                        qs = slice(Q * 512 + u * 128, Q * 512 + (u + 1) * 128)
                        out = ps[:, u * 128:(u + 1) * 128]
                        if u > us:
                            mm(out, "below", qs)
                        elif u < us:
                            mm(out, "above", qs)
                        else:
                            mm(out, "diag", qs, start=True, stop=False)
                            fw.op("pe", lambda e, out=out, hh=hh: e.matmul(out, lhsT=ctx.ident_bf, rhs=DT[:, hh, :], start=False, stop=True),
                                  reads=[ctx.r_const, r_d], writes=[ctx.r_ps[sb + j]])

        for Q in range(4):
            qcols = slice(Q * 512, (Q + 1) * 512)
            if BAND[h] is None:
                Ls = list(range(64))
            else:
                Ls = [(4 * Q + d_) % 64 for d_ in range(-BAND[h], BAND[h] + 4)]
            nL = len(Ls)
            s_mm(Q, Ls[0], 0)
            for li, L in enumerate(Ls):
                sb = 2 * (li % 2)
                if li + 1 < nL:
                    s_mm(Q, Ls[li + 1], 2 * ((li + 1) % 2))
                pi = it["n"] % 3
                it["n"] += 1
                if li == 3 and pend["tail"] is not None:
                    pend["tail"]()
                    pend["tail"] = None
                for j in range(2):
                    fw.op("act", lambda e, sb=sb, pi=pi, j=j: e.activation(out=PT[pi][:, j * 512:(j + 1) * 512], in_=bank(ctx, sb + j), func=AF.Exp),
                          reads=[ctx.r_ps[sb + j]], writes=[r_pt[pi][j]])
                for j in range(2):
                    fw.op("pe", lambda e, L=L, j=j, pi=pi, li=li, nL=nL: e.matmul(bank(ctx, 4 + j), lhsT=V[:, L, :], rhs=PT[pi][:, j * 512:(j + 1) * 512],
                                                                   start=(li == 0), stop=(li == nL - 1)),
                          reads=r_v + [r_pt[pi][j]], writes=[ctx.r_ps[4 + j]])
                    if j == 0:
                        fw.op("pe", lambda e, pi=pi, li=li, nL=nL: e.matmul(bank(ctx, 6), lhsT=ctx.ones_bf, rhs=PT[pi][:, 0:512],
                                                                      start=(li == 0), stop=(li == nL - 1)),
                              reads=[ctx.r_const, r_pt[pi][0]], writes=[ctx.r_ps[6]])
                if li == 0:
                    fw.op("dve", lambda e, pi=pi: e.tensor_copy(out=ACC[:, 512:1024], in_=PT[pi][:, 512:1024]),
                          reads=[r_pt[pi][1]], writes=[r_acc2[1]])
                else:
                    fw.op("dve", lambda e, pi=pi: e.tensor_tensor(out=ACC[:, 512:1024], in0=ACC[:, 512:1024],
                                                                  in1=PT[pi][:, 512:1024], op=ALU.add),
                          reads=[r_pt[pi][1]], writes=[r_acc2[1]])
            fw.op("pe", lambda e: e.matmul(bank(ctx, 7), lhsT=ctx.ones_f, rhs=ACC[:, 512:1024], start=True, stop=True),
                  reads=[ctx.r_const, r_acc2[1]], writes=[ctx.r_ps[7]])
            fw.op("act", lambda e: e.copy(out=OS[0], in_=bank(ctx, 4)), reads=[ctx.r_ps[4]], writes=[r_os[0]])
            fw.op("act", lambda e: e.copy(out=OS[1], in_=bank(ctx, 5)), reads=[ctx.r_ps[5]], writes=[r_os[1]])
            fw.op("dve", lambda e: e.reciprocal(out=FT[0], in_=bank(ctx, 6)), reads=[ctx.r_ps[6]], writes=[r_ft[0]])
            fw.op("dve", lambda e: e.reciprocal(out=FT[1], in_=bank(ctx, 7)), reads=[ctx.r_ps[7]], writes=[r_ft[1]])
            fw.op("dve", lambda e: e.tensor_tensor(out=FT[0], in0=OS[0], in1=FT[0], op=ALU.mult),
                  reads=[r_os[0], r_ft[0]], writes=[r_ft[0]])
            fw.op("dve", lambda e: e.tensor_tensor(out=FT[1], in0=OS[1], in1=FT[1], op=ALU.mult),
                  reads=[r_os[1], r_ft[1]], writes=[r_ft[1]])
            fw.op("dve", lambda e: e.scalar_tensor_tensor(out=FT[2], in0=FT[1], scalar=NEGLAM, in1=FT[0],
                                                          op0=ALU.mult, op1=ALU.add),
                  reads=[r_ft[0], r_ft[1], r_lam], writes=[r_ft[2]])
            fw.op("act", lambda e: e.activation(out=SQH, in_=FT[2], func=AF.Square), reads=[r_ft[2]], writes=[r_sqh])
            def tail(h=h, qcols=qcols, t=Q):
                fw.op("pe", lambda e: e.matmul(bank(ctx, 7), lhsT=ctx.ones_bf, rhs=SQH, start=True, stop=True),
                      reads=[ctx.r_const, r_sqh], writes=[ctx.r_ps[7]])
                fw.op("act", lambda e: e.activation(out=FT[3], in_=bank(ctx, 7), func=AF.Sqrt, bias=ctx.eps_col, scale=1.0 / 128.0),
                      reads=[ctx.r_ps[7], ctx.r_const], writes=[r_ft[3]])
                fw.op("dve", lambda e: e.reciprocal(out=FT[3], in_=FT[3]), reads=[r_ft[3]], writes=[r_ft[3]])
                fw.op("dve", lambda e: e.scalar_tensor_tensor(
                    out=ctx.HY[:, 4 + h, qcols], in0=FT[2], scalar=GN[:, h:h + 1], in1=FT[3], op0=ALU.mult, op1=ALU.mult),
                    reads=[r_ft[2], r_ft[3], r_lam], writes=[ctx.r_hy[4 + h][t]])
            pend["tail"] = tail
    if pend["tail"] is not None:
        pend["tail"]()
        pend["tail"] = None


def emit_wout(ctx, l, io):
    fw = ctx.fw
    WO = carve(ctx, 4096, [KC, 1024], BF16)
    r_wo = [fw.res("wo") for _ in range(2)]
    wv = io["w_out"][l].rearrange("(k p) d -> p k d", p=128)
    for hlf in range(2):
        fw.op("pool", lambda e, hlf=hlf: e.dma_start(out=WO[:, hlf * 4:(hlf + 1) * 4, :], in_=wv[:, hlf * 4:(hlf + 1) * 4, :]),
              writes=[r_wo[hlf]], kind="d")
    n = 0
    for dc in range(KC):
        for t in range(NTC):
            b = n % 4
            n += 1
            ps = bank(ctx, b)
            for k in range(KC):
                fw.op("pe", lambda e, k=k, dc=dc, t=t, ps=ps: e.matmul(
                    ps, lhsT=WO[:, k, dc * 128:(dc + 1) * 128], rhs=ctx.HY[:, k, t * 512:(t + 1) * 512],
                    start=(k == 0), stop=(k == KC - 1)),
                    reads=[r_wo[k // 4], ctx.r_hy[k][t]], writes=[ctx.r_ps[b]])
            fw.op("dve", lambda e, dc=dc, t=t, ps=ps: e.tensor_tensor(
                out=ctx.XT[:, dc, t * 512:(t + 1) * 512], in0=ps, in1=ctx.XT[:, dc, t * 512:(t + 1) * 512], op=ALU.add),
                reads=[ctx.r_ps[b]], writes=[ctx.r_xt[dc][t]])


BIG_A = {"ffn2_w_gate": [D_MODEL, D_FF], "ffn2_w_up": [D_MODEL, D_FF], "ffn2_w_down": [D_FF, D_MODEL],
         "w_out": [D_MODEL, D_MODEL]}
BIG_B = {"ffn1_w_gate": [D_MODEL, D_FF], "ffn1_w_up": [D_MODEL, D_FF], "ffn1_w_down": [D_FF, D_MODEL],
         "w_in": [D_MODEL, 2048]}
SMALL_A = {"ffn2_norm": [D_MODEL], "pool_w": [4, 64, 64], "fourier_w": [256, 256], "pool_scale_t": [128, 2],
           "head_norm_t": [128, 4], "lamvec": [256], "laminit": [128, 2]}
SMALL_B = {"ffn1_norm": [D_MODEL], "mix_norm": [D_MODEL]}
TABLE_SPECS = {
    "pcorr": ([128, 2, 16], F32), "pmask": ([128, 8], F32),
    "f_wa": ([64, 128], BF16),
    "f_w3": ([128, 64, 3, 32], BF16), "f_c64": ([64, 2, 64], BF16),
    "dtile": ([128, 4, 128], BF16), "ident": ([128, 128], BF16),
    "kaug0": ([4, 9, SEQ], BF16), "qaug0": ([4, 9, TOK], BF16),
}
PAY_SPECS = {
    "kpay": ([4, 128, TOK], BF16), "vpay": ([4, 128, 16, 128], BF16), "fpay": ([4, TOK, 64], BF16),
    "hpay": ([2, 128, 16], F32), "qc_out": ([128, 4, TOK], BF16), "et_out": ([128, 2, TOK], F32),
    "x_out": ([D_MODEL, TOK], F32),
}
GATH_SPECS = {
    "kg": ([4, 128, SEQ], BF16), "vg": ([4, 128, 64, 128], BF16), "fg": ([4, SEQ, 64], BF16),
    "halo_all": ([2, 128, 4, 16], F32), "qc_in": ([128, 4, TOK], BF16), "et_in": ([128, 2, TOK], F32),
}


class _One:
    def __init__(self, ap):
        self.ap = ap

    def __getitem__(self, _):
        return self.ap


def build_launch(kind, dbg_phases=None):
    nc = bass.Bass("TRN2", target_bir_lowering=False)
    io = {}

    def inp(name, shp, dt=F32):
        return nc.dram_tensor(name, shp, dt, kind="ExternalInput").ap()

    io["x_in"] = inp("x_in", [D_MODEL, TOK])
    if kind != "first":
        for n, shp in {**BIG_A, **SMALL_A}.items():
            io[n] = _One(inp(n, shp))
        for n, (shp, dt) in TABLE_SPECS.items():
            io[n] = inp(n, shp, dt)
        for n, (shp, dt) in GATH_SPECS.items():
            io[n] = inp(n, shp, dt)
    if kind != "last":
        for n, shp in {**BIG_B, **SMALL_B}.items():
            io[n] = _One(inp(n, shp))
        for n, (shp, dt) in PAY_SPECS.items():
            io[n] = nc.dram_tensor(n, shp, dt, kind="ExternalOutput").ap()
    else:
        io["final_norm"] = inp("final_norm", [D_MODEL])
        io["y"] = nc.dram_tensor("y", [D_MODEL, TOK], F32, kind="ExternalOutput").ap()
    with ExitStack() as stack:
        ctx = Ctx()
        ctx.nc = nc
        fw = ctx.fw = FW(nc, stack)
        setup_memory(nc, stack, ctx)
        emit_consts(ctx)
        emit_load_x(ctx, io["x_in"])
        if kind != "first":
            ET = carve(ctx, OFF_ET, [2, ETW], F32)
            QC = carve(ctx, OFF_QC, [4, TOK], BF16)
            ctx.r_et = [fw.res("et") for _ in range(2)]
            ctx.r_qc = [fw.res("qc") for _ in range(4)]
            for ck in range(2):
                fw.op("sp", lambda e, ck=ck: e.dma_start(out=ET[:, ck, 8:8 + TOK], in_=io["et_in"][:, ck, :]),
                      writes=[ctx.r_et[ck]], kind="d")
            for h in range(4):
                fw.op("sp", lambda e, h=h: e.dma_start(out=QC[:, h, :], in_=io["qc_in"][:, h, :]),
                      writes=[ctx.r_qc[h]], kind="d")
            if dbg_phases is None or "pool" in dbg_phases:
                emit_pool(ctx, 0, io)
                fw.barrier()
            if dbg_phases is None or "fourier" in dbg_phases:
                emit_fourier(ctx, 0, io)
                fw.barrier()
            if dbg_phases is None or "attn" in dbg_phases:
                emit_attn(ctx, 0, io)
                fw.barrier()
            if dbg_phases is not None:
                hy_out = nc.dram_tensor("hy_out", [128, KC, TOK], BF16, kind="ExternalOutput").ap()
                for k in range(KC):
                    fw.op("sp", lambda e, k=k: e.dma_start(out=hy_out[:, k, :], in_=ctx.HY[:, k, :]),
                          reads=ctx.r_hy[k], kind="d")
                fw.barrier()
                fw.op("sp", None)
                fw.emit()
                return nc
            emit_wout(ctx, 0, io)
            fw.barrier()
            emit_ffn(ctx, io["ffn2_norm"][0], io["ffn2_w_gate"][0], io["ffn2_w_up"][0], io["ffn2_w_down"][0], OFF_DYN)
            fw.barrier()
            fw.new_epoch()
        if kind == "last":
            emit_final_norm(ctx, io["final_norm"], io["y"], OFF_DYN)
        else:
            emit_ffn(ctx, io["ffn1_norm"][0], io["ffn1_w_gate"][0], io["ffn1_w_up"][0], io["ffn1_w_down"][0], OFF_DYN)
            fw.barrier()
            emit_proj(ctx, 0, io)
            ET = carve(ctx, OFF_ET, [2, ETW], F32)
            QC = carve(ctx, OFF_QC, [4, TOK], BF16)
            for ck in range(2):
                fw.op("sp", lambda e, ck=ck: e.dma_start(out=io["et_out"][:, ck, :], in_=ET[:, ck, 8:8 + TOK]),
                      reads=[ctx.r_et[ck]], kind="d")
            for h in range(4):
                fw.op("sp", lambda e, h=h: e.dma_start(out=io["qc_out"][:, h, :], in_=QC[:, h, :]),
                      reads=[ctx.r_qc[h]], kind="d")
            fw.barrier()
            emit_store_x(ctx, io["x_out"])
        fw.barrier()
        fw.op("sp", None)
        fw.emit()
        nc._fw_stats = (len(fw.ops), fw.n_waits, dict(fw.count_log), max(dma_v for dma_v in [0]))
    return nc


def _bf(a):
    return np.asarray(a, dtype=np.float32).astype(ml_dtypes.bfloat16)


def make_tables(r):
    t = {}
    pcorr = np.ones((128, 2, 16), np.float32)
    for ck in range(2):
        for p in range(128):
            w = POOL_W[2 * ck + p // 64]
            left = w // 2
            right = w - 1 - left
            for i in range(8):
                if r == 0:
                    tt = i
                    cnt = min(tt + right + 1, SEQ) - max(tt - left, 0)
                    pcorr[p, ck, i] = w / cnt
                if r == 3:
                    tt = SEQ - 8 + i
                    cnt = min(tt + right + 1, SEQ) - max(tt - left, 0)
                    pcorr[p, ck, 8 + i] = w / cnt
    t["pcorr"] = pcorr
    pmask = np.zeros((128, 8), np.float32)
    if r - 1 >= 0:
        pmask[:, r - 1] = 1.0
    if r + 1 <= 3:
        pmask[:, 4 + r + 1] = 1.0
    t["pmask"] = pmask
    s1 = np.arange(64)[:, None]
    k1 = np.arange(64)[None, :]
    ang = 2 * np.pi * ((s1 * k1) % 64) / 64.0
    t["f_wa"] = _bf(np.concatenate([np.cos(ang), -np.sin(ang)], axis=1))
    s2 = np.arange(128)[:, None]
    ang = 2 * np.pi * ((s2 * k1) % SEQ) / float(SEQ)
    t["f_tr"] = (np.cos(ang) * FNORM).astype(np.float32)
    t["f_ti"] = (-np.sin(ang) * FNORM).astype(np.float32)
    del t["f_tr"], t["f_ti"]
    s2c = np.arange(128, dtype=np.int64)[:, None, None]
    k1c = np.arange(64, dtype=np.int64)[None, :, None]
    k2c = (32 * r + np.arange(32, dtype=np.int64))[None, None, :]
    ph = 2 * np.pi * (((k1c * s2c) + 64 * (k2c * s2c)) % SEQ) / float(SEQ)
    wr_ = np.cos(ph) * FNORM
    wi_ = -np.sin(ph) * FNORM
    t["f_w3"] = _bf(np.stack([wr_, wi_, -wi_], axis=2))
    c = np.arange(64)[:, None]
    cp = np.arange(64)[None, :]
    ang = 2 * np.pi * ((c * cp) % 64) / 64.0
    t["f_c64"] = _bf(np.stack([np.cos(ang), np.sin(ang)], axis=1))
    p = np.arange(128)
    dt = np.zeros((128, 4, 128), np.float32)
    for h in range(4):
        dt[:, h, :] = -SLOPES[h] * np.abs(p[:, None] - p[None, :])
    t["dtile"] = _bf(dt)
    t["ident"] = _bf(np.eye(128))
    L = np.arange(64)
    n = (16 * r + L) % 64
    sig = np.where(L < 16, 1.0, np.where(n < 16 * r, 1.0, -1.0))
    ncol = np.repeat(n, 128).astype(np.float64)
    sigc = np.repeat(sig, 128)
    pcol = np.tile(p, 64).astype(np.float64)
    kaug0 = np.zeros((4, 9, SEQ), np.float32)
    kaug1 = np.zeros((4, 64, SEQ), np.float32)
    qaug0 = np.zeros((4, 9, TOK), np.float32)
    qaug1 = np.zeros((4, 64, TOK), np.float32)
    tq = 2048 * r + np.arange(TOK)
    nq = (tq // 256).astype(np.float64)
    bq = (tq % 256).astype(np.float64)
    for h in range(4):
        m = SLOPES[h]
        A = np.stack([sigc * m * 128.0 * ncol, sigc * m * pcol, sigc, sigc])
        B = np.stack([np.ones(TOK), np.ones(TOK), -m * 256.0 * nq, -m * bq])
        kaug0[h, 0] = 1.0
        kaug0[h, 1:5] = A
        kaug0[h, 5:9] = A
        kaug1[h, 0:4] = A
        kaug1[h, 32:36] = A
        qaug0[h, 0] = 0.0
        qaug0[h, 1:5] = B
        qaug0[h, 5:9] = -2.0 * B
        qaug1[h, 32:36] = B
        qaug1[h, 0:4] = -2.0 * B
    for nm, a in (("kaug0", kaug0), ("qaug0", qaug0)):
        b = _bf(a)
        assert np.array_equal(b.astype(np.float32), a), nm
        t[nm] = b
    return t


def _layer_small(inputs, l):
    f32 = np.float32
    d = {}
    d["pool_scale_t"] = np.ascontiguousarray(np.asarray(inputs["pool_scale"][l], f32).reshape(2, 128).T)
    d["head_norm_t"] = np.ascontiguousarray(np.asarray(inputs["attn_head_norm"][l], f32).reshape(4, 128).T)
    d["lamvec"] = np.concatenate([np.asarray(inputs[k][l], f32) for k in ("lam_q1", "lam_k1", "lam_q2", "lam_k2")])
    li = lambda_init_fn(l)
    d["laminit"] = np.tile(np.array([[-li, 1.0 - li]], f32), (128, 1))
    return d


def _run(nc, in_maps):
    res = run_bass_kernel_spmd(nc, in_maps, core_ids=list(range(NCORES)))
    return res.results


class _Lay:
    def __init__(self, ap):
        self.ap = ap

    def __getitem__(self, l):
        return self.ap[l]


FUSED_W = {
    "ffn1_norm": [DEPTH, D_MODEL], "ffn1_w_gate": [DEPTH, D_MODEL, D_FF], "ffn1_w_up": [DEPTH, D_MODEL, D_FF],
    "ffn1_w_down": [DEPTH, D_FF, D_MODEL], "mix_norm": [DEPTH, D_MODEL], "w_in": [DEPTH, D_MODEL, 2048],
    "pool_w": [DEPTH, 4, 64, 64], "fourier_w": [DEPTH, 256, 256], "w_out": [DEPTH, D_MODEL, D_MODEL],
    "ffn2_norm": [DEPTH, D_MODEL], "ffn2_w_gate": [DEPTH, D_MODEL, D_FF], "ffn2_w_up": [DEPTH, D_MODEL, D_FF],
    "ffn2_w_down": [DEPTH, D_FF, D_MODEL],
    "pool_scale_t": [DEPTH, 128, 2], "head_norm_t": [DEPTH, 128, 4], "lamvec": [DEPTH, 256], "laminit": [DEPTH, 128, 2],
}
GROUPS = [[0, 1, 2, 3], [4, 5, 6, 7]]


def build_fused(depth=DEPTH):
    nc = bass.Bass("TRN2", target_bir_lowering=False)
    io = {}

    def inp(name, shp, dt=F32):
        return nc.dram_tensor(name, shp, dt, kind="ExternalInput").ap()

    io["x_in"] = inp("x_in", [D_MODEL, TOK])
    for n, shp in FUSED_W.items():
        io[n] = _Lay(inp(n, shp))
    io["final_norm"] = inp("final_norm", [D_MODEL])
    for n, (shp, dt) in TABLE_SPECS.items():
        io[n] = inp(n, shp, dt)
    io["y"] = nc.dram_tensor("y", [D_MODEL, TOK], F32, kind="ExternalOutput").ap()
    pay_kv = [nc.dram_tensor(f"pay_kv{h}", [256, TOK], BF16) for h in range(4)]
    kvg = [nc.dram_tensor(f"kvg{h}", [4 * 256, TOK], BF16) for h in range(4)]
    pay_f = nc.dram_tensor("pay_f", [4 * TOK, 64], BF16)
    fgat = nc.dram_tensor("fgat", [4 * 4 * TOK, 64], BF16)
    pay_h = nc.dram_tensor("pay_h", [256, 16], F32)
    hgat = nc.dram_tensor("hgat", [4 * 256, 16], F32)
    io["kpay"] = [pay_kv[h].ap()[0:128, :] for h in range(4)]
    io["vpay"] = [pay_kv[h].ap()[128:256, :].rearrange("p (t e) -> p t e", e=128) for h in range(4)]
    io["fpay"] = [pay_f.ap()[g * TOK:(g + 1) * TOK, :] for g in range(4)]
    io["hpay"] = pay_h.ap().rearrange("(k p) j -> k p j", p=128)
    io["halo_all"] = hgat.ap().rearrange("(r k p) j -> k p r j", r=4, k=2)
    with ExitStack() as stack:
        ctx = Ctx()
        ctx.nc = nc
        fw = ctx.fw = FW(nc, stack)
        setup_memory(nc, stack, ctx)
        rp = {"f": fw.res("pay_f"), "halo": fw.res("pay_h")}
        rg = {"f": fw.res("fgat"), "halo": fw.res("hgat")}
        for h in range(4):
            rp[("kv", h)] = fw.res("pay_kv")
            rg[("kv", h)] = fw.res("kvg")
        io["r_pay"] = rp
        io["r_gath"] = rg

        def cc(src, dst, key, extra=()):
            fw.op("pool", lambda e: e.collective_compute("AllGather", ALU.bypass, replica_groups=GROUPS,
                                                         ins=[src.ap().opt()], outs=[dst.ap().opt()]),
                  reads=[rp[key]] + list(extra), writes=[rg[key]], kind="cc")

        HEAD_ORDER = [3, 2, 1, 0]

        def after_f():
            cc(pay_h, hgat, "halo")
            emit_pool_halo_load(ctx, io)

        def after_kv():
            h0 = HEAD_ORDER[0]
            cc(pay_kv[h0], kvg[h0], ("kv", h0))

        def after_first_loads(r_k, r_v):
            for h in HEAD_ORDER[1:]:
                cc(pay_kv[h], kvg[h], ("kv", h), extra=list(r_k) + list(r_v))
            cc(pay_f, fgat, "f", extra=list(r_k) + list(r_v))

        def load_xs(g, XS, r_xs):
            fv = fgat.ap()
            for j in range(4):
                base = j * 4 * TOK + g * TOK
                fw.op("sp", lambda e, j=j, base=base: e.dma_start(
                    out=XS[16 * j:16 * (j + 1)], in_=fv[base:base + TOK, :].rearrange("(s1 s2) c -> s1 s2 c", s2=128)),
                    reads=[rg["f"]], writes=[r_xs], kind="d")

        def load_kv(h, K0, K1, V, r_k, r_v):
            kv = kvg[h].ap()
            for i in range(4):
                def mk(i=i, part=0):
                    def f(e):
                        rank = (ctx.pid + i) % 4
                        if part == 0:
                            return e.dma_start(out=K0[0:64, i * TOK:(i + 1) * TOK], in_=kv[bass.ds(rank * 256, 64), :])
                        if part == 1:
                            return e.dma_start(out=K1[0:64, i * TOK:(i + 1) * TOK], in_=kv[bass.ds(rank * 256 + 64, 64), :])
                        return e.dma_start(out=V[:, 16 * i:16 * (i + 1), :],
                                           in_=kv[bass.ds(rank * 256 + 128, 128), :].rearrange("p (t e) -> p t e", e=128))
                    return f
                fw.op("sp", mk(i, 0), reads=[rg[("kv", h)]], writes=[r_k[i]], kind="d")
                fw.op("sp", mk(i, 1), reads=[rg[("kv", h)]], writes=[r_k[4 + i]], kind="d")
                fw.op("sp", mk(i, 2), reads=[rg[("kv", h)]], writes=[r_v[i]], kind="d")

        io["load_xs"] = load_xs
        io["load_kv"] = load_kv
        io["head_order"] = HEAD_ORDER
        io["after_first_loads"] = after_first_loads

        def _pro(e):
            ctx.pid = nc.partition_id([mybir.EngineType.SP])
        fw.sp_prologue = _pro
        io["fg"] = None
        emit_consts(ctx)
        emit_load_x(ctx, io["x_in"])
        for l in range(depth):
            emit_ffn(ctx, io["ffn1_norm"][l], io["ffn1_w_gate"][l], io["ffn1_w_up"][l], io["ffn1_w_down"][l], OFF_DYN)
            fw.barrier()
            emit_pool_loads(ctx, l, io)
            emit_proj(ctx, l, io, after_f=after_f, after_kv=after_kv)
            fw.barrier()
            fw.new_epoch()
            emit_pool(ctx, l, io, preloaded=True)
            fw.barrier()
            emit_attn(ctx, l, io)
            fw.barrier()
            emit_fourier(ctx, l, io)
            fw.barrier()
            emit_wout(ctx, l, io)
            fw.barrier()
            emit_ffn(ctx, io["ffn2_norm"][l], io["ffn2_w_gate"][l], io["ffn2_w_up"][l], io["ffn2_w_down"][l], OFF_DYN)
            fw.barrier()
            fw.new_epoch()
        emit_final_norm(ctx, io["final_norm"], io["y"], OFF_DYN)
        fw.barrier()
        fw.op("sp", None)
        fw.emit()
        nc._fw_stats = (len(fw.ops), fw.n_waits, dict(fw.count_log))
    return nc


def fused_inputs(inputs, depth=DEPTH):
    f32 = np.float32
    x = np.asarray(inputs["x"], f32)
    shared = {}
    for n in FUSED_W:
        if n in inputs:
            shared[n] = np.ascontiguousarray(np.asarray(inputs[n], f32))
    sm = [_layer_small(inputs, l) for l in range(DEPTH)]
    for n in ("pool_scale_t", "head_norm_t", "lamvec", "laminit"):
        shared[n] = np.ascontiguousarray(np.stack([sm[l][n] for l in range(DEPTH)]).astype(f32))
    shared["final_norm"] = np.asarray(inputs["final_norm"], f32)
    tables = [make_tables(r) for r in range(4)]
    in_maps = []
    for c in range(NCORES):
        b, r = c // 4, c % 4
        d = {"x_in": np.ascontiguousarray(x[b, r * TOK:(r + 1) * TOK, :].T)}
        d.update(shared)
        d.update(tables[r])
        in_maps.append(d)
    return in_maps


def kernel(**inputs):
    in_maps = fused_inputs(inputs)
    outs = _run(_prog("fused"), in_maps)
    out = np.empty((BATCH, SEQ, D_MODEL), np.float32)
    for c in range(NCORES):
        b, r = c // 4, c % 4
        out[b, r * TOK:(r + 1) * TOK, :] = np.asarray(outs[c]["y"], np.float32).T
    return out


_PROGS = {}


def _prog(kind):
    if kind not in _PROGS:
        _PROGS[kind] = build_fused() if kind == "fused" else build_launch(kind)
    return _PROGS[kind]


def kernel_unfused(**inputs):
    f32 = np.float32
    x = np.asarray(inputs["x"], f32)
    tables = [make_tables(r) for r in range(4)]

    def wA(l):
        d = {n: np.ascontiguousarray(np.asarray(inputs[n][l], f32)) for n in BIG_A}
        d["ffn2_norm"] = np.asarray(inputs["ffn2_norm"][l], f32)
        d["pool_w"] = np.asarray(inputs["pool_w"][l], f32)
        d["fourier_w"] = np.asarray(inputs["fourier_w"][l], f32)
        d.update(_layer_small(inputs, l))
        return d

    def wB(l):
        d = {n: np.ascontiguousarray(np.asarray(inputs[n][l], f32)) for n in BIG_B}
        d["ffn1_norm"] = np.asarray(inputs["ffn1_norm"][l], f32)
        d["mix_norm"] = np.asarray(inputs["mix_norm"][l], f32)
        return d

    def gathered(outs):
        g = []
        for c in range(NCORES):
            b, r = c // 4, c % 4
            grp = [outs[4 * b + j] for j in range(4)]
            rot = [grp[(r + i) % 4] for i in range(4)]
            d = {}
            d["kg"] = np.ascontiguousarray(np.concatenate([o["kpay"] for o in rot], axis=2))
            d["vg"] = np.ascontiguousarray(np.concatenate([o["vpay"] for o in rot], axis=2))
            d["fg"] = np.ascontiguousarray(np.concatenate([o["fpay"] for o in grp], axis=1))
            d["halo_all"] = np.ascontiguousarray(np.stack([o["hpay"] for o in grp], axis=2))
            d["qc_in"] = outs[c]["qc_out"]
            d["et_in"] = outs[c]["et_out"]
            d["x_in"] = outs[c]["x_out"]
            g.append(d)
        return g

    b0 = wB(0)
    in_maps = []
    for c in range(NCORES):
        b, r = c // 4, c % 4
        d = {"x_in": np.ascontiguousarray(x[b, r * TOK:(r + 1) * TOK, :].T)}
        d.update(b0)
        in_maps.append(d)
    outs = _run(_prog("first"), in_maps)
    for l in range(1, DEPTH):
        g = gathered(outs)
        a, bb = wA(l - 1), wB(l)
        in_maps = []
        for c in range(NCORES):
            d = dict(g[c])
            d.update(a)
            d.update(bb)
            d.update(tables[c % 4])
            in_maps.append(d)
        outs = _run(_prog("mid"), in_maps)
    g = gathered(outs)
    a = wA(DEPTH - 1)
    in_maps = []
    for c in range(NCORES):
        d = dict(g[c])
        d.update(a)
        d.update(tables[c % 4])
        d["final_norm"] = np.asarray(inputs["final_norm"], f32)
        in_maps.append(d)
    outs = _run(_prog("last"), in_maps)
    out = np.empty((BATCH, SEQ, D_MODEL), f32)
    for c in range(NCORES):
        b, r = c // 4, c % 4
        out[b, r * TOK:(r + 1) * TOK, :] = np.asarray(outs[c]["y"], f32).T
    return out
```

```python
import math
from contextlib import ExitStack

import numpy as np
import ml_dtypes

import concourse.bass as bass
import concourse.mybir as mybir
from concourse.bass_utils import run_bass_kernel_spmd

F32 = mybir.dt.float32
BF16 = mybir.dt.bfloat16
AF = mybir.ActivationFunctionType
ALU = mybir.AluOpType

D_MODEL = 1024
BATCH = 2
SEQ = 8192
DEPTH = 4
D_FF = 2816
NCORES = 8
TOK = 2048
NTC = 4
KC = 8
EPS = 1e-6
HEADS = 4
SLOPES = [2.0 ** (-8.0 * (i + 1) / HEADS) for i in range(HEADS)]
POOL_W = (2, 4, 8, 16)
BAND = [4, 16, None, None]


def lambda_init_fn(layer_idx):
    return 0.8 - 0.6 * math.exp(-0.3 * layer_idx)


class Res:
    __slots__ = ("name", "last_w", "readers")

    def __init__(self, name):
        self.name = name
        self.last_w = None
        self.readers = []


class Op:
    __slots__ = ("eng", "fn", "deps", "kind", "signal", "has_dep", "idx")

    def __init__(self, eng, fn, kind):
        self.eng = eng
        self.fn = fn
        self.deps = set()
        self.kind = kind
        self.signal = None
        self.has_dep = False
        self.idx = -1


ENGS = ("pe", "act", "dve", "pool", "sp")


class FW:
    def __init__(self, nc, stack, n_dma_sems=24, n_cc_sems=4):
        self.nc = nc
        self.stack = stack
        self.ops = []
        self.last_op = {e: None for e in ENGS}
        self.pending = {e: [] for e in ENGS}
        self.outstanding_dma = []
        self.n_dma_sems = n_dma_sems
        self.n_cc_sems = n_cc_sems
        self.epoch_marks = []

    def res(self, name="r"):
        return Res(name)

    def op(self, eng, fn, reads=(), writes=(), kind="c", after_barrier=True):
        o = Op(eng, fn, kind)
        o.idx = len(self.ops)
        for r in reads:
            if r.last_w is not None:
                o.deps.add(r.last_w)
            if kind == "c":
                r.readers = [x for x in r.readers if not (x.kind == "c" and x.eng == eng)]
            r.readers.append(o)
        for w in writes:
            if w.last_w is not None:
                o.deps.add(w.last_w)
            for rd in w.readers:
                if rd is not o:
                    o.deps.add(rd)
            w.last_w = o
            w.readers = []
        if after_barrier and self.pending[eng]:
            o.deps.update(self.pending[eng])
            self.pending[eng] = []
        o.deps.discard(o)
        self.ops.append(o)
        if kind != "cc":
            self.last_op[eng] = o
        if kind == "d":
            self.outstanding_dma.append(o)
        return o

    def barrier(self):
        col = [o for o in self.last_op.values() if o is not None] + list(self.outstanding_dma)
        for e in ENGS:
            self.pending[e] = list(col) + self.pending[e]
        self.outstanding_dma = []

    def new_epoch(self):
        self.epoch_marks.append(len(self.ops))

    def emit(self):
        nc = self.nc
        st = self.stack
        for o in self.ops:
            keep = set()
            for p in o.deps:
                if p.eng == "pe" and o.eng == "pe" and p.kind == "c" and o.kind == "c":
                    continue
                keep.add(p)
                p.has_dep = True
            o.deps = keep
        n_epochs = len(self.epoch_marks) + 1
        eng_sems = {e: [st.enter_context(nc.semaphore(f"s_{e}_{k}")) for k in range(n_epochs)] for e in ENGS}
        dma_sems = [st.enter_context(nc.semaphore(f"s_dma_{k}")) for k in range(self.n_dma_sems)]
        n_sw = 8
        pool_of = {"pool": list(range(0, n_sw)), "sp": list(range(n_sw, self.n_dma_sems))}
        rr = {"pool": 0, "sp": 0}
        cc_sems = [st.enter_context(nc.semaphore(f"s_cc_{k}")) for k in range(self.n_cc_sems)]
        epoch = 0
        marks = list(self.epoch_marks)
        counters = {e: 0 for e in ENGS}
        dma_rr = 0
        cc_rr = 0
        dma_tot = [0] * self.n_dma_sems
        dma_prev = [None] * self.n_dma_sems
        cc_tot = [0] * self.n_cc_sems
        cc_prev = [None] * self.n_cc_sems
        pre_wait = {}
        for o in self.ops:
            while marks and o.idx >= marks[0]:
                marks.pop(0)
                epoch += 1
                counters = {e: 0 for e in ENGS}
            if o.kind == "d":
                lst = pool_of[o.eng]
                k = lst[rr[o.eng] % len(lst)]
                rr[o.eng] += 1
                if dma_prev[k] is not None:
                    pre_wait[o] = dma_prev[k].signal
                dma_tot[k] += 16
                o.signal = (dma_sems[k], dma_tot[k])
                dma_prev[k] = o
            elif o.kind == "cc":
                k = cc_rr
                cc_rr = (cc_rr + 1) % self.n_cc_sems
                if cc_prev[k] is not None:
                    pre_wait[o] = cc_prev[k].signal
                cc_tot[k] += 1
                o.signal = (cc_sems[k], cc_tot[k])
                cc_prev[k] = o
            elif o.has_dep:
                counters[o.eng] += 1
                o.signal = (eng_sems[o.eng][epoch], counters[o.eng])
                self.max_count = max(getattr(self, "max_count", 0), counters[o.eng])
                self.count_log = getattr(self, "count_log", {})
                self.count_log[(epoch, o.eng)] = counters[o.eng]
        by_eng = {e: [o for o in self.ops if o.eng == e] for e in ENGS}
        self.n_waits = 0

        def run(eng_name, eng):
            waited = {}
            for o in by_eng[eng_name]:
                need = [p.signal for p in o.deps]
                if o in pre_wait:
                    need.append(pre_wait[o])
                for (sem, val) in need:
                    key = id(sem)
                    if waited.get(key, 0) < val:
                        eng.wait_ge(sem, val)
                        waited[key] = val
                        self.n_waits += 1
                if o.fn is None:
                    continue
                ins = o.fn(eng)
                if o.kind == "d":
                    ins.then_inc(o.signal[0], 16)
                elif o.kind == "cc":
                    ins.then_inc(o.signal[0], 1)
                elif o.has_dep:
                    ins.then_inc(o.signal[0], 1)

        with nc.Block() as block:
            @block.tensor
            def _(e):
                run("pe", e)

            @block.scalar
            def _(e):
                run("act", e)

            @block.vector
            def _(e):
                run("dve", e)

            @block.gpsimd
            def _(e):
                run("pool", e)

            @block.sync
            def _(e):
                if getattr(self, "sp_prologue", None) is not None:
                    self.sp_prologue(e)
                run("sp", e)


ARENA_BYTES = 111 * 1024


class Ctx:
    pass


def carve(ctx, off_bytes, shape, dtype):
    esz = 2 if dtype == BF16 else 4
    n = int(np.prod(shape))
    assert off_bytes % 4 == 0 and off_bytes + n * esz <= ARENA_BYTES, (off_bytes, shape)
    v = ctx.arena[:, off_bytes // 2: off_bytes // 2 + n * esz // 2]
    if dtype != BF16:
        v = v.bitcast(dtype)
    if len(shape) == 1:
        return v
    names = " ".join(f"d{i}" for i in range(len(shape)))
    kw = {f"d{i}": shape[i] for i in range(len(shape))}
    return v.rearrange(f"p ({names}) -> p {names}", **kw)


OFF_CONST = 0
OFF_DYN = 4096


def setup_memory(nc, stack, ctx):
    ctx.XT = stack.enter_context(nc.sbuf_tensor("XT", [128, KC, TOK], F32))
    ctx.HY = stack.enter_context(nc.sbuf_tensor("HY", [128, KC, TOK], BF16))
    ctx.arena = stack.enter_context(nc.sbuf_tensor("ARENA", [128, ARENA_BYTES // 2], BF16))
    ctx.PS = stack.enter_context(nc.psum_tensor("PS", [128, 8 * 512], F32))
    fw = ctx.fw
    ctx.r_xt = [[fw.res(f"xt{k}_{t}") for t in range(NTC)] for k in range(KC)]
    ctx.r_hy = [[fw.res(f"hy{k}_{t}") for t in range(NTC)] for k in range(KC)]
    ctx.r_ps = [fw.res(f"ps{b}") for b in range(8)]
    ctx.ones_bf = carve(ctx, OFF_CONST + 0, [128], BF16)
    ctx.ident_bf = carve(ctx, OFF_CONST + 256, [128], BF16)
    ctx.gains = carve(ctx, OFF_CONST + 512, [KC], F32)
    ctx.eps_col = carve(ctx, OFF_CONST + 768, [1], F32)
    ctx.ones_f = carve(ctx, OFF_CONST + 1024, [128], F32)
    ctx.r_const = fw.res("const")
    ctx.r_gain = fw.res("gain")


def bank(ctx, b):
    return ctx.PS[:, b * 512:(b + 1) * 512]


def emit_consts(ctx):
    fw = ctx.fw
    fw.op("pool", lambda e: e.memset(ctx.ones_bf, 1.0), writes=[ctx.r_const])
    fw.op("pool", lambda e: e.memset(ctx.eps_col, EPS), writes=[ctx.r_const])
    fw.op("pool", lambda e: e.memset(ctx.ones_f, 1.0), writes=[ctx.r_const])


def emit_load_x(ctx, x_dram):
    fw = ctx.fw
    src = x_dram.rearrange("(k p) t -> p k t", p=128)
    for k in range(KC):
        fw.op("sp", lambda e, k=k: e.dma_start(out=ctx.XT[:, k, :], in_=src[:, k, :]),
              writes=ctx.r_xt[k], kind="d")


def emit_store_x(ctx, y_dram):
    fw = ctx.fw
    dst = y_dram.rearrange("(k p) t -> p k t", p=128)
    ops = []
    for k in range(KC):
        ops.append(fw.op("sp", lambda e, k=k: e.dma_start(out=dst[:, k, :], in_=ctx.XT[:, k, :]),
                         reads=ctx.r_xt[k], kind="d"))
    return ops


def emit_rmsnorm(ctx, gain_dram_row, off):
    fw = ctx.fw
    rstd = carve(ctx, off, [NTC, 512], F32)
    sq = [carve(ctx, off + 8192 + i * 1024, [512], BF16) for i in range(4)]
    r_rstd = [fw.res(f"rstd{t}") for t in range(NTC)]
    r_sq = [fw.res(f"sq{i}") for i in range(4)]
    g_src = gain_dram_row.rearrange("(k p) -> p k", p=128)
    fw.op("sp", lambda e: e.dma_start(out=ctx.gains, in_=g_src, allow_slow_non_contiguous=True), writes=[ctx.r_gain], kind="d")
    cnt = 0
    for t in range(NTC):
        pb = 6 + (t % 2)
        ps = bank(ctx, pb)
        for k in range(KC):
            i = cnt % 4
            cnt += 1
            fw.op("act", lambda e, k=k, t=t, i=i: e.activation(out=sq[i], in_=ctx.XT[:, k, t * 512:(t + 1) * 512],
                                                              func=AF.Square),
                  reads=[ctx.r_xt[k][t]], writes=[r_sq[i]])
            fw.op("pe", lambda e, k=k, i=i, ps=ps: e.matmul(ps, lhsT=ctx.ones_bf, rhs=sq[i], start=(k == 0), stop=(k == KC - 1)),
                  reads=[r_sq[i], ctx.r_const], writes=[ctx.r_ps[pb]])
        fw.op("act", lambda e, t=t, ps=ps: e.activation(out=rstd[:, t, :], in_=ps, func=AF.Sqrt, bias=ctx.eps_col,
                                                       scale=1.0 / D_MODEL),
              reads=[ctx.r_ps[pb], ctx.r_const], writes=[r_rstd[t]])
        fw.op("dve", lambda e, t=t: e.reciprocal(out=rstd[:, t, :], in_=rstd[:, t, :]),
              reads=[r_rstd[t]], writes=[r_rstd[t]])
        for k in range(KC):
            eng = "dve"
            fw.op(eng, lambda e, k=k, t=t: e.scalar_tensor_tensor(
                out=ctx.HY[:, k, t * 512:(t + 1) * 512], in0=ctx.XT[:, k, t * 512:(t + 1) * 512],
                scalar=ctx.gains[:, k:k + 1], in1=rstd[:, t, :], op0=ALU.mult, op1=ALU.mult),
                reads=[ctx.r_xt[k][t], r_rstd[t], ctx.r_gain], writes=[ctx.r_hy[k][t]])


FF_GROUPS = [(0, 4), (4, 4), (8, 4), (12, 4), (16, 4), (20, 2)]


def emit_ffn(ctx, norm_row, wg, wu, wd, off):
    fw = ctx.fw
    emit_rmsnorm(ctx, norm_row, off)
    o = off + 12288
    WG = [carve(ctx, o + s * 16384, [KC, 512], BF16) for s in range(2)]
    WU = [carve(ctx, o + s * 16384 + 8192, [KC, 512], BF16) for s in range(2)]
    o += 32768
    WD = [carve(ctx, o + s * 8192, [4, 1024], BF16) for s in range(2)]
    o += 16384
    AT = [carve(ctx, o + s * 16384, [4, TOK], BF16) for s in range(2)]
    o += 32768
    SG = [carve(ctx, o + s * 2048, [512], F32) for s in range(2)]
    o += 4096
    r_wg = [fw.res("wg") for _ in range(2)]
    r_wu = [fw.res("wu") for _ in range(2)]
    r_wd = [fw.res("wd") for _ in range(2)]
    r_at = [[[fw.res("at") for _ in range(NTC)] for _ in range(4)] for _ in range(2)]
    r_sg = [fw.res("sg") for _ in range(2)]
    wg_v = wg.rearrange("(k p) f -> p k f", p=128)
    wu_v = wu.rearrange("(k p) f -> p k f", p=128)
    wd_v = wd.rearrange("(c p) d -> p c d", p=128)
    state = {"sg": 0, "gu": 0, "y": 0}

    def load_w(g):
        f0, n = FF_GROUPS[g]
        s = g % 2
        c0, c1 = f0 * 128, (f0 + n) * 128
        fw.op("pool", lambda e: e.dma_start(out=WG[s][:, :, 0:c1 - c0], in_=wg_v[:, :, c0:c1]),
              writes=[r_wg[s]], kind="d", after_barrier=True)
        fw.op("pool", lambda e: e.dma_start(out=WU[s][:, :, 0:c1 - c0], in_=wu_v[:, :, c0:c1]),
              writes=[r_wu[s]], kind="d")
        fw.op("pool", lambda e: e.dma_start(out=WD[s][:, 0:n, :], in_=wd_v[:, f0:f0 + n, :]),
              writes=[r_wd[s]], kind="d")

    def up(g):
        f0, n = FF_GROUPS[g]
        s = g % 2
        for fc in range(n):
            for t in range(NTC):
                gb = 2 * (state["gu"] % 2)
                state["gu"] += 1
                gps, ups = bank(ctx, gb), bank(ctx, gb + 1)
                for k in range(KC):
                    fw.op("pe", lambda e, k=k, fc=fc, t=t, gps=gps: e.matmul(
                        gps, lhsT=WG[s][:, k, fc * 128:(fc + 1) * 128], rhs=ctx.HY[:, k, t * 512:(t + 1) * 512],
                        start=(k == 0), stop=(k == KC - 1)),
                        reads=[r_wg[s], ctx.r_hy[k][t]], writes=[ctx.r_ps[gb]])
                for k in range(KC):
                    fw.op("pe", lambda e, k=k, fc=fc, t=t, ups=ups: e.matmul(
                        ups, lhsT=WU[s][:, k, fc * 128:(fc + 1) * 128], rhs=ctx.HY[:, k, t * 512:(t + 1) * 512],
                        start=(k == 0), stop=(k == KC - 1)),
                        reads=[r_wu[s], ctx.r_hy[k][t]], writes=[ctx.r_ps[gb + 1]])
                si = state["sg"] % 2
                state["sg"] += 1
                fw.op("act", lambda e, gps=gps, si=si: e.activation(out=SG[si], in_=gps, func=AF.Silu),
                      reads=[ctx.r_ps[gb]], writes=[r_sg[si]])
                fw.op("dve", lambda e, ups=ups, si=si, fc=fc, t=t: e.tensor_tensor(
                    out=AT[s][:, fc, t * 512:(t + 1) * 512], in0=SG[si], in1=ups, op=ALU.mult),
                    reads=[ctx.r_ps[gb + 1], r_sg[si]], writes=[r_at[s][fc][t]])

    def down(g):
        f0, n = FF_GROUPS[g]
        s = g % 2
        for dc in range(KC):
            for t in range(NTC):
                yb = 4 + (state["y"] % 2)
                state["y"] += 1
                yps = bank(ctx, yb)
                for fc in range(n):
                    fw.op("pe", lambda e, fc=fc, dc=dc, t=t, yps=yps: e.matmul(
                        yps, lhsT=WD[s][:, fc, dc * 128:(dc + 1) * 128], rhs=AT[s][:, fc, t * 512:(t + 1) * 512],
                        start=(fc == 0), stop=(fc == n - 1)),
                        reads=[r_wd[s], r_at[s][fc][t]], writes=[ctx.r_ps[yb]])
                fw.op("dve", lambda e, dc=dc, t=t, yps=yps: e.scalar_tensor_tensor(
                    out=ctx.XT[:, dc, t * 512:(t + 1) * 512], in0=yps, scalar=0.5,
                    in1=ctx.XT[:, dc, t * 512:(t + 1) * 512], op0=ALU.mult, op1=ALU.add),
                    reads=[ctx.r_ps[yb]], writes=[ctx.r_xt[dc][t]])

    ng = len(FF_GROUPS)
    load_w(0)
    load_w(1)
    up(0)
    for g in range(1, ng):
        up(g)
        down(g - 1)
        if g + 1 < ng:
            load_w(g + 1)
    down(ng - 1)


OFF_ET = 32768
OFF_QC = 49280
OFF_HI = 65664
ETW = TOK + 16
FNORM = 1.0 / math.sqrt(SEQ * 64.0)


def emit_final_norm(ctx, gain_row, y_dram, off):
    fw = ctx.fw
    rstd = carve(ctx, off, [NTC, 512], F32)
    sq = [carve(ctx, off + 8192 + i * 1024, [512], BF16) for i in range(4)]
    ob = [carve(ctx, off + 12288 + i * 2048, [512], F32) for i in range(4)]
    r_rstd = [fw.res("rstd") for t in range(NTC)]
    r_sq = [fw.res("sq") for i in range(4)]
    r_ob = [fw.res("ob") for i in range(4)]
    g_src = gain_row.rearrange("(k p) -> p k", p=128)
    fw.op("sp", lambda e: e.dma_start(out=ctx.gains, in_=g_src, allow_slow_non_contiguous=True),
          writes=[ctx.r_gain], kind="d")
    dst = y_dram.rearrange("(k p) t -> p k t", p=128)
    cnt = 0
    outs = []
    for t in range(NTC):
        pb = 6 + (t % 2)
        ps = bank(ctx, pb)
        for k in range(KC):
            i = cnt % 4
            cnt += 1
            fw.op("act", lambda e, k=k, t=t, i=i: e.activation(out=sq[i], in_=ctx.XT[:, k, t * 512:(t + 1) * 512],
                                                              func=AF.Square),
                  reads=[ctx.r_xt[k][t]], writes=[r_sq[i]])
            fw.op("pe", lambda e, k=k, i=i, ps=ps: e.matmul(ps, lhsT=ctx.ones_bf, rhs=sq[i], start=(k == 0), stop=(k == KC - 1)),
                  reads=[r_sq[i], ctx.r_const], writes=[ctx.r_ps[pb]])
        fw.op("act", lambda e, t=t, ps=ps: e.activation(out=rstd[:, t, :], in_=ps, func=AF.Sqrt, bias=ctx.eps_col,
                                                       scale=1.0 / D_MODEL),
              reads=[ctx.r_ps[pb], ctx.r_const], writes=[r_rstd[t]])
        fw.op("dve", lambda e, t=t: e.reciprocal(out=rstd[:, t, :], in_=rstd[:, t, :]),
              reads=[r_rstd[t]], writes=[r_rstd[t]])
        for k in range(KC):
            i = (t * KC + k) % 4
            fw.op("dve", lambda e, k=k, t=t, i=i: e.scalar_tensor_tensor(
                out=ob[i], in0=ctx.XT[:, k, t * 512:(t + 1) * 512],
                scalar=ctx.gains[:, k:k + 1], in1=rstd[:, t, :], op0=ALU.mult, op1=ALU.mult),
                reads=[ctx.r_xt[k][t], r_rstd[t], ctx.r_gain], writes=[r_ob[i]])
            outs.append(fw.op("sp", lambda e, k=k, t=t, i=i: e.dma_start(out=dst[:, k, t * 512:(t + 1) * 512], in_=ob[i]),
                              reads=[r_ob[i]], kind="d"))
    return outs


def emit_proj(ctx, l, io, after_f=None, after_kv=None):
    fw = ctx.fw
    rp = io.get("r_pay", None)
    wr = (lambda key: [rp[key]]) if rp is not None else (lambda key: [])
    emit_rmsnorm(ctx, io["mix_norm"][l], OFF_DYN)
    WB = [carve(ctx, 16384 + s * 8192, [KC, 512], BF16) for s in range(2)]
    r_wb = [fw.res("wb") for _ in range(2)]
    ET = carve(ctx, OFF_ET, [2, ETW], F32)
    QC = carve(ctx, OFF_QC, [4, TOK], BF16)
    KST = carve(ctx, OFF_HI, [4, TOK], BF16)
    VST = carve(ctx, OFF_HI + 16384, [4, 16, 128], BF16)
    FST = carve(ctx, OFF_HI + 32768, [16, 256], BF16)
    ctx.r_et = [fw.res("et") for _ in range(2)]
    ctx.r_qc = [fw.res("qc") for _ in range(4)]
    r_kst = [fw.res("kst") for _ in range(4)]
    r_vst = fw.res("vst")
    r_fst = fw.res("fst")
    win = io["w_in"][l].rearrange("(k p) f -> p k f", p=128)
    st = {"b": 0}

    def load(blk):
        s = blk % 2
        fw.op("pool", lambda e: e.dma_start(out=WB[s], in_=win[:, :, blk * 512:(blk + 1) * 512]),
              writes=[r_wb[s]], kind="d")

    def nb():
        b = st["b"] % 4
        st["b"] += 1
        return b

    def fmajor(blk, c0, nchunk, evac):
        s = blk % 2
        for ck in range(nchunk):
            for t in range(NTC):
                b = nb()
                ps = bank(ctx, b)
                for k in range(KC):
                    fw.op("pe", lambda e, k=k, ck=ck, t=t, ps=ps: e.matmul(
                        ps, lhsT=WB[s][:, k, c0 + ck * 128:c0 + (ck + 1) * 128], rhs=ctx.HY[:, k, t * 512:(t + 1) * 512],
                        start=(k == 0), stop=(k == KC - 1)),
                        reads=[r_wb[s], ctx.r_hy[k][t]], writes=[ctx.r_ps[b]])
                evac(ck, t, ps, b)

    def tmajor(blk, c0, ncol, evac):
        s = blk % 2
        for tile in range(16):
            b = nb()
            ps = bank(ctx, b)[:, 0:ncol]
            t = tile // 4
            for k in range(KC):
                fw.op("pe", lambda e, k=k, tile=tile, ps=ps: e.matmul(
                    ps, lhsT=ctx.HY[:, k, tile * 128:(tile + 1) * 128], rhs=WB[s][:, k, c0:c0 + ncol],
                    start=(k == 0), stop=(k == KC - 1)),
                    reads=[r_wb[s], ctx.r_hy[k][t]], writes=[ctx.r_ps[b]])
            evac(tile, ps, b)

    outs = []
    load(0)
    load(3)
    fmajor(0, 0, 2, lambda ck, t, ps, b: fw.op(
        "act", lambda e: e.copy(out=ET[:, ck, 8 + t * 512:8 + (t + 1) * 512], in_=ps),
        reads=[ctx.r_ps[b]], writes=[ctx.r_et[ck]]))
    for ck in range(2):
        outs.append(fw.op("sp", lambda e, ck=ck: e.dma_start(out=io["hpay"][ck, :, 0:8], in_=ET[:, ck, 8:16]),
                          reads=[ctx.r_et[ck]], writes=wr("halo"), kind="d"))
        outs.append(fw.op("sp", lambda e, ck=ck: e.dma_start(out=io["hpay"][ck, :, 8:16], in_=ET[:, ck, TOK:TOK + 8]),
                          reads=[ctx.r_et[ck]], writes=wr("halo"), kind="d"))
    if after_f is not None:
        after_f()
    tmajor(0, 256, 256, lambda tile, ps, b: fw.op(
        "dve", lambda e: e.tensor_copy(out=FST[:, tile, :], in_=ps), reads=[ctx.r_ps[b]], writes=[r_fst]))
    for g in range(4):
        outs.append(fw.op("sp", lambda e, g=g: e.dma_start(
            out=io["fpay"][g].rearrange("(t p) c -> p t c", p=128), in_=FST[:, :, g * 64:(g + 1) * 64]),
            reads=[r_fst], writes=wr("f"), kind="d"))
    load(2)
    tmajor(3, 0, 512, lambda tile, ps, b: fw.op(
        "act" if tile % 2 else "dve",
        (lambda e: e.copy(out=VST[:, :, tile, :], in_=ps.rearrange("p (h e) -> p h e", h=4))) if tile % 2
        else (lambda e: e.tensor_copy(out=VST[:, :, tile, :], in_=ps.rearrange("p (h e) -> p h e", h=4))),
        reads=[ctx.r_ps[b]], writes=[r_vst]))
    load(1)
    fmajor(2, 0, 4, lambda ck, t, ps, b: fw.op(
        "dve", lambda e: e.tensor_copy(out=KST[:, ck, t * 512:(t + 1) * 512], in_=ps),
        reads=[ctx.r_ps[b]], writes=[r_kst[ck]]))
    for h in range(4):
        outs.append(fw.op("sp", lambda e, h=h: e.dma_start(out=io["kpay"][h], in_=KST[:, h, :]),
                          reads=[r_kst[h]], writes=wr(("kv", h)), kind="d"))
        outs.append(fw.op("sp", lambda e, h=h: e.dma_start(out=io["vpay"][h], in_=VST[:, h, :, :]),
                          reads=[r_vst], writes=wr(("kv", h)), kind="d"))
    if after_kv is not None:
        after_kv()
    fmajor(1, 0, 4, lambda ck, t, ps, b: fw.op(
        "act", lambda e: e.mul(out=QC[:, ck, t * 512:(t + 1) * 512], in_=ps, mul=0.125),
        reads=[ctx.r_ps[b]], writes=[ctx.r_qc[ck]]))
    return outs


def _stash_view(ctx):
    return ctx.HY[:, 0:4, :].rearrange("p k t -> p (k t)").bitcast(F32).rearrange("p (c t) -> p c t", c=2)


def emit_stash_et(ctx):
    fw = ctx.fw
    ET = carve(ctx, OFF_ET, [2, ETW], F32)
    SV = _stash_view(ctx)
    for ck in range(2):
        fw.op("dve" if ck == 0 else "act",
              (lambda e, ck=ck: e.tensor_copy(out=SV[:, ck, :], in_=ET[:, ck, 8:8 + TOK])) if ck == 0
              else (lambda e, ck=ck: e.copy(out=SV[:, ck, :], in_=ET[:, ck, 8:8 + TOK])),
              reads=[ctx.r_et[ck]], writes=[r for k in (2 * ck, 2 * ck + 1) for r in ctx.r_hy[k]])


def emit_restore_et(ctx):
    fw = ctx.fw
    ET = carve(ctx, OFF_ET, [2, ETW], F32)
    SV = _stash_view(ctx)
    for ck in range(2):
        fw.op("dve" if ck == 0 else "act",
              (lambda e, ck=ck: e.tensor_copy(out=ET[:, ck, 8:8 + TOK], in_=SV[:, ck, :])) if ck == 0
              else (lambda e, ck=ck: e.copy(out=ET[:, ck, 8:8 + TOK], in_=SV[:, ck, :])),
              reads=[r for k in (2 * ck, 2 * ck + 1) for r in ctx.r_hy[k]], writes=[ctx.r_et[ck]])


def _pool_bufs(ctx):
    fw = ctx.fw
    o = OFF_CONST + 1536
    d = dict(
        PW=carve(ctx, o, [2, 128], BF16),
        PWF=carve(ctx, o + 512, [2, 128], F32),
        PSC=carve(ctx, o + 1536, [2], F32),
        CORR=carve(ctx, o + 1544, [2, 16], F32),
        MASK=carve(ctx, o + 1672, [8], F32),
        HALO=carve(ctx, o + 1704, [2, 4, 16], F32),
    )
    if not hasattr(ctx, "r_pool"):
        ctx.r_pool = {n: fw.res(n) for n in ("pw", "pcst", "halo")}
    return d


def emit_pool_loads(ctx, l, io):
    fw = ctx.fw
    B = _pool_bufs(ctx)
    PW, PWF, PSC, CORR, MASK = B["PW"], B["PWF"], B["PSC"], B["CORR"], B["MASK"]
    r_pw, r_cst = ctx.r_pool["pw"], ctx.r_pool["pcst"]
    fw.op("pool", lambda e: e.memset(PWF, 0.0), writes=[r_pw])
    for g in range(4):
        ck, gl = g // 2, g % 2
        fw.op("sp", lambda e, g=g, ck=ck, gl=gl: e.dma_start(
            out=PWF[gl * 64:(gl + 1) * 64, ck, gl * 64:(gl + 1) * 64], in_=io["pool_w"][l][g]),
            writes=[r_pw], kind="d")
    fw.op("dve", lambda e: e.tensor_copy(out=PW, in_=PWF), reads=[r_pw], writes=[r_pw])
    fw.op("sp", lambda e: e.dma_start(out=PSC, in_=io["pool_scale_t"][l]), writes=[r_cst], kind="d")
    fw.op("sp", lambda e: e.dma_start(out=CORR, in_=io["pcorr"]), writes=[r_cst], kind="d")
    fw.op("sp", lambda e: e.dma_start(out=MASK, in_=io["pmask"]), writes=[r_cst], kind="d")


def emit_pool_halo_load(ctx, io):
    fw = ctx.fw
    HALO = _pool_bufs(ctx)["HALO"]
    hrd = [io["r_gath"]["halo"]] if "r_gath" in io else []
    for k_ in range(2):
        for r_ in range(4):
            fw.op("sp", lambda e, k_=k_, r_=r_: e.dma_start(out=HALO[:, k_, r_, :], in_=io["halo_all"][k_, :, r_, :]),
                  reads=hrd, writes=[ctx.r_pool["halo"]], kind="d")


def emit_pool(ctx, l, io, preloaded=False):
    fw = ctx.fw
    ET = carve(ctx, OFF_ET, [2, ETW], F32)
    T1 = carve(ctx, OFF_HI, [ETW], F32)
    T2 = carve(ctx, OFF_HI + 8256, [ETW], F32)
    o = OFF_HI + 16512
    MT = carve(ctx, o + 2304, [2, TOK], BF16)
    WIN = carve(ctx, o + 2304 + 8192, [TOK], F32)
    B = _pool_bufs(ctx)
    PW, PSC, CORR, MASK, HALO = B["PW"], B["PSC"], B["CORR"], B["MASK"], B["HALO"]
    if not preloaded:
        emit_pool_loads(ctx, l, io)
        emit_pool_halo_load(ctx, io)
    r_pw, r_cst, r_halo = ctx.r_pool["pw"], ctx.r_pool["pcst"], ctx.r_pool["halo"]
    r_t1, r_t2, r_win = (fw.res(n) for n in ("t1", "t2", "win"))
    r_mt = [fw.res("mt") for _ in range(2)]
    for ck in range(2):
        fw.op("dve", lambda e, ck=ck: e.tensor_scalar(out=ET[:, ck, 0:8], in0=HALO[:, ck, 0, 8:16], scalar1=MASK[:, 0:1],
                                                      scalar2=None, op0=ALU.mult),
              reads=[r_halo, r_cst], writes=[ctx.r_et[ck]])
        fw.op("dve", lambda e, ck=ck: e.tensor_scalar(out=ET[:, ck, TOK + 8:TOK + 16], in0=HALO[:, ck, 0, 0:8],
                                                      scalar1=MASK[:, 4:5], scalar2=None, op0=ALU.mult),
              reads=[r_halo, r_cst], writes=[ctx.r_et[ck]])
        for r in range(1, 4):
            fw.op("dve", lambda e, ck=ck, r=r: e.scalar_tensor_tensor(
                out=ET[:, ck, 0:8], in0=HALO[:, ck, r, 8:16], scalar=MASK[:, r:r + 1], in1=ET[:, ck, 0:8],
                op0=ALU.mult, op1=ALU.add), reads=[r_halo, r_cst], writes=[ctx.r_et[ck]])
            fw.op("dve", lambda e, ck=ck, r=r: e.scalar_tensor_tensor(
                out=ET[:, ck, TOK + 8:TOK + 16], in0=HALO[:, ck, r, 0:8], scalar=MASK[:, 4 + r:5 + r],
                in1=ET[:, ck, TOK + 8:TOK + 16], op0=ALU.mult, op1=ALU.add),
                reads=[r_halo, r_cst], writes=[ctx.r_et[ck]])
        E = ET[:, ck, :]
        fw.op("pool", lambda e, E=E: e.tensor_tensor(out=T1[:, 0:ETW - 1], in0=E[:, 0:ETW - 1], in1=E[:, 1:ETW], op=ALU.add),
              reads=[ctx.r_et[ck]], writes=[r_t1])
        if ck == 0:
            lv = {0: (T1, r_t1, 2)}
            fw.op("pool", lambda e: e.tensor_tensor(out=T2[:, 0:ETW - 3], in0=T1[:, 0:ETW - 3], in1=T1[:, 2:ETW - 1], op=ALU.add),
                  reads=[r_t1], writes=[r_t2])
            lv[1] = (T2, r_t2, 4)
        else:
            fw.op("pool", lambda e: e.tensor_tensor(out=T2[:, 0:ETW - 3], in0=T1[:, 0:ETW - 3], in1=T1[:, 2:ETW - 1], op=ALU.add),
                  reads=[r_t1], writes=[r_t2])
            fw.op("pool", lambda e: e.tensor_tensor(out=T1[:, 0:ETW - 7], in0=T2[:, 0:ETW - 7], in1=T2[:, 4:ETW - 3], op=ALU.add),
                  reads=[r_t2], writes=[r_t1])
            fw.op("pool", lambda e: e.tensor_tensor(out=T2[:, 0:ETW - 15], in0=T1[:, 0:ETW - 15], in1=T1[:, 8:ETW - 7], op=ALU.add),
                  reads=[r_t1], writes=[r_t2])
            lv = {0: (T1, r_t1, 8), 1: (T2, r_t2, 16)}
        for gl in range(2):
            src, rsrc, w = lv[gl]
            sh = 8 - w // 2
            ps_ = slice(gl * 64, (gl + 1) * 64)
            fw.op("dve", lambda e, src=src, sh=sh, ps_=ps_: e.tensor_copy(out=WIN[ps_, :], in_=src[ps_, sh:sh + TOK]),
                  reads=[rsrc], writes=[r_win])
            fw.op("dve", lambda e, ck=ck, ps_=ps_: e.tensor_tensor(out=WIN[ps_, 0:8], in0=WIN[ps_, 0:8],
                                                                  in1=CORR[ps_, ck, 0:8], op=ALU.mult),
                  reads=[r_cst], writes=[r_win])
            fw.op("dve", lambda e, ck=ck, ps_=ps_: e.tensor_tensor(out=WIN[ps_, TOK - 8:TOK], in0=WIN[ps_, TOK - 8:TOK],
                                                                  in1=CORR[ps_, ck, 8:16], op=ALU.mult),
                  reads=[r_cst], writes=[r_win])
            fw.op("dve", lambda e, ck=ck, ps_=ps_, w=w: e.scalar_tensor_tensor(
                out=MT[ps_, ck, :], in0=WIN[ps_, :], scalar=1.0 / w, in1=ET[ps_, ck, 8:8 + TOK],
                op0=ALU.mult, op1=ALU.subtract), reads=[r_win, ctx.r_et[ck]], writes=[r_mt[ck]])
    for ck in range(2):
        for t in range(NTC):
            b = t % 4
            ps = bank(ctx, b)
            fw.op("pe", lambda e, ck=ck, t=t, ps=ps: e.matmul(ps, lhsT=PW[:, ck, :], rhs=MT[:, ck, t * 512:(t + 1) * 512],
                                                            start=True, stop=True),
                  reads=[r_pw, r_mt[ck]], writes=[ctx.r_ps[b]])
            fw.op("act", lambda e, ck=ck, t=t, ps=ps: e.mul(out=ctx.HY[:, ck, t * 512:(t + 1) * 512], in_=ps, mul=PSC[:, ck:ck + 1]),
                  reads=[ctx.r_ps[b], r_cst], writes=[ctx.r_hy[ck][t]])


def emit_fourier(ctx, l, io):
    fw = ctx.fw
    XS = carve(ctx, 4096, [128, 64], BF16)
    BB = carve(ctx, 20480, [2, 64, 64], BF16)
    W3 = carve(ctx, 36864, [64, 3, 32], BF16)
    UT = carve(ctx, OFF_HI, [4, 2, TOK], BF16)
    MM = carve(ctx, OFF_HI + 32768, [4, 2, 256], BF16)
    WFS = carve(ctx, OFF_HI + 36864, [4, 256], BF16)
    tb = OFF_HI + 38912
    WA = carve(ctx, tb, [128], BF16)
    C64 = carve(ctx, tb + 960, [2, 64], BF16)
    r_tab, r_wfs, r_mm, r_xs, r_bb = (fw.res(n) for n in ("ftab", "wfs", "mm", "xs", "bb"))
    r_ut = [fw.res("ut") for _ in range(4)]
    fw.op("sp", lambda e: e.dma_start(out=WA[0:64, :], in_=io["f_wa"]), writes=[r_tab], kind="d")
    fw.op("sp", lambda e: e.dma_start(out=W3, in_=io["f_w3"]), writes=[r_tab], kind="d")
    fw.op("sp", lambda e: e.dma_start(out=C64[0:64], in_=io["f_c64"]), writes=[r_tab], kind="d")
    fw.op("pool", lambda e: e.dma_start(out=WFS[0:64], in_=io["fourier_w"][l].rearrange("(g p) c -> p g c", p=64)),
          writes=[r_wfs], kind="d")
    for g in range(4):
        for comp in range(2):
            b = (g * 2 + comp) % 4
            ps = bank(ctx, b)[0:64, 0:256]
            fw.op("pe", lambda e, g=g, comp=comp, ps=ps: e.matmul(ps, lhsT=C64[0:64, comp, :], rhs=WFS[0:64, g, :],
                                                                 start=True, stop=True),
                  reads=[r_tab, r_wfs], writes=[ctx.r_ps[b]])
            fw.op("dve", lambda e, g=g, comp=comp, ps=ps: e.tensor_copy(out=MM[0:64, g, comp, :], in_=ps),
                  reads=[ctx.r_ps[b]], writes=[r_mm])
    for g in range(4):
        if "load_xs" in io:
            io["load_xs"](g, XS, r_xs)
        else:
            fw.op("sp", lambda e, g=g: e.dma_start(out=XS[0:64], in_=io["fg"][g].rearrange("(s1 s2) c -> s1 s2 c", s2=128)),
                  writes=[r_xs], kind="d")
        for rd in range(4):
            pb0 = 4 * (rd % 2)
            PSV = ctx.PS[:, pb0 * 512:(pb0 + 4) * 512].rearrange("p (c x) -> p c x", x=128)
            rps = [ctx.r_ps[pb0 + i] for i in range(4)]
            for ci in range(16):
                c = rd * 16 + ci
                fw.op("pe", lambda e, c=c, ci=ci, PSV=PSV: e.matmul(PSV[:, ci, :], lhsT=XS[0:64, :, c], rhs=WA[0:64, :],
                                                                   start=True, stop=True),
                      reads=[r_xs, r_tab], writes=[rps[ci // 4]])
            AR = PSV[:, :, 0:64]
            AI = PSV[:, :, 64:128]
            cs = slice(rd * 16, (rd + 1) * 16)
            BRv = BB[:, 0, :, cs].rearrange("p k c -> p c k")
            BIv = BB[:, 1, :, cs].rearrange("p k c -> p c k")
            fw.op("act", lambda e, AR=AR, BRv=BRv: e.copy(out=BRv, in_=AR), reads=rps, writes=[r_bb])
            fw.op("dve", lambda e, AI=AI, BIv=BIv: e.tensor_copy(out=BIv, in_=AI), reads=rps, writes=[r_bb])
        for q4 in range(4):
            pr, pi = (q4 % 2) * 2, (q4 % 2) * 2 + 1
            for kk in range(16):
                k1 = q4 * 16 + kk
                outr = bank(ctx, pr)[0:64, kk * 32:(kk + 1) * 32]
                outi = bank(ctx, pi)[0:64, kk * 32:(kk + 1) * 32]
                fw.op("pe", lambda e, k1=k1, outr=outr: e.matmul(outr, lhsT=BB[:, 0, k1, :], rhs=W3[:, k1, 0, :], start=True, stop=False),
                      reads=[r_bb, r_tab], writes=[ctx.r_ps[pr]])
                fw.op("pe", lambda e, k1=k1, outr=outr: e.matmul(outr, lhsT=BB[:, 1, k1, :], rhs=W3[:, k1, 2, :], start=False, stop=True),
                      reads=[r_bb, r_tab], writes=[ctx.r_ps[pr]])
                fw.op("pe", lambda e, k1=k1, outi=outi: e.matmul(outi, lhsT=BB[:, 1, k1, :], rhs=W3[:, k1, 0, :], start=True, stop=False),
                      reads=[r_bb, r_tab], writes=[ctx.r_ps[pi]])
                fw.op("pe", lambda e, k1=k1, outi=outi: e.matmul(outi, lhsT=BB[:, 0, k1, :], rhs=W3[:, k1, 1, :], start=False, stop=True),
                      reads=[r_bb, r_tab], writes=[ctx.r_ps[pi]])
            for comp, pbk in ((0, pr), (1, pi)):
                src = bank(ctx, pbk)[0:64, :].rearrange("p (k j) -> p k j", j=32)
                dstv = UT[0:64, g, comp, :].rearrange("p (j k) -> p k j", k=64)[:, q4 * 16:(q4 + 1) * 16, :]
                fw.op("act" if comp else "dve",
                      (lambda e, src=src, dstv=dstv: e.copy(out=dstv, in_=src)) if comp
                      else (lambda e, src=src, dstv=dstv: e.tensor_copy(out=dstv, in_=src)),
                      reads=[ctx.r_ps[pbk]], writes=[r_ut[g]])
    for ck in range(2):
        for t in range(NTC):
            b = 4 + (ck * NTC + t) % 4
            ps = bank(ctx, b)
            n = 0
            for g in range(4):
                for comp in range(2):
                    fw.op("pe", lambda e, g=g, comp=comp, ck=ck, t=t, ps=ps, n=n: e.matmul(
                        ps, lhsT=MM[0:64, g, comp, ck * 128:(ck + 1) * 128], rhs=UT[0:64, g, comp, t * 512:(t + 1) * 512],
                        start=(n == 0), stop=(n == 7)),
                        reads=[r_mm, r_ut[g]], writes=[ctx.r_ps[b]])
                    n += 1
            fw.op("act", lambda e, ck=ck, t=t, ps=ps: e.copy(out=ctx.HY[:, 2 + ck, t * 512:(t + 1) * 512], in_=ps),
                  reads=[ctx.r_ps[b]], writes=[ctx.r_hy[2 + ck][t]])


def emit_attn(ctx, l, io):
    fw = ctx.fw
    K0 = carve(ctx, 4096, [SEQ], BF16)
    K1 = carve(ctx, 20480, [SEQ], BF16)
    Q0 = carve(ctx, 36864, [TOK], BF16)
    Q1 = carve(ctx, 40960, [TOK], BF16)
    DT = carve(ctx, 45056, [4, 128], BF16)
    o = 46080
    LAMV = carve(ctx, o, [256], F32)
    LTMP = carve(ctx, o + 1024, [64], F32)
    LS = carve(ctx, o + 1280, [8], F32)
    GN = carve(ctx, o + 1312, [4], F32)
    SQH = carve(ctx, o + 1344, [512], BF16)
    QC = carve(ctx, OFF_QC, [4, TOK], BF16)
    V = carve(ctx, OFF_HI, [64, 128], BF16)
    PT = [carve(ctx, OFF_HI + 16384 + i * 2048, [1024], BF16) for i in range(3)]
    FT = [carve(ctx, OFF_HI + 22528 + i * 2048, [512], F32) for i in range(4)]
    ACC = carve(ctx, OFF_HI + 30720, [1024], F32)
    OS = [carve(ctx, OFF_HI + 34816 + i * 2048, [512], F32) for i in range(2)]
    r_os = [fw.res("os0"), fw.res("os1")]
    r_acc2 = [fw.res("acc0"), fw.res("acc1")]
    r_d, r_lam = (fw.res(n) for n in ("dt", "lam"))
    r_k = [fw.res("k") for _ in range(10)]
    r_v = [fw.res("v") for _ in range(4)]
    r_q = [fw.res("q") for _ in range(4)]
    r_pt = [[fw.res("pt0"), fw.res("pt1")] for _ in range(3)]
    r_ft = [fw.res("ft") for _ in range(4)]
    r_sqh = fw.res("sqh")
    LI = carve(ctx, o + 2368, [2], F32)
    fw.op("sp", lambda e: e.dma_start(out=DT, in_=io["dtile"]), writes=[r_d], kind="d")
    fw.op("sp", lambda e: e.dma_start(out=ctx.ident_bf, in_=io["ident"]), writes=[ctx.r_const], kind="d")
    fw.op("sp", lambda e: e.dma_start(out=LAMV, in_=io["lamvec"][l].partition_broadcast(128)), writes=[r_lam], kind="d")
    fw.op("sp", lambda e: e.dma_start(out=GN, in_=io["head_norm_t"][l]), writes=[r_lam], kind="d")
    fw.op("sp", lambda e: e.dma_start(out=LI, in_=io["laminit"][l]), writes=[r_lam], kind="d")
    for i in range(2):
        fw.op("dve", lambda e, i=i: e.tensor_tensor(out=LTMP, in0=LAMV[:, i * 128:i * 128 + 64],
                                                    in1=LAMV[:, i * 128 + 64:i * 128 + 128], op=ALU.mult),
              reads=[r_lam], writes=[r_lam])
        fw.op("dve", lambda e, i=i: e.reduce_sum(out=LS[:, i:i + 1], in_=LTMP, axis=mybir.AxisListType.X),
              reads=[r_lam], writes=[r_lam])
    fw.op("act", lambda e: e.activation(out=LS[:, 2:4], in_=LS[:, 0:2], func=AF.Exp), reads=[r_lam], writes=[r_lam])
    fw.op("dve", lambda e: e.tensor_tensor(out=LS[:, 4:5], in0=LS[:, 3:4], in1=LS[:, 2:3], op=ALU.subtract),
          reads=[r_lam], writes=[r_lam])
    fw.op("dve", lambda e: e.tensor_tensor(out=LS[:, 5:6], in0=LS[:, 4:5], in1=LI[:, 0:1], op=ALU.add),
          reads=[r_lam], writes=[r_lam])
    fw.op("dve", lambda e: e.tensor_scalar(out=GN, in0=GN, scalar1=LI[:, 1:2], scalar2=None, op0=ALU.mult),
          reads=[r_lam], writes=[r_lam])
    NEGLAM = LS[:, 5:6]

    it = {"n": 0}
    pend = {"tail": None}
    for hi_, h in enumerate(io.get("head_order", list(range(HEADS)))):
        if "load_kv" in io:
            io["load_kv"](h, K0, K1, V, r_k, r_v)
        else:
            fw.op("sp", lambda e, h=h: e.dma_start(out=K0[0:64, :], in_=io["kg"][h, 0:64, :]), writes=r_k[0:4], kind="d")
            fw.op("sp", lambda e, h=h: e.dma_start(out=K1[0:64, :], in_=io["kg"][h, 64:128, :]), writes=r_k[4:8], kind="d")
            fw.op("sp", lambda e, h=h: e.dma_start(out=V, in_=io["vg"][h]), writes=r_v, kind="d")
        fw.op("sp", lambda e, h=h: e.dma_start(out=K0[64:73, :], in_=io["kaug0"][h]), writes=[r_k[8]], kind="d")
        fw.op("sp", lambda e, h=h: e.dma_start(out=K1[64:73, :], in_=io["kaug0"][h]), writes=[r_k[9]], kind="d")
        if hi_ == 0 and "after_first_loads" in io:
            io["after_first_loads"](r_k, r_v)
        fw.op("sp", lambda e, h=h: e.dma_start(out=Q0[64:73, :], in_=io["qaug0"][h]), writes=[r_q[0]], kind="d")
        fw.op("sp", lambda e, h=h: e.dma_start(out=Q1[64:73, :], in_=io["qaug0"][h]), writes=[r_q[1]], kind="d")
        fw.op("dve", lambda e, h=h: e.tensor_copy(out=Q0[0:64, :], in_=QC[0:64, h, :]), reads=[ctx.r_qc[h]], writes=[r_q[2]])
        fw.op("sp", lambda e, h=h: e.dma_start(out=Q1[0:64, :], in_=QC[64:128, h, :]), reads=[ctx.r_qc[h]], writes=[r_q[3]], kind="d")

        def s_mm(Q, L, sb, hh=h):
            ks = slice(L * 128, (L + 1) * 128)
            for j in range(2):
                KT, QT = (K0, Q0) if j == 0 else (K1, Q1)
                ps = bank(ctx, sb + j)

                def rng(mode):
                    return {"diag": slice(0, 65), "below": slice(0, 69), "above": slice(0, 73)}[mode]

                def mm(out, mode, qs, start=True, stop=True, KT=KT, QT=QT, j=j):
                    pr = rng(mode)
                    rb = ctx.r_ps[sb + j]
                    fw.op("pe", lambda e: e.matmul(out, lhsT=KT[pr, ks], rhs=QT[pr, qs], start=start, stop=stop),
                          reads=r_k + r_q, writes=[rb])

                if L >= 16 or L < 4 * Q:
                    mm(ps, "below", slice(Q * 512, (Q + 1) * 512))
                elif L >= 4 * Q + 4:
                    mm(ps, "above", slice(Q * 512, (Q + 1) * 512))
                else:
                    us = L - 4 * Q
                    for u in range(4):
                        qs = slice(Q * 512 + u * 128, Q * 512 + (u + 1) * 128)
                        out = ps[:, u * 128:(u + 1) * 128]
                        if u > us:
                            mm(out, "below", qs)
                        elif u < us:
                            mm(out, "above", qs)
                        else:
                            mm(out, "diag", qs, start=True, stop=False)
                            fw.op("pe", lambda e, out=out, hh=hh: e.matmul(out, lhsT=ctx.ident_bf, rhs=DT[:, hh, :], start=False, stop=True),
                                  reads=[ctx.r_const, r_d], writes=[ctx.r_ps[sb + j]])

        for Q in range(4):
            qcols = slice(Q * 512, (Q + 1) * 512)
            if BAND[h] is None:
                Ls = list(range(64))
            else:
                Ls = [(4 * Q + d_) % 64 for d_ in range(-BAND[h], BAND[h] + 4)]
            nL = len(Ls)
            s_mm(Q, Ls[0], 0)
            for li, L in enumerate(Ls):
                sb = 2 * (li % 2)
                if li + 1 < nL:
                    s_mm(Q, Ls[li + 1], 2 * ((li + 1) % 2))
                pi = it["n"] % 3
                it["n"] += 1
                if li == 3 and pend["tail"] is not None:
                    pend["tail"]()
                    pend["tail"] = None
                for j in range(2):
                    fw.op("act", lambda e, sb=sb, pi=pi, j=j: e.activation(out=PT[pi][:, j * 512:(j + 1) * 512], in_=bank(ctx, sb + j), func=AF.Exp),
                          reads=[ctx.r_ps[sb + j]], writes=[r_pt[pi][j]])
                for j in range(2):
                    fw.op("pe", lambda e, L=L, j=j, pi=pi, li=li, nL=nL: e.matmul(bank(ctx, 4 + j), lhsT=V[:, L, :], rhs=PT[pi][:, j * 512:(j + 1) * 512],
                                                                   start=(li == 0), stop=(li == nL - 1)),
                          reads=r_v + [r_pt[pi][j]], writes=[ctx.r_ps[4 + j]])
                    if j == 0:
                        fw.op("pe", lambda e, pi=pi, li=li, nL=nL: e.matmul(bank(ctx, 6), lhsT=ctx.ones_bf, rhs=PT[pi][:, 0:512],
                                                                      start=(li == 0), stop=(li == nL - 1)),
                              reads=[ctx.r_const, r_pt[pi][0]], writes=[ctx.r_ps[6]])
                if li == 0:
                    fw.op("dve", lambda e, pi=pi: e.tensor_copy(out=ACC[:, 512:1024], in_=PT[pi][:, 512:1024]),
                          reads=[r_pt[pi][1]], writes=[r_acc2[1]])
                else:
                    fw.op("dve", lambda e, pi=pi: e.tensor_tensor(out=ACC[:, 512:1024], in0=ACC[:, 512:1024],
                                                                  in1=PT[pi][:, 512:1024], op=ALU.add),
                          reads=[r_pt[pi][1]], writes=[r_acc2[1]])
            fw.op("pe", lambda e: e.matmul(bank(ctx, 7), lhsT=ctx.ones_f, rhs=ACC[:, 512:1024], start=True, stop=True),
                  reads=[ctx.r_const, r_acc2[1]], writes=[ctx.r_ps[7]])
            fw.op("dve", lambda e: e.reciprocal(out=FT[0], in_=bank(ctx, 6)), reads=[ctx.r_ps[6]], writes=[r_ft[0]])
            fw.op("dve", lambda e: e.reciprocal(out=FT[1], in_=bank(ctx, 7)), reads=[ctx.r_ps[7]], writes=[r_ft[1]])
            fw.op("dve", lambda e: e.tensor_tensor(out=FT[0], in0=bank(ctx, 4), in1=FT[0], op=ALU.mult),
                  reads=[ctx.r_ps[4], r_ft[0]], writes=[r_ft[0]])
            fw.op("dve", lambda e: e.tensor_tensor(out=FT[1], in0=bank(ctx, 5), in1=FT[1], op=ALU.mult),
                  reads=[ctx.r_ps[5], r_ft[1]], writes=[r_ft[1]])
            fw.op("dve", lambda e: e.scalar_tensor_tensor(out=FT[2], in0=FT[1], scalar=NEGLAM, in1=FT[0],
                                                          op0=ALU.mult, op1=ALU.add),
                  reads=[r_ft[0], r_ft[1], r_lam], writes=[r_ft[2]])
            fw.op("act", lambda e: e.activation(out=SQH, in_=FT[2], func=AF.Square), reads=[r_ft[2]], writes=[r_sqh])
            def tail(h=h, qcols=qcols, t=Q):
                fw.op("pe", lambda e: e.matmul(bank(ctx, 7), lhsT=ctx.ones_bf, rhs=SQH, start=True, stop=True),
                      reads=[ctx.r_const, r_sqh], writes=[ctx.r_ps[7]])
                fw.op("act", lambda e: e.activation(out=FT[3], in_=bank(ctx, 7), func=AF.Sqrt, bias=ctx.eps_col, scale=1.0 / 128.0),
                      reads=[ctx.r_ps[7], ctx.r_const], writes=[r_ft[3]])
                fw.op("dve", lambda e: e.reciprocal(out=FT[3], in_=FT[3]), reads=[r_ft[3]], writes=[r_ft[3]])
                fw.op("dve", lambda e: e.scalar_tensor_tensor(
                    out=ctx.HY[:, 4 + h, qcols], in0=FT[2], scalar=GN[:, h:h + 1], in1=FT[3], op0=ALU.mult, op1=ALU.mult),
                    reads=[r_ft[2], r_ft[3], r_lam], writes=[ctx.r_hy[4 + h][t]])
            pend["tail"] = tail
    if pend["tail"] is not None:
        pend["tail"]()
        pend["tail"] = None


def emit_wout(ctx, l, io):
    fw = ctx.fw
    WO = carve(ctx, 4096, [KC, 1024], BF16)
    r_wo = [fw.res("wo") for _ in range(2)]
    wv = io["w_out"][l].rearrange("(k p) d -> p k d", p=128)
    for hlf in range(2):
        fw.op("pool", lambda e, hlf=hlf: e.dma_start(out=WO[:, hlf * 4:(hlf + 1) * 4, :], in_=wv[:, hlf * 4:(hlf + 1) * 4, :]),
              writes=[r_wo[hlf]], kind="d")
    n = 0
    for dc in range(KC):
        for t in range(NTC):
            b = n % 4
            n += 1
            ps = bank(ctx, b)
            for k in range(KC):
                fw.op("pe", lambda e, k=k, dc=dc, t=t, ps=ps: e.matmul(
                    ps, lhsT=WO[:, k, dc * 128:(dc + 1) * 128], rhs=ctx.HY[:, k, t * 512:(t + 1) * 512],
                    start=(k == 0), stop=(k == KC - 1)),
                    reads=[r_wo[k // 4], ctx.r_hy[k][t]], writes=[ctx.r_ps[b]])
            fw.op("dve", lambda e, dc=dc, t=t, ps=ps: e.tensor_tensor(
                out=ctx.XT[:, dc, t * 512:(t + 1) * 512], in0=ps, in1=ctx.XT[:, dc, t * 512:(t + 1) * 512], op=ALU.add),
                reads=[ctx.r_ps[b]], writes=[ctx.r_xt[dc][t]])


BIG_A = {"ffn2_w_gate": [D_MODEL, D_FF], "ffn2_w_up": [D_MODEL, D_FF], "ffn2_w_down": [D_FF, D_MODEL],
         "w_out": [D_MODEL, D_MODEL]}
BIG_B = {"ffn1_w_gate": [D_MODEL, D_FF], "ffn1_w_up": [D_MODEL, D_FF], "ffn1_w_down": [D_FF, D_MODEL],
         "w_in": [D_MODEL, 2048]}
SMALL_A = {"ffn2_norm": [D_MODEL], "pool_w": [4, 64, 64], "fourier_w": [256, 256], "pool_scale_t": [128, 2],
           "head_norm_t": [128, 4], "lamvec": [256], "laminit": [128, 2]}
SMALL_B = {"ffn1_norm": [D_MODEL], "mix_norm": [D_MODEL]}
TABLE_SPECS = {
    "pcorr": ([128, 2, 16], F32), "pmask": ([128, 8], F32),
    "f_wa": ([64, 128], BF16),
    "f_w3": ([128, 64, 3, 32], BF16), "f_c64": ([64, 2, 64], BF16),
    "dtile": ([128, 4, 128], BF16), "ident": ([128, 128], BF16),
    "kaug0": ([4, 9, SEQ], BF16), "qaug0": ([4, 9, TOK], BF16),
}
PAY_SPECS = {
    "kpay": ([4, 128, TOK], BF16), "vpay": ([4, 128, 16, 128], BF16), "fpay": ([4, TOK, 64], BF16),
    "hpay": ([2, 128, 16], F32), "qc_out": ([128, 4, TOK], BF16), "et_out": ([128, 2, TOK], F32),
    "x_out": ([D_MODEL, TOK], F32),
}
GATH_SPECS = {
    "kg": ([4, 128, SEQ], BF16), "vg": ([4, 128, 64, 128], BF16), "fg": ([4, SEQ, 64], BF16),
    "halo_all": ([2, 128, 4, 16], F32), "qc_in": ([128, 4, TOK], BF16), "et_in": ([128, 2, TOK], F32),
}


class _One:
    def __init__(self, ap):
        self.ap = ap

    def __getitem__(self, _):
        return self.ap


def build_launch(kind, dbg_phases=None):
    nc = bass.Bass("TRN2", target_bir_lowering=False)
    io = {}

    def inp(name, shp, dt=F32):
        return nc.dram_tensor(name, shp, dt, kind="ExternalInput").ap()

    io["x_in"] = inp("x_in", [D_MODEL, TOK])
    if kind != "first":
        for n, shp in {**BIG_A, **SMALL_A}.items():
            io[n] = _One(inp(n, shp))
        for n, (shp, dt) in TABLE_SPECS.items():
            io[n] = inp(n, shp, dt)
        for n, (shp, dt) in GATH_SPECS.items():
            io[n] = inp(n, shp, dt)
    if kind != "last":
        for n, shp in {**BIG_B, **SMALL_B}.items():
            io[n] = _One(inp(n, shp))
        for n, (shp, dt) in PAY_SPECS.items():
            io[n] = nc.dram_tensor(n, shp, dt, kind="ExternalOutput").ap()
    else:
        io["final_norm"] = inp("final_norm", [D_MODEL])
        io["y"] = nc.dram_tensor("y", [D_MODEL, TOK], F32, kind="ExternalOutput").ap()
    with ExitStack() as stack:
        ctx = Ctx()
        ctx.nc = nc
        fw = ctx.fw = FW(nc, stack)
        setup_memory(nc, stack, ctx)
        emit_consts(ctx)
        emit_load_x(ctx, io["x_in"])
        if kind != "first":
            ET = carve(ctx, OFF_ET, [2, ETW], F32)
            QC = carve(ctx, OFF_QC, [4, TOK], BF16)
            ctx.r_et = [fw.res("et") for _ in range(2)]
            ctx.r_qc = [fw.res("qc") for _ in range(4)]
            for ck in range(2):
                fw.op("sp", lambda e, ck=ck: e.dma_start(out=ET[:, ck, 8:8 + TOK], in_=io["et_in"][:, ck, :]),
                      writes=[ctx.r_et[ck]], kind="d")
            for h in range(4):
                fw.op("sp", lambda e, h=h: e.dma_start(out=QC[:, h, :], in_=io["qc_in"][:, h, :]),
                      writes=[ctx.r_qc[h]], kind="d")
            if dbg_phases is None or "pool" in dbg_phases:
                emit_pool(ctx, 0, io)
                fw.barrier()
            if dbg_phases is None or "fourier" in dbg_phases:
                emit_fourier(ctx, 0, io)
                fw.barrier()
            if dbg_phases is None or "attn" in dbg_phases:
                emit_attn(ctx, 0, io)
                fw.barrier()
            if dbg_phases is not None:
                hy_out = nc.dram_tensor("hy_out", [128, KC, TOK], BF16, kind="ExternalOutput").ap()
                for k in range(KC):
                    fw.op("sp", lambda e, k=k: e.dma_start(out=hy_out[:, k, :], in_=ctx.HY[:, k, :]),
                          reads=ctx.r_hy[k], kind="d")
                fw.barrier()
                fw.op("sp", None)
                fw.emit()
                return nc
            emit_wout(ctx, 0, io)
            fw.barrier()
            emit_ffn(ctx, io["ffn2_norm"][0], io["ffn2_w_gate"][0], io["ffn2_w_up"][0], io["ffn2_w_down"][0], OFF_DYN)
            fw.barrier()
            fw.new_epoch()
        if kind == "last":
            emit_final_norm(ctx, io["final_norm"], io["y"], OFF_DYN)
        else:
            emit_ffn(ctx, io["ffn1_norm"][0], io["ffn1_w_gate"][0], io["ffn1_w_up"][0], io["ffn1_w_down"][0], OFF_DYN)
            fw.barrier()
            emit_proj(ctx, 0, io)
            ET = carve(ctx, OFF_ET, [2, ETW], F32)
            QC = carve(ctx, OFF_QC, [4, TOK], BF16)
            for ck in range(2):
                fw.op("sp", lambda e, ck=ck: e.dma_start(out=io["et_out"][:, ck, :], in_=ET[:, ck, 8:8 + TOK]),
                      reads=[ctx.r_et[ck]], kind="d")
            for h in range(4):
                fw.op("sp", lambda e, h=h: e.dma_start(out=io["qc_out"][:, h, :], in_=QC[:, h, :]),
                      reads=[ctx.r_qc[h]], kind="d")
            fw.barrier()
            emit_store_x(ctx, io["x_out"])
        fw.barrier()
        fw.op("sp", None)
        fw.emit()
        nc._fw_stats = (len(fw.ops), fw.n_waits, dict(fw.count_log), max(dma_v for dma_v in [0]))
    return nc


def _bf(a):
    return np.asarray(a, dtype=np.float32).astype(ml_dtypes.bfloat16)


def make_tables(r):
    t = {}
    pcorr = np.ones((128, 2, 16), np.float32)
    for ck in range(2):
        for p in range(128):
            w = POOL_W[2 * ck + p // 64]
            left = w // 2
            right = w - 1 - left
            for i in range(8):
                if r == 0:
                    tt = i
                    cnt = min(tt + right + 1, SEQ) - max(tt - left, 0)
                    pcorr[p, ck, i] = w / cnt
                if r == 3:
                    tt = SEQ - 8 + i
                    cnt = min(tt + right + 1, SEQ) - max(tt - left, 0)
                    pcorr[p, ck, 8 + i] = w / cnt
    t["pcorr"] = pcorr
    pmask = np.zeros((128, 8), np.float32)
    if r - 1 >= 0:
        pmask[:, r - 1] = 1.0
    if r + 1 <= 3:
        pmask[:, 4 + r + 1] = 1.0
    t["pmask"] = pmask
    s1 = np.arange(64)[:, None]
    k1 = np.arange(64)[None, :]
    ang = 2 * np.pi * ((s1 * k1) % 64) / 64.0
    t["f_wa"] = _bf(np.concatenate([np.cos(ang), -np.sin(ang)], axis=1))
    s2 = np.arange(128)[:, None]
    ang = 2 * np.pi * ((s2 * k1) % SEQ) / float(SEQ)
    t["f_tr"] = (np.cos(ang) * FNORM).astype(np.float32)
    t["f_ti"] = (-np.sin(ang) * FNORM).astype(np.float32)
    del t["f_tr"], t["f_ti"]
    s2c = np.arange(128, dtype=np.int64)[:, None, None]
    k1c = np.arange(64, dtype=np.int64)[None, :, None]
    k2c = (32 * r + np.arange(32, dtype=np.int64))[None, None, :]
    ph = 2 * np.pi * (((k1c * s2c) + 64 * (k2c * s2c)) % SEQ) / float(SEQ)
    wr_ = np.cos(ph) * FNORM
    wi_ = -np.sin(ph) * FNORM
    t["f_w3"] = _bf(np.stack([wr_, wi_, -wi_], axis=2))
    c = np.arange(64)[:, None]
    cp = np.arange(64)[None, :]
    ang = 2 * np.pi * ((c * cp) % 64) / 64.0
    t["f_c64"] = _bf(np.stack([np.cos(ang), np.sin(ang)], axis=1))
    p = np.arange(128)
    dt = np.zeros((128, 4, 128), np.float32)
    for h in range(4):
        dt[:, h, :] = -SLOPES[h] * np.abs(p[:, None] - p[None, :])
    t["dtile"] = _bf(dt)
    t["ident"] = _bf(np.eye(128))
    L = np.arange(64)
    n = (16 * r + L) % 64
    sig = np.where(L < 16, 1.0, np.where(n < 16 * r, 1.0, -1.0))
    ncol = np.repeat(n, 128).astype(np.float64)
    sigc = np.repeat(sig, 128)
    pcol = np.tile(p, 64).astype(np.float64)
    kaug0 = np.zeros((4, 9, SEQ), np.float32)
    kaug1 = np.zeros((4, 64, SEQ), np.float32)
    qaug0 = np.zeros((4, 9, TOK), np.float32)
    qaug1 = np.zeros((4, 64, TOK), np.float32)
    tq = 2048 * r + np.arange(TOK)
    nq = (tq // 256).astype(np.float64)
    bq = (tq % 256).astype(np.float64)
    for h in range(4):
        m = SLOPES[h]
        A = np.stack([sigc * m * 128.0 * ncol, sigc * m * pcol, sigc, sigc])
        B = np.stack([np.ones(TOK), np.ones(TOK), -m * 256.0 * nq, -m * bq])
        kaug0[h, 0] = 1.0
        kaug0[h, 1:5] = A
        kaug0[h, 5:9] = A
        kaug1[h, 0:4] = A
        kaug1[h, 32:36] = A
        qaug0[h, 0] = 0.0
        qaug0[h, 1:5] = B
        qaug0[h, 5:9] = -2.0 * B
        qaug1[h, 32:36] = B
        qaug1[h, 0:4] = -2.0 * B
    for nm, a in (("kaug0", kaug0), ("qaug0", qaug0)):
        b = _bf(a)
        assert np.array_equal(b.astype(np.float32), a), nm
        t[nm] = b
    return t


def _layer_small(inputs, l):
    f32 = np.float32
    d = {}
    d["pool_scale_t"] = np.ascontiguousarray(np.asarray(inputs["pool_scale"][l], f32).reshape(2, 128).T)
    d["head_norm_t"] = np.ascontiguousarray(np.asarray(inputs["attn_head_norm"][l], f32).reshape(4, 128).T)
    d["lamvec"] = np.concatenate([np.asarray(inputs[k][l], f32) for k in ("lam_q1", "lam_k1", "lam_q2", "lam_k2")])
    li = lambda_init_fn(l)
    d["laminit"] = np.tile(np.array([[-li, 1.0 - li]], f32), (128, 1))
    return d


def _run(nc, in_maps):
    res = run_bass_kernel_spmd(nc, in_maps, core_ids=list(range(NCORES)))
    return res.results


class _Lay:
    def __init__(self, ap):
        self.ap = ap

    def __getitem__(self, l):
        return self.ap[l]


FUSED_W = {
    "ffn1_norm": [DEPTH, D_MODEL], "ffn1_w_gate": [DEPTH, D_MODEL, D_FF], "ffn1_w_up": [DEPTH, D_MODEL, D_FF],
    "ffn1_w_down": [DEPTH, D_FF, D_MODEL], "mix_norm": [DEPTH, D_MODEL], "w_in": [DEPTH, D_MODEL, 2048],
    "pool_w": [DEPTH, 4, 64, 64], "fourier_w": [DEPTH, 256, 256], "w_out": [DEPTH, D_MODEL, D_MODEL],
    "ffn2_norm": [DEPTH, D_MODEL], "ffn2_w_gate": [DEPTH, D_MODEL, D_FF], "ffn2_w_up": [DEPTH, D_MODEL, D_FF],
    "ffn2_w_down": [DEPTH, D_FF, D_MODEL],
    "pool_scale_t": [DEPTH, 128, 2], "head_norm_t": [DEPTH, 128, 4], "lamvec": [DEPTH, 256], "laminit": [DEPTH, 128, 2],
}
GROUPS = [[0, 1, 2, 3], [4, 5, 6, 7]]


def build_fused(depth=DEPTH):
    nc = bass.Bass("TRN2", target_bir_lowering=False)
    io = {}

    def inp(name, shp, dt=F32):
        return nc.dram_tensor(name, shp, dt, kind="ExternalInput").ap()

    io["x_in"] = inp("x_in", [D_MODEL, TOK])
    for n, shp in FUSED_W.items():
        io[n] = _Lay(inp(n, shp))
    io["final_norm"] = inp("final_norm", [D_MODEL])
    for n, (shp, dt) in TABLE_SPECS.items():
        io[n] = inp(n, shp, dt)
    io["y"] = nc.dram_tensor("y", [D_MODEL, TOK], F32, kind="ExternalOutput").ap()
    pay_kv = [nc.dram_tensor(f"pay_kv{h}", [256, TOK], BF16) for h in range(4)]
    kvg = [nc.dram_tensor(f"kvg{h}", [4 * 256, TOK], BF16) for h in range(4)]
    pay_f = nc.dram_tensor("pay_f", [4 * TOK, 64], BF16)
    fgat = nc.dram_tensor("fgat", [4 * 4 * TOK, 64], BF16)
    pay_h = nc.dram_tensor("pay_h", [256, 16], F32)
    hgat = nc.dram_tensor("hgat", [4 * 256, 16], F32)
    io["kpay"] = [pay_kv[h].ap()[0:128, :] for h in range(4)]
    io["vpay"] = [pay_kv[h].ap()[128:256, :].rearrange("p (t e) -> p t e", e=128) for h in range(4)]
    io["fpay"] = [pay_f.ap()[g * TOK:(g + 1) * TOK, :] for g in range(4)]
    io["hpay"] = pay_h.ap().rearrange("(k p) j -> k p j", p=128)
    io["halo_all"] = hgat.ap().rearrange("(r k p) j -> k p r j", r=4, k=2)
    with ExitStack() as stack:
        ctx = Ctx()
        ctx.nc = nc
        fw = ctx.fw = FW(nc, stack)
        setup_memory(nc, stack, ctx)
        rp = {"f": fw.res("pay_f"), "halo": fw.res("pay_h")}
        rg = {"f": fw.res("fgat"), "halo": fw.res("hgat")}
        for h in range(4):
            rp[("kv", h)] = fw.res("pay_kv")
            rg[("kv", h)] = fw.res("kvg")
        io["r_pay"] = rp
        io["r_gath"] = rg

        def cc(src, dst, key, extra=()):
            fw.op("pool", lambda e: e.collective_compute("AllGather", ALU.bypass, replica_groups=GROUPS,
                                                         ins=[src.ap().opt()], outs=[dst.ap().opt()]),
                  reads=[rp[key]] + list(extra), writes=[rg[key]], kind="cc")

        HEAD_ORDER = [3, 2, 1, 0]

        def after_f():
            cc(pay_h, hgat, "halo")
            emit_pool_halo_load(ctx, io)

        def after_kv():
            h0 = HEAD_ORDER[0]
            cc(pay_kv[h0], kvg[h0], ("kv", h0))

        def after_first_loads(r_k, r_v):
            for h in HEAD_ORDER[1:]:
                cc(pay_kv[h], kvg[h], ("kv", h), extra=list(r_k) + list(r_v))
            cc(pay_f, fgat, "f", extra=list(r_k) + list(r_v))

        def load_xs(g, XS, r_xs):
            fv = fgat.ap()
            for j in range(4):
                base = j * 4 * TOK + g * TOK
                fw.op("sp", lambda e, j=j, base=base: e.dma_start(
                    out=XS[16 * j:16 * (j + 1)], in_=fv[base:base + TOK, :].rearrange("(s1 s2) c -> s1 s2 c", s2=128)),
                    reads=[rg["f"]], writes=[r_xs], kind="d")

        def load_kv(h, K0, K1, V, r_k, r_v):
            kv = kvg[h].ap()
            for i in range(4):
                def mk(i=i, part=0):
                    def f(e):
                        rank = (ctx.pid + i) % 4
                        if part == 0:
                            return e.dma_start(out=K0[0:64, i * TOK:(i + 1) * TOK], in_=kv[bass.ds(rank * 256, 64), :])
                        if part == 1:
                            return e.dma_start(out=K1[0:64, i * TOK:(i + 1) * TOK], in_=kv[bass.ds(rank * 256 + 64, 64), :])
                        return e.dma_start(out=V[:, 16 * i:16 * (i + 1), :],
                                           in_=kv[bass.ds(rank * 256 + 128, 128), :].rearrange("p (t e) -> p t e", e=128))
                    return f
                fw.op("sp", mk(i, 0), reads=[rg[("kv", h)]], writes=[r_k[i]], kind="d")
                fw.op("sp", mk(i, 1), reads=[rg[("kv", h)]], writes=[r_k[4 + i]], kind="d")
                fw.op("sp", mk(i, 2), reads=[rg[("kv", h)]], writes=[r_v[i]], kind="d")

        io["load_xs"] = load_xs
        io["load_kv"] = load_kv
        io["head_order"] = HEAD_ORDER
        io["after_first_loads"] = after_first_loads

        def _pro(e):
            ctx.pid = nc.partition_id([mybir.EngineType.SP])
        fw.sp_prologue = _pro
        io["fg"] = None
        emit_consts(ctx)
        emit_load_x(ctx, io["x_in"])
        for l in range(depth):
            emit_ffn(ctx, io["ffn1_norm"][l], io["ffn1_w_gate"][l], io["ffn1_w_up"][l], io["ffn1_w_down"][l], OFF_DYN)
            fw.barrier()
            emit_pool_loads(ctx, l, io)
            emit_proj(ctx, l, io, after_f=after_f, after_kv=after_kv)
            fw.barrier()
            fw.new_epoch()
            emit_pool(ctx, l, io, preloaded=True)
            fw.barrier()
            emit_attn(ctx, l, io)
            fw.barrier()
            emit_fourier(ctx, l, io)
            fw.barrier()
            emit_wout(ctx, l, io)
            fw.barrier()
            emit_ffn(ctx, io["ffn2_norm"][l], io["ffn2_w_gate"][l], io["ffn2_w_up"][l], io["ffn2_w_down"][l], OFF_DYN)
            fw.barrier()
            fw.new_epoch()
        emit_final_norm(ctx, io["final_norm"], io["y"], OFF_DYN)
        fw.barrier()
        fw.op("sp", None)
        fw.emit()
        nc._fw_stats = (len(fw.ops), fw.n_waits, dict(fw.count_log))
    return nc


def fused_inputs(inputs, depth=DEPTH):
    f32 = np.float32
    x = np.asarray(inputs["x"], f32)
    shared = {}
    for n in FUSED_W:
        if n in inputs:
            shared[n] = np.ascontiguousarray(np.asarray(inputs[n], f32))
    sm = [_layer_small(inputs, l) for l in range(DEPTH)]
    for n in ("pool_scale_t", "head_norm_t", "lamvec", "laminit"):
        shared[n] = np.ascontiguousarray(np.stack([sm[l][n] for l in range(DEPTH)]).astype(f32))
    shared["final_norm"] = np.asarray(inputs["final_norm"], f32)
    tables = [make_tables(r) for r in range(4)]
    in_maps = []
    for c in range(NCORES):
        b, r = c // 4, c % 4
        d = {"x_in": np.ascontiguousarray(x[b, r * TOK:(r + 1) * TOK, :].T)}
        d.update(shared)
        d.update(tables[r])
        in_maps.append(d)
    return in_maps


def kernel(**inputs):
    in_maps = fused_inputs(inputs)
    outs = _run(_prog("fused"), in_maps)
    out = np.empty((BATCH, SEQ, D_MODEL), np.float32)
    for c in range(NCORES):
        b, r = c // 4, c % 4
        out[b, r * TOK:(r + 1) * TOK, :] = np.asarray(outs[c]["y"], np.float32).T
    return out


_PROGS = {}


def _prog(kind):
    if kind not in _PROGS:
        _PROGS[kind] = build_fused() if kind == "fused" else build_launch(kind)
    return _PROGS[kind]


def kernel_unfused(**inputs):
    f32 = np.float32
    x = np.asarray(inputs["x"], f32)
    tables = [make_tables(r) for r in range(4)]

    def wA(l):
        d = {n: np.ascontiguousarray(np.asarray(inputs[n][l], f32)) for n in BIG_A}
        d["ffn2_norm"] = np.asarray(inputs["ffn2_norm"][l], f32)
        d["pool_w"] = np.asarray(inputs["pool_w"][l], f32)
        d["fourier_w"] = np.asarray(inputs["fourier_w"][l], f32)
        d.update(_layer_small(inputs, l))
        return d

    def wB(l):
        d = {n: np.ascontiguousarray(np.asarray(inputs[n][l], f32)) for n in BIG_B}
        d["ffn1_norm"] = np.asarray(inputs["ffn1_norm"][l], f32)
        d["mix_norm"] = np.asarray(inputs["mix_norm"][l], f32)
        return d

    def gathered(outs):
        g = []
        for c in range(NCORES):
            b, r = c // 4, c % 4
            grp = [outs[4 * b + j] for j in range(4)]
            rot = [grp[(r + i) % 4] for i in range(4)]
            d = {}
            d["kg"] = np.ascontiguousarray(np.concatenate([o["kpay"] for o in rot], axis=2))
            d["vg"] = np.ascontiguousarray(np.concatenate([o["vpay"] for o in rot], axis=2))
            d["fg"] = np.ascontiguousarray(np.concatenate([o["fpay"] for o in grp], axis=1))
            d["halo_all"] = np.ascontiguousarray(np.stack([o["hpay"] for o in grp], axis=2))
            d["qc_in"] = outs[c]["qc_out"]
            d["et_in"] = outs[c]["et_out"]
            d["x_in"] = outs[c]["x_out"]
            g.append(d)
        return g

    b0 = wB(0)
    in_maps = []
    for c in range(NCORES):
        b, r = c // 4, c % 4
        d = {"x_in": np.ascontiguousarray(x[b, r * TOK:(r + 1) * TOK, :].T)}
        d.update(b0)
        in_maps.append(d)
    outs = _run(_prog("first"), in_maps)
    for l in range(1, DEPTH):
        g = gathered(outs)
        a, bb = wA(l - 1), wB(l)
        in_maps = []
        for c in range(NCORES):
            d = dict(g[c])
            d.update(a)
            d.update(bb)
            d.update(tables[c % 4])
            in_maps.append(d)
        outs = _run(_prog("mid"), in_maps)
    g = gathered(outs)
    a = wA(DEPTH - 1)
    in_maps = []
    for c in range(NCORES):
        d = dict(g[c])
        d.update(a)
        d.update(tables[c % 4])
        d["final_norm"] = np.asarray(inputs["final_norm"], f32)
        in_maps.append(d)
    outs = _run(_prog("last"), in_maps)
    out = np.empty((BATCH, SEQ, D_MODEL), f32)
    for c in range(NCORES):
        b, r = c // 4, c % 4
        out[b, r * TOK:(r + 1) * TOK, :] = np.asarray(outs[c]["y"], f32).T
    return out
```

```python
import math
from contextlib import ExitStack

import numpy as np
import ml_dtypes

import concourse.bass as bass
import concourse.mybir as mybir
from concourse.bass_utils import run_bass_kernel_spmd

F32 = mybir.dt.float32
BF16 = mybir.dt.bfloat16
AF = mybir.ActivationFunctionType
ALU = mybir.AluOpType

D_MODEL = 1024
BATCH = 2
SEQ = 8192
DEPTH = 4
D_FF = 2816
NCORES = 8
TOK = 2048
NTC = 4
KC = 8
EPS = 1e-6
HEADS = 4
SLOPES = [2.0 ** (-8.0 * (i + 1) / HEADS) for i in range(HEADS)]
POOL_W = (2, 4, 8, 16)
BAND = [4, 16, None, None]


def lambda_init_fn(layer_idx):
    return 0.8 - 0.6 * math.exp(-0.3 * layer_idx)


class Res:
    __slots__ = ("name", "last_w", "readers")

    def __init__(self, name):
        self.name = name
        self.last_w = None
        self.readers = []


class Op:
    __slots__ = ("eng", "fn", "deps", "kind", "signal", "has_dep", "idx")

    def __init__(self, eng, fn, kind):
        self.eng = eng
        self.fn = fn
        self.deps = set()
        self.kind = kind
        self.signal = None
        self.has_dep = False
        self.idx = -1


ENGS = ("pe", "act", "dve", "pool", "sp")


class FW:
    def __init__(self, nc, stack, n_dma_sems=24, n_cc_sems=4):
        self.nc = nc
        self.stack = stack
        self.ops = []
        self.last_op = {e: None for e in ENGS}
        self.pending = {e: [] for e in ENGS}
        self.outstanding_dma = []
        self.n_dma_sems = n_dma_sems
        self.n_cc_sems = n_cc_sems
        self.epoch_marks = []

    def res(self, name="r"):
        return Res(name)

    def op(self, eng, fn, reads=(), writes=(), kind="c", after_barrier=True):
        o = Op(eng, fn, kind)
        o.idx = len(self.ops)
        for r in reads:
            if r.last_w is not None:
                o.deps.add(r.last_w)
            if kind == "c":
                r.readers = [x for x in r.readers if not (x.kind == "c" and x.eng == eng)]
            r.readers.append(o)
        for w in writes:
            if w.last_w is not None:
                o.deps.add(w.last_w)
            for rd in w.readers:
                if rd is not o:
                    o.deps.add(rd)
            w.last_w = o
            w.readers = []
        if after_barrier and self.pending[eng]:
            o.deps.update(self.pending[eng])
            self.pending[eng] = []
        o.deps.discard(o)
        self.ops.append(o)
        if kind != "cc":
            self.last_op[eng] = o
        if kind == "d":
            self.outstanding_dma.append(o)
        return o

    def barrier(self):
        col = [o for o in self.last_op.values() if o is not None] + list(self.outstanding_dma)
        for e in ENGS:
            self.pending[e] = list(col) + self.pending[e]
        self.outstanding_dma = []

    def new_epoch(self):
        self.epoch_marks.append(len(self.ops))

    def emit(self):
        nc = self.nc
        st = self.stack
        for o in self.ops:
            keep = set()
            for p in o.deps:
                if p.eng == "pe" and o.eng == "pe" and p.kind == "c" and o.kind == "c":
                    continue
                keep.add(p)
                p.has_dep = True
            o.deps = keep
        n_epochs = len(self.epoch_marks) + 1
        eng_sems = {e: [st.enter_context(nc.semaphore(f"s_{e}_{k}")) for k in range(n_epochs)] for e in ENGS}
        dma_sems = [st.enter_context(nc.semaphore(f"s_dma_{k}")) for k in range(self.n_dma_sems)]
        n_sw = 8
        pool_of = {"pool": list(range(0, n_sw)), "sp": list(range(n_sw, self.n_dma_sems))}
        rr = {"pool": 0, "sp": 0}
        cc_sems = [st.enter_context(nc.semaphore(f"s_cc_{k}")) for k in range(self.n_cc_sems)]
        epoch = 0
        marks = list(self.epoch_marks)
        counters = {e: 0 for e in ENGS}
        dma_rr = 0
        cc_rr = 0
        dma_tot = [0] * self.n_dma_sems
        dma_prev = [None] * self.n_dma_sems
        cc_tot = [0] * self.n_cc_sems
        cc_prev = [None] * self.n_cc_sems
        pre_wait = {}
        for o in self.ops:
            while marks and o.idx >= marks[0]:
                marks.pop(0)
                epoch += 1
                counters = {e: 0 for e in ENGS}
            if o.kind == "d":
                lst = pool_of[o.eng]
                k = lst[rr[o.eng] % len(lst)]
                rr[o.eng] += 1
                if dma_prev[k] is not None:
                    pre_wait[o] = dma_prev[k].signal
                dma_tot[k] += 16
                o.signal = (dma_sems[k], dma_tot[k])
                dma_prev[k] = o
            elif o.kind == "cc":
                k = cc_rr
                cc_rr = (cc_rr + 1) % self.n_cc_sems
                if cc_prev[k] is not None:
                    pre_wait[o] = cc_prev[k].signal
                cc_tot[k] += 1
                o.signal = (cc_sems[k], cc_tot[k])
                cc_prev[k] = o
            elif o.has_dep:
                counters[o.eng] += 1
                o.signal = (eng_sems[o.eng][epoch], counters[o.eng])
                self.max_count = max(getattr(self, "max_count", 0), counters[o.eng])
                self.count_log = getattr(self, "count_log", {})
                self.count_log[(epoch, o.eng)] = counters[o.eng]
        by_eng = {e: [o for o in self.ops if o.eng == e] for e in ENGS}
        self.n_waits = 0

        def run(eng_name, eng):
            waited = {}
            for o in by_eng[eng_name]:
                need = [p.signal for p in o.deps]
                if o in pre_wait:
                    need.append(pre_wait[o])
                for (sem, val) in need:
                    key = id(sem)
                    if waited.get(key, 0) < val:
                        eng.wait_ge(sem, val)
                        waited[key] = val
                        self.n_waits += 1
                if o.fn is None:
                    continue
                ins = o.fn(eng)
                if o.kind == "d":
                    ins.then_inc(o.signal[0], 16)
                elif o.kind == "cc":
                    ins.then_inc(o.signal[0], 1)
                elif o.has_dep:
                    ins.then_inc(o.signal[0], 1)

        with nc.Block() as block:
            @block.tensor
            def _(e):
                run("pe", e)

            @block.scalar
            def _(e):
                run("act", e)

            @block.vector
            def _(e):
                run("dve", e)

            @block.gpsimd
            def _(e):
                run("pool", e)

            @block.sync
            def _(e):
                if getattr(self, "sp_prologue", None) is not None:
                    self.sp_prologue(e)
                run("sp", e)


ARENA_BYTES = 111 * 1024


class Ctx:
    pass


def carve(ctx, off_bytes, shape, dtype):
    esz = 2 if dtype == BF16 else 4
    n = int(np.prod(shape))
    assert off_bytes % 4 == 0 and off_bytes + n * esz <= ARENA_BYTES, (off_bytes, shape)
    v = ctx.arena[:, off_bytes // 2: off_bytes // 2 + n * esz // 2]
    if dtype != BF16:
        v = v.bitcast(dtype)
    if len(shape) == 1:
        return v
    names = " ".join(f"d{i}" for i in range(len(shape)))
    kw = {f"d{i}": shape[i] for i in range(len(shape))}
    return v.rearrange(f"p ({names}) -> p {names}", **kw)


OFF_CONST = 0
OFF_DYN = 4096


def setup_memory(nc, stack, ctx):
    ctx.XT = stack.enter_context(nc.sbuf_tensor("XT", [128, KC, TOK], F32))
    ctx.HY = stack.enter_context(nc.sbuf_tensor("HY", [128, KC, TOK], BF16))
    ctx.arena = stack.enter_context(nc.sbuf_tensor("ARENA", [128, ARENA_BYTES // 2], BF16))
    ctx.PS = stack.enter_context(nc.psum_tensor("PS", [128, 8 * 512], F32))
    fw = ctx.fw
    ctx.r_xt = [[fw.res(f"xt{k}_{t}") for t in range(NTC)] for k in range(KC)]
    ctx.r_hy = [[fw.res(f"hy{k}_{t}") for t in range(NTC)] for k in range(KC)]
    ctx.r_ps = [fw.res(f"ps{b}") for b in range(8)]
    ctx.ones_bf = carve(ctx, OFF_CONST + 0, [128], BF16)
    ctx.ident_bf = carve(ctx, OFF_CONST + 256, [128], BF16)
    ctx.gains = carve(ctx, OFF_CONST + 512, [KC], F32)
    ctx.eps_col = carve(ctx, OFF_CONST + 768, [1], F32)
    ctx.ones_f = carve(ctx, OFF_CONST + 1024, [128], F32)
    ctx.r_const = fw.res("const")
    ctx.r_gain = fw.res("gain")


def bank(ctx, b):
    return ctx.PS[:, b * 512:(b + 1) * 512]


def emit_consts(ctx):
    fw = ctx.fw
    fw.op("pool", lambda e: e.memset(ctx.ones_bf, 1.0), writes=[ctx.r_const])
    fw.op("pool", lambda e: e.memset(ctx.eps_col, EPS), writes=[ctx.r_const])
    fw.op("pool", lambda e: e.memset(ctx.ones_f, 1.0), writes=[ctx.r_const])


def emit_load_x(ctx, x_dram):
    fw = ctx.fw
    src = x_dram.rearrange("(k p) t -> p k t", p=128)
    for k in range(KC):
        fw.op("sp", lambda e, k=k: e.dma_start(out=ctx.XT[:, k, :], in_=src[:, k, :]),
              writes=ctx.r_xt[k], kind="d")


def emit_store_x(ctx, y_dram):
    fw = ctx.fw
    dst = y_dram.rearrange("(k p) t -> p k t", p=128)
    ops = []
    for k in range(KC):
        ops.append(fw.op("sp", lambda e, k=k: e.dma_start(out=dst[:, k, :], in_=ctx.XT[:, k, :]),
                         reads=ctx.r_xt[k], kind="d"))
    return ops


def emit_rmsnorm(ctx, gain_dram_row, off):
    fw = ctx.fw
    rstd = carve(ctx, off, [NTC, 512], F32)
    sq = [carve(ctx, off + 8192 + i * 1024, [512], BF16) for i in range(4)]
    r_rstd = [fw.res(f"rstd{t}") for t in range(NTC)]
    r_sq = [fw.res(f"sq{i}") for i in range(4)]
    g_src = gain_dram_row.rearrange("(k p) -> p k", p=128)
    fw.op("sp", lambda e: e.dma_start(out=ctx.gains, in_=g_src, allow_slow_non_contiguous=True), writes=[ctx.r_gain], kind="d")
    cnt = 0
    for t in range(NTC):
        pb = 6 + (t % 2)
        ps = bank(ctx, pb)
        for k in range(KC):
            i = cnt % 4
            cnt += 1
            fw.op("act", lambda e, k=k, t=t, i=i: e.activation(out=sq[i], in_=ctx.XT[:, k, t * 512:(t + 1) * 512],
                                                              func=AF.Square),
                  reads=[ctx.r_xt[k][t]], writes=[r_sq[i]])
            fw.op("pe", lambda e, k=k, i=i, ps=ps: e.matmul(ps, lhsT=ctx.ones_bf, rhs=sq[i], start=(k == 0), stop=(k == KC - 1)),
                  reads=[r_sq[i], ctx.r_const], writes=[ctx.r_ps[pb]])
        fw.op("act", lambda e, t=t, ps=ps: e.activation(out=rstd[:, t, :], in_=ps, func=AF.Sqrt, bias=ctx.eps_col,
                                                       scale=1.0 / D_MODEL),
              reads=[ctx.r_ps[pb], ctx.r_const], writes=[r_rstd[t]])
        fw.op("dve", lambda e, t=t: e.reciprocal(out=rstd[:, t, :], in_=rstd[:, t, :]),
              reads=[r_rstd[t]], writes=[r_rstd[t]])
        for k in range(KC):
            eng = "dve"
            fw.op(eng, lambda e, k=k, t=t: e.scalar_tensor_tensor(
                out=ctx.HY[:, k, t * 512:(t + 1) * 512], in0=ctx.XT[:, k, t * 512:(t + 1) * 512],
                scalar=ctx.gains[:, k:k + 1], in1=rstd[:, t, :], op0=ALU.mult, op1=ALU.mult),
                reads=[ctx.r_xt[k][t], r_rstd[t], ctx.r_gain], writes=[ctx.r_hy[k][t]])


FF_GROUPS = [(0, 4), (4, 4), (8, 4), (12, 4), (16, 4), (20, 2)]


def emit_ffn(ctx, norm_row, wg, wu, wd, off):
    fw = ctx.fw
    emit_rmsnorm(ctx, norm_row, off)
    o = off + 12288
    WG = [carve(ctx, o + s * 16384, [KC, 512], BF16) for s in range(2)]
    WU = [carve(ctx, o + s * 16384 + 8192, [KC, 512], BF16) for s in range(2)]
    o += 32768
    WD = [carve(ctx, o + s * 8192, [4, 1024], BF16) for s in range(2)]
    o += 16384
    AT = [carve(ctx, o + s * 16384, [4, TOK], BF16) for s in range(2)]
    o += 32768
    SG = [carve(ctx, o + s * 2048, [512], F32) for s in range(2)]
    o += 4096
    r_wg = [fw.res("wg") for _ in range(2)]
    r_wu = [fw.res("wu") for _ in range(2)]
    r_wd = [fw.res("wd") for _ in range(2)]
    r_at = [[[fw.res("at") for _ in range(NTC)] for _ in range(4)] for _ in range(2)]
    r_sg = [fw.res("sg") for _ in range(2)]
    wg_v = wg.rearrange("(k p) f -> p k f", p=128)
    wu_v = wu.rearrange("(k p) f -> p k f", p=128)
    wd_v = wd.rearrange("(c p) d -> p c d", p=128)
    state = {"sg": 0, "gu": 0, "y": 0}

    def load_w(g):
        f0, n = FF_GROUPS[g]
        s = g % 2
        c0, c1 = f0 * 128, (f0 + n) * 128
        fw.op("pool", lambda e: e.dma_start(out=WG[s][:, :, 0:c1 - c0], in_=wg_v[:, :, c0:c1]),
              writes=[r_wg[s]], kind="d", after_barrier=True)
        fw.op("pool", lambda e: e.dma_start(out=WU[s][:, :, 0:c1 - c0], in_=wu_v[:, :, c0:c1]),
              writes=[r_wu[s]], kind="d")
        fw.op("pool", lambda e: e.dma_start(out=WD[s][:, 0:n, :], in_=wd_v[:, f0:f0 + n, :]),
              writes=[r_wd[s]], kind="d")

    def up(g):
        f0, n = FF_GROUPS[g]
        s = g % 2
        for fc in range(n):
            for t in range(NTC):
                gb = 2 * (state["gu"] % 2)
                state["gu"] += 1
                gps, ups = bank(ctx, gb), bank(ctx, gb + 1)
                for k in range(KC):
                    fw.op("pe", lambda e, k=k, fc=fc, t=t, gps=gps: e.matmul(
                        gps, lhsT=WG[s][:, k, fc * 128:(fc + 1) * 128], rhs=ctx.HY[:, k, t * 512:(t + 1) * 512],
                        start=(k == 0), stop=(k == KC - 1)),
                        reads=[r_wg[s], ctx.r_hy[k][t]], writes=[ctx.r_ps[gb]])
                for k in range(KC):
                    fw.op("pe", lambda e, k=k, fc=fc, t=t, ups=ups: e.matmul(
                        ups, lhsT=WU[s][:, k, fc * 128:(fc + 1) * 128], rhs=ctx.HY[:, k, t * 512:(t + 1) * 512],
                        start=(k == 0), stop=(k == KC - 1)),
                        reads=[r_wu[s], ctx.r_hy[k][t]], writes=[ctx.r_ps[gb + 1]])
                si = state["sg"] % 2
                state["sg"] += 1
                fw.op("act", lambda e, gps=gps, si=si: e.activation(out=SG[si], in_=gps, func=AF.Silu),
                      reads=[ctx.r_ps[gb]], writes=[r_sg[si]])
                fw.op("dve", lambda e, ups=ups, si=si, fc=fc, t=t: e.tensor_tensor(
                    out=AT[s][:, fc, t * 512:(t + 1) * 512], in0=SG[si], in1=ups, op=ALU.mult),
                    reads=[ctx.r_ps[gb + 1], r_sg[si]], writes=[r_at[s][fc][t]])

    def down(g):
        f0, n = FF_GROUPS[g]
        s = g % 2
        for dc in range(KC):
            for t in range(NTC):
                yb = 4 + (state["y"] % 2)
                state["y"] += 1
                yps = bank(ctx, yb)
                for fc in range(n):
                    fw.op("pe", lambda e, fc=fc, dc=dc, t=t, yps=yps: e.matmul(
                        yps, lhsT=WD[s][:, fc, dc * 128:(dc + 1) * 128], rhs=AT[s][:, fc, t * 512:(t + 1) * 512],
                        start=(fc == 0), stop=(fc == n - 1)),
                        reads=[r_wd[s], r_at[s][fc][t]], writes=[ctx.r_ps[yb]])
                fw.op("dve", lambda e, dc=dc, t=t, yps=yps: e.scalar_tensor_tensor(
                    out=ctx.XT[:, dc, t * 512:(t + 1) * 512], in0=yps, scalar=0.5,
                    in1=ctx.XT[:, dc, t * 512:(t + 1) * 512], op0=ALU.mult, op1=ALU.add),
                    reads=[ctx.r_ps[yb]], writes=[ctx.r_xt[dc][t]])

    ng = len(FF_GROUPS)
    load_w(0)
    load_w(1)
    up(0)
    for g in range(1, ng):
        up(g)
        down(g - 1)
        if g + 1 < ng:
            load_w(g + 1)
    down(ng - 1)


OFF_ET = 32768
OFF_QC = 49280
OFF_HI = 65664
ETW = TOK + 16
FNORM = 1.0 / math.sqrt(SEQ * 64.0)


def emit_final_norm(ctx, gain_row, y_dram, off):
    fw = ctx.fw
    rstd = carve(ctx, off, [NTC, 512], F32)
    sq = [carve(ctx, off + 8192 + i * 1024, [512], BF16) for i in range(4)]
    ob = [carve(ctx, off + 12288 + i * 2048, [512], F32) for i in range(4)]
    r_rstd = [fw.res("rstd") for t in range(NTC)]
    r_sq = [fw.res("sq") for i in range(4)]
    r_ob = [fw.res("ob") for i in range(4)]
    g_src = gain_row.rearrange("(k p) -> p k", p=128)
    fw.op("sp", lambda e: e.dma_start(out=ctx.gains, in_=g_src, allow_slow_non_contiguous=True),
          writes=[ctx.r_gain], kind="d")
    dst = y_dram.rearrange("(k p) t -> p k t", p=128)
    cnt = 0
    outs = []
    for t in range(NTC):
        pb = 6 + (t % 2)
        ps = bank(ctx, pb)
        for k in range(KC):
            i = cnt % 4
            cnt += 1
            fw.op("act", lambda e, k=k, t=t, i=i: e.activation(out=sq[i], in_=ctx.XT[:, k, t * 512:(t + 1) * 512],
                                                              func=AF.Square),
                  reads=[ctx.r_xt[k][t]], writes=[r_sq[i]])
            fw.op("pe", lambda e, k=k, i=i, ps=ps: e.matmul(ps, lhsT=ctx.ones_bf, rhs=sq[i], start=(k == 0), stop=(k == KC - 1)),
                  reads=[r_sq[i], ctx.r_const], writes=[ctx.r_ps[pb]])
        fw.op("act", lambda e, t=t, ps=ps: e.activation(out=rstd[:, t, :], in_=ps, func=AF.Sqrt, bias=ctx.eps_col,
                                                       scale=1.0 / D_MODEL),
              reads=[ctx.r_ps[pb], ctx.r_const], writes=[r_rstd[t]])
        fw.op("dve", lambda e, t=t: e.reciprocal(out=rstd[:, t, :], in_=rstd[:, t, :]),
              reads=[r_rstd[t]], writes=[r_rstd[t]])
        for k in range(KC):
            i = (t * KC + k) % 4
            fw.op("dve", lambda e, k=k, t=t, i=i: e.scalar_tensor_tensor(
                out=ob[i], in0=ctx.XT[:, k, t * 512:(t + 1) * 512],
                scalar=ctx.gains[:, k:k + 1], in1=rstd[:, t, :], op0=ALU.mult, op1=ALU.mult),
                reads=[ctx.r_xt[k][t], r_rstd[t], ctx.r_gain], writes=[r_ob[i]])
            outs.append(fw.op("sp", lambda e, k=k, t=t, i=i: e.dma_start(out=dst[:, k, t * 512:(t + 1) * 512], in_=ob[i]),
                              reads=[r_ob[i]], kind="d"))
    return outs


def emit_proj(ctx, l, io, after_f=None, after_kv=None):
    fw = ctx.fw
    rp = io.get("r_pay", None)
    wr = (lambda key: [rp[key]]) if rp is not None else (lambda key: [])
    emit_rmsnorm(ctx, io["mix_norm"][l], OFF_DYN)
    WB = [carve(ctx, 16384 + s * 8192, [KC, 512], BF16) for s in range(2)]
    r_wb = [fw.res("wb") for _ in range(2)]
    ET = carve(ctx, OFF_ET, [2, ETW], F32)
    QC = carve(ctx, OFF_QC, [4, TOK], BF16)
    KST = carve(ctx, OFF_HI, [4, TOK], BF16)
    VST = carve(ctx, OFF_HI + 16384, [4, 16, 128], BF16)
    FST = carve(ctx, OFF_HI + 32768, [16, 256], BF16)
    ctx.r_et = [fw.res("et") for _ in range(2)]
    ctx.r_qc = [fw.res("qc") for _ in range(4)]
    r_kst = [fw.res("kst") for _ in range(4)]
    r_vst = fw.res("vst")
    r_fst = fw.res("fst")
    win = io["w_in"][l].rearrange("(k p) f -> p k f", p=128)
    st = {"b": 0}

    def load(blk):
        s = blk % 2
        fw.op("pool", lambda e: e.dma_start(out=WB[s], in_=win[:, :, blk * 512:(blk + 1) * 512]),
              writes=[r_wb[s]], kind="d")

    def nb():
        b = st["b"] % 4
        st["b"] += 1
        return b

    def fmajor(blk, c0, nchunk, evac):
        s = blk % 2
        for ck in range(nchunk):
            for t in range(NTC):
                b = nb()
                ps = bank(ctx, b)
                for k in range(KC):
                    fw.op("pe", lambda e, k=k, ck=ck, t=t, ps=ps: e.matmul(
                        ps, lhsT=WB[s][:, k, c0 + ck * 128:c0 + (ck + 1) * 128], rhs=ctx.HY[:, k, t * 512:(t + 1) * 512],
                        start=(k == 0), stop=(k == KC - 1)),
                        reads=[r_wb[s], ctx.r_hy[k][t]], writes=[ctx.r_ps[b]])
                evac(ck, t, ps, b)

    def tmajor(blk, c0, ncol, evac):
        s = blk % 2
        for tile in range(16):
            b = nb()
            ps = bank(ctx, b)[:, 0:ncol]
            t = tile // 4
            for k in range(KC):
                fw.op("pe", lambda e, k=k, tile=tile, ps=ps: e.matmul(
                    ps, lhsT=ctx.HY[:, k, tile * 128:(tile + 1) * 128], rhs=WB[s][:, k, c0:c0 + ncol],
                    start=(k == 0), stop=(k == KC - 1)),
                    reads=[r_wb[s], ctx.r_hy[k][t]], writes=[ctx.r_ps[b]])
            evac(tile, ps, b)

    outs = []
    load(0)
    load(3)
    fmajor(0, 0, 2, lambda ck, t, ps, b: fw.op(
        "act", lambda e: e.copy(out=ET[:, ck, 8 + t * 512:8 + (t + 1) * 512], in_=ps),
        reads=[ctx.r_ps[b]], writes=[ctx.r_et[ck]]))
    for ck in range(2):
        outs.append(fw.op("sp", lambda e, ck=ck: e.dma_start(out=io["hpay"][ck, :, 0:8], in_=ET[:, ck, 8:16]),
                          reads=[ctx.r_et[ck]], writes=wr("halo"), kind="d"))
        outs.append(fw.op("sp", lambda e, ck=ck: e.dma_start(out=io["hpay"][ck, :, 8:16], in_=ET[:, ck, TOK:TOK + 8]),
                          reads=[ctx.r_et[ck]], writes=wr("halo"), kind="d"))
    if after_f is not None:
        after_f()
    tmajor(0, 256, 256, lambda tile, ps, b: fw.op(
        "dve", lambda e: e.tensor_copy(out=FST[:, tile, :], in_=ps), reads=[ctx.r_ps[b]], writes=[r_fst]))
    for g in range(4):
        outs.append(fw.op("sp", lambda e, g=g: e.dma_start(
            out=io["fpay"][g].rearrange("(t p) c -> p t c", p=128), in_=FST[:, :, g * 64:(g + 1) * 64]),
            reads=[r_fst], writes=wr("f"), kind="d"))
    load(2)
    tmajor(3, 0, 512, lambda tile, ps, b: fw.op(
        "act" if tile % 2 else "dve",
        (lambda e: e.copy(out=VST[:, :, tile, :], in_=ps.rearrange("p (h e) -> p h e", h=4))) if tile % 2
        else (lambda e: e.tensor_copy(out=VST[:, :, tile, :], in_=ps.rearrange("p (h e) -> p h e", h=4))),
        reads=[ctx.r_ps[b]], writes=[r_vst]))
    load(1)
    fmajor(2, 0, 4, lambda ck, t, ps, b: fw.op(
        "dve", lambda e: e.tensor_copy(out=KST[:, ck, t * 512:(t + 1) * 512], in_=ps),
        reads=[ctx.r_ps[b]], writes=[r_kst[ck]]))
    for h in range(4):
        outs.append(fw.op("sp", lambda e, h=h: e.dma_start(out=io["kpay"][h], in_=KST[:, h, :]),
                          reads=[r_kst[h]], writes=wr(("kv", h)), kind="d"))
        outs.append(fw.op("sp", lambda e, h=h: e.dma_start(out=io["vpay"][h], in_=VST[:, h, :, :]),
                          reads=[r_vst], writes=wr(("kv", h)), kind="d"))
    if after_kv is not None:
        after_kv()
    fmajor(1, 0, 4, lambda ck, t, ps, b: fw.op(
        "act", lambda e: e.mul(out=QC[:, ck, t * 512:(t + 1) * 512], in_=ps, mul=0.125),
        reads=[ctx.r_ps[b]], writes=[ctx.r_qc[ck]]))
    return outs


def _stash_view(ctx):
    return ctx.HY[:, 0:4, :].rearrange("p k t -> p (k t)").bitcast(F32).rearrange("p (c t) -> p c t", c=2)


def emit_stash_et(ctx):
    fw = ctx.fw
    ET = carve(ctx, OFF_ET, [2, ETW], F32)
    SV = _stash_view(ctx)
    for ck in range(2):
        fw.op("dve" if ck == 0 else "act",
              (lambda e, ck=ck: e.tensor_copy(out=SV[:, ck, :], in_=ET[:, ck, 8:8 + TOK])) if ck == 0
              else (lambda e, ck=ck: e.copy(out=SV[:, ck, :], in_=ET[:, ck, 8:8 + TOK])),
              reads=[ctx.r_et[ck]], writes=[r for k in (2 * ck, 2 * ck + 1) for r in ctx.r_hy[k]])


def emit_restore_et(ctx):
    fw = ctx.fw
    ET = carve(ctx, OFF_ET, [2, ETW], F32)
    SV = _stash_view(ctx)
    for ck in range(2):
        fw.op("dve" if ck == 0 else "act",
              (lambda e, ck=ck: e.tensor_copy(out=ET[:, ck, 8:8 + TOK], in_=SV[:, ck, :])) if ck == 0
              else (lambda e, ck=ck: e.copy(out=ET[:, ck, 8:8 + TOK], in_=SV[:, ck, :])),
              reads=[r for k in (2 * ck, 2 * ck + 1) for r in ctx.r_hy[k]], writes=[ctx.r_et[ck]])


def _pool_bufs(ctx):
    fw = ctx.fw
    o = OFF_CONST + 1536
    d = dict(
        PW=carve(ctx, o, [2, 128], BF16),
        PWF=carve(ctx, o + 512, [2, 128], F32),
        PSC=carve(ctx, o + 1536, [2], F32),
        CORR=carve(ctx, o + 1544, [2, 16], F32),
        MASK=carve(ctx, o + 1672, [8], F32),
        HALO=carve(ctx, o + 1704, [2, 4, 16], F32),
    )
    if not hasattr(ctx, "r_pool"):
        ctx.r_pool = {n: fw.res(n) for n in ("pw", "pcst", "halo")}
    return d


def emit_pool_loads(ctx, l, io):
    fw = ctx.fw
    B = _pool_bufs(ctx)
    PW, PWF, PSC, CORR, MASK = B["PW"], B["PWF"], B["PSC"], B["CORR"], B["MASK"]
    r_pw, r_cst = ctx.r_pool["pw"], ctx.r_pool["pcst"]
    fw.op("pool", lambda e: e.memset(PWF, 0.0), writes=[r_pw])
    for g in range(4):
        ck, gl = g // 2, g % 2
        fw.op("sp", lambda e, g=g, ck=ck, gl=gl: e.dma_start(
            out=PWF[gl * 64:(gl + 1) * 64, ck, gl * 64:(gl + 1) * 64], in_=io["pool_w"][l][g]),
            writes=[r_pw], kind="d")
    fw.op("dve", lambda e: e.tensor_copy(out=PW, in_=PWF), reads=[r_pw], writes=[r_pw])
    fw.op("sp", lambda e: e.dma_start(out=PSC, in_=io["pool_scale_t"][l]), writes=[r_cst], kind="d")
    fw.op("sp", lambda e: e.dma_start(out=CORR, in_=io["pcorr"]), writes=[r_cst], kind="d")
    fw.op("sp", lambda e: e.dma_start(out=MASK, in_=io["pmask"]), writes=[r_cst], kind="d")


def emit_pool_halo_load(ctx, io):
    fw = ctx.fw
    HALO = _pool_bufs(ctx)["HALO"]
    hrd = [io["r_gath"]["halo"]] if "r_gath" in io else []
    for k_ in range(2):
        for r_ in range(4):
            fw.op("sp", lambda e, k_=k_, r_=r_: e.dma_start(out=HALO[:, k_, r_, :], in_=io["halo_all"][k_, :, r_, :]),
                  reads=hrd, writes=[ctx.r_pool["halo"]], kind="d")


def emit_pool(ctx, l, io, preloaded=False):
    fw = ctx.fw
    ET = carve(ctx, OFF_ET, [2, ETW], F32)
    T1 = carve(ctx, OFF_HI, [ETW], F32)
    T2 = carve(ctx, OFF_HI + 8256, [ETW], F32)
    o = OFF_HI + 16512
    MT = carve(ctx, o + 2304, [2, TOK], BF16)
    WIN = carve(ctx, o + 2304 + 8192, [TOK], F32)
    B = _pool_bufs(ctx)
    PW, PSC, CORR, MASK, HALO = B["PW"], B["PSC"], B["CORR"], B["MASK"], B["HALO"]
    if not preloaded:
        emit_pool_loads(ctx, l, io)
        emit_pool_halo_load(ctx, io)
    r_pw, r_cst, r_halo = ctx.r_pool["pw"], ctx.r_pool["pcst"], ctx.r_pool["halo"]
    r_t1, r_t2, r_win = (fw.res(n) for n in ("t1", "t2", "win"))
    r_mt = [fw.res("mt") for _ in range(2)]
    for ck in range(2):
        fw.op("dve", lambda e, ck=ck: e.tensor_scalar(out=ET[:, ck, 0:8], in0=HALO[:, ck, 0, 8:16], scalar1=MASK[:, 0:1],
                                                      scalar2=None, op0=ALU.mult),
              reads=[r_halo, r_cst], writes=[ctx.r_et[ck]])
        fw.op("dve", lambda e, ck=ck: e.tensor_scalar(out=ET[:, ck, TOK + 8:TOK + 16], in0=HALO[:, ck, 0, 0:8],
                                                      scalar1=MASK[:, 4:5], scalar2=None, op0=ALU.mult),
              reads=[r_halo, r_cst], writes=[ctx.r_et[ck]])
        for r in range(1, 4):
            fw.op("dve", lambda e, ck=ck, r=r: e.scalar_tensor_tensor(
                out=ET[:, ck, 0:8], in0=HALO[:, ck, r, 8:16], scalar=MASK[:, r:r + 1], in1=ET[:, ck, 0:8],
                op0=ALU.mult, op1=ALU.add), reads=[r_halo, r_cst], writes=[ctx.r_et[ck]])
            fw.op("dve", lambda e, ck=ck, r=r: e.scalar_tensor_tensor(
                out=ET[:, ck, TOK + 8:TOK + 16], in0=HALO[:, ck, r, 0:8], scalar=MASK[:, 4 + r:5 + r],
                in1=ET[:, ck, TOK + 8:TOK + 16], op0=ALU.mult, op1=ALU.add),
                reads=[r_halo, r_cst], writes=[ctx.r_et[ck]])
        E = ET[:, ck, :]
        fw.op("pool", lambda e, E=E: e.tensor_tensor(out=T1[:, 0:ETW - 1], in0=E[:, 0:ETW - 1], in1=E[:, 1:ETW], op=ALU.add),
              reads=[ctx.r_et[ck]], writes=[r_t1])
        if ck == 0:
            lv = {0: (T1, r_t1, 2)}
            fw.op("pool", lambda e: e.tensor_tensor(out=T2[:, 0:ETW - 3], in0=T1[:, 0:ETW - 3], in1=T1[:, 2:ETW - 1], op=ALU.add),
                  reads=[r_t1], writes=[r_t2])
            lv[1] = (T2, r_t2, 4)
        else:
            fw.op("pool", lambda e: e.tensor_tensor(out=T2[:, 0:ETW - 3], in0=T1[:, 0:ETW - 3], in1=T1[:, 2:ETW - 1], op=ALU.add),
                  reads=[r_t1], writes=[r_t2])
            fw.op("pool", lambda e: e.tensor_tensor(out=T1[:, 0:ETW - 7], in0=T2[:, 0:ETW - 7], in1=T2[:, 4:ETW - 3], op=ALU.add),
                  reads=[r_t2], writes=[r_t1])
            fw.op("pool", lambda e: e.tensor_tensor(out=T2[:, 0:ETW - 15], in0=T1[:, 0:ETW - 15], in1=T1[:, 8:ETW - 7], op=ALU.add),
                  reads=[r_t1], writes=[r_t2])
            lv = {0: (T1, r_t1, 8), 1: (T2, r_t2, 16)}
        for gl in range(2):
            src, rsrc, w = lv[gl]
            sh = 8 - w // 2
            ps_ = slice(gl * 64, (gl + 1) * 64)
            fw.op("dve", lambda e, src=src, sh=sh, ps_=ps_: e.tensor_copy(out=WIN[ps_, :], in_=src[ps_, sh:sh + TOK]),
                  reads=[rsrc], writes=[r_win])
            fw.op("dve", lambda e, ck=ck, ps_=ps_: e.tensor_tensor(out=WIN[ps_, 0:8], in0=WIN[ps_, 0:8],
                                                                  in1=CORR[ps_, ck, 0:8], op=ALU.mult),
                  reads=[r_cst], writes=[r_win])
            fw.op("dve", lambda e, ck=ck, ps_=ps_: e.tensor_tensor(out=WIN[ps_, TOK - 8:TOK], in0=WIN[ps_, TOK - 8:TOK],
                                                                  in1=CORR[ps_, ck, 8:16], op=ALU.mult),
                  reads=[r_cst], writes=[r_win])
            fw.op("dve", lambda e, ck=ck, ps_=ps_, w=w: e.scalar_tensor_tensor(
                out=MT[ps_, ck, :], in0=WIN[ps_, :], scalar=1.0 / w, in1=ET[ps_, ck, 8:8 + TOK],
                op0=ALU.mult, op1=ALU.subtract), reads=[r_win, ctx.r_et[ck]], writes=[r_mt[ck]])
    for ck in range(2):
        for t in range(NTC):
            b = t % 4
            ps = bank(ctx, b)
            fw.op("pe", lambda e, ck=ck, t=t, ps=ps: e.matmul(ps, lhsT=PW[:, ck, :], rhs=MT[:, ck, t * 512:(t + 1) * 512],
                                                            start=True, stop=True),
                  reads=[r_pw, r_mt[ck]], writes=[ctx.r_ps[b]])
            fw.op("act", lambda e, ck=ck, t=t, ps=ps: e.mul(out=ctx.HY[:, ck, t * 512:(t + 1) * 512], in_=ps, mul=PSC[:, ck:ck + 1]),
                  reads=[ctx.r_ps[b], r_cst], writes=[ctx.r_hy[ck][t]])


def emit_fourier(ctx, l, io):
    fw = ctx.fw
    XS = carve(ctx, 4096, [128, 64], BF16)
    BB = carve(ctx, 20480, [2, 64, 64], BF16)
    W3 = carve(ctx, 36864, [64, 3, 32], BF16)
    UT = carve(ctx, OFF_HI, [4, 2, TOK], BF16)
    MM = carve(ctx, OFF_HI + 32768, [4, 2, 256], BF16)
    WFS = carve(ctx, OFF_HI + 36864, [4, 256], BF16)
    tb = OFF_HI + 38912
    WA = carve(ctx, tb, [128], BF16)
    C64 = carve(ctx, tb + 960, [2, 64], BF16)
    r_tab, r_wfs, r_mm, r_bb, r_w3 = (fw.res(n) for n in ("ftab", "wfs", "mm", "bb", "w3"))
    r_xs = [fw.res("xs") for _ in range(4)]
    r_ut = [fw.res("ut") for _ in range(4)]
    fw.op("sp", lambda e: e.dma_start(out=WA[0:64, :], in_=io["f_wa"]), writes=[r_tab], kind="d")
    fw.op("sp", lambda e: e.dma_start(out=W3, in_=io["f_w3"]), writes=[r_w3], kind="d")
    fw.op("sp", lambda e: e.dma_start(out=C64[0:64], in_=io["f_c64"]), writes=[r_tab], kind="d")
    fw.op("pool", lambda e: e.dma_start(out=WFS[0:64], in_=io["fourier_w"][l].rearrange("(g p) c -> p g c", p=64)),
          writes=[r_wfs], kind="d")
    for g in range(4):
        for comp in range(2):
            b = (g * 2 + comp) % 4
            ps = bank(ctx, b)[0:64, 0:256]
            fw.op("pe", lambda e, g=g, comp=comp, ps=ps: e.matmul(ps, lhsT=C64[0:64, comp, :], rhs=WFS[0:64, g, :],
                                                                 start=True, stop=True),
                  reads=[r_tab, r_wfs], writes=[ctx.r_ps[b]])
            fw.op("dve", lambda e, g=g, comp=comp, ps=ps: e.tensor_copy(out=MM[0:64, g, comp, :], in_=ps),
                  reads=[ctx.r_ps[b]], writes=[r_mm])
    for g in range(4):
        if "load_xs" in io:
            io["load_xs"](g, XS, r_xs)
        else:
            fw.op("sp", lambda e, g=g: e.dma_start(out=XS[0:64], in_=io["fg"][g].rearrange("(s1 s2) c -> s1 s2 c", s2=128)),
                  writes=r_xs, kind="d")
        for rd in range(4):
            pb0 = 4 * (rd % 2)
            PSV = ctx.PS[:, pb0 * 512:(pb0 + 4) * 512].rearrange("p (c x) -> p c x", x=128)
            rps = [ctx.r_ps[pb0 + i] for i in range(4)]
            for ci in range(16):
                c = rd * 16 + ci
                fw.op("pe", lambda e, c=c, ci=ci, PSV=PSV: e.matmul(PSV[:, ci, :], lhsT=XS[0:64, :, c], rhs=WA[0:64, :],
                                                                   start=True, stop=True),
                      reads=r_xs + [r_tab], writes=[rps[ci // 4]])
            AR = PSV[:, :, 0:64]
            AI = PSV[:, :, 64:128]
            cs = slice(rd * 16, (rd + 1) * 16)
            BRv = BB[:, 0, :, cs].rearrange("p k c -> p c k")
            BIv = BB[:, 1, :, cs].rearrange("p k c -> p c k")
            fw.op("act", lambda e, AR=AR, BRv=BRv: e.copy(out=BRv, in_=AR), reads=rps, writes=[r_bb])
            fw.op("dve", lambda e, AI=AI, BIv=BIv: e.tensor_copy(out=BIv, in_=AI), reads=rps, writes=[r_bb])
        for q4 in range(4):
            pr, pi = (q4 % 2) * 2, (q4 % 2) * 2 + 1
            for kk in range(16):
                k1 = q4 * 16 + kk
                outr = bank(ctx, pr)[0:64, kk * 32:(kk + 1) * 32]
                outi = bank(ctx, pi)[0:64, kk * 32:(kk + 1) * 32]
                fw.op("pe", lambda e, k1=k1, outr=outr: e.matmul(outr, lhsT=BB[:, 0, k1, :], rhs=W3[:, k1, 0, :], start=True, stop=False),
                      reads=[r_bb, r_w3], writes=[ctx.r_ps[pr]])
                fw.op("pe", lambda e, k1=k1, outr=outr: e.matmul(outr, lhsT=BB[:, 1, k1, :], rhs=W3[:, k1, 2, :], start=False, stop=True),
                      reads=[r_bb, r_w3], writes=[ctx.r_ps[pr]])
                fw.op("pe", lambda e, k1=k1, outi=outi: e.matmul(outi, lhsT=BB[:, 1, k1, :], rhs=W3[:, k1, 0, :], start=True, stop=False),
                      reads=[r_bb, r_w3], writes=[ctx.r_ps[pi]])
                fw.op("pe", lambda e, k1=k1, outi=outi: e.matmul(outi, lhsT=BB[:, 0, k1, :], rhs=W3[:, k1, 1, :], start=False, stop=True),
                      reads=[r_bb, r_w3], writes=[ctx.r_ps[pi]])
            for comp, pbk in ((0, pr), (1, pi)):
                src = bank(ctx, pbk)[0:64, :].rearrange("p (k j) -> p k j", j=32)
                dstv = UT[0:64, g, comp, :].rearrange("p (j k) -> p k j", k=64)[:, q4 * 16:(q4 + 1) * 16, :]
                fw.op("act" if comp else "dve",
                      (lambda e, src=src, dstv=dstv: e.copy(out=dstv, in_=src)) if comp
                      else (lambda e, src=src, dstv=dstv: e.tensor_copy(out=dstv, in_=src)),
                      reads=[ctx.r_ps[pbk]], writes=[r_ut[g]])
    for ck in range(2):
        for t in range(NTC):
            b = 4 + (ck * NTC + t) % 4
            ps = bank(ctx, b)
            n = 0
            for g in range(4):
                for comp in range(2):
                    fw.op("pe", lambda e, g=g, comp=comp, ck=ck, t=t, ps=ps, n=n: e.matmul(
                        ps, lhsT=MM[0:64, g, comp, ck * 128:(ck + 1) * 128], rhs=UT[0:64, g, comp, t * 512:(t + 1) * 512],
                        start=(n == 0), stop=(n == 7)),
                        reads=[r_mm, r_ut[g]], writes=[ctx.r_ps[b]])
                    n += 1
            fw.op("act", lambda e, ck=ck, t=t, ps=ps: e.copy(out=ctx.HY[:, 2 + ck, t * 512:(t + 1) * 512], in_=ps),
                  reads=[ctx.r_ps[b]], writes=[ctx.r_hy[2 + ck][t]])


def emit_attn(ctx, l, io):
    fw = ctx.fw
    K0 = carve(ctx, 4096, [SEQ], BF16)
    K1 = carve(ctx, 20480, [SEQ], BF16)
    Q0 = carve(ctx, 36864, [TOK], BF16)
    Q1 = carve(ctx, 40960, [TOK], BF16)
    DT = carve(ctx, 45056, [4, 128], BF16)
    o = 46080
    LAMV = carve(ctx, o, [256], F32)
    LTMP = carve(ctx, o + 1024, [64], F32)
    LS = carve(ctx, o + 1280, [8], F32)
    GN = carve(ctx, o + 1312, [4], F32)
    SQH = carve(ctx, o + 1344, [512], BF16)
    QC = carve(ctx, OFF_QC, [4, TOK], BF16)
    V = carve(ctx, OFF_HI, [64, 128], BF16)
    PT = [carve(ctx, OFF_HI + 16384 + i * 2048, [1024], BF16) for i in range(3)]
    FT = [carve(ctx, OFF_HI + 22528 + i * 2048, [512], F32) for i in range(4)]
    ACC = carve(ctx, OFF_HI + 30720, [1024], F32)
    r_acc2 = [fw.res("acc0"), fw.res("acc1")]
    r_d, r_lam = (fw.res(n) for n in ("dt", "lam"))
    r_k = [fw.res("k") for _ in range(10)]
    r_v = [fw.res("v") for _ in range(4)]
    r_q = [fw.res("q") for _ in range(4)]
    r_pt = [[fw.res("pt0"), fw.res("pt1")] for _ in range(3)]
    r_ft = [fw.res("ft") for _ in range(4)]
    r_sqh = fw.res("sqh")
    LI = carve(ctx, o + 2368, [2], F32)
    fw.op("sp", lambda e: e.dma_start(out=DT, in_=io["dtile"]), writes=[r_d], kind="d")
    fw.op("sp", lambda e: e.dma_start(out=ctx.ident_bf, in_=io["ident"]), writes=[ctx.r_const], kind="d")
    fw.op("sp", lambda e: e.dma_start(out=LAMV, in_=io["lamvec"][l].partition_broadcast(128)), writes=[r_lam], kind="d")
    fw.op("sp", lambda e: e.dma_start(out=GN, in_=io["head_norm_t"][l]), writes=[r_lam], kind="d")
    fw.op("sp", lambda e: e.dma_start(out=LI, in_=io["laminit"][l]), writes=[r_lam], kind="d")
    for i in range(2):
        fw.op("dve", lambda e, i=i: e.tensor_tensor(out=LTMP, in0=LAMV[:, i * 128:i * 128 + 64],
                                                    in1=LAMV[:, i * 128 + 64:i * 128 + 128], op=ALU.mult),
              reads=[r_lam], writes=[r_lam])
        fw.op("dve", lambda e, i=i: e.reduce_sum(out=LS[:, i:i + 1], in_=LTMP, axis=mybir.AxisListType.X),
              reads=[r_lam], writes=[r_lam])
    fw.op("act", lambda e: e.activation(out=LS[:, 2:4], in_=LS[:, 0:2], func=AF.Exp), reads=[r_lam], writes=[r_lam])
    fw.op("dve", lambda e: e.tensor_tensor(out=LS[:, 4:5], in0=LS[:, 3:4], in1=LS[:, 2:3], op=ALU.subtract),
          reads=[r_lam], writes=[r_lam])
    fw.op("dve", lambda e: e.tensor_tensor(out=LS[:, 5:6], in0=LS[:, 4:5], in1=LI[:, 0:1], op=ALU.add),
          reads=[r_lam], writes=[r_lam])
    fw.op("dve", lambda e: e.tensor_scalar(out=GN, in0=GN, scalar1=LI[:, 1:2], scalar2=None, op0=ALU.mult),
          reads=[r_lam], writes=[r_lam])
    NEGLAM = LS[:, 5:6]

    it = {"n": 0}
    for hi_, h in enumerate(io.get("head_order", list(range(HEADS)))):
        if "load_kv" in io:
            io["load_kv"](h, K0, K1, V, r_k, r_v)
        else:
            fw.op("sp", lambda e, h=h: e.dma_start(out=K0[0:64, :], in_=io["kg"][h, 0:64, :]), writes=r_k[0:4], kind="d")
            fw.op("sp", lambda e, h=h: e.dma_start(out=K1[0:64, :], in_=io["kg"][h, 64:128, :]), writes=r_k[4:8], kind="d")
            fw.op("sp", lambda e, h=h: e.dma_start(out=V, in_=io["vg"][h]), writes=r_v, kind="d")
        fw.op("sp", lambda e, h=h: e.dma_start(out=K0[64:73, :], in_=io["kaug0"][h]), writes=[r_k[8]], kind="d")
        fw.op("sp", lambda e, h=h: e.dma_start(out=K1[64:73, :], in_=io["kaug0"][h]), writes=[r_k[9]], kind="d")
        if hi_ == 0 and "after_first_loads" in io:
            io["after_first_loads"](r_k, r_v)
        fw.op("sp", lambda e, h=h: e.dma_start(out=Q0[64:73, :], in_=io["qaug0"][h]), writes=[r_q[0]], kind="d")
        fw.op("sp", lambda e, h=h: e.dma_start(out=Q1[64:73, :], in_=io["qaug0"][h]), writes=[r_q[1]], kind="d")
        fw.op("dve", lambda e, h=h: e.tensor_copy(out=Q0[0:64, :], in_=QC[0:64, h, :]), reads=[ctx.r_qc[h]], writes=[r_q[2]])
        fw.op("sp", lambda e, h=h: e.dma_start(out=Q1[0:64, :], in_=QC[64:128, h, :]), reads=[ctx.r_qc[h]], writes=[r_q[3]], kind="d")

        def s_mm(Q, L, sb, hh=h):
            ks = slice(L * 128, (L + 1) * 128)
            for j in range(2):
                KT, QT = (K0, Q0) if j == 0 else (K1, Q1)
                ps = bank(ctx, sb + j)

                def rng(mode):
                    return {"diag": slice(0, 65), "below": slice(0, 69), "above": slice(0, 73)}[mode]

                def mm(out, mode, qs, start=True, stop=True, KT=KT, QT=QT, j=j):
                    pr = rng(mode)
                    rb = ctx.r_ps[sb + j]
                    fw.op("pe", lambda e: e.matmul(out, lhsT=KT[pr, ks], rhs=QT[pr, qs], start=start, stop=stop),
                          reads=r_k + r_q, writes=[rb])

                if L >= 16 or L < 4 * Q:
                    mm(ps, "below", slice(Q * 512, (Q + 1) * 512))
                elif L >= 4 * Q + 4:
                    mm(ps, "above", slice(Q * 512, (Q + 1) * 512))
                else:
                    us = L - 4 * Q
                    for u in range(4):
                        qs = slice(Q * 512 + u * 128, Q * 512 + (u + 1) * 128)
                        out = ps[:, u * 128:(u + 1) * 128]
                        if u > us:
                            mm(out, "below", qs)
                        elif u < us:
                            mm(out, "above", qs)
                        else:
                            mm(out, "diag", qs, start=True, stop=False)
                            fw.op("pe", lambda e, out=out, hh=hh: e.matmul(out, lhsT=ctx.ident_bf, rhs=DT[:, hh, :], start=False, stop=True),
                                  reads=[ctx.r_const, r_d], writes=[ctx.r_ps[sb + j]])

        for Q in range(4):
            qcols = slice(Q * 512, (Q + 1) * 512)
            if BAND[h] is None:
                Ls = list(range(64))
            else:
                Ls = [(4 * Q + d_) % 64 for d_ in range(-BAND[h], BAND[h] + 4)]
            nL = len(Ls)
            s_mm(Q, Ls[0], 0)
            for li, L in enumerate(Ls):
                sb = 2 * (li % 2)
                if li + 1 < nL:
                    s_mm(Q, Ls[li + 1], 2 * ((li + 1) % 2))
                pi = it["n"] % 3
                it["n"] += 1
                for j in range(2):
                    fw.op("act", lambda e, sb=sb, pi=pi, j=j: e.activation(out=PT[pi][:, j * 512:(j + 1) * 512], in_=bank(ctx, sb + j), func=AF.Exp),
                          reads=[ctx.r_ps[sb + j]], writes=[r_pt[pi][j]])
                for j in range(2):
                    fw.op("pe", lambda e, L=L, j=j, pi=pi, li=li, nL=nL: e.matmul(bank(ctx, 4 + j), lhsT=V[:, L, :], rhs=PT[pi][:, j * 512:(j + 1) * 512],
                                                                   start=(li == 0), stop=(li == nL - 1)),
                          reads=r_v + [r_pt[pi][j]], writes=[ctx.r_ps[4 + j]])
                    if j == 0:
                        fw.op("pe", lambda e, pi=pi, li=li, nL=nL: e.matmul(bank(ctx, 6), lhsT=ctx.ones_bf, rhs=PT[pi][:, 0:512],
                                                                      start=(li == 0), stop=(li == nL - 1)),
                              reads=[ctx.r_const, r_pt[pi][0]], writes=[ctx.r_ps[6]])
                if li == 0:
                    fw.op("dve", lambda e, pi=pi: e.tensor_copy(out=ACC[:, 512:1024], in_=PT[pi][:, 512:1024]),
                          reads=[r_pt[pi][1]], writes=[r_acc2[1]])
                else:
                    fw.op("dve", lambda e, pi=pi: e.tensor_tensor(out=ACC[:, 512:1024], in0=ACC[:, 512:1024],
                                                                  in1=PT[pi][:, 512:1024], op=ALU.add),
                          reads=[r_pt[pi][1]], writes=[r_acc2[1]])
            fw.op("pe", lambda e: e.matmul(bank(ctx, 7), lhsT=ctx.ones_f, rhs=ACC[:, 512:1024], start=True, stop=True),
                  reads=[ctx.r_const, r_acc2[1]], writes=[ctx.r_ps[7]])
            fw.op("dve", lambda e: e.reciprocal(out=FT[0], in_=bank(ctx, 6)), reads=[ctx.r_ps[6]], writes=[r_ft[0]])
            fw.op("dve", lambda e: e.reciprocal(out=FT[1], in_=bank(ctx, 7)), reads=[ctx.r_ps[7]], writes=[r_ft[1]])
            fw.op("dve", lambda e: e.tensor_tensor(out=FT[0], in0=bank(ctx, 4), in1=FT[0], op=ALU.mult),
                  reads=[ctx.r_ps[4], r_ft[0]], writes=[r_ft[0]])
            fw.op("dve", lambda e: e.tensor_tensor(out=FT[1], in0=bank(ctx, 5), in1=FT[1], op=ALU.mult),
                  reads=[ctx.r_ps[5], r_ft[1]], writes=[r_ft[1]])
            fw.op("dve", lambda e: e.scalar_tensor_tensor(out=FT[2], in0=FT[1], scalar=NEGLAM, in1=FT[0],
                                                          op0=ALU.mult, op1=ALU.add),
                  reads=[r_ft[0], r_ft[1], r_lam], writes=[r_ft[2]])
            fw.op("act", lambda e: e.activation(out=SQH, in_=FT[2], func=AF.Square), reads=[r_ft[2]], writes=[r_sqh])
            fw.op("pe", lambda e: e.matmul(bank(ctx, 6), lhsT=ctx.ones_bf, rhs=SQH, start=True, stop=True),
                  reads=[ctx.r_const, r_sqh], writes=[ctx.r_ps[6]])
            fw.op("act", lambda e: e.activation(out=FT[3], in_=bank(ctx, 6), func=AF.Sqrt, bias=ctx.eps_col, scale=1.0 / 128.0),
                  reads=[ctx.r_ps[6], ctx.r_const], writes=[r_ft[3]])
            fw.op("dve", lambda e: e.reciprocal(out=FT[3], in_=FT[3]), reads=[r_ft[3]], writes=[r_ft[3]])
            t = Q
            fw.op("dve", lambda e, h=h, qcols=qcols: e.scalar_tensor_tensor(
                out=ctx.HY[:, 4 + h, qcols], in0=FT[2], scalar=GN[:, h:h + 1], in1=FT[3], op0=ALU.mult, op1=ALU.mult),
                reads=[r_ft[2], r_ft[3], r_lam], writes=[ctx.r_hy[4 + h][t]])


def emit_wout(ctx, l, io):
    fw = ctx.fw
    WO = carve(ctx, 4096, [KC, 1024], BF16)
    r_wo = [fw.res("wo") for _ in range(2)]
    wv = io["w_out"][l].rearrange("(k p) d -> p k d", p=128)
    for hlf in range(2):
        fw.op("pool", lambda e, hlf=hlf: e.dma_start(out=WO[:, hlf * 4:(hlf + 1) * 4, :], in_=wv[:, hlf * 4:(hlf + 1) * 4, :]),
              writes=[r_wo[hlf]], kind="d")
    n = 0
    for dc in range(KC):
        for t in range(NTC):
            b = n % 4
            n += 1
            ps = bank(ctx, b)
            for k in range(KC):
                fw.op("pe", lambda e, k=k, dc=dc, t=t, ps=ps: e.matmul(
                    ps, lhsT=WO[:, k, dc * 128:(dc + 1) * 128], rhs=ctx.HY[:, k, t * 512:(t + 1) * 512],
                    start=(k == 0), stop=(k == KC - 1)),
                    reads=[r_wo[k // 4], ctx.r_hy[k][t]], writes=[ctx.r_ps[b]])
            fw.op("dve", lambda e, dc=dc, t=t, ps=ps: e.tensor_tensor(
                out=ctx.XT[:, dc, t * 512:(t + 1) * 512], in0=ps, in1=ctx.XT[:, dc, t * 512:(t + 1) * 512], op=ALU.add),
                reads=[ctx.r_ps[b]], writes=[ctx.r_xt[dc][t]])


BIG_A = {"ffn2_w_gate": [D_MODEL, D_FF], "ffn2_w_up": [D_MODEL, D_FF], "ffn2_w_down": [D_FF, D_MODEL],
         "w_out": [D_MODEL, D_MODEL]}
BIG_B = {"ffn1_w_gate": [D_MODEL, D_FF], "ffn1_w_up": [D_MODEL, D_FF], "ffn1_w_down": [D_FF, D_MODEL],
         "w_in": [D_MODEL, 2048]}
SMALL_A = {"ffn2_norm": [D_MODEL], "pool_w": [4, 64, 64], "fourier_w": [256, 256], "pool_scale_t": [128, 2],
           "head_norm_t": [128, 4], "lamvec": [256], "laminit": [128, 2]}
SMALL_B = {"ffn1_norm": [D_MODEL], "mix_norm": [D_MODEL]}
TABLE_SPECS = {
    "pcorr": ([128, 2, 16], F32), "pmask": ([128, 8], F32),
    "f_wa": ([64, 128], BF16),
    "f_w3": ([128, 64, 3, 32], BF16), "f_c64": ([64, 2, 64], BF16),
    "dtile": ([128, 4, 128], BF16), "ident": ([128, 128], BF16),
    "kaug0": ([4, 9, SEQ], BF16), "qaug0": ([4, 9, TOK], BF16),
}
PAY_SPECS = {
    "kpay": ([4, 128, TOK], BF16), "vpay": ([4, 128, 16, 128], BF16), "fpay": ([4, TOK, 64], BF16),
    "hpay": ([2, 128, 16], F32), "qc_out": ([128, 4, TOK], BF16), "et_out": ([128, 2, TOK], F32),
    "x_out": ([D_MODEL, TOK], F32),
}
GATH_SPECS = {
    "kg": ([4, 128, SEQ], BF16), "vg": ([4, 128, 64, 128], BF16), "fg": ([4, SEQ, 64], BF16),
    "halo_all": ([2, 128, 4, 16], F32), "qc_in": ([128, 4, TOK], BF16), "et_in": ([128, 2, TOK], F32),
}


class _One:
    def __init__(self, ap):
        self.ap = ap

    def __getitem__(self, _):
        return self.ap


def build_launch(kind, dbg_phases=None):
    nc = bass.Bass("TRN2", target_bir_lowering=False)
    io = {}

    def inp(name, shp, dt=F32):
        return nc.dram_tensor(name, shp, dt, kind="ExternalInput").ap()

    io["x_in"] = inp("x_in", [D_MODEL, TOK])
    if kind != "first":
        for n, shp in {**BIG_A, **SMALL_A}.items():
            io[n] = _One(inp(n, shp))
        for n, (shp, dt) in TABLE_SPECS.items():
            io[n] = inp(n, shp, dt)
        for n, (shp, dt) in GATH_SPECS.items():
            io[n] = inp(n, shp, dt)
    if kind != "last":
        for n, shp in {**BIG_B, **SMALL_B}.items():
            io[n] = _One(inp(n, shp))
        for n, (shp, dt) in PAY_SPECS.items():
            io[n] = nc.dram_tensor(n, shp, dt, kind="ExternalOutput").ap()
    else:
        io["final_norm"] = inp("final_norm", [D_MODEL])
        io["y"] = nc.dram_tensor("y", [D_MODEL, TOK], F32, kind="ExternalOutput").ap()
    with ExitStack() as stack:
        ctx = Ctx()
        ctx.nc = nc
        fw = ctx.fw = FW(nc, stack)
        setup_memory(nc, stack, ctx)
        emit_consts(ctx)
        emit_load_x(ctx, io["x_in"])
        if kind != "first":
            ET = carve(ctx, OFF_ET, [2, ETW], F32)
            QC = carve(ctx, OFF_QC, [4, TOK], BF16)
            ctx.r_et = [fw.res("et") for _ in range(2)]
            ctx.r_qc = [fw.res("qc") for _ in range(4)]
            for ck in range(2):
                fw.op("sp", lambda e, ck=ck: e.dma_start(out=ET[:, ck, 8:8 + TOK], in_=io["et_in"][:, ck, :]),
                      writes=[ctx.r_et[ck]], kind="d")
            for h in range(4):
                fw.op("sp", lambda e, h=h: e.dma_start(out=QC[:, h, :], in_=io["qc_in"][:, h, :]),
                      writes=[ctx.r_qc[h]], kind="d")
            if dbg_phases is None or "pool" in dbg_phases:
                emit_pool(ctx, 0, io)
                fw.barrier()
            if dbg_phases is None or "fourier" in dbg_phases:
                emit_fourier(ctx, 0, io)
                fw.barrier()
            if dbg_phases is None or "attn" in dbg_phases:
                emit_attn(ctx, 0, io)
                fw.barrier()
            if dbg_phases is not None:
                hy_out = nc.dram_tensor("hy_out", [128, KC, TOK], BF16, kind="ExternalOutput").ap()
                for k in range(KC):
                    fw.op("sp", lambda e, k=k: e.dma_start(out=hy_out[:, k, :], in_=ctx.HY[:, k, :]),
                          reads=ctx.r_hy[k], kind="d")
                fw.barrier()
                fw.op("sp", None)
                fw.emit()
                return nc
            emit_wout(ctx, 0, io)
            fw.barrier()
            emit_ffn(ctx, io["ffn2_norm"][0], io["ffn2_w_gate"][0], io["ffn2_w_up"][0], io["ffn2_w_down"][0], OFF_DYN)
            fw.barrier()
            fw.new_epoch()
        if kind == "last":
            emit_final_norm(ctx, io["final_norm"], io["y"], OFF_DYN)
        else:
            emit_ffn(ctx, io["ffn1_norm"][0], io["ffn1_w_gate"][0], io["ffn1_w_up"][0], io["ffn1_w_down"][0], OFF_DYN)
            fw.barrier()
            emit_proj(ctx, 0, io)
            ET = carve(ctx, OFF_ET, [2, ETW], F32)
            QC = carve(ctx, OFF_QC, [4, TOK], BF16)
            for ck in range(2):
                fw.op("sp", lambda e, ck=ck: e.dma_start(out=io["et_out"][:, ck, :], in_=ET[:, ck, 8:8 + TOK]),
                      reads=[ctx.r_et[ck]], kind="d")
            for h in range(4):
                fw.op("sp", lambda e, h=h: e.dma_start(out=io["qc_out"][:, h, :], in_=QC[:, h, :]),
                      reads=[ctx.r_qc[h]], kind="d")
            fw.barrier()
            emit_store_x(ctx, io["x_out"])
        fw.barrier()
        fw.op("sp", None)
        fw.emit()
        nc._fw_stats = (len(fw.ops), fw.n_waits, dict(fw.count_log), max(dma_v for dma_v in [0]))
    return nc


def _bf(a):
    return np.asarray(a, dtype=np.float32).astype(ml_dtypes.bfloat16)


def make_tables(r):
    t = {}
    pcorr = np.ones((128, 2, 16), np.float32)
    for ck in range(2):
        for p in range(128):
            w = POOL_W[2 * ck + p // 64]
            left = w // 2
            right = w - 1 - left
            for i in range(8):
                if r == 0:
                    tt = i
                    cnt = min(tt + right + 1, SEQ) - max(tt - left, 0)
                    pcorr[p, ck, i] = w / cnt
                if r == 3:
                    tt = SEQ - 8 + i
                    cnt = min(tt + right + 1, SEQ) - max(tt - left, 0)
                    pcorr[p, ck, 8 + i] = w / cnt
    t["pcorr"] = pcorr
    pmask = np.zeros((128, 8), np.float32)
    if r - 1 >= 0:
        pmask[:, r - 1] = 1.0
    if r + 1 <= 3:
        pmask[:, 4 + r + 1] = 1.0
    t["pmask"] = pmask
    s1 = np.arange(64)[:, None]
    k1 = np.arange(64)[None, :]
    ang = 2 * np.pi * ((s1 * k1) % 64) / 64.0
    t["f_wa"] = _bf(np.concatenate([np.cos(ang), -np.sin(ang)], axis=1))
    s2 = np.arange(128)[:, None]
    ang = 2 * np.pi * ((s2 * k1) % SEQ) / float(SEQ)
    t["f_tr"] = (np.cos(ang) * FNORM).astype(np.float32)
    t["f_ti"] = (-np.sin(ang) * FNORM).astype(np.float32)
    del t["f_tr"], t["f_ti"]
    s2c = np.arange(128, dtype=np.int64)[:, None, None]
    k1c = np.arange(64, dtype=np.int64)[None, :, None]
    k2c = (32 * r + np.arange(32, dtype=np.int64))[None, None, :]
    ph = 2 * np.pi * (((k1c * s2c) + 64 * (k2c * s2c)) % SEQ) / float(SEQ)
    wr_ = np.cos(ph) * FNORM
    wi_ = -np.sin(ph) * FNORM
    t["f_w3"] = _bf(np.stack([wr_, wi_, -wi_], axis=2))
    c = np.arange(64)[:, None]
    cp = np.arange(64)[None, :]
    ang = 2 * np.pi * ((c * cp) % 64) / 64.0
    t["f_c64"] = _bf(np.stack([np.cos(ang), np.sin(ang)], axis=1))
    p = np.arange(128)
    dt = np.zeros((128, 4, 128), np.float32)
    for h in range(4):
        dt[:, h, :] = -SLOPES[h] * np.abs(p[:, None] - p[None, :])
    t["dtile"] = _bf(dt)
    t["ident"] = _bf(np.eye(128))
    L = np.arange(64)
    n = (16 * r + L) % 64
    sig = np.where(L < 16, 1.0, np.where(n < 16 * r, 1.0, -1.0))
    ncol = np.repeat(n, 128).astype(np.float64)
    sigc = np.repeat(sig, 128)
    pcol = np.tile(p, 64).astype(np.float64)
    kaug0 = np.zeros((4, 9, SEQ), np.float32)
    kaug1 = np.zeros((4, 64, SEQ), np.float32)
    qaug0 = np.zeros((4, 9, TOK), np.float32)
    qaug1 = np.zeros((4, 64, TOK), np.float32)
    tq = 2048 * r + np.arange(TOK)
    nq = (tq // 256).astype(np.float64)
    bq = (tq % 256).astype(np.float64)
    for h in range(4):
        m = SLOPES[h]
        A = np.stack([sigc * m * 128.0 * ncol, sigc * m * pcol, sigc, sigc])
        B = np.stack([np.ones(TOK), np.ones(TOK), -m * 256.0 * nq, -m * bq])
        kaug0[h, 0] = 1.0
        kaug0[h, 1:5] = A
        kaug0[h, 5:9] = A
        kaug1[h, 0:4] = A
        kaug1[h, 32:36] = A
        qaug0[h, 0] = 0.0
        qaug0[h, 1:5] = B
        qaug0[h, 5:9] = -2.0 * B
        qaug1[h, 32:36] = B
        qaug1[h, 0:4] = -2.0 * B
    for nm, a in (("kaug0", kaug0), ("qaug0", qaug0)):
        b = _bf(a)
        assert np.array_equal(b.astype(np.float32), a), nm
        t[nm] = b
    return t


def _layer_small(inputs, l):
    f32 = np.float32
    d = {}
    d["pool_scale_t"] = np.ascontiguousarray(np.asarray(inputs["pool_scale"][l], f32).reshape(2, 128).T)
    d["head_norm_t"] = np.ascontiguousarray(np.asarray(inputs["attn_head_norm"][l], f32).reshape(4, 128).T)
    d["lamvec"] = np.concatenate([np.asarray(inputs[k][l], f32) for k in ("lam_q1", "lam_k1", "lam_q2", "lam_k2")])
    li = lambda_init_fn(l)
    d["laminit"] = np.tile(np.array([[-li, 1.0 - li]], f32), (128, 1))
    return d


def _run(nc, in_maps):
    res = run_bass_kernel_spmd(nc, in_maps, core_ids=list(range(NCORES)))
    return res.results


class _Lay:
    def __init__(self, ap):
        self.ap = ap

    def __getitem__(self, l):
        return self.ap[l]


FUSED_W = {
    "ffn1_norm": [DEPTH, D_MODEL], "ffn1_w_gate": [DEPTH, D_MODEL, D_FF], "ffn1_w_up": [DEPTH, D_MODEL, D_FF],
    "ffn1_w_down": [DEPTH, D_FF, D_MODEL], "mix_norm": [DEPTH, D_MODEL], "w_in": [DEPTH, D_MODEL, 2048],
    "pool_w": [DEPTH, 4, 64, 64], "fourier_w": [DEPTH, 256, 256], "w_out": [DEPTH, D_MODEL, D_MODEL],
    "ffn2_norm": [DEPTH, D_MODEL], "ffn2_w_gate": [DEPTH, D_MODEL, D_FF], "ffn2_w_up": [DEPTH, D_MODEL, D_FF],
    "ffn2_w_down": [DEPTH, D_FF, D_MODEL],
    "pool_scale_t": [DEPTH, 128, 2], "head_norm_t": [DEPTH, 128, 4], "lamvec": [DEPTH, 256], "laminit": [DEPTH, 128, 2],
}
GROUPS = [[0, 1, 2, 3], [4, 5, 6, 7]]


def build_fused(depth=DEPTH):
    nc = bass.Bass("TRN2", target_bir_lowering=False)
    io = {}

    def inp(name, shp, dt=F32):
        return nc.dram_tensor(name, shp, dt, kind="ExternalInput").ap()

    io["x_in"] = inp("x_in", [D_MODEL, TOK])
    for n, shp in FUSED_W.items():
        io[n] = _Lay(inp(n, shp))
    io["final_norm"] = inp("final_norm", [D_MODEL])
    for n, (shp, dt) in TABLE_SPECS.items():
        io[n] = inp(n, shp, dt)
    io["y"] = nc.dram_tensor("y", [D_MODEL, TOK], F32, kind="ExternalOutput").ap()
    pay_kv = [nc.dram_tensor(f"pay_kv{h}", [256, TOK], BF16) for h in range(4)]
    kvg = [nc.dram_tensor(f"kvg{h}", [4 * 256, TOK], BF16) for h in range(4)]
    pay_f = nc.dram_tensor("pay_f", [4 * TOK, 64], BF16)
    fgat = nc.dram_tensor("fgat", [4 * 4 * TOK, 64], BF16)
    pay_h = nc.dram_tensor("pay_h", [256, 16], F32)
    hgat = nc.dram_tensor("hgat", [4 * 256, 16], F32)
    io["kpay"] = [pay_kv[h].ap()[0:128, :] for h in range(4)]
    io["vpay"] = [pay_kv[h].ap()[128:256, :].rearrange("p (t e) -> p t e", e=128) for h in range(4)]
    io["fpay"] = [pay_f.ap()[g * TOK:(g + 1) * TOK, :] for g in range(4)]
    io["hpay"] = pay_h.ap().rearrange("(k p) j -> k p j", p=128)
    io["halo_all"] = hgat.ap().rearrange("(r k p) j -> k p r j", r=4, k=2)
    with ExitStack() as stack:
        ctx = Ctx()
        ctx.nc = nc
        fw = ctx.fw = FW(nc, stack)
        setup_memory(nc, stack, ctx)
        rp = {"f": fw.res("pay_f"), "halo": fw.res("pay_h")}
        rg = {"f": fw.res("fgat"), "halo": fw.res("hgat")}
        for h in range(4):
            rp[("kv", h)] = fw.res("pay_kv")
            rg[("kv", h)] = fw.res("kvg")
        io["r_pay"] = rp
        io["r_gath"] = rg

        def cc(src, dst, key, extra=()):
            fw.op("pool", lambda e: e.collective_compute("AllGather", ALU.bypass, replica_groups=GROUPS,
                                                         ins=[src.ap().opt()], outs=[dst.ap().opt()]),
                  reads=[rp[key]] + list(extra), writes=[rg[key]], kind="cc")

        HEAD_ORDER = [3, 2, 1, 0]

        def after_f():
            cc(pay_h, hgat, "halo")
            emit_pool_halo_load(ctx, io)

        def after_kv():
            h0 = HEAD_ORDER[0]
            cc(pay_kv[h0], kvg[h0], ("kv", h0))

        def after_first_loads(r_k, r_v):
            for h in HEAD_ORDER[1:]:
                cc(pay_kv[h], kvg[h], ("kv", h), extra=list(r_k) + list(r_v))
            cc(pay_f, fgat, "f", extra=list(r_k) + list(r_v))

        def load_xs(g, XS, r_xs):
            fv = fgat.ap()
            for j in range(4):
                base = j * 4 * TOK + g * TOK
                fw.op("sp", lambda e, j=j, base=base: e.dma_start(
                    out=XS[16 * j:16 * (j + 1)], in_=fv[base:base + TOK, :].rearrange("(s1 s2) c -> s1 s2 c", s2=128)),
                    reads=[rg["f"]], writes=[r_xs[j]], kind="d")

        def load_kv(h, K0, K1, V, r_k, r_v):
            kv = kvg[h].ap()
            for i in range(4):
                def mk(i=i, part=0):
                    def f(e):
                        rank = (ctx.pid + i) % 4
                        if part == 0:
                            return e.dma_start(out=K0[0:64, i * TOK:(i + 1) * TOK], in_=kv[bass.ds(rank * 256, 64), :])
                        if part == 1:
                            return e.dma_start(out=K1[0:64, i * TOK:(i + 1) * TOK], in_=kv[bass.ds(rank * 256 + 64, 64), :])
                        return e.dma_start(out=V[:, 16 * i:16 * (i + 1), :],
                                           in_=kv[bass.ds(rank * 256 + 128, 128), :].rearrange("p (t e) -> p t e", e=128))
                    return f
                fw.op("sp", mk(i, 0), reads=[rg[("kv", h)]], writes=[r_k[i]], kind="d")
                fw.op("sp", mk(i, 1), reads=[rg[("kv", h)]], writes=[r_k[4 + i]], kind="d")
                fw.op("sp", mk(i, 2), reads=[rg[("kv", h)]], writes=[r_v[i]], kind="d")

        io["load_xs"] = load_xs
        io["load_kv"] = load_kv
        io["head_order"] = HEAD_ORDER
        io["after_first_loads"] = after_first_loads

        def _pro(e):
            ctx.pid = nc.partition_id([mybir.EngineType.SP])
        fw.sp_prologue = _pro
        io["fg"] = None
        emit_consts(ctx)
        emit_load_x(ctx, io["x_in"])
        for l in range(depth):
            emit_ffn(ctx, io["ffn1_norm"][l], io["ffn1_w_gate"][l], io["ffn1_w_up"][l], io["ffn1_w_down"][l], OFF_DYN)
            fw.barrier()
            emit_pool_loads(ctx, l, io)
            emit_proj(ctx, l, io, after_f=after_f, after_kv=after_kv)
            fw.barrier()
            fw.new_epoch()
            emit_pool(ctx, l, io, preloaded=True)
            fw.barrier()
            emit_attn(ctx, l, io)
            fw.barrier()
            emit_fourier(ctx, l, io)
            fw.barrier()
            emit_wout(ctx, l, io)
            fw.barrier()
            emit_ffn(ctx, io["ffn2_norm"][l], io["ffn2_w_gate"][l], io["ffn2_w_up"][l], io["ffn2_w_down"][l], OFF_DYN)
            fw.barrier()
            fw.new_epoch()
        emit_final_norm(ctx, io["final_norm"], io["y"], OFF_DYN)
        fw.barrier()
        fw.op("sp", None)
        fw.emit()
        nc._fw_stats = (len(fw.ops), fw.n_waits, dict(fw.count_log))
    return nc


def fused_inputs(inputs, depth=DEPTH):
    f32 = np.float32
    x = np.asarray(inputs["x"], f32)
    shared = {}
    for n in FUSED_W:
        if n in inputs:
            shared[n] = np.ascontiguousarray(np.asarray(inputs[n], f32))
    sm = [_layer_small(inputs, l) for l in range(DEPTH)]
    for n in ("pool_scale_t", "head_norm_t", "lamvec", "laminit"):
        shared[n] = np.ascontiguousarray(np.stack([sm[l][n] for l in range(DEPTH)]).astype(f32))
    shared["final_norm"] = np.asarray(inputs["final_norm"], f32)
    tables = [make_tables(r) for r in range(4)]
    in_maps = []
    for c in range(NCORES):
        b, r = c // 4, c % 4
        d = {"x_in": np.ascontiguousarray(x[b, r * TOK:(r + 1) * TOK, :].T)}
        d.update(shared)
        d.update(tables[r])
        in_maps.append(d)
    return in_maps


def kernel(**inputs):
    in_maps = fused_inputs(inputs)
    outs = _run(_prog("fused"), in_maps)
    out = np.empty((BATCH, SEQ, D_MODEL), np.float32)
    for c in range(NCORES):
        b, r = c // 4, c % 4
        out[b, r * TOK:(r + 1) * TOK, :] = np.asarray(outs[c]["y"], np.float32).T
    return out


_PROGS = {}


def _prog(kind):
    if kind not in _PROGS:
        _PROGS[kind] = build_fused() if kind == "fused" else build_launch(kind)
    return _PROGS[kind]


def kernel_unfused(**inputs):
    f32 = np.float32
    x = np.asarray(inputs["x"], f32)
    tables = [make_tables(r) for r in range(4)]

    def wA(l):
        d = {n: np.ascontiguousarray(np.asarray(inputs[n][l], f32)) for n in BIG_A}
        d["ffn2_norm"] = np.asarray(inputs["ffn2_norm"][l], f32)
        d["pool_w"] = np.asarray(inputs["pool_w"][l], f32)
        d["fourier_w"] = np.asarray(inputs["fourier_w"][l], f32)
        d.update(_layer_small(inputs, l))
        return d

    def wB(l):
        d = {n: np.ascontiguousarray(np.asarray(inputs[n][l], f32)) for n in BIG_B}
        d["ffn1_norm"] = np.asarray(inputs["ffn1_norm"][l], f32)
        d["mix_norm"] = np.asarray(inputs["mix_norm"][l], f32)
        return d

    def gathered(outs):
        g = []
        for c in range(NCORES):
            b, r = c // 4, c % 4
            grp = [outs[4 * b + j] for j in range(4)]
            rot = [grp[(r + i) % 4] for i in range(4)]
            d = {}
            d["kg"] = np.ascontiguousarray(np.concatenate([o["kpay"] for o in rot], axis=2))
            d["vg"] = np.ascontiguousarray(np.concatenate([o["vpay"] for o in rot], axis=2))
            d["fg"] = np.ascontiguousarray(np.concatenate([o["fpay"] for o in grp], axis=1))
            d["halo_all"] = np.ascontiguousarray(np.stack([o["hpay"] for o in grp], axis=2))
            d["qc_in"] = outs[c]["qc_out"]
            d["et_in"] = outs[c]["et_out"]
            d["x_in"] = outs[c]["x_out"]
            g.append(d)
        return g

    b0 = wB(0)
    in_maps = []
    for c in range(NCORES):
        b, r = c // 4, c % 4
        d = {"x_in": np.ascontiguousarray(x[b, r * TOK:(r + 1) * TOK, :].T)}
        d.update(b0)
        in_maps.append(d)
    outs = _run(_prog("first"), in_maps)
    for l in range(1, DEPTH):
        g = gathered(outs)
        a, bb = wA(l - 1), wB(l)
        in_maps = []
        for c in range(NCORES):
            d = dict(g[c])
            d.update(a)
            d.update(bb)
            d.update(tables[c % 4])
            in_maps.append(d)
        outs = _run(_prog("mid"), in_maps)
    g = gathered(outs)
    a = wA(DEPTH - 1)
    in_maps = []
    for c in range(NCORES):
        d = dict(g[c])
        d.update(a)
        d.update(tables[c % 4])
        d["final_norm"] = np.asarray(inputs["final_norm"], f32)
        in_maps.append(d)
    outs = _run(_prog("last"), in_maps)
    out = np.empty((BATCH, SEQ, D_MODEL), f32)
    for c in range(NCORES):
        b, r = c // 4, c % 4
        out[b, r * TOK:(r + 1) * TOK, :] = np.asarray(outs[c]["y"], f32).T
    return out
```
